# Optimizing a Trainium2 kernel written in Bass

```python
import math
import jax, jax.numpy as jnp
from jax import lax
import numpy as np

D_MODEL = 1024
BATCH = 8
SEQ = 4096
DEPTH = 4

CHUNK = 64
Q_BLOCK = 128
N_EVEN = (DEPTH + 1) // 2
N_ODD = DEPTH // 2
DIFF_HEADS = 4
DIFF_HEAD_DIM = 64
DIFF_V_DIM = 2 * DIFF_HEAD_DIM
DIFF_WIDTH = DIFF_HEADS * DIFF_V_DIM
LRU_WIDTH = 512
LRU_BLOCKS = 8
LRU_BLOCK_DIM = LRU_WIDTH // LRU_BLOCKS
LRU_C = 8.0
CONV_WIDTH = 4
AB_IN = 3 * DIFF_WIDTH + 2 * LRU_WIDTH
AB_OUT = DIFF_WIDTH + LRU_WIDTH
SGU_WIDTH = D_MODEL
SGU_GROUPS = 8
SGU_GROUP_DIM = SGU_WIDTH // SGU_GROUPS
SGU_BLOCK = 128
D_FF = 4 * D_MODEL
EPS = 1e-6

kernel_name = 'hybrid_diffattn_rglru_gmlp_trunk'


def rms_norm(x, g):
    xf = x.astype(jnp.float32)
    y = xf * lax.rsqrt(jnp.mean(xf * xf, axis=-1, keepdims=True) + EPS)
    return (y * g.astype(jnp.float32)).astype(x.dtype)


def layer_norm(x, g, b):
    xf = x.astype(jnp.float32)
    mu = jnp.mean(xf, axis=-1, keepdims=True)
    var = jnp.mean(jnp.square(xf - mu), axis=-1, keepdims=True)
    y = (xf - mu) * lax.rsqrt(var + EPS)
    return (y * g.astype(jnp.float32) + b.astype(jnp.float32)).astype(x.dtype)


def diff_attention(q, k, v, lam, lam_init, subln_g):
    b, s, _ = q.shape
    nb = s // Q_BLOCK
    qb = q.reshape(b, nb, Q_BLOCK, DIFF_HEADS, 2, DIFF_HEAD_DIM).swapaxes(0, 1)
    k = k.reshape(b, s, DIFF_HEADS, 2, DIFF_HEAD_DIM)
    v = v.reshape(b, s, DIFF_HEADS, DIFF_V_DIM)
    scale = DIFF_HEAD_DIM ** -0.5
    k_chunk = jnp.arange(s) // CHUNK

    def block(args):
        q_blk, i = args
        scores = jnp.einsum('bqhmd,bkhmd->bhmqk', q_blk, k).astype(jnp.float32) * scale
        q_chunk = (i * Q_BLOCK + jnp.arange(Q_BLOCK)) // CHUNK
        mask = k_chunk[None, :] <= q_chunk[:, None]
        probs = jax.nn.softmax(jnp.where(mask, scores, -jnp.inf), axis=-1)
        w = probs[:, :, 0] - lam * probs[:, :, 1]
        return jnp.einsum('bhqk,bkhv->bqhv', w.astype(v.dtype), v)

    out = lax.map(block, (qb, jnp.arange(nb)))
    out = out.swapaxes(0, 1).reshape(b, s, DIFF_HEADS, DIFF_V_DIM)
    out = rms_norm(out, subln_g) * (1.0 - lam_init)
    return out.reshape(b, s, DIFF_WIDTH)


def rg_lru_branch(xb, gate, conv_w, conv_b, wa, ba, wx, bx, lam):
    c = xb.shape[-1]
    xc = lax.conv_general_dilated(
        xb, conv_w[:, None, :].astype(xb.dtype), window_strides=(1,),
        padding=[(CONV_WIDTH - 1, 0)], dimension_numbers=('NWC', 'WIO', 'NWC'),
        feature_group_count=c) + conv_b
    b, s, _ = xc.shape
    xh = xc.reshape(b, s, LRU_BLOCKS, LRU_BLOCK_DIM)
    r = jax.nn.sigmoid(jnp.einsum('bshi,hij->bshj', xh, wa).reshape(b, s, c) + ba)
    i = jax.nn.sigmoid(jnp.einsum('bshi,hij->bshj', xh, wx).reshape(b, s, c) + bx)
    log_a = -LRU_C * r.astype(jnp.float32) * jax.nn.softplus(-lam.astype(jnp.float32))
    a = jnp.exp(log_a)
    bt = jnp.sqrt(-jnp.expm1(2.0 * log_a)) * (i * xc).astype(jnp.float32)

    def combine(left, right):
        a1, b1 = left
        a2, b2 = right
        return a1 * a2, a2 * b1 + b2

    _, h = lax.associative_scan(combine, (a, bt), axis=1)
    return jax.nn.gelu(gate) * h.astype(xb.dtype)


def spatial_gating(z, ln_g, ln_b, w_s, b_s):
    u, v = jnp.split(z, 2, axis=-1)
    v = layer_norm(v, ln_g, ln_b)
    b, s, _ = v.shape
    vb = v.reshape(b, s // SGU_BLOCK, SGU_BLOCK, SGU_GROUPS, SGU_GROUP_DIM)
    pos_chunk = jnp.arange(SGU_BLOCK) // CHUNK
    mask = pos_chunk[None, :] <= pos_chunk[:, None]
    w = jnp.where(mask[None], w_s, 0)
    mixed = jnp.einsum('gpq,bnqgc->bnpgc', w, vb) + b_s.T[None, None, :, :, None]
    return u * mixed.reshape(b, s, SGU_WIDTH)


def setup_inputs(seed: int = 0) -> dict:
    key = jax.random.key(seed)
    ks = iter(jax.random.split(key, 40))

    def nrm(shape, scale):
        return jax.random.normal(next(ks), shape, jnp.float32) * scale

    x = nrm((BATCH, SEQ, D_MODEL), 1.0)
    norm_mix = 1.0 + nrm((DEPTH, D_MODEL), 0.02)
    norm_ffn = 1.0 + nrm((DEPTH, D_MODEL), 0.02)
    norm_final = 1.0 + nrm((D_MODEL,), 0.02)
    ab_w_in = nrm((N_EVEN, D_MODEL, AB_IN), D_MODEL ** -0.5)
    diff_lq1 = nrm((N_EVEN, DIFF_HEAD_DIM), 0.1)
    diff_lk1 = nrm((N_EVEN, DIFF_HEAD_DIM), 0.1)
    diff_lq2 = nrm((N_EVEN, DIFF_HEAD_DIM), 0.1)
    diff_lk2 = nrm((N_EVEN, DIFF_HEAD_DIM), 0.1)
    diff_subln = 1.0 + nrm((N_EVEN, DIFF_V_DIM), 0.02)
    lru_conv_w = nrm((N_EVEN, CONV_WIDTH, LRU_WIDTH), CONV_WIDTH ** -0.5)
    lru_conv_b = nrm((N_EVEN, LRU_WIDTH), 0.01)
    lru_wa = nrm((N_EVEN, LRU_BLOCKS, LRU_BLOCK_DIM, LRU_BLOCK_DIM), LRU_BLOCK_DIM ** -0.5)
    lru_ba = nrm((N_EVEN, LRU_WIDTH), 0.01)
    lru_wx = nrm((N_EVEN, LRU_BLOCKS, LRU_BLOCK_DIM, LRU_BLOCK_DIM), LRU_BLOCK_DIM ** -0.5)
    lru_bx = nrm((N_EVEN, LRU_WIDTH), 0.01)
    a0 = jax.random.uniform(next(ks), (N_EVEN, LRU_WIDTH), jnp.float32, 0.9, 0.999)
    s0 = a0 ** (1.0 / LRU_C)
    lru_lambda = jnp.log(s0) - jnp.log1p(-s0)
    ab_w_out = nrm((N_EVEN, AB_OUT, D_MODEL), AB_OUT ** -0.5)
    c_w_in = nrm((N_ODD, D_MODEL, 2 * SGU_WIDTH), D_MODEL ** -0.5)
    c_ln_g = 1.0 + nrm((N_ODD, SGU_WIDTH), 0.02)
    c_ln_b = nrm((N_ODD, SGU_WIDTH), 0.01)
    c_w_s = nrm((N_ODD, SGU_GROUPS, SGU_BLOCK, SGU_BLOCK), SGU_BLOCK ** -0.5)
    c_b_s = 1.0 + nrm((N_ODD, SGU_GROUPS, SGU_BLOCK), 0.01)
    c_w_out = nrm((N_ODD, SGU_WIDTH, D_MODEL), SGU_WIDTH ** -0.5)
    ffn_w1 = nrm((DEPTH, D_MODEL, D_FF), D_MODEL ** -0.5)
    ffn_w2 = nrm((DEPTH, D_FF, D_MODEL), D_FF ** -0.5)
    return {'x': x, 'norm_mix': norm_mix, 'norm_ffn': norm_ffn, 'norm_final': norm_final,
            'ab_w_in': ab_w_in, 'diff_lq1': diff_lq1, 'diff_lk1': diff_lk1,
            'diff_lq2': diff_lq2, 'diff_lk2': diff_lk2, 'diff_subln': diff_subln,
            'lru_conv_w': lru_conv_w, 'lru_conv_b': lru_conv_b, 'lru_wa': lru_wa,
            'lru_ba': lru_ba, 'lru_wx': lru_wx, 'lru_bx': lru_bx, 'lru_lambda': lru_lambda,
            'ab_w_out': ab_w_out, 'c_w_in': c_w_in, 'c_ln_g': c_ln_g, 'c_ln_b': c_ln_b,
            'c_w_s': c_w_s, 'c_b_s': c_b_s, 'c_w_out': c_w_out,
            'ffn_w1': ffn_w1, 'ffn_w2': ffn_w2}


def reference(x, norm_mix, norm_ffn, norm_final, ab_w_in, diff_lq1, diff_lk1,
              diff_lq2, diff_lk2, diff_subln, lru_conv_w, lru_conv_b, lru_wa,
              lru_ba, lru_wx, lru_bx, lru_lambda, ab_w_out, c_w_in, c_ln_g, c_ln_b,
              c_w_s, c_b_s, c_w_out, ffn_w1, ffn_w2):
    splits = [DIFF_WIDTH, 2 * DIFF_WIDTH, 3 * DIFF_WIDTH, 3 * DIFF_WIDTH + LRU_WIDTH]
    for layer in range(DEPTH):
        h = rms_norm(x, norm_mix[layer])
        if layer % 2 == 0:
            e = layer // 2
            proj = h @ ab_w_in[e]
            q, k, v, xb, gate = jnp.split(proj, splits, axis=-1)
            lam_init = 0.8 - 0.6 * math.exp(-0.3 * layer)
            lam = (jnp.exp(jnp.sum(diff_lq1[e].astype(jnp.float32) * diff_lk1[e].astype(jnp.float32)))
                   - jnp.exp(jnp.sum(diff_lq2[e].astype(jnp.float32) * diff_lk2[e].astype(jnp.float32)))
                   + lam_init)
            ya = diff_attention(q, k, v, lam, lam_init, diff_subln[e])
            yb = rg_lru_branch(xb, gate, lru_conv_w[e], lru_conv_b[e], lru_wa[e], lru_ba[e],
                               lru_wx[e], lru_bx[e], lru_lambda[e])
            mix = jnp.concatenate([ya, yb], axis=-1) @ ab_w_out[e]
        else:
            o = layer // 2
            z = jax.nn.gelu(h @ c_w_in[o])
            mix = spatial_gating(z, c_ln_g[o], c_ln_b[o], c_w_s[o], c_b_s[o]) @ c_w_out[o]
        x = x + mix
        h = rms_norm(x, norm_ffn[layer])
        x = x + jnp.square(jax.nn.relu(h @ ffn_w1[layer])) @ ffn_w2[layer]
    return rms_norm(x, norm_final)
```

```python
import contextlib
import math

import numpy as np
import concourse.bass as bass
import concourse.mybir as mybir
from concourse.bass_utils import run_bass_kernel_spmd

F32 = mybir.dt.float32
BF16 = mybir.dt.bfloat16
U8 = mybir.dt.uint8
AF = mybir.ActivationFunctionType
ALU = mybir.AluOpType

D = 1024
TB = 256
EPS = 1e-6
ENGS = ('pe', 'act', 'dve', 'pool', 'sp')
SEM_CAP = 30000
DBG = {'ffn': 9, 'noW': 0}


def A(*a, **k):
    return (a, k)


class Op:
    __slots__ = ('eng', 'fn', 'deps', 'is_dma', 'ndma', 'slot', 'target', 'epoch', 'msval')


class Sched:
    def __init__(self, nslots=8):
        self.ops = {e: [] for e in ENGS}
        self.lastw = {}
        self.readers = {}
        self.nslots = nslots
        self.dma_slots = {}
        self.dma_count = {e: 0 for e in ENGS}
        self.fence_deps = set()

    def add(self, eng, fn, reads=(), writes=(), dma=0):
        op = Op()
        op.eng = eng
        op.fn = fn
        op.is_dma = dma > 0
        op.ndma = dma
        op.slot = None
        op.target = None
        op.epoch = None
        op.msval = None
        deps = set(self.fence_deps)
        for k in reads:
            w = self.lastw.get(k)
            if w is not None:
                deps.add(w)
        for k in writes:
            w = self.lastw.get(k)
            if w is not None:
                deps.add(w)
            for r in self.readers.get(k, ()):
                deps.add(r)
        if dma:
            q = self.dma_slots.setdefault(eng, [None] * self.nslots)
            s = self.dma_count[eng] % self.nslots
            self.dma_count[eng] += 1
            if q[s] is not None:
                deps.add(q[s])
            q[s] = op
            op.slot = (eng, s)
        deps.discard(op)
        op.deps = deps
        for k in reads:
            self.readers.setdefault(k, []).append(op)
        for k in writes:
            self.lastw[k] = op
            self.readers[k] = []
        self.ops[eng].append(op)
        return op

    def fence(self):
        deps = set()
        for e in ENGS:
            last = None
            for op in reversed(self.ops[e]):
                if not op.is_dma:
                    last = op
                    break
            if last is not None:
                deps.add(last)
        for q in self.dma_slots.values():
            for op in q:
                if op is not None:
                    deps.add(op)
        self.fence_deps = deps
        self.lastw = {}
        self.readers = {}

    def emit(self, nc, final_wait_ops=()):
        needed = set()
        for e in ENGS:
            for op in self.ops[e]:
                for d in op.deps:
                    if e == 'pe' and d.eng == 'pe' and not d.is_dma:
                        continue
                    needed.add(d)
        for d in final_wait_ops:
            needed.add(d)
        n_epochs = {}
        for e in ENGS:
            c = 0
            ep = 0
            for op in self.ops[e]:
                if (not op.is_dma) and op in needed:
                    if c >= SEM_CAP:
                        ep += 1
                        c = 0
                    c += 1
                    op.epoch = ep
                    op.msval = c
            n_epochs[e] = ep + 1
        slot_cnt = {}
        for e in ENGS:
            for op in self.ops[e]:
                if op.is_dma:
                    slot_cnt[op.slot] = slot_cnt.get(op.slot, 0) + 16 * op.ndma
                    op.target = slot_cnt[op.slot]
        with contextlib.ExitStack() as st:
            esem = {}
            for e in ENGS:
                for ep in range(n_epochs[e]):
                    esem[(e, ep)] = st.enter_context(nc.semaphore(f"s_{e}_{ep}"))
            dsem = {}
            for slot in slot_cnt:
                dsem[slot] = st.enter_context(nc.semaphore(f"d_{slot[0]}_{slot[1]}"))
            block = st.enter_context(nc.Block())
            ops = self.ops

            def run(e, eng):
                seen_e = {}
                seen_d = {}
                for op in ops[e]:
                    self._waits(e, eng, op.deps, seen_e, seen_d, esem, dsem)
                    name, (pa, kw) = op.fn
                    if op.is_dma:
                        getattr(eng, name)(*pa, **kw).then_inc(dsem[op.slot], 16)
                    else:
                        ins = getattr(eng, name)(*pa, **kw)
                        if op.msval is not None:
                            ins.then_inc(esem[(e, op.epoch)], 1)
                if e == 'sp' and final_wait_ops:
                    self._waits(e, eng, final_wait_ops, seen_e, seen_d, esem, dsem)

            @block.tensor
            def _(eng):
                run('pe', eng)

            @block.scalar
            def _(eng):
                run('act', eng)

            @block.vector
            def _(eng):
                run('dve', eng)

            @block.gpsimd
            def _(eng):
                run('pool', eng)

            @block.sync
            def _(eng):
                run('sp', eng)

    @staticmethod
    def _waits(e, eng, deps, seen_e, seen_d, esem, dsem):
        best_e = {}
        best_d = {}
        for d in deps:
            if d.is_dma:
                if best_d.get(d.slot, 0) < d.target:
                    best_d[d.slot] = d.target
            else:
                if e == 'pe' and d.eng == 'pe':
                    continue
                v = (d.epoch, d.msval)
                if best_e.get(d.eng, (-1, 0)) < v:
                    best_e[d.eng] = v
        for slot, t in best_d.items():
            if seen_d.get(slot, 0) >= t:
                continue
            seen_d[slot] = t
            eng.wait_ge(dsem[slot], t)
        for de, v in best_e.items():
            if seen_e.get(de, (-1, 0)) >= v:
                continue
            seen_e[de] = v
            eng.wait_ge(esem[(de, v[0])], v[1])


class Region:
    def __init__(self, raw, nbytes):
        self.raw = raw
        self.nbytes = nbytes
        self.off = 0

    def reset(self):
        self.off = 0

    def alloc(self, cols, dt):
        esz = 4 if dt == F32 else 2
        nb = cols * esz
        nb_al = (nb + 63) // 64 * 64
        assert self.off + nb_al <= self.nbytes, (self.off, nb_al, self.nbytes)
        v = self.raw[:, self.off:self.off + nb].bitcast(dt)
        self.off += nb_al
        return v


def build_program(S_LEN=4096, layers=(0, 1, 2, 3), final_norm=True, passes=None):
    nblk = S_LEN // TB
    nc = bass.Bass("TRN2", target_bir_lowering=False)

    def din(name, shape):
        return nc.dram_tensor(name, list(shape), F32, kind="ExternalInput").ap()

    x_in = din("x", [S_LEN, D])
    norm_mix = din("norm_mix", [4, D])
    norm_ffn = din("norm_ffn", [4, D])
    norm_final = din("norm_final", [D])
    ab_w_in = din("ab_w_in", [2, D, 2560])
    ab_w_out = din("ab_w_out", [2, D, D])
    diff_l = din("diff_l", [2, 256])
    pvec = din("pvec", [2, 128, 33])
    lru_wa = din("lru_wa", [2, 8, 64, 64])
    lru_wx = din("lru_wx", [2, 8, 64, 64])
    c_w_in = din("c_w_in", [2, D, 2048])
    c_ln_g = din("c_ln_g", [2, D])
    c_ln_b = din("c_ln_b", [2, D])
    c_w_sT = din("c_w_sT", [2, 128, 8, 128])
    c_b_sT = din("c_b_sT", [2, 128, 8])
    c_w_out = din("c_w_out", [2, D, D])
    ffn_w1 = din("ffn_w1", [4, D, 4096])
    ffn_w2 = din("ffn_w2", [4, 4096, D])
    out = nc.dram_tensor("out", [S_LEN, D], F32, kind="ExternalOutput").ap()
    scrA = nc.dram_tensor("scrA", [S_LEN, D], F32, kind="Internal").ap()
    scrB = nc.dram_tensor("scrB", [S_LEN, D], F32, kind="Internal").ap()

    S = Sched()
    st = contextlib.ExitStack()
    with st:
        def sb(name, shape, dt):
            return st.enter_context(nc.sbuf_tensor(name, shape, dt))

        WBYTES = DBG.get('WBYTES', 126976)
        ABYTES = 77824
        wraw = sb("wraw", [128, WBYTES], U8)
        araw = sb("araw", [128, ABYTES], U8)
        WR = Region(wraw, WBYTES)
        AR = Region(araw, ABYTES)
        ident = sb("ident", [128, 128], BF16)
        identf = sb("identf", [128, 128], F32)
        ones_bf = sb("ones_bf", [128, 128], BF16)
        ones_f = sb("ones_f", [128, 128], F32)
        neghalf = sb("neghalf", [128, 1], F32)
        poshalf = sb("poshalf", [128, 1], F32)
        expbias = sb("expbias", [128, 1], F32)
        banks = [st.enter_context(nc.psum_tensor(f"bank{i}", [128, 512], F32)) for i in range(8)]

        def bank_bf(i):
            return banks[i][:, :].bitcast(BF16).rearrange("p (k t) -> p k t", k=8)

        S.add('pool', ('memset', A(identf[:], 1.0)), writes=['identf'])
        S.add('pool', ('affine_select', A(out=identf[:], in_=identf[:], pattern=[[-1, 128]],
                                                compare_op=ALU.is_equal, fill=0.0, base=0,
                                                channel_multiplier=1)), reads=['identf'], writes=['identf'])
        S.add('pool', ('tensor_copy', A(out=ident[:], in_=identf[:])), reads=['identf'], writes=['ident'])
        S.add('pool', ('memset', A(ones_bf[:], 1.0)), writes=['ones_bf'])
        S.add('pool', ('memset', A(ones_f[:], 1.0)), writes=['ones_f'])
        S.add('pool', ('memset', A(neghalf[:], -0.5)), writes=['neghalf'])
        S.add('pool', ('memset', A(poshalf[:], 0.5)), writes=['poshalf'])
        S.add('pool', ('memset', A(expbias[:], 0.0)), writes=['expbias'])
        S.add('pool', ('memset', A(expbias[64:128, :], -30000.0)), reads=['expbias'], writes=['expbias'])
        CONST_KEYS = ['ident', 'ones_bf', 'ones_f', 'neghalf', 'poshalf', 'expbias']

        def after_fence_consts():
            pass

        def load_w(dst3, src2, name, kc_n, cols):
            src3 = src2.rearrange("(kc p) n -> p kc n", p=128)
            cstep = cols
            while cstep > 2048:
                cstep //= 2
            kstep = max(1, 2048 // cstep)
            keys = []
            for k0 in range(0, kc_n, kstep):
                for c0 in range(0, cols, cstep):
                    key = f"{name}_{k0}_{c0}"
                    keys.append(key)
                    S.add('pool', ('dma_start', A(
                        out=dst3[:, k0:k0 + kstep, c0:c0 + cstep],
                        in_=src3[:, k0:k0 + kstep, c0:c0 + cstep])),
                        writes=[key], dma=1)
            return keys

        def join(keys, name, dummy):
            S.add('pool', ('memset', A(dummy, 0.0)), reads=keys, writes=[name])

        def load_bcast(dst, src1d, name):
            S.add('sp', ('dma_start', A(out=dst, in_=src1d.partition_broadcast(128))),
                  writes=[name], dma=1)

        def load_x(xt3, src, blk, key):
            S.add('sp', ('dma_start', A(
                out=xt3, in_=src[blk * TB:(blk + 1) * TB, :].rearrange("(s p) d -> p s d", p=128))),
                writes=[key], dma=1)

        def store_x(dst, xo3, blk, key):
            return S.add('sp', ('dma_start', A(
                out=dst[blk * TB:(blk + 1) * TB, :].rearrange("(s p) d -> p s d", p=128), in_=xo3)),
                reads=[key], dma=1)

        def rms_norm(x2, gbc, gkey, h2, xkey, hkey, tmp, tkey):
            stt = tmp[:, 0:12]
            mv = tmp[:, 12:14]
            ms = tmp[:, 14:15]
            rs = tmp[:, 15:16]
            S.add('dve', ('bn_stats', A(out=stt[:, 0:6], in_=x2[:, 0:512])), reads=[xkey], writes=[tkey + 'a'])
            S.add('dve', ('bn_stats', A(out=stt[:, 6:12], in_=x2[:, 512:1024])), reads=[xkey], writes=[tkey + 'b'])
            S.add('dve', ('bn_aggr', A(out=mv, in_=stt)), reads=[tkey + 'a', tkey + 'b'], writes=[tkey + 'mv'])
            S.add('dve', ('tensor_scalar', A(out=ms, in0=mv[:, 0:1], scalar1=mv[:, 0:1], scalar2=mv[:, 1:2],
                                                   op0=ALU.mult, op1=ALU.add)), reads=[tkey + 'mv'], writes=[tkey + 'ms'])
            S.add('dve', ('tensor_scalar', A(out=ms, in0=ms, scalar1=EPS, scalar2=None, op0=ALU.add)),
                  reads=[tkey + 'ms'], writes=[tkey + 'ms'])
            S.add('pool', ('tensor_tensor', A(out=rs, in0=ms, in1=neghalf[:, 0:1], op=ALU.pow)),
                  reads=[tkey + 'ms', 'neghalf'], writes=[tkey + 'rs'])
            S.add('dve', ('scalar_tensor_tensor', A(out=h2, in0=x2, scalar=rs, in1=gbc, op0=ALU.mult, op1=ALU.mult)),
                  reads=[xkey, tkey + 'rs', gkey], writes=[hkey])

        def transpose8(h2, hkey, hT3, col0, hTkey, bank_i, evac_eng):
            tp = bank_bf(bank_i)
            bkey = f"bank{bank_i}"
            for kc in range(8):
                S.add('pe', ('transpose', A(out=tp[:, kc, :], in_=h2[:, kc * 128:(kc + 1) * 128],
                                                         identity=ident[:])),
                      reads=[hkey, 'ident'], writes=[bkey])
            if evac_eng == 'act':
                S.add('act', ('copy', A(out=hT3[:, :, col0:col0 + 128], in_=tp)), writes=[bkey, hTkey])
            else:
                S.add('dve', ('tensor_copy', A(out=hT3[:, :, col0:col0 + 128], in_=tp)), writes=[bkey, hTkey])

        final_ops = []

        def BK(i):
            return f'bank{i}'

        def norm_stats(x2, xkeys, t, tk):
            stt = t[:, 0:12]
            mv = t[:, 12:14]
            ms = t[:, 14:15]
            rs = t[:, 15:16]
            S.add('dve', ('bn_stats', A(out=stt[:, 0:6], in_=x2[:, 0:512])), reads=xkeys, writes=[tk + 'a'])
            S.add('dve', ('bn_stats', A(out=stt[:, 6:12], in_=x2[:, 512:1024])), reads=xkeys, writes=[tk + 'b'])
            S.add('dve', ('bn_aggr', A(out=mv, in_=stt)), reads=[tk + 'a', tk + 'b'], writes=[tk + 'mv'])
            S.add('dve', ('tensor_scalar', A(out=ms, in0=mv[:, 0:1], scalar1=mv[:, 0:1], scalar2=mv[:, 1:2],
                                                   op0=ALU.mult, op1=ALU.add)), reads=[tk + 'mv'], writes=[tk + 'ms'])
            S.add('dve', ('tensor_scalar', A(out=ms, in0=ms, scalar1=EPS, scalar2=None, op0=ALU.add)),
                  reads=[tk + 'ms'], writes=[tk + 'ms'])
            S.add('pool', ('tensor_tensor', A(out=rs, in0=ms, in1=neghalf[:, 0:1], op=ALU.pow)),
                  reads=[tk + 'ms', 'neghalf'], writes=[tk + 'rs'])
            return rs

        def ffn_pass(layer, src, dst, apply_final):
            S.fence()
            WR.reset()
            AR.reset()
            W1 = WR.alloc(8 * 4096, BF16).rearrange("p (k f) -> p k f", k=8)
            W2a = WR.alloc(30 * 1024, BF16).rearrange("p (k f) -> p k f", k=30)
            W2b = AR.alloc(2 * 1024, BF16).rearrange("p (k f) -> p k f", k=2)
            W2v = [W2a[:, fc, :] if fc < 30 else W2b[:, fc - 30, :] for fc in range(32)]
            gbc = AR.alloc(1024, F32)
            gfin = AR.alloc(1024, F32) if apply_final else None
            dummy = AR.alloc(16, F32)
            xts = [AR.alloc(2048, F32).rearrange("p (s d) -> p s d", s=2) for _ in range(2)]
            xos = [AR.alloc(2048, F32).rearrange("p (s d) -> p s d", s=2) for _ in range(2)]
            hb = [AR.alloc(1024, BF16) for _ in range(2)]
            hT = [AR.alloc(8 * TB, BF16).rearrange("p (k t) -> p k t", k=8) for _ in range(2)]
            uT = AR.alloc(32 * TB, BF16).rearrange("p (k t) -> p k t", k=32)
            rt = [AR.alloc(TB, F32) for _ in range(2)]
            tmp = [AR.alloc(16, F32) for _ in range(2)]

            load_bcast(gbc, norm_ffn[layer], 'gbc')
            if apply_final:
                load_bcast(gfin, norm_final, 'gfin')
            load_x(xts[0], src, 0, 'xt0')
            k1 = load_w(W1, ffn_w1[layer], 'W1', 8, 4096)
            k2 = load_w(W2a, ffn_w2[layer][0:30 * 128, :], 'W2a', 30, 1024)
            k2 += load_w(W2b, ffn_w2[layer][30 * 128:32 * 128, :], 'W2b', 2, 1024)
            join(k1, 'W1', dummy[:, 0:4])
            join(k2, 'W2', dummy[:, 4:8])

            for b in range(nblk):
                p = b % 2
                xt = xts[p]
                xo = xos[p]
                if b + 1 < nblk:
                    load_x(xts[1 - p], src, b + 1, f'xt{1 - p}')
                for s in range(2):
                    rms_norm(xt[:, s, :], gbc, 'gbc', hb[s], f'xt{p}', f'hb{s}', tmp[s], f'tmp{s}')
                    transpose8(hb[s], f'hb{s}', hT[p], s * 128, f'hT{p}', s, 'act')
                for fc in range(32):
                    bi = 2 + fc % 4
                    ups = banks[bi][:, 0:TB]
                    for kc in range(8):
                        S.add('pe', ('matmul', A(
                            out=ups, lhsT=W1[:, kc, fc * 128:(fc + 1) * 128], rhs=hT[p][:, kc, :],
                            start=(kc == 0), stop=(kc == 7))),
                            reads=['W1', f'hT{p}'], writes=[BK(bi)])
                    r = rt[fc % 2]
                    S.add('act', ('activation', A(out=r, in_=ups, func=AF.Relu)),
                          writes=[BK(bi), f'rt{fc % 2}'])
                    S.add('dve', ('tensor_tensor', A(out=uT[:, fc, :], in0=r, in1=r, op=ALU.mult)),
                          reads=[f'rt{fc % 2}'], writes=[f'uT{fc}'])
                gi = 0
                for ts in range(2):
                    for dh in range(2):
                        bi = 6 + gi % 2
                        gi += 1
                        acc = banks[bi]
                        for fc in range(32):
                            S.add('pe', ('matmul', A(
                                out=acc[:, :], lhsT=uT[:, fc, ts * 128:(ts + 1) * 128],
                                rhs=W2v[fc][:, dh * 512:(dh + 1) * 512], start=(fc == 0), stop=(fc == 31))),
                                reads=['W2', f'uT{fc}'], writes=[BK(bi)])
                        S.add('dve', ('tensor_tensor', A(
                            out=xo[:, ts, dh * 512:(dh + 1) * 512], in0=acc[:, :], in1=xt[:, ts, dh * 512:(dh + 1) * 512],
                            op=ALU.add)), reads=[f'xt{p}'], writes=[BK(bi), f'xo{p}_{ts}'])
                okeys = [f'xo{p}_0', f'xo{p}_1']
                if apply_final:
                    for s in range(2):
                        x2 = xo[:, s, :]
                        rs = norm_stats(x2, [f'xo{p}_{s}'], tmp[s], f'tmp{s}')
                        S.add('dve', ('scalar_tensor_tensor', A(out=x2, in0=x2, scalar=rs, in1=gfin,
                                                                                     op0=ALU.mult, op1=ALU.mult)),
                              reads=[f'tmp{s}rs', 'gfin'], writes=[f'xo{p}_{s}'])
                op = S.add('sp', ('dma_start', A(
                    out=dst[b * TB:(b + 1) * TB, :].rearrange("(s p) d -> p s d", p=128), in_=xo)),
                    reads=okeys, dma=1)
                if apply_final:
                    final_ops.append(op)

        def odd_pass(layer, src, dst):
            o = layer // 2
            S.fence()
            WR.reset()
            AR.reset()
            Wci = WR.alloc(8 * 2048, BF16).rearrange("p (k f) -> p k f", k=8)
            Wco = WR.alloc(8 * 1024, BF16).rearrange("p (k f) -> p k f", k=8)
            WsT = WR.alloc(8 * 128, BF16).rearrange("p (g t) -> p g t", g=8)
            lng = WR.alloc(1024, F32)
            lnb = WR.alloc(1024, F32)
            gbc = WR.alloc(1024, F32)
            bsT = WR.alloc(8, F32)
            dummy = WR.alloc(16, F32)
            zu = [WR.alloc(1024, F32) for _ in range(2)]
            zv = [WR.alloc(1024, F32) for _ in range(2)]
            vt = [WR.alloc(1024, F32) for _ in range(2)]
            xts = [AR.alloc(2048, F32).rearrange("p (s d) -> p s d", s=2) for _ in range(2)]
            xos = [AR.alloc(2048, F32).rearrange("p (s d) -> p s d", s=2) for _ in range(2)]
            hb = [AR.alloc(1024, BF16) for _ in range(2)]
            hT = [AR.alloc(8 * 128, BF16).rearrange("p (k t) -> p k t", k=8) for _ in range(2)]
            vln = [AR.alloc(1024, BF16) for _ in range(2)]
            sg = [AR.alloc(1024, BF16) for _ in range(2)]
            sT = [AR.alloc(8 * 128, BF16).rearrange("p (k t) -> p k t", k=8) for _ in range(2)]
            tmp = [AR.alloc(16, F32) for _ in range(2)]
            tmp2 = [AR.alloc(16, F32) for _ in range(2)]

            load_bcast(gbc, norm_mix[layer], 'gbc')
            load_bcast(lng, c_ln_g[o], 'lng')
            load_bcast(lnb, c_ln_b[o], 'lnb')
            S.add('sp', ('dma_start', A(out=bsT, in_=c_b_sT[o])), writes=['bsT'], dma=1)
            load_x(xts[0], src, 0, 'xt0')
            k1 = load_w(Wci, c_w_in[o], 'Wci', 8, 2048)
            k2 = load_w(Wco, c_w_out[o], 'Wco', 8, 1024)
            S.add('pool', ('dma_start', A(out=WsT, in_=c_w_sT[o])), writes=['WsT_raw'], dma=1)
            join(k1, 'Wci', dummy[:, 0:4])
            join(k2, 'Wco', dummy[:, 4:8])
            S.add('pool', ('memset', A(WsT[64:128, :, 0:64], 0.0)), reads=['WsT_raw'], writes=['WsT'])

            for b in range(nblk):
                p = b % 2
                xt = xts[p]
                xo = xos[p]
                if b + 1 < nblk:
                    load_x(xts[1 - p], src, b + 1, f'xt{1 - p}')
                for s in range(2):
                    x2 = xt[:, s, :]
                    rms_norm(x2, gbc, 'gbc', hb[s], f'xt{p}', f'hb{s}', tmp[s], f'tmp{s}')
                    transpose8(hb[s], f'hb{s}', hT[s], 0, f'hT{s}', s, 'act')
                    for cg in range(4):
                        bi = 2 + cg
                        for kc in range(8):
                            S.add('pe', ('matmul', A(
                                out=banks[bi][:, :], lhsT=hT[s][:, kc, :], rhs=Wci[:, kc, cg * 512:(cg + 1) * 512],
                                start=(kc == 0), stop=(kc == 7))), reads=['Wci', f'hT{s}'], writes=[BK(bi)])
                        dstz = (zu[s] if cg < 2 else zv[s])[:, (cg % 2) * 512:(cg % 2 + 1) * 512]
                        S.add('act', ('activation', A(out=dstz, in_=banks[bi][:, :],
                                                                               func=AF.Gelu_apprx_tanh)),
                              writes=[BK(bi), f'z{s}_{cg}'])
                    t2 = tmp2[s]
                    stt = t2[:, 0:12]
                    mv = t2[:, 12:14]
                    ve = t2[:, 14:15]
                    rs = t2[:, 15:16]
                    zvs = zv[s]
                    S.add('dve', ('bn_stats', A(out=stt[:, 0:6], in_=zvs[:, 0:512])),
                          reads=[f'z{s}_2'], writes=[f't2{s}a'])
                    S.add('dve', ('bn_stats', A(out=stt[:, 6:12], in_=zvs[:, 512:1024])),
                          reads=[f'z{s}_3'], writes=[f't2{s}b'])
                    S.add('dve', ('bn_aggr', A(out=mv, in_=stt)), reads=[f't2{s}a', f't2{s}b'],
                          writes=[f't2{s}mv'])
                    S.add('dve', ('tensor_scalar', A(out=ve, in0=mv[:, 1:2], scalar1=EPS, scalar2=None,
                                                                         op0=ALU.add)), reads=[f't2{s}mv'], writes=[f't2{s}ve'])
                    S.add('pool', ('tensor_tensor', A(out=rs, in0=ve, in1=neghalf[:, 0:1], op=ALU.pow)),
                          reads=[f't2{s}ve', 'neghalf'], writes=[f't2{s}rs'])
                    vts = vt[s]
                    S.add('dve', ('tensor_scalar', A(
                        out=vts, in0=zvs, scalar1=mv[:, 0:1], scalar2=rs, op0=ALU.subtract, op1=ALU.mult)),
                        reads=[f'z{s}_2', f'z{s}_3', f't2{s}mv', f't2{s}rs'], writes=[f'vt{s}'])
                    S.add('pool', ('tensor_tensor', A(out=vts, in0=vts, in1=lng, op=ALU.mult)),
                          reads=[f'vt{s}', 'lng'], writes=[f'vt{s}'])
                    vl = vln[s]
                    S.add('pool', ('tensor_tensor', A(out=vl, in0=vts, in1=lnb, op=ALU.add)),
                          reads=[f'vt{s}', 'lnb'], writes=[f'vln{s}'])
                    sgs = sg[s]
                    zus = zu[s]
                    for gh in range(2):
                        bi = 6 + gh
                        for g in range(4 * gh, 4 * gh + 4):
                            mps = banks[bi][:, (g % 4) * 128:(g % 4 + 1) * 128]
                            S.add('pe', ('matmul', A(out=mps, lhsT=WsT[:, g, :],
                                                                               rhs=vl[:, g * 128:(g + 1) * 128],
                                                                               start=True, stop=True)),
                                  reads=['WsT', f'vln{s}'], writes=[BK(bi)])
                        for g in range(4 * gh, 4 * gh + 4):
                            mps = banks[bi][:, (g % 4) * 128:(g % 4 + 1) * 128]
                            S.add('dve', ('scalar_tensor_tensor', A(
                                out=sgs[:, g * 128:(g + 1) * 128], in0=mps, scalar=bsT[:, g:g + 1],
                                in1=zus[:, g * 128:(g + 1) * 128], op0=ALU.add, op1=ALU.mult)),
                                reads=['bsT', f'z{s}_{g // 4}'], writes=[BK(bi), f'sg{s}_{g}'])
                    tp = bank_bf(s)
                    for kc in range(8):
                        S.add('pe', ('transpose', A(
                            out=tp[:, kc, :], in_=sgs[:, kc * 128:(kc + 1) * 128], identity=ident[:])),
                            reads=[f'sg{s}_{kc}', 'ident'], writes=[BK(s)])
                    S.add('act', ('copy', A(out=sT[s], in_=tp)), writes=[BK(s), f'sT{s}'])
                    for dh in range(2):
                        bi = 2 + dh
                        for kc in range(8):
                            S.add('pe', ('matmul', A(
                                out=banks[bi][:, :], lhsT=sT[s][:, kc, :], rhs=Wco[:, kc, dh * 512:(dh + 1) * 512],
                                start=(kc == 0), stop=(kc == 7))), reads=['Wco', f'sT{s}'], writes=[BK(bi)])
                        S.add('dve', ('tensor_tensor', A(
                            out=xo[:, s, dh * 512:(dh + 1) * 512], in0=banks[bi][:, :],
                            in1=xt[:, s, dh * 512:(dh + 1) * 512], op=ALU.add)),
                            reads=[f'xt{p}'], writes=[BK(bi), f'xo{p}_{s}'])
                S.add('sp', ('dma_start', A(
                    out=dst[b * TB:(b + 1) * TB, :].rearrange("(s p) d -> p s d", p=128), in_=xo)),
                    reads=[f'xo{p}_0', f'xo{p}_1'], dma=1)

        def even_pass(layer, src, dst):
            ev = layer // 2
            lam_init = 0.8 - 0.6 * math.exp(-0.3 * layer)
            NCH = S_LEN // 128
            S.fence()
            WR.reset()
            AR.reset()
            Win = WR.alloc(8 * 2560, BF16).rearrange("p (k f) -> p k f", k=8)
            Wout = WR.alloc(8 * 1024, BF16).rearrange("p (k f) -> p k f", k=8)
            kT = WR.alloc(4 * S_LEN, BF16).rearrange("p (h t) -> p h t", h=4)
            V = WR.alloc(NCH * 512, BF16).rearrange("p (c v) -> p c v", c=NCH)
            WA = WR.alloc(4 * 128, BF16).rearrange("p (c j) -> p c j", c=4)
            WX = WR.alloc(4 * 128, BF16).rearrange("p (c j) -> p c j", c=4)
            gbc = AR.alloc(1024, F32)
            pv = AR.alloc(33, F32)
            ldl = AR.alloc(256, F32)
            sm = AR.alloc(64, F32)
            dummy = AR.alloc(16, F32)
            ltmp = AR.alloc(64, F32)
            hprev = AR.alloc(4, F32)
            tmp = [AR.alloc(16, F32) for _ in range(2)]
            xt = AR.alloc(2048, F32).rearrange("p (s d) -> p s d", s=2)
            hb = [AR.alloc(1024, BF16) for _ in range(2)]
            hT = AR.alloc(8 * TB, BF16).rearrange("p (k t) -> p k t", k=8)
            qc = AR.alloc(4 * 2 * TB, BF16).rearrange("p (h t) -> p h t", h=4)
            xbh = AR.alloc(4 * (TB + 3), F32).rearrange("p (c t) -> p c t", c=4)
            gg = AR.alloc(4 * TB, F32).rearrange("p (c t) -> p c t", c=4)
            xc = AR.alloc(4 * TB, F32).rearrange("p (c t) -> p c t", c=4)
            xcb = AR.alloc(4 * TB, BF16).rearrange("p (c t) -> p c t", c=4)
            Tr = AR.alloc(4 * TB, F32).rearrange("p (c t) -> p c t", c=4)
            Ti = AR.alloc(4 * TB, F32).rearrange("p (c t) -> p c t", c=4)
            aa = AR.alloc(4 * TB, F32).rearrange("p (c t) -> p c t", c=4)
            a2 = AR.alloc(4 * TB, F32).rearrange("p (c t) -> p c t", c=4)
            bt = Ti
            hs = Tr
            yT = AR.alloc(8 * TB, BF16).rearrange("p (k t) -> p k t", k=8)
            NPT = 4
            pT = [AR.alloc(2 * TB, BF16) for _ in range(NPT)]
            fr = AR.alloc(2 * TB, F32)
            ft = AR.alloc(2 * TB, F32)
            fo = AR.alloc(TB, F32)
            fsq = AR.alloc(TB, F32)
            frs = AR.alloc(TB, F32)

            s1 = sm[:, 0:1]
            s2 = sm[:, 1:2]
            e1 = sm[:, 2:3]
            e2 = sm[:, 3:4]
            neglam = sm[:, 4:5]
            g1 = sm[:, 5:6]
            yv = sm[:, 8:12]
            tv = sm[:, 12:16]
            sc = sm[:, 16:20]
            sch = sm[:, 20:24]
            hba = sm[:, 24:28]
            hbx = sm[:, 28:32]
            cw = lambda w, c: pv[:, w * 4 + c:w * 4 + c + 1]
            cb = lambda c: pv[:, 16 + c:17 + c]

            load_bcast(gbc, norm_mix[layer], 'gbc')
            load_bcast(ldl, diff_l[ev], 'ldl')
            S.add('sp', ('dma_start', A(out=pv, in_=pvec[ev])), writes=['pv'], dma=1)
            S.add('sp', ('dma_start', A(
                out=xt, in_=src[0:TB, :].rearrange("(s p) d -> p s d", p=128))), writes=['xt'], dma=1)
            k1 = load_w(Win, ab_w_in[ev], 'Win', 8, 2560)
            k2 = load_w(Wout, ab_w_out[ev], 'Wout', 8, 1024)
            join(k1, 'Win', dummy[:, 0:4])
            join(k2, 'Wout', dummy[:, 4:8])
            S.add('pool', ('memset', A(WA, 0.0)), writes=['WA0'])
            S.add('pool', ('memset', A(WX, 0.0)), writes=['WX0'])
            for nm, wt, srcw in (('WA', WA, lru_wa), ('WX', WX, lru_wx)):
                sv = srcw[ev].rearrange("(c j) i o -> j i c o", j=2)
                wk = []
                for j in range(2):
                    key = f'{nm}_{j}'
                    wk.append(key)
                    S.add('pool', ('dma_start', A(
                        out=wt[j * 64:(j + 1) * 64, :, j * 64:(j + 1) * 64], in_=sv[j])),
                        reads=[nm + '0'], writes=[key], dma=1)
                join(wk, nm, dummy[:, 8:10] if nm == 'WA' else dummy[:, 10:12])
            S.add('pool', ('memset', A(qc[64:128, :, 0:TB], 0.0)), writes=['qcz0'])
            S.add('pool', ('memset', A(qc[0:64, :, TB:2 * TB], 0.0)), writes=['qcz1'])
            S.add('dve', ('tensor_tensor', A(out=ltmp, in0=ldl[:, 0:64], in1=ldl[:, 64:128], op=ALU.mult)),
                  reads=['ldl'], writes=['ltmp'])
            S.add('dve', ('tensor_reduce', A(out=s1, in_=ltmp, axis=mybir.AxisListType.X, op=ALU.add)),
                  reads=['ltmp'], writes=['s1'])
            S.add('dve', ('tensor_tensor', A(out=ltmp, in0=ldl[:, 128:192], in1=ldl[:, 192:256], op=ALU.mult)),
                  reads=['ldl', 's1'], writes=['ltmp'])
            S.add('dve', ('tensor_reduce', A(out=s2, in_=ltmp, axis=mybir.AxisListType.X, op=ALU.add)),
                  reads=['ltmp'], writes=['s2'])
            S.add('act', ('activation', A(out=e1, in_=s1, func=AF.Exp)), reads=['s1'], writes=['e1'])
            S.add('act', ('activation', A(out=e2, in_=s2, func=AF.Exp)), reads=['s2'], writes=['e2'])
            S.add('dve', ('tensor_tensor', A(out=neglam, in0=e2, in1=e1, op=ALU.subtract)), reads=['e1', 'e2'],
                  writes=['neglam'])
            S.add('dve', ('tensor_scalar', A(out=neglam, in0=neglam, scalar1=-lam_init, scalar2=None, op0=ALU.add)),
                  reads=['neglam'], writes=['neglam'])
            S.add('dve', ('tensor_scalar', A(out=g1, in0=pv[:, 32:33], scalar1=(1.0 - lam_init), scalar2=None,
                                                   op0=ALU.mult)), reads=['pv'], writes=['g1'])
            S.add('act', ('activation', A(out=yv, in_=pv[:, 28:32], func=AF.Exp, scale=-1.0)), reads=['pv'], writes=['yv'])
            S.add('dve', ('tensor_scalar', A(out=tv, in0=yv, scalar1=-0.25, scalar2=1.0 / 3.0, op0=ALU.mult, op1=ALU.add)),
                  reads=['yv'], writes=['tv'])
            S.add('dve', ('tensor_tensor', A(out=tv, in0=tv, in1=yv, op=ALU.mult)), reads=['tv', 'yv'], writes=['tv'])
            S.add('dve', ('tensor_scalar', A(out=tv, in0=tv, scalar1=-1.0, scalar2=0.5, op0=ALU.mult, op1=ALU.add)),
                  reads=['tv'], writes=['tv'])
            S.add('dve', ('tensor_tensor', A(out=tv, in0=tv, in1=yv, op=ALU.mult)), reads=['tv', 'yv'], writes=['tv'])
            S.add('dve', ('tensor_scalar', A(out=tv, in0=tv, scalar1=-1.0, scalar2=1.0, op0=ALU.mult, op1=ALU.add)),
                  reads=['tv'], writes=['tv'])
            S.add('dve', ('tensor_tensor', A(out=tv, in0=tv, in1=yv, op=ALU.mult)), reads=['tv', 'yv'], writes=['tv'])
            S.add('dve', ('tensor_scalar', A(out=sc, in0=tv, scalar1=-8.0, scalar2=None, op0=ALU.mult)),
                  reads=['tv'], writes=['sc'])
            S.add('dve', ('tensor_scalar', A(out=sch, in0=tv, scalar1=-4.0, scalar2=None, op0=ALU.mult)),
                  reads=['tv'], writes=['sch'])
            S.add('dve', ('tensor_scalar', A(out=hba, in0=pv[:, 20:24], scalar1=0.5, scalar2=None, op0=ALU.mult)),
                  reads=['pv'], writes=['hba'])
            S.add('dve', ('tensor_scalar', A(out=hbx, in0=pv[:, 24:28], scalar1=0.5, scalar2=None, op0=ALU.mult)),
                  reads=['pv'], writes=['hbx'])
            S.add('pool', ('memset', A(xbh[:, :, 0:3], 0.0)), writes=['xbh_halo'])
            S.add('pool', ('memset', A(hprev, 0.0)), writes=['hprev'])

            gcount = [0]

            def gbank():
                bi = 1 + gcount[0] % 3
                gcount[0] += 1
                return bi

            pcount = [0]
            for b in range(nblk):
                if b > 0:
                    S.add('sp', ('dma_start', A(
                        out=xt, in_=src[b * TB:(b + 1) * TB, :].rearrange("(s p) d -> p s d", p=128))),
                        writes=['xt'], dma=1)
                for s in range(2):
                    rms_norm(xt[:, s, :], gbc, 'gbc', hb[s], 'xt', f'hb{s}', tmp[s], f'tmp{s}')
                    transpose8(hb[s], f'hb{s}', hT, s * 128, 'hT', 0, 'act')

                def proj_fm(col0, evac):
                    bi = gbank()
                    ps_ = banks[bi][:, 0:TB]
                    for kc in range(8):
                        S.add('pe', ('matmul', A(out=ps_, lhsT=Win[:, kc, col0:col0 + 128],
                                                                       rhs=hT[:, kc, :], start=(kc == 0), stop=(kc == 7))),
                              reads=['Win', 'hT'], writes=[BK(bi)])
                    evac(ps_, BK(bi))

                def evac_q(ps_, bk, h):
                    S.add('dve', ('tensor_copy', A(out=qc[0:64, h, 0:TB], in_=ps_[0:64, :])),
                          reads=['qcz0', 'qcz1'], writes=[bk, f'qc{h}'])
                    S.add('dve', ('tensor_copy', A(out=qc[64:128, h, TB:2 * TB], in_=ps_[64:128, :])),
                          writes=[bk, f'qc{h}'])

                for h in range(4):
                    proj_fm(h * 128, lambda ps_, bk, h=h: evac_q(ps_, bk, h))
                for h in range(4):
                    proj_fm(512 + h * 128, lambda ps_, bk, h=h: S.add(
                        'dve', ('tensor_copy', A(out=kT[:, h, b * TB:(b + 1) * TB], in_=ps_)),
                        writes=[bk, f'kT{h}_{b}']))
                for ts in range(2):
                    bi = gbank()
                    for kc in range(8):
                        S.add('pe', ('matmul', A(
                            out=banks[bi][:, :], lhsT=hT[:, kc, ts * 128:(ts + 1) * 128],
                            rhs=Win[:, kc, 1024:1536], start=(kc == 0), stop=(kc == 7))),
                            reads=['Win', 'hT'], writes=[BK(bi)])
                    S.add('dve', ('tensor_copy', A(out=V[:, b * 2 + ts, :], in_=banks[bi][:, :])),
                          writes=[BK(bi), f'V{b * 2 + ts}'])
                for c in range(4):
                    proj_fm(1536 + c * 128, lambda ps_, bk, c=c: S.add(
                        'act', ('copy', A(out=xbh[:, c, 3:3 + TB], in_=ps_)), reads=['xbh_halo'],
                        writes=[bk, f'xbh{c}']))
                for c in range(4):
                    proj_fm(2048 + c * 128, lambda ps_, bk, c=c: S.add(
                        'act', ('activation', A(out=gg[:, c, :], in_=ps_, func=AF.Gelu_apprx_tanh)),
                        writes=[bk, f'gg{c}']))
                for c in range(4):
                    S.add('dve', ('tensor_scalar', A(out=xc[:, c, :], in0=xbh[:, c, 0:TB], scalar1=cw(0, c),
                                                                scalar2=cb(c), op0=ALU.mult, op1=ALU.add)),
                          reads=[f'xbh{c}', 'xbh_halo', 'pv'], writes=[f'xc{c}'])
                    for w in range(1, 4):
                        S.add('dve', ('scalar_tensor_tensor', A(
                            out=xc[:, c, :], in0=xbh[:, c, w:w + TB], scalar=cw(w, c), in1=xc[:, c, :],
                            op0=ALU.mult, op1=ALU.add)), reads=[f'xbh{c}', 'xbh_halo', 'pv'], writes=[f'xc{c}'])
                    S.add('pool', ('tensor_copy', A(out=xcb[:, c, :], in_=xc[:, c, :])), reads=[f'xc{c}'],
                          writes=[f'xcb{c}'])
                S.add('pool', ('tensor_copy', A(out=xbh[:, :, 0:3], in_=xbh[:, :, TB:TB + 3])),
                      reads=[f'xbh{c}' for c in range(4)] + [f'xc{c}' for c in range(4)], writes=['xbh_halo'])
                for c in range(4):
                    for nm, wt, Tt, hbias in (('r', WA, Tr, hba), ('i', WX, Ti, hbx)):
                        bi = gbank()
                        ps_ = banks[bi][:, 0:TB]
                        S.add('pe', ('matmul', A(out=ps_, lhsT=wt[:, c, :], rhs=xcb[:, c, :],
                                                                            start=True, stop=True)),
                              reads=['WA' if nm == 'r' else 'WX', f'xcb{c}'], writes=[BK(bi)])
                        S.add('act', ('activation', A(
                            out=Tt[:, c, :], in_=ps_, func=AF.Tanh, bias=hbias[:, c:c + 1], scale=0.5)),
                            reads=['hba', 'hbx'],
                            writes=[BK(bi), f'T{nm}{c}'] + ([f'hs{c}'] if nm == 'r' else [f'bt{c}']))
                for c in range(4):
                    S.add('act', ('activation', A(out=aa[:, c, :], in_=Tr[:, c, :], func=AF.Exp,
                                                             bias=sch[:, c:c + 1], scale=sch[:, c:c + 1])),
                          reads=[f'Tr{c}', 'sch'], writes=[f'aa{c}'])
                    S.add('act', ('activation', A(out=a2[:, c, :], in_=Tr[:, c, :], func=AF.Exp,
                                                             bias=sc[:, c:c + 1], scale=sc[:, c:c + 1])),
                          reads=[f'Tr{c}', 'sc'], writes=[f'a2{c}'])
                for c in range(4):
                    S.add('dve', ('tensor_scalar', A(out=a2[:, c, :], in0=a2[:, c, :], scalar1=-0.25, scalar2=0.25,
                                                                op0=ALU.mult, op1=ALU.add)), reads=[f'a2{c}'], writes=[f'a2{c}'])
                    S.add('pool', ('tensor_tensor', A(out=a2[:, c, :], in0=a2[:, c, :],
                                                                 in1=poshalf[:, 0:1].to_broadcast([128, TB]), op=ALU.pow)),
                          reads=[f'a2{c}', 'poshalf'], writes=[f'a2{c}'])
                    S.add('dve', ('scalar_tensor_tensor', A(out=bt[:, c, :], in0=Ti[:, c, :], scalar=1.0,
                                                                       in1=xc[:, c, :], op0=ALU.add, op1=ALU.mult)),
                          reads=[f'Ti{c}', f'xc{c}'], writes=[f'bt{c}', f'Ti{c}'])
                    S.add('dve', ('tensor_tensor', A(out=bt[:, c, :], in0=bt[:, c, :], in1=a2[:, c, :], op=ALU.mult)),
                          reads=[f'a2{c}'], writes=[f'bt{c}'])
                    S.add('dve', ('tensor_tensor_scan', A(out=hs[:, c, :], data0=aa[:, c, :], data1=bt[:, c, :],
                                                                     initial=hprev[:, c:c + 1], op0=ALU.mult, op1=ALU.add)),
                          reads=[f'aa{c}', f'bt{c}', 'hprev'], writes=[f'hs{c}', f'Tr{c}'])
                    S.add('dve', ('tensor_tensor', A(out=yT[:, 4 + c, :], in0=gg[:, c, :], in1=hs[:, c, :], op=ALU.mult)),
                          reads=[f'gg{c}', f'hs{c}'], writes=[f'yT{4 + c}'])
                S.add('pool', ('tensor_copy', A(out=hprev, in_=hs[:, :, TB - 1])),
                      reads=[f'hs{c}' for c in range(4)], writes=['hprev'])
                nkc = 2 * b + 2
                for h in range(4):
                    ob = 4 + h % 2
                    sb_ = 6 + h % 2
                    for kc in range(nkc):
                        j = kc - 2 * b
                        si = 2 + kc % 2
                        S.add('pe', ('matmul', A(
                            out=banks[si][:, :], lhsT=kT[:, h, kc * 128:(kc + 1) * 128], rhs=qc[:, h, :],
                            start=True, stop=True)), reads=[f'kT{h}_{kc // 2}', f'qc{h}'], writes=[BK(si)])
                        pi = pcount[0] % NPT
                        pcount[0] += 1
                        pt = pT[pi]
                        pkey = f'pT{pi}'
                        S.add('act', ('activation', A(out=pt, in_=banks[si][:, :], func=AF.Exp, scale=0.125)),
                              writes=[BK(si), pkey])
                        if j >= 0:
                            c0 = j * 128
                            if c0 > 0:
                                S.add('pool', ('memset', A(pt[:, 0:c0], 0.0)), writes=[pkey])
                                S.add('pool', ('memset', A(pt[:, TB:TB + c0], 0.0)), writes=[pkey])
                            S.add('pool', ('memset', A(pt[64:128, c0:c0 + 64], 0.0)), writes=[pkey])
                            S.add('pool', ('memset', A(pt[64:128, TB + c0:TB + c0 + 64], 0.0)), writes=[pkey])
                        S.add('pe', ('matmul', A(
                            out=banks[ob][:, :], lhsT=V[:, kc, h * 128:(h + 1) * 128], rhs=pt,
                            start=(kc == 0), stop=(kc == nkc - 1))), reads=[f'V{kc}', pkey], writes=[BK(ob)])
                        S.add('pe', ('matmul', A(
                            out=banks[sb_][:, :], lhsT=ones_bf[:], rhs=pt,
                            start=(kc == 0), stop=(kc == nkc - 1))), reads=['ones_bf', pkey], writes=[BK(sb_)])
                    S.add('dve', ('reciprocal', A(out=fr, in_=banks[sb_][:, :])), writes=[BK(sb_), 'fr'])
                    S.add('dve', ('tensor_tensor', A(out=ft, in0=banks[ob][:, :], in1=fr, op=ALU.mult)),
                          reads=['fr'], writes=[BK(ob), 'ft'])
                    S.add('dve', ('scalar_tensor_tensor', A(out=fo, in0=ft[:, TB:2 * TB], scalar=neglam, in1=ft[:, 0:TB],
                                                                  op0=ALU.mult, op1=ALU.add)),
                          reads=['ft', 'neglam'], writes=['fo'])
                    S.add('dve', ('tensor_tensor', A(out=fsq, in0=fo, in1=fo, op=ALU.mult)), reads=['fo'], writes=['fsq'])
                    bi = gbank()
                    mps = banks[bi][:, 0:TB]
                    S.add('pe', ('matmul', A(out=mps, lhsT=ones_f[:], rhs=fsq, start=True, stop=True)),
                          reads=['ones_f', 'fsq'], writes=[BK(bi)])
                    S.add('dve', ('tensor_scalar', A(out=frs, in0=mps, scalar1=1.0 / 128.0, scalar2=EPS,
                                                                    op0=ALU.mult, op1=ALU.add)),
                          writes=[BK(bi), 'frs'])
                    S.add('pool', ('tensor_tensor', A(out=frs, in0=frs, in1=neghalf[:, 0:1].to_broadcast([128, TB]),
                                                            op=ALU.pow)), reads=['neghalf'], writes=['frs'])
                    S.add('dve', ('scalar_tensor_tensor', A(out=yT[:, h, :], in0=fo, scalar=g1, in1=frs,
                                                                       op0=ALU.mult, op1=ALU.mult)),
                          reads=['fo', 'g1', 'frs'], writes=[f'yT{h}'])
                for ts in range(2):
                    for dh in range(2):
                        bi = gbank()
                        for kc in range(8):
                            S.add('pe', ('matmul', A(
                                out=banks[bi][:, :], lhsT=yT[:, kc, ts * 128:(ts + 1) * 128],
                                rhs=Wout[:, kc, dh * 512:(dh + 1) * 512], start=(kc == 0), stop=(kc == 7))),
                                reads=['Wout', f'yT{kc}'], writes=[BK(bi)])
                        S.add('dve', ('tensor_tensor', A(
                            out=xt[:, ts, dh * 512:(dh + 1) * 512], in0=banks[bi][:, :],
                            in1=xt[:, ts, dh * 512:(dh + 1) * 512], op=ALU.add)),
                            writes=[BK(bi), 'xt'])
                S.add('sp', ('dma_start', A(
                    out=dst[b * TB:(b + 1) * TB, :].rearrange("(s p) d -> p s d", p=128), in_=xt)),
                    reads=['xt'], dma=1)

        bufs = [scrA, scrB]
        cur = x_in
        nb = 0
        if passes is None:
            passes_l = []
            for layer in layers:
                passes_l.append(('even' if layer % 2 == 0 else 'odd', layer))
                passes_l.append(('ffn', layer))
        else:
            passes_l = list(passes)
        for pi, (kind, layer) in enumerate(passes_l):
            last = (pi == len(passes_l) - 1)
            d = out if last else bufs[nb % 2]
            nb += 1
            if kind == 'even':
                even_pass(layer, cur, d)
            elif kind == 'odd':
                odd_pass(layer, cur, d)
            else:
                ffn_pass(layer, cur, d, apply_final=(last and final_norm))
            cur = d
        if not final_ops:
            for q in S.dma_slots.values():
                for op in q:
                    if op is not None:
                        final_ops.append(op)
        else:
            for q in S.dma_slots.values():
                for op in q:
                    if op is not None and op not in final_ops:
                        final_ops.append(op)
        S.emit(nc, final_wait_ops=final_ops)
    return nc


def prep_weights(inp):
    f = lambda a: np.ascontiguousarray(np.asarray(a, dtype=np.float32))
    w = {}
    for k in ('norm_mix', 'norm_ffn', 'norm_final', 'ab_w_in', 'ab_w_out', 'lru_wa', 'lru_wx', 'c_w_in', 'c_ln_g',
              'c_ln_b', 'c_w_out', 'ffn_w1', 'ffn_w2'):
        w[k] = f(inp[k])
    w['diff_l'] = f(np.concatenate([inp['diff_lq1'], inp['diff_lk1'], inp['diff_lq2'], inp['diff_lk2']], axis=1))
    pv = []
    for e in range(2):
        cols = []
        cwv = np.asarray(inp['lru_conv_w'][e])
        cols.append(cwv.reshape(4, 4, 128).transpose(2, 0, 1).reshape(128, 16))
        for nm in ('lru_conv_b', 'lru_ba', 'lru_bx', 'lru_lambda'):
            cols.append(np.asarray(inp[nm][e]).reshape(4, 128).T)
        cols.append(np.asarray(inp['diff_subln'][e]).reshape(128, 1))
        pv.append(np.concatenate(cols, axis=1))
    w['pvec'] = f(np.stack(pv))
    w['c_w_sT'] = f(np.transpose(np.asarray(inp['c_w_s']), (0, 3, 1, 2)))
    w['c_b_sT'] = f(np.transpose(np.asarray(inp['c_b_s']), (0, 2, 1)))
    return w


_NC_CACHE = {}


def kernel(**inputs):
    x = np.asarray(inputs['x'], dtype=np.float32)
    B, S_LEN, _ = x.shape
    w = prep_weights(inputs)
    key = (S_LEN,)
    if key not in _NC_CACHE:
        _NC_CACHE[key] = build_program(S_LEN)
    nc = _NC_CACHE[key]
    in_maps = []
    for c in range(B):
        m = dict(w)
        m['x'] = np.ascontiguousarray(x[c])
        in_maps.append(m)
    res = run_bass_kernel_spmd(nc, in_maps, core_ids=list(range(B)))
    return np.stack([np.asarray(r['out'], dtype=np.float32) for r in res.results], axis=0)
```

```python
import contextlib
import math

import numpy as np
import concourse.bass as bass
import concourse.mybir as mybir
from concourse.bass_utils import run_bass_kernel_spmd

F32 = mybir.dt.float32
BF16 = mybir.dt.bfloat16
U8 = mybir.dt.uint8
AF = mybir.ActivationFunctionType
ALU = mybir.AluOpType

D = 1024
TB = 256
EPS = 1e-6
ENGS = ('pe', 'act', 'dve', 'pool', 'sp')
SEM_CAP = 30000
DBG = {'ffn': 9, 'noW': 0}


def A(*a, **k):
    return (a, k)


class Op:
    __slots__ = ('eng', 'fn', 'deps', 'is_dma', 'ndma', 'slot', 'target', 'epoch', 'msval')


class Sched:
    def __init__(self, nslots=8):
        self.ops = {e: [] for e in ENGS}
        self.lastw = {}
        self.readers = {}
        self.nslots = nslots
        self.dma_slots = {}
        self.dma_count = {e: 0 for e in ENGS}
        self.fence_deps = set()

    def add(self, eng, fn, reads=(), writes=(), dma=0):
        op = Op()
        op.eng = eng
        op.fn = fn
        op.is_dma = dma > 0
        op.ndma = dma
        op.slot = None
        op.target = None
        op.epoch = None
        op.msval = None
        deps = set(self.fence_deps)
        for k in reads:
            w = self.lastw.get(k)
            if w is not None:
                deps.add(w)
        for k in writes:
            w = self.lastw.get(k)
            if w is not None:
                deps.add(w)
            for r in self.readers.get(k, ()):
                deps.add(r)
        if dma:
            q = self.dma_slots.setdefault(eng, [None] * self.nslots)
            s = self.dma_count[eng] % self.nslots
            self.dma_count[eng] += 1
            if q[s] is not None:
                deps.add(q[s])
            q[s] = op
            op.slot = (eng, s)
        deps.discard(op)
        op.deps = deps
        for k in reads:
            self.readers.setdefault(k, []).append(op)
        for k in writes:
            self.lastw[k] = op
            self.readers[k] = []
        self.ops[eng].append(op)
        return op

    def fence(self):
        deps = set()
        for e in ENGS:
            last = None
            for op in reversed(self.ops[e]):
                if not op.is_dma:
                    last = op
                    break
            if last is not None:
                deps.add(last)
        for q in self.dma_slots.values():
            for op in q:
                if op is not None:
                    deps.add(op)
        self.fence_deps = deps
        self.lastw = {}
        self.readers = {}

    def emit(self, nc, final_wait_ops=()):
        needed = set()
        for e in ENGS:
            for op in self.ops[e]:
                for d in op.deps:
                    if e == 'pe' and d.eng == 'pe' and not d.is_dma:
                        continue
                    needed.add(d)
        for d in final_wait_ops:
            needed.add(d)
        n_epochs = {}
        for e in ENGS:
            c = 0
            ep = 0
            for op in self.ops[e]:
                if (not op.is_dma) and op in needed:
                    if c >= SEM_CAP:
                        ep += 1
                        c = 0
                    c += 1
                    op.epoch = ep
                    op.msval = c
            n_epochs[e] = ep + 1
        slot_cnt = {}
        for e in ENGS:
            for op in self.ops[e]:
                if op.is_dma:
                    slot_cnt[op.slot] = slot_cnt.get(op.slot, 0) + 16 * op.ndma
                    op.target = slot_cnt[op.slot]
        with contextlib.ExitStack() as st:
            esem = {}
            for e in ENGS:
                for ep in range(n_epochs[e]):
                    esem[(e, ep)] = st.enter_context(nc.semaphore(f"s_{e}_{ep}"))
            dsem = {}
            for slot in slot_cnt:
                dsem[slot] = st.enter_context(nc.semaphore(f"d_{slot[0]}_{slot[1]}"))
            block = st.enter_context(nc.Block())
            ops = self.ops

            def run(e, eng):
                seen_e = {}
                seen_d = {}
                for op in ops[e]:
                    self._waits(e, eng, op.deps, seen_e, seen_d, esem, dsem)
                    name, (pa, kw) = op.fn
                    if op.is_dma:
                        getattr(eng, name)(*pa, **kw).then_inc(dsem[op.slot], 16)
                    else:
                        ins = getattr(eng, name)(*pa, **kw)
                        if op.msval is not None:
                            ins.then_inc(esem[(e, op.epoch)], 1)
                if e == 'sp' and final_wait_ops:
                    self._waits(e, eng, final_wait_ops, seen_e, seen_d, esem, dsem)

            @block.tensor
            def _(eng):
                run('pe', eng)

            @block.scalar
            def _(eng):
                run('act', eng)

            @block.vector
            def _(eng):
                run('dve', eng)

            @block.gpsimd
            def _(eng):
                run('pool', eng)

            @block.sync
            def _(eng):
                run('sp', eng)

    @staticmethod
    def _waits(e, eng, deps, seen_e, seen_d, esem, dsem):
        best_e = {}
        best_d = {}
        for d in deps:
            if d.is_dma:
                if best_d.get(d.slot, 0) < d.target:
                    best_d[d.slot] = d.target
            else:
                if e == 'pe' and d.eng == 'pe':
                    continue
                v = (d.epoch, d.msval)
                if best_e.get(d.eng, (-1, 0)) < v:
                    best_e[d.eng] = v
        for slot, t in best_d.items():
            if seen_d.get(slot, 0) >= t:
                continue
            seen_d[slot] = t
            eng.wait_ge(dsem[slot], t)
        for de, v in best_e.items():
            if seen_e.get(de, (-1, 0)) >= v:
                continue
            seen_e[de] = v
            eng.wait_ge(esem[(de, v[0])], v[1])


class Region:
    def __init__(self, raw, nbytes):
        self.raw = raw
        self.nbytes = nbytes
        self.off = 0

    def reset(self):
        self.off = 0

    def alloc(self, cols, dt):
        esz = 4 if dt == F32 else 2
        nb = cols * esz
        nb_al = (nb + 63) // 64 * 64
        assert self.off + nb_al <= self.nbytes, (self.off, nb_al, self.nbytes)
        v = self.raw[:, self.off:self.off + nb].bitcast(dt)
        self.off += nb_al
        return v


def build_program(S_LEN=4096, layers=(0, 1, 2, 3), final_norm=True, passes=None):
    nblk = S_LEN // TB
    nc = bass.Bass("TRN2", target_bir_lowering=False)

    def din(name, shape):
        return nc.dram_tensor(name, list(shape), F32, kind="ExternalInput").ap()

    x_in = din("x", [S_LEN, D])
    norm_mix = din("norm_mix", [4, D])
    norm_ffn = din("norm_ffn", [4, D])
    norm_final = din("norm_final", [D])
    ab_w_in = din("ab_w_in", [2, D, 2560])
    ab_w_out = din("ab_w_out", [2, D, D])
    diff_l = din("diff_l", [2, 256])
    pvec = din("pvec", [2, 128, 33])
    lru_wa = din("lru_wa", [2, 8, 64, 64])
    lru_wx = din("lru_wx", [2, 8, 64, 64])
    c_w_in = din("c_w_in", [2, D, 2048])
    c_ln_g = din("c_ln_g", [2, D])
    c_ln_b = din("c_ln_b", [2, D])
    c_w_sT = din("c_w_sT", [2, 128, 8, 128])
    c_b_sT = din("c_b_sT", [2, 128, 8])
    c_w_out = din("c_w_out", [2, D, D])
    ffn_w1 = din("ffn_w1", [4, D, 4096])
    ffn_w2 = din("ffn_w2", [4, 4096, D])
    out = nc.dram_tensor("out", [S_LEN, D], F32, kind="ExternalOutput").ap()
    scrA = nc.dram_tensor("scrA", [S_LEN, D], F32, kind="Internal").ap()
    scrB = nc.dram_tensor("scrB", [S_LEN, D], F32, kind="Internal").ap()

    S = Sched()
    st = contextlib.ExitStack()
    with st:
        def sb(name, shape, dt):
            return st.enter_context(nc.sbuf_tensor(name, shape, dt))

        WBYTES = DBG.get('WBYTES', 126976)
        ABYTES = 77824
        wraw = sb("wraw", [128, WBYTES], U8)
        araw = sb("araw", [128, ABYTES], U8)
        WR = Region(wraw, WBYTES)
        AR = Region(araw, ABYTES)
        ident = sb("ident", [128, 128], BF16)
        identf = sb("identf", [128, 128], F32)
        ones_bf = sb("ones_bf", [128, 128], BF16)
        ones_f = sb("ones_f", [128, 128], F32)
        neghalf = sb("neghalf", [128, 1], F32)
        poshalf = sb("poshalf", [128, 1], F32)
        expbias = sb("expbias", [128, 1], F32)
        qbias = sb("qbias", [128, 1], F32)
        epsb = sb("epsb", [128, 1], F32)
        banks = [st.enter_context(nc.psum_tensor(f"bank{i}", [128, 512], F32)) for i in range(8)]

        def bank_bf(i):
            return banks[i][:, :].bitcast(BF16).rearrange("p (k t) -> p k t", k=8)

        S.add('pool', ('memset', A(identf[:], 1.0)), writes=['identf'])
        S.add('pool', ('affine_select', A(out=identf[:], in_=identf[:], pattern=[[-1, 128]],
                                                compare_op=ALU.is_equal, fill=0.0, base=0,
                                                channel_multiplier=1)), reads=['identf'], writes=['identf'])
        S.add('pool', ('tensor_copy', A(out=ident[:], in_=identf[:])), reads=['identf'], writes=['ident'])
        S.add('pool', ('memset', A(ones_bf[:], 1.0)), writes=['ones_bf'])
        S.add('pool', ('memset', A(ones_f[:], 1.0)), writes=['ones_f'])
        S.add('pool', ('memset', A(neghalf[:], -0.5)), writes=['neghalf'])
        S.add('pool', ('memset', A(poshalf[:], 0.5)), writes=['poshalf'])
        S.add('pool', ('memset', A(expbias[:], 0.0)), writes=['expbias'])
        S.add('pool', ('memset', A(qbias[:], 0.25)), writes=['qbias'])
        S.add('pool', ('memset', A(epsb[:], EPS)), writes=['epsb'])
        S.add('pool', ('memset', A(expbias[64:128, :], -30000.0)), reads=['expbias'], writes=['expbias'])
        CONST_KEYS = ['ident', 'ones_bf', 'ones_f', 'neghalf', 'poshalf', 'expbias']

        def after_fence_consts():
            pass

        def load_w(dst3, src2, name, kc_n, cols):
            src3 = src2.rearrange("(kc p) n -> p kc n", p=128)
            cstep = cols
            while cstep > 2048:
                cstep //= 2
            kstep = max(1, 2048 // cstep)
            keys = []
            for k0 in range(0, kc_n, kstep):
                for c0 in range(0, cols, cstep):
                    key = f"{name}_{k0}_{c0}"
                    keys.append(key)
                    S.add('pool', ('dma_start', A(
                        out=dst3[:, k0:k0 + kstep, c0:c0 + cstep],
                        in_=src3[:, k0:k0 + kstep, c0:c0 + cstep])),
                        writes=[key], dma=1)
            return keys

        def join(keys, name, dummy):
            S.add('pool', ('memset', A(dummy, 0.0)), reads=keys, writes=[name])

        def load_bcast(dst, src1d, name):
            S.add('sp', ('dma_start', A(out=dst, in_=src1d.partition_broadcast(128))),
                  writes=[name], dma=1)

        def load_x(xt3, src, blk, key):
            S.add('sp', ('dma_start', A(
                out=xt3, in_=src[blk * TB:(blk + 1) * TB, :].rearrange("(s p) d -> p s d", p=128))),
                writes=[key], dma=1)

        def store_x(dst, xo3, blk, key):
            return S.add('sp', ('dma_start', A(
                out=dst[blk * TB:(blk + 1) * TB, :].rearrange("(s p) d -> p s d", p=128), in_=xo3)),
                reads=[key], dma=1)

        def rms_norm(x2, gbc, gkey, h2, xkey, hkey, tmp, tkey):
            stt = tmp[:, 0:12]
            mv = tmp[:, 12:14]
            ms = tmp[:, 14:15]
            rs = tmp[:, 15:16]
            S.add('dve', ('bn_stats', A(out=stt[:, 0:6], in_=x2[:, 0:512])), reads=[xkey], writes=[tkey + 'a'])
            S.add('dve', ('bn_stats', A(out=stt[:, 6:12], in_=x2[:, 512:1024])), reads=[xkey], writes=[tkey + 'b'])
            S.add('dve', ('bn_aggr', A(out=mv, in_=stt)), reads=[tkey + 'a', tkey + 'b'], writes=[tkey + 'mv'])
            S.add('dve', ('tensor_scalar', A(out=ms, in0=mv[:, 0:1], scalar1=mv[:, 0:1], scalar2=mv[:, 1:2],
                                                   op0=ALU.mult, op1=ALU.add)), reads=[tkey + 'mv'], writes=[tkey + 'ms'])
            S.add('dve', ('tensor_scalar', A(out=ms, in0=ms, scalar1=EPS, scalar2=None, op0=ALU.add)),
                  reads=[tkey + 'ms'], writes=[tkey + 'ms'])
            S.add('pool', ('tensor_tensor', A(out=rs, in0=ms, in1=neghalf[:, 0:1], op=ALU.pow)),
                  reads=[tkey + 'ms', 'neghalf'], writes=[tkey + 'rs'])
            S.add('dve', ('scalar_tensor_tensor', A(out=h2, in0=x2, scalar=rs, in1=gbc, op0=ALU.mult, op1=ALU.mult)),
                  reads=[xkey, tkey + 'rs', gkey], writes=[hkey])

        def transpose8(h2, hkey, hT3, col0, hTkey, bank_i, evac_eng):
            tp = bank_bf(bank_i)
            bkey = f"bank{bank_i}"
            for kc in range(8):
                S.add('pe', ('transpose', A(out=tp[:, kc, :], in_=h2[:, kc * 128:(kc + 1) * 128],
                                                         identity=ident[:])),
                      reads=[hkey, 'ident'], writes=[bkey])
            if evac_eng == 'act':
                S.add('act', ('copy', A(out=hT3[:, :, col0:col0 + 128], in_=tp)), writes=[bkey, hTkey])
            else:
                S.add('dve', ('tensor_copy', A(out=hT3[:, :, col0:col0 + 128], in_=tp)), writes=[bkey, hTkey])

        final_ops = []

        def BK(i):
            return f'bank{i}'

        def norm_stats(x2, xkeys, t, tk):
            stt = t[:, 0:12]
            mv = t[:, 12:14]
            ms = t[:, 14:15]
            rs = t[:, 15:16]
            S.add('dve', ('bn_stats', A(out=stt[:, 0:6], in_=x2[:, 0:512])), reads=xkeys, writes=[tk + 'a'])
            S.add('dve', ('bn_stats', A(out=stt[:, 6:12], in_=x2[:, 512:1024])), reads=xkeys, writes=[tk + 'b'])
            S.add('dve', ('bn_aggr', A(out=mv, in_=stt)), reads=[tk + 'a', tk + 'b'], writes=[tk + 'mv'])
            S.add('dve', ('tensor_scalar', A(out=ms, in0=mv[:, 0:1], scalar1=mv[:, 0:1], scalar2=mv[:, 1:2],
                                                   op0=ALU.mult, op1=ALU.add)), reads=[tk + 'mv'], writes=[tk + 'ms'])
            S.add('dve', ('tensor_scalar', A(out=ms, in0=ms, scalar1=EPS, scalar2=None, op0=ALU.add)),
                  reads=[tk + 'ms'], writes=[tk + 'ms'])
            S.add('pool', ('tensor_tensor', A(out=rs, in0=ms, in1=neghalf[:, 0:1], op=ALU.pow)),
                  reads=[tk + 'ms', 'neghalf'], writes=[tk + 'rs'])
            return rs

        def ffn_pass(layer, src, dst, apply_final):
            S.fence()
            WR.reset()
            AR.reset()
            W1 = WR.alloc(8 * 4096, BF16).rearrange("p (k f) -> p k f", k=8)
            W2a = WR.alloc(30 * 1024, BF16).rearrange("p (k f) -> p k f", k=30)
            W2b = AR.alloc(2 * 1024, BF16).rearrange("p (k f) -> p k f", k=2)
            W2v = [W2a[:, fc, :] if fc < 30 else W2b[:, fc - 30, :] for fc in range(32)]
            gbc = AR.alloc(1024, F32)
            gfin = AR.alloc(1024, F32) if apply_final else None
            dummy = AR.alloc(16, F32)
            xts = [AR.alloc(2048, F32).rearrange("p (s d) -> p s d", s=2) for _ in range(2)]
            xos = [AR.alloc(2048, F32).rearrange("p (s d) -> p s d", s=2) for _ in range(2)]
            hb = [AR.alloc(1024, BF16) for _ in range(2)]
            hT = [AR.alloc(8 * TB, BF16).rearrange("p (k t) -> p k t", k=8) for _ in range(2)]
            uT = AR.alloc(32 * TB, BF16).rearrange("p (k t) -> p k t", k=32)
            rt = [AR.alloc(TB, F32) for _ in range(2)]
            tmp = [AR.alloc(16, F32) for _ in range(2)]

            load_bcast(gbc, norm_ffn[layer], 'gbc')
            if apply_final:
                load_bcast(gfin, norm_final, 'gfin')
            load_x(xts[0], src, 0, 'xt0')
            k1 = load_w(W1, ffn_w1[layer], 'W1', 8, 4096)
            k2 = load_w(W2a, ffn_w2[layer][0:30 * 128, :], 'W2a', 30, 1024)
            k2 += load_w(W2b, ffn_w2[layer][30 * 128:32 * 128, :], 'W2b', 2, 1024)
            join(k1, 'W1', dummy[:, 0:4])
            join(k2, 'W2', dummy[:, 4:8])

            for b in range(nblk):
                p = b % 2
                xt = xts[p]
                xo = xos[p]
                if b + 1 < nblk:
                    load_x(xts[1 - p], src, b + 1, f'xt{1 - p}')
                for s in range(2):
                    rms_norm(xt[:, s, :], gbc, 'gbc', hb[s], f'xt{p}', f'hb{s}', tmp[s], f'tmp{s}')
                    transpose8(hb[s], f'hb{s}', hT[p], s * 128, f'hT{p}', s, 'act')
                for fc in range(32):
                    bi = 2 + fc % 4
                    ups = banks[bi][:, 0:TB]
                    for kc in range(8):
                        S.add('pe', ('matmul', A(
                            out=ups, lhsT=W1[:, kc, fc * 128:(fc + 1) * 128], rhs=hT[p][:, kc, :],
                            start=(kc == 0), stop=(kc == 7))),
                            reads=['W1', f'hT{p}'], writes=[BK(bi)])
                    r = rt[fc % 2]
                    S.add('act', ('activation', A(out=r, in_=ups, func=AF.Relu)),
                          writes=[BK(bi), f'rt{fc % 2}'])
                    S.add('dve', ('tensor_tensor', A(out=uT[:, fc, :], in0=r, in1=r, op=ALU.mult)),
                          reads=[f'rt{fc % 2}'], writes=[f'uT{fc}'])
                gi = 0
                for ts in range(2):
                    for dh in range(2):
                        bi = 6 + gi % 2
                        gi += 1
                        acc = banks[bi]
                        for fc in range(32):
                            S.add('pe', ('matmul', A(
                                out=acc[:, :], lhsT=uT[:, fc, ts * 128:(ts + 1) * 128],
                                rhs=W2v[fc][:, dh * 512:(dh + 1) * 512], start=(fc == 0), stop=(fc == 31))),
                                reads=['W2', f'uT{fc}'], writes=[BK(bi)])
                        S.add('dve', ('tensor_tensor', A(
                            out=xo[:, ts, dh * 512:(dh + 1) * 512], in0=acc[:, :], in1=xt[:, ts, dh * 512:(dh + 1) * 512],
                            op=ALU.add)), reads=[f'xt{p}'], writes=[BK(bi), f'xo{p}_{ts}'])
                okeys = [f'xo{p}_0', f'xo{p}_1']
                if apply_final:
                    for s in range(2):
                        x2 = xo[:, s, :]
                        rs = norm_stats(x2, [f'xo{p}_{s}'], tmp[s], f'tmp{s}')
                        S.add('dve', ('scalar_tensor_tensor', A(out=x2, in0=x2, scalar=rs, in1=gfin,
                                                                                     op0=ALU.mult, op1=ALU.mult)),
                              reads=[f'tmp{s}rs', 'gfin'], writes=[f'xo{p}_{s}'])
                op = S.add('sp', ('dma_start', A(
                    out=dst[b * TB:(b + 1) * TB, :].rearrange("(s p) d -> p s d", p=128), in_=xo)),
                    reads=okeys, dma=1)
                if apply_final:
                    final_ops.append(op)

        def odd_pass(layer, src, dst):
            o = layer // 2
            S.fence()
            WR.reset()
            AR.reset()
            Wci = WR.alloc(8 * 2048, BF16).rearrange("p (k f) -> p k f", k=8)
            Wco = WR.alloc(8 * 1024, BF16).rearrange("p (k f) -> p k f", k=8)
            WsT = WR.alloc(8 * 128, BF16).rearrange("p (g t) -> p g t", g=8)
            lng = WR.alloc(1024, F32)
            lnb = WR.alloc(1024, F32)
            gbc = WR.alloc(1024, F32)
            bsT = WR.alloc(8, F32)
            dummy = WR.alloc(16, F32)
            zu = [WR.alloc(1024, F32) for _ in range(2)]
            zv = [WR.alloc(1024, F32) for _ in range(2)]
            vt = [WR.alloc(1024, F32) for _ in range(2)]
            xts = [AR.alloc(2048, F32).rearrange("p (s d) -> p s d", s=2) for _ in range(2)]
            xos = [AR.alloc(2048, F32).rearrange("p (s d) -> p s d", s=2) for _ in range(2)]
            hb = [AR.alloc(1024, BF16) for _ in range(2)]
            hT = [AR.alloc(8 * 128, BF16).rearrange("p (k t) -> p k t", k=8) for _ in range(2)]
            vln = [AR.alloc(1024, BF16) for _ in range(2)]
            sg = [AR.alloc(1024, BF16) for _ in range(2)]
            sT = [AR.alloc(8 * 128, BF16).rearrange("p (k t) -> p k t", k=8) for _ in range(2)]
            tmp = [AR.alloc(16, F32) for _ in range(2)]
            tmp2 = [AR.alloc(16, F32) for _ in range(2)]

            load_bcast(gbc, norm_mix[layer], 'gbc')
            load_bcast(lng, c_ln_g[o], 'lng')
            load_bcast(lnb, c_ln_b[o], 'lnb')
            S.add('sp', ('dma_start', A(out=bsT, in_=c_b_sT[o])), writes=['bsT'], dma=1)
            load_x(xts[0], src, 0, 'xt0')
            k1 = load_w(Wci, c_w_in[o], 'Wci', 8, 2048)
            k2 = load_w(Wco, c_w_out[o], 'Wco', 8, 1024)
            S.add('pool', ('dma_start', A(out=WsT, in_=c_w_sT[o])), writes=['WsT_raw'], dma=1)
            join(k1, 'Wci', dummy[:, 0:4])
            join(k2, 'Wco', dummy[:, 4:8])
            S.add('pool', ('memset', A(WsT[64:128, :, 0:64], 0.0)), reads=['WsT_raw'], writes=['WsT'])

            for b in range(nblk):
                p = b % 2
                xt = xts[p]
                xo = xos[p]
                if b + 1 < nblk:
                    load_x(xts[1 - p], src, b + 1, f'xt{1 - p}')
                for s in range(2):
                    x2 = xt[:, s, :]
                    rms_norm(x2, gbc, 'gbc', hb[s], f'xt{p}', f'hb{s}', tmp[s], f'tmp{s}')
                    transpose8(hb[s], f'hb{s}', hT[s], 0, f'hT{s}', s, 'act')
                    for cg in range(4):
                        bi = 2 + cg
                        for kc in range(8):
                            S.add('pe', ('matmul', A(
                                out=banks[bi][:, :], lhsT=hT[s][:, kc, :], rhs=Wci[:, kc, cg * 512:(cg + 1) * 512],
                                start=(kc == 0), stop=(kc == 7))), reads=['Wci', f'hT{s}'], writes=[BK(bi)])
                        dstz = (zu[s] if cg < 2 else zv[s])[:, (cg % 2) * 512:(cg % 2 + 1) * 512]
                        S.add('act', ('activation', A(out=dstz, in_=banks[bi][:, :],
                                                                               func=AF.Gelu_apprx_tanh)),
                              writes=[BK(bi), f'z{s}_{cg}'])
                    t2 = tmp2[s]
                    stt = t2[:, 0:12]
                    mv = t2[:, 12:14]
                    ve = t2[:, 14:15]
                    rs = t2[:, 15:16]
                    zvs = zv[s]
                    S.add('dve', ('bn_stats', A(out=stt[:, 0:6], in_=zvs[:, 0:512])),
                          reads=[f'z{s}_2'], writes=[f't2{s}a'])
                    S.add('dve', ('bn_stats', A(out=stt[:, 6:12], in_=zvs[:, 512:1024])),
                          reads=[f'z{s}_3'], writes=[f't2{s}b'])
                    S.add('dve', ('bn_aggr', A(out=mv, in_=stt)), reads=[f't2{s}a', f't2{s}b'],
                          writes=[f't2{s}mv'])
                    S.add('dve', ('tensor_scalar', A(out=ve, in0=mv[:, 1:2], scalar1=EPS, scalar2=None,
                                                                         op0=ALU.add)), reads=[f't2{s}mv'], writes=[f't2{s}ve'])
                    S.add('pool', ('tensor_tensor', A(out=rs, in0=ve, in1=neghalf[:, 0:1], op=ALU.pow)),
                          reads=[f't2{s}ve', 'neghalf'], writes=[f't2{s}rs'])
                    vts = vt[s]
                    S.add('dve', ('tensor_scalar', A(
                        out=vts, in0=zvs, scalar1=mv[:, 0:1], scalar2=rs, op0=ALU.subtract, op1=ALU.mult)),
                        reads=[f'z{s}_2', f'z{s}_3', f't2{s}mv', f't2{s}rs'], writes=[f'vt{s}'])
                    S.add('pool', ('tensor_tensor', A(out=vts, in0=vts, in1=lng, op=ALU.mult)),
                          reads=[f'vt{s}', 'lng'], writes=[f'vt{s}'])
                    vl = vln[s]
                    S.add('pool', ('tensor_tensor', A(out=vl, in0=vts, in1=lnb, op=ALU.add)),
                          reads=[f'vt{s}', 'lnb'], writes=[f'vln{s}'])
                    sgs = sg[s]
                    zus = zu[s]
                    for gh in range(2):
                        bi = 6 + gh
                        for g in range(4 * gh, 4 * gh + 4):
                            mps = banks[bi][:, (g % 4) * 128:(g % 4 + 1) * 128]
                            S.add('pe', ('matmul', A(out=mps, lhsT=WsT[:, g, :],
                                                                               rhs=vl[:, g * 128:(g + 1) * 128],
                                                                               start=True, stop=True)),
                                  reads=['WsT', f'vln{s}'], writes=[BK(bi)])
                        for g in range(4 * gh, 4 * gh + 4):
                            mps = banks[bi][:, (g % 4) * 128:(g % 4 + 1) * 128]
                            S.add('dve', ('scalar_tensor_tensor', A(
                                out=sgs[:, g * 128:(g + 1) * 128], in0=mps, scalar=bsT[:, g:g + 1],
                                in1=zus[:, g * 128:(g + 1) * 128], op0=ALU.add, op1=ALU.mult)),
                                reads=['bsT', f'z{s}_{g // 4}'], writes=[BK(bi), f'sg{s}_{g}'])
                    tp = bank_bf(s)
                    for kc in range(8):
                        S.add('pe', ('transpose', A(
                            out=tp[:, kc, :], in_=sgs[:, kc * 128:(kc + 1) * 128], identity=ident[:])),
                            reads=[f'sg{s}_{kc}', 'ident'], writes=[BK(s)])
                    S.add('act', ('copy', A(out=sT[s], in_=tp)), writes=[BK(s), f'sT{s}'])
                    for dh in range(2):
                        bi = 2 + dh
                        for kc in range(8):
                            S.add('pe', ('matmul', A(
                                out=banks[bi][:, :], lhsT=sT[s][:, kc, :], rhs=Wco[:, kc, dh * 512:(dh + 1) * 512],
                                start=(kc == 0), stop=(kc == 7))), reads=['Wco', f'sT{s}'], writes=[BK(bi)])
                        S.add('dve', ('tensor_tensor', A(
                            out=xo[:, s, dh * 512:(dh + 1) * 512], in0=banks[bi][:, :],
                            in1=xt[:, s, dh * 512:(dh + 1) * 512], op=ALU.add)),
                            reads=[f'xt{p}'], writes=[BK(bi), f'xo{p}_{s}'])
                S.add('sp', ('dma_start', A(
                    out=dst[b * TB:(b + 1) * TB, :].rearrange("(s p) d -> p s d", p=128), in_=xo)),
                    reads=[f'xo{p}_0', f'xo{p}_1'], dma=1)

        def even_pass(layer, src, dst):
            ev = layer // 2
            lam_init = 0.8 - 0.6 * math.exp(-0.3 * layer)
            NCH = S_LEN // 128
            S.fence()
            WR.reset()
            AR.reset()
            Win = WR.alloc(8 * 2560, BF16).rearrange("p (k f) -> p k f", k=8)
            Wout = WR.alloc(8 * 1024, BF16).rearrange("p (k f) -> p k f", k=8)
            kT = WR.alloc(4 * S_LEN, BF16).rearrange("p (h t) -> p h t", h=4)
            V = WR.alloc(NCH * 512, BF16).rearrange("p (c v) -> p c v", c=NCH)
            WA = WR.alloc(4 * 128, BF16).rearrange("p (c j) -> p c j", c=4)
            WX = WR.alloc(4 * 128, BF16).rearrange("p (c j) -> p c j", c=4)
            gbc = AR.alloc(1024, F32)
            pv = AR.alloc(33, F32)
            ldl = AR.alloc(256, F32)
            sm = AR.alloc(64, F32)
            dummy = AR.alloc(16, F32)
            ltmp = AR.alloc(64, F32)
            hprev = AR.alloc(4, F32)
            tmp = [AR.alloc(16, F32) for _ in range(2)]
            xt = AR.alloc(2048, F32).rearrange("p (s d) -> p s d", s=2)
            hb = [AR.alloc(1024, BF16) for _ in range(2)]
            hT = AR.alloc(8 * TB, BF16).rearrange("p (k t) -> p k t", k=8)
            qc = AR.alloc(4 * 2 * TB, BF16).rearrange("p (h t) -> p h t", h=4)
            xbh = AR.alloc(4 * (TB + 3), F32).rearrange("p (c t) -> p c t", c=4)
            gg = AR.alloc(4 * TB, F32).rearrange("p (c t) -> p c t", c=4)
            xc = AR.alloc(4 * TB, F32).rearrange("p (c t) -> p c t", c=4)
            xcb = AR.alloc(4 * TB, BF16).rearrange("p (c t) -> p c t", c=4)
            Tr = AR.alloc(4 * TB, F32).rearrange("p (c t) -> p c t", c=4)
            Ti = AR.alloc(4 * TB, F32).rearrange("p (c t) -> p c t", c=4)
            aa = AR.alloc(4 * TB, F32).rearrange("p (c t) -> p c t", c=4)
            a2 = AR.alloc(4 * TB, F32).rearrange("p (c t) -> p c t", c=4)
            bt = Ti
            hs = Tr
            yT = AR.alloc(8 * TB, BF16).rearrange("p (k t) -> p k t", k=8)
            NPT = 4
            pT = [AR.alloc(2 * TB, BF16) for _ in range(NPT)]
            fr = AR.alloc(2 * TB, F32)
            ft = AR.alloc(2 * TB, F32)
            fo4 = AR.alloc(4 * TB, F32).rearrange("p (h t) -> p h t", h=4)
            sd = AR.alloc(4 * TB, F32).rearrange("p (h t) -> p h t", h=4)

            s1 = sm[:, 0:1]
            s2 = sm[:, 1:2]
            e1 = sm[:, 2:3]
            e2 = sm[:, 3:4]
            neglam = sm[:, 4:5]
            g1 = sm[:, 5:6]
            yv = sm[:, 8:12]
            tv = sm[:, 12:16]
            sc = sm[:, 16:20]
            sch = sm[:, 20:24]
            hba = sm[:, 24:28]
            hbx = sm[:, 28:32]
            cw = lambda w, c: pv[:, w * 4 + c:w * 4 + c + 1]
            cb = lambda c: pv[:, 16 + c:17 + c]

            load_bcast(gbc, norm_mix[layer], 'gbc')
            load_bcast(ldl, diff_l[ev], 'ldl')
            S.add('sp', ('dma_start', A(out=pv, in_=pvec[ev])), writes=['pv'], dma=1)
            S.add('sp', ('dma_start', A(
                out=xt, in_=src[0:TB, :].rearrange("(s p) d -> p s d", p=128))), writes=['xt'], dma=1)
            k1 = load_w(Win, ab_w_in[ev], 'Win', 8, 2560)
            k2 = load_w(Wout, ab_w_out[ev], 'Wout', 8, 1024)
            join(k1, 'Win', dummy[:, 0:4])
            join(k2, 'Wout', dummy[:, 4:8])
            S.add('pool', ('memset', A(WA, 0.0)), writes=['WA0'])
            S.add('pool', ('memset', A(WX, 0.0)), writes=['WX0'])
            for nm, wt, srcw in (('WA', WA, lru_wa), ('WX', WX, lru_wx)):
                sv = srcw[ev].rearrange("(c j) i o -> j i c o", j=2)
                wk = []
                for j in range(2):
                    key = f'{nm}_{j}'
                    wk.append(key)
                    S.add('pool', ('dma_start', A(
                        out=wt[j * 64:(j + 1) * 64, :, j * 64:(j + 1) * 64], in_=sv[j])),
                        reads=[nm + '0'], writes=[key], dma=1)
                join(wk, nm, dummy[:, 8:10] if nm == 'WA' else dummy[:, 10:12])
            S.add('pool', ('memset', A(qc[64:128, :, 0:TB], 0.0)), writes=['qcz0'])
            S.add('pool', ('memset', A(qc[0:64, :, TB:2 * TB], 0.0)), writes=['qcz1'])
            S.add('dve', ('tensor_tensor', A(out=ltmp, in0=ldl[:, 0:64], in1=ldl[:, 64:128], op=ALU.mult)),
                  reads=['ldl'], writes=['ltmp'])
            S.add('dve', ('tensor_reduce', A(out=s1, in_=ltmp, axis=mybir.AxisListType.X, op=ALU.add)),
                  reads=['ltmp'], writes=['s1'])
            S.add('dve', ('tensor_tensor', A(out=ltmp, in0=ldl[:, 128:192], in1=ldl[:, 192:256], op=ALU.mult)),
                  reads=['ldl', 's1'], writes=['ltmp'])
            S.add('dve', ('tensor_reduce', A(out=s2, in_=ltmp, axis=mybir.AxisListType.X, op=ALU.add)),
                  reads=['ltmp'], writes=['s2'])
            S.add('act', ('activation', A(out=e1, in_=s1, func=AF.Exp)), reads=['s1'], writes=['e1'])
            S.add('act', ('activation', A(out=e2, in_=s2, func=AF.Exp)), reads=['s2'], writes=['e2'])
            S.add('dve', ('tensor_tensor', A(out=neglam, in0=e2, in1=e1, op=ALU.subtract)), reads=['e1', 'e2'],
                  writes=['neglam'])
            S.add('dve', ('tensor_scalar', A(out=neglam, in0=neglam, scalar1=-lam_init, scalar2=None, op0=ALU.add)),
                  reads=['neglam'], writes=['neglam'])
            S.add('dve', ('tensor_scalar', A(out=g1, in0=pv[:, 32:33], scalar1=(1.0 - lam_init), scalar2=None,
                                                   op0=ALU.mult)), reads=['pv'], writes=['g1'])
            S.add('act', ('activation', A(out=yv, in_=pv[:, 28:32], func=AF.Exp, scale=-1.0)), reads=['pv'], writes=['yv'])
            S.add('dve', ('tensor_scalar', A(out=tv, in0=yv, scalar1=-0.25, scalar2=1.0 / 3.0, op0=ALU.mult, op1=ALU.add)),
                  reads=['yv'], writes=['tv'])
            S.add('dve', ('tensor_tensor', A(out=tv, in0=tv, in1=yv, op=ALU.mult)), reads=['tv', 'yv'], writes=['tv'])
            S.add('dve', ('tensor_scalar', A(out=tv, in0=tv, scalar1=-1.0, scalar2=0.5, op0=ALU.mult, op1=ALU.add)),
                  reads=['tv'], writes=['tv'])
            S.add('dve', ('tensor_tensor', A(out=tv, in0=tv, in1=yv, op=ALU.mult)), reads=['tv', 'yv'], writes=['tv'])
            S.add('dve', ('tensor_scalar', A(out=tv, in0=tv, scalar1=-1.0, scalar2=1.0, op0=ALU.mult, op1=ALU.add)),
                  reads=['tv'], writes=['tv'])
            S.add('dve', ('tensor_tensor', A(out=tv, in0=tv, in1=yv, op=ALU.mult)), reads=['tv', 'yv'], writes=['tv'])
            S.add('dve', ('tensor_scalar', A(out=sc, in0=tv, scalar1=-8.0, scalar2=None, op0=ALU.mult)),
                  reads=['tv'], writes=['sc'])
            S.add('dve', ('tensor_scalar', A(out=sch, in0=tv, scalar1=-4.0, scalar2=None, op0=ALU.mult)),
                  reads=['tv'], writes=['sch'])
            S.add('dve', ('tensor_scalar', A(out=hba, in0=pv[:, 20:24], scalar1=0.5, scalar2=None, op0=ALU.mult)),
                  reads=['pv'], writes=['hba'])
            S.add('dve', ('tensor_scalar', A(out=hbx, in0=pv[:, 24:28], scalar1=0.5, scalar2=None, op0=ALU.mult)),
                  reads=['pv'], writes=['hbx'])
            S.add('pool', ('memset', A(xbh[:, :, 0:3], 0.0)), writes=['xbh_halo'])
            S.add('pool', ('memset', A(hprev, 0.0)), writes=['hprev'])

            gcount = [0]

            def gbank():
                bi = 1 + gcount[0] % 3
                gcount[0] += 1
                return bi

            pcount = [0]
            for b in range(nblk):
                if b > 0:
                    S.add('sp', ('dma_start', A(
                        out=xt, in_=src[b * TB:(b + 1) * TB, :].rearrange("(s p) d -> p s d", p=128))),
                        writes=['xt'], dma=1)
                for s in range(2):
                    rms_norm(xt[:, s, :], gbc, 'gbc', hb[s], 'xt', f'hb{s}', tmp[s], f'tmp{s}')
                    transpose8(hb[s], f'hb{s}', hT, s * 128, 'hT', 0, 'act')

                def proj_fm(col0, evac):
                    bi = gbank()
                    ps_ = banks[bi][:, 0:TB]
                    for kc in range(8):
                        S.add('pe', ('matmul', A(out=ps_, lhsT=Win[:, kc, col0:col0 + 128],
                                                                       rhs=hT[:, kc, :], start=(kc == 0), stop=(kc == 7))),
                              reads=['Win', 'hT'], writes=[BK(bi)])
                    evac(ps_, BK(bi))

                def evac_q(ps_, bk, h):
                    S.add('dve', ('tensor_copy', A(out=qc[0:64, h, 0:TB], in_=ps_[0:64, :])),
                          reads=['qcz0', 'qcz1'], writes=[bk, f'qc{h}'])
                    S.add('dve', ('tensor_copy', A(out=qc[64:128, h, TB:2 * TB], in_=ps_[64:128, :])),
                          writes=[bk, f'qc{h}'])

                for h in range(4):
                    proj_fm(h * 128, lambda ps_, bk, h=h: evac_q(ps_, bk, h))
                for h in range(4):
                    proj_fm(512 + h * 128, lambda ps_, bk, h=h: S.add(
                        'dve', ('tensor_copy', A(out=kT[:, h, b * TB:(b + 1) * TB], in_=ps_)),
                        writes=[bk, f'kT{h}_{b}']))
                for ts in range(2):
                    bi = gbank()
                    for kc in range(8):
                        S.add('pe', ('matmul', A(
                            out=banks[bi][:, :], lhsT=hT[:, kc, ts * 128:(ts + 1) * 128],
                            rhs=Win[:, kc, 1024:1536], start=(kc == 0), stop=(kc == 7))),
                            reads=['Win', 'hT'], writes=[BK(bi)])
                    S.add('dve', ('tensor_copy', A(out=V[:, b * 2 + ts, :], in_=banks[bi][:, :])),
                          writes=[BK(bi), f'V{b * 2 + ts}'])
                for c in range(4):
                    proj_fm(1536 + c * 128, lambda ps_, bk, c=c: S.add(
                        'act', ('copy', A(out=xbh[:, c, 3:3 + TB], in_=ps_)), reads=['xbh_halo'],
                        writes=[bk, f'xbh{c}']))
                for c in range(4):
                    proj_fm(2048 + c * 128, lambda ps_, bk, c=c: S.add(
                        'act', ('activation', A(out=gg[:, c, :], in_=ps_, func=AF.Gelu_apprx_tanh)),
                        writes=[bk, f'gg{c}']))
                for c in range(4):
                    S.add('dve', ('tensor_scalar', A(out=xc[:, c, :], in0=xbh[:, c, 0:TB], scalar1=cw(0, c),
                                                                scalar2=cb(c), op0=ALU.mult, op1=ALU.add)),
                          reads=[f'xbh{c}', 'xbh_halo', 'pv'], writes=[f'xc{c}'])
                    for w in range(1, 4):
                        S.add('dve', ('scalar_tensor_tensor', A(
                            out=xc[:, c, :], in0=xbh[:, c, w:w + TB], scalar=cw(w, c), in1=xc[:, c, :],
                            op0=ALU.mult, op1=ALU.add)), reads=[f'xbh{c}', 'xbh_halo', 'pv'], writes=[f'xc{c}'])
                    S.add('pool', ('tensor_copy', A(out=xcb[:, c, :], in_=xc[:, c, :])), reads=[f'xc{c}'],
                          writes=[f'xcb{c}'])
                S.add('pool', ('tensor_copy', A(out=xbh[:, :, 0:3], in_=xbh[:, :, TB:TB + 3])),
                      reads=[f'xbh{c}' for c in range(4)] + [f'xc{c}' for c in range(4)], writes=['xbh_halo'])
                for c in range(4):
                    for nm, wt, Tt, hbias in (('r', WA, Tr, hba), ('i', WX, Ti, hbx)):
                        bi = gbank()
                        ps_ = banks[bi][:, 0:TB]
                        S.add('pe', ('matmul', A(out=ps_, lhsT=wt[:, c, :], rhs=xcb[:, c, :],
                                                                            start=True, stop=True)),
                              reads=['WA' if nm == 'r' else 'WX', f'xcb{c}'], writes=[BK(bi)])
                        S.add('act', ('activation', A(
                            out=Tt[:, c, :], in_=ps_, func=AF.Tanh, bias=hbias[:, c:c + 1], scale=0.5)),
                            reads=['hba', 'hbx'],
                            writes=[BK(bi), f'T{nm}{c}'] + ([f'hs{c}'] if nm == 'r' else [f'bt{c}']))
                for c in range(4):
                    S.add('act', ('activation', A(out=aa[:, c, :], in_=Tr[:, c, :], func=AF.Exp,
                                                             bias=sch[:, c:c + 1], scale=sch[:, c:c + 1])),
                          reads=[f'Tr{c}', 'sch'], writes=[f'aa{c}'])
                    S.add('act', ('activation', A(out=a2[:, c, :], in_=Tr[:, c, :], func=AF.Exp,
                                                             bias=sc[:, c:c + 1], scale=sc[:, c:c + 1])),
                          reads=[f'Tr{c}', 'sc'], writes=[f'a2{c}'])
                S.add('act', ('activation', A(out=a2.rearrange("p c t -> p (c t)"), in_=a2.rearrange("p c t -> p (c t)"),
                                              func=AF.Sqrt, bias=qbias[:, 0:1], scale=-0.25)),
                      reads=['qbias'], writes=[f'a2{c}' for c in range(4)])
                for c in range(4):
                    S.add('dve', ('scalar_tensor_tensor', A(out=bt[:, c, :], in0=Ti[:, c, :], scalar=1.0,
                                                                       in1=xc[:, c, :], op0=ALU.add, op1=ALU.mult)),
                          reads=[f'Ti{c}', f'xc{c}'], writes=[f'bt{c}', f'Ti{c}'])
                    S.add('dve', ('tensor_tensor', A(out=bt[:, c, :], in0=bt[:, c, :], in1=a2[:, c, :], op=ALU.mult)),
                          reads=[f'a2{c}'], writes=[f'bt{c}'])
                    S.add('dve', ('tensor_tensor_scan', A(out=hs[:, c, :], data0=aa[:, c, :], data1=bt[:, c, :],
                                                                     initial=hprev[:, c:c + 1], op0=ALU.mult, op1=ALU.add)),
                          reads=[f'aa{c}', f'bt{c}', 'hprev'], writes=[f'hs{c}', f'Tr{c}'])
                    S.add('dve', ('tensor_tensor', A(out=yT[:, 4 + c, :], in0=gg[:, c, :], in1=hs[:, c, :], op=ALU.mult)),
                          reads=[f'gg{c}', f'hs{c}'], writes=[f'yT{4 + c}'])
                S.add('pool', ('tensor_copy', A(out=hprev, in_=hs[:, :, TB - 1])),
                      reads=[f'hs{c}' for c in range(4)], writes=['hprev'])
                nkc = 2 * b + 2
                for h in range(4):
                    ob = 4 + h % 2
                    sb_ = 6 + h % 2
                    for kc in range(nkc):
                        j = kc - 2 * b
                        si = 2 + kc % 2
                        S.add('pe', ('matmul', A(
                            out=banks[si][:, :], lhsT=kT[:, h, kc * 128:(kc + 1) * 128], rhs=qc[:, h, :],
                            start=True, stop=True)), reads=[f'kT{h}_{kc // 2}', f'qc{h}'], writes=[BK(si)])
                        pi = pcount[0] % NPT
                        pcount[0] += 1
                        pt = pT[pi]
                        pkey = f'pT{pi}'
                        S.add('act', ('activation', A(out=pt, in_=banks[si][:, :], func=AF.Exp, scale=0.125)),
                              writes=[BK(si), pkey])
                        if j >= 0:
                            c0 = j * 128
                            if c0 > 0:
                                S.add('pool', ('memset', A(pt[:, 0:c0], 0.0)), writes=[pkey])
                                S.add('pool', ('memset', A(pt[:, TB:TB + c0], 0.0)), writes=[pkey])
                            S.add('pool', ('memset', A(pt[64:128, c0:c0 + 64], 0.0)), writes=[pkey])
                            S.add('pool', ('memset', A(pt[64:128, TB + c0:TB + c0 + 64], 0.0)), writes=[pkey])
                        S.add('pe', ('matmul', A(
                            out=banks[ob][:, :], lhsT=V[:, kc, h * 128:(h + 1) * 128], rhs=pt,
                            start=(kc == 0), stop=(kc == nkc - 1))), reads=[f'V{kc}', pkey], writes=[BK(ob)])
                        S.add('pe', ('matmul', A(
                            out=banks[sb_][:, :], lhsT=ones_bf[:], rhs=pt,
                            start=(kc == 0), stop=(kc == nkc - 1))), reads=['ones_bf', pkey], writes=[BK(sb_)])
                    S.add('dve', ('reciprocal', A(out=fr, in_=banks[sb_][:, :])), writes=[BK(sb_), 'fr'])
                    S.add('dve', ('tensor_tensor', A(out=ft, in0=banks[ob][:, :], in1=fr, op=ALU.mult)),
                          reads=['fr'], writes=[BK(ob), 'ft'])
                    S.add('dve', ('scalar_tensor_tensor', A(out=fo4[:, h, :], in0=ft[:, TB:2 * TB], scalar=neglam,
                                                           in1=ft[:, 0:TB], op0=ALU.mult, op1=ALU.add)),
                          reads=['ft', 'neglam'], writes=[f'fo{h}'])
                    S.add('dve', ('tensor_tensor', A(out=sd[:, h, :], in0=fo4[:, h, :], in1=fo4[:, h, :], op=ALU.mult)),
                          reads=[f'fo{h}'], writes=[f'sd{h}'])
                    mb = 1 - h // 2
                    S.add('pe', ('matmul', A(out=banks[mb][:, (h % 2) * TB:(h % 2 + 1) * TB], lhsT=ones_f[:], rhs=sd[:, h, :],
                                             start=True, stop=True)),
                          reads=['ones_f', f'sd{h}'], writes=[BK(mb)])
                for hh in range(2):
                    mb = 1 - hh
                    S.add('act', ('activation', A(out=sd[:, 2 * hh:2 * hh + 2, :].rearrange("p h t -> p (h t)"),
                                                  in_=banks[mb][:, :], func=AF.Sqrt, bias=epsb[:, 0:1], scale=1.0 / 128.0)),
                          reads=['epsb'], writes=[BK(mb), f'sd{2 * hh}', f'sd{2 * hh + 1}'])
                S.add('dve', ('reciprocal', A(out=sd.rearrange("p h t -> p (h t)"), in_=sd.rearrange("p h t -> p (h t)"))),
                      writes=[f'sd{h}' for h in range(4)])
                S.add('dve', ('scalar_tensor_tensor', A(out=yT[:, 0:4, :].rearrange("p h t -> p (h t)"),
                                                       in0=fo4.rearrange("p h t -> p (h t)"), scalar=g1,
                                                       in1=sd.rearrange("p h t -> p (h t)"), op0=ALU.mult, op1=ALU.mult)),
                      reads=[f'fo{h}' for h in range(4)] + ['g1'] + [f'sd{h}' for h in range(4)],
                      writes=[f'yT{h}' for h in range(4)])
                for ts in range(2):
                    for dh in range(2):
                        bi = gbank()
                        for kc in range(8):
                            S.add('pe', ('matmul', A(
                                out=banks[bi][:, :], lhsT=yT[:, kc, ts * 128:(ts + 1) * 128],
                                rhs=Wout[:, kc, dh * 512:(dh + 1) * 512], start=(kc == 0), stop=(kc == 7))),
                                reads=['Wout', f'yT{kc}'], writes=[BK(bi)])
                        S.add('dve', ('tensor_tensor', A(
                            out=xt[:, ts, dh * 512:(dh + 1) * 512], in0=banks[bi][:, :],
                            in1=xt[:, ts, dh * 512:(dh + 1) * 512], op=ALU.add)),
                            writes=[BK(bi), 'xt'])
                S.add('sp', ('dma_start', A(
                    out=dst[b * TB:(b + 1) * TB, :].rearrange("(s p) d -> p s d", p=128), in_=xt)),
                    reads=['xt'], dma=1)

        bufs = [scrA, scrB]
        cur = x_in
        nb = 0
        if passes is None:
            passes_l = []
            for layer in layers:
                passes_l.append(('even' if layer % 2 == 0 else 'odd', layer))
                passes_l.append(('ffn', layer))
        else:
            passes_l = list(passes)
        for pi, (kind, layer) in enumerate(passes_l):
            last = (pi == len(passes_l) - 1)
            d = out if last else bufs[nb % 2]
            nb += 1
            if kind == 'even':
                even_pass(layer, cur, d)
            elif kind == 'odd':
                odd_pass(layer, cur, d)
            else:
                ffn_pass(layer, cur, d, apply_final=(last and final_norm))
            cur = d
        if not final_ops:
            for q in S.dma_slots.values():
                for op in q:
                    if op is not None:
                        final_ops.append(op)
        else:
            for q in S.dma_slots.values():
                for op in q:
                    if op is not None and op not in final_ops:
                        final_ops.append(op)
        S.emit(nc, final_wait_ops=final_ops)
    return nc


def prep_weights(inp):
    f = lambda a: np.ascontiguousarray(np.asarray(a, dtype=np.float32))
    w = {}
    for k in ('norm_mix', 'norm_ffn', 'norm_final', 'ab_w_in', 'ab_w_out', 'lru_wa', 'lru_wx', 'c_w_in', 'c_ln_g',
              'c_ln_b', 'c_w_out', 'ffn_w1', 'ffn_w2'):
        w[k] = f(inp[k])
    w['diff_l'] = f(np.concatenate([inp['diff_lq1'], inp['diff_lk1'], inp['diff_lq2'], inp['diff_lk2']], axis=1))
    pv = []
    for e in range(2):
        cols = []
        cwv = np.asarray(inp['lru_conv_w'][e])
        cols.append(cwv.reshape(4, 4, 128).transpose(2, 0, 1).reshape(128, 16))
        for nm in ('lru_conv_b', 'lru_ba', 'lru_bx', 'lru_lambda'):
            cols.append(np.asarray(inp[nm][e]).reshape(4, 128).T)
        cols.append(np.asarray(inp['diff_subln'][e]).reshape(128, 1))
        pv.append(np.concatenate(cols, axis=1))
    w['pvec'] = f(np.stack(pv))
    w['c_w_sT'] = f(np.transpose(np.asarray(inp['c_w_s']), (0, 3, 1, 2)))
    w['c_b_sT'] = f(np.transpose(np.asarray(inp['c_b_s']), (0, 2, 1)))
    return w


_NC_CACHE = {}


def kernel(**inputs):
    x = np.asarray(inputs['x'], dtype=np.float32)
    B, S_LEN, _ = x.shape
    w = prep_weights(inputs)
    key = (S_LEN,)
    if key not in _NC_CACHE:
        _NC_CACHE[key] = build_program(S_LEN)
    nc = _NC_CACHE[key]
    in_maps = []
    for c in range(B):
        m = dict(w)
        m['x'] = np.ascontiguousarray(x[c])
        in_maps.append(m)
    res = run_bass_kernel_spmd(nc, in_maps, core_ids=list(range(B)))
    return np.stack([np.asarray(r['out'], dtype=np.float32) for r in res.results], axis=0)
```

```python
import contextlib
import math

import numpy as np
import concourse.bass as bass
import concourse.mybir as mybir
from concourse.bass_utils import run_bass_kernel_spmd

F32 = mybir.dt.float32
BF16 = mybir.dt.bfloat16
U8 = mybir.dt.uint8
AF = mybir.ActivationFunctionType
ALU = mybir.AluOpType

D = 1024
TB = 256
EPS = 1e-6
ENGS = ('pe', 'act', 'dve', 'pool', 'sp')
SEM_CAP = 30000
DBG = {'ffn': 9, 'noW': 0}


def A(*a, **k):
    return (a, k)


class Op:
    __slots__ = ('eng', 'fn', 'deps', 'is_dma', 'ndma', 'slot', 'target', 'epoch', 'msval')


class Sched:
    def __init__(self, nslots=8):
        self.ops = {e: [] for e in ENGS}
        self.lastw = {}
        self.readers = {}
        self.nslots = nslots
        self.dma_slots = {}
        self.dma_count = {e: 0 for e in ENGS}
        self.fence_deps = set()

    def add(self, eng, fn, reads=(), writes=(), dma=0):
        op = Op()
        op.eng = eng
        op.fn = fn
        op.is_dma = dma > 0
        op.ndma = dma
        op.slot = None
        op.target = None
        op.epoch = None
        op.msval = None
        deps = set(self.fence_deps)
        for k in reads:
            w = self.lastw.get(k)
            if w is not None:
                deps.add(w)
        for k in writes:
            w = self.lastw.get(k)
            if w is not None:
                deps.add(w)
            for r in self.readers.get(k, ()):
                deps.add(r)
        if dma:
            q = self.dma_slots.setdefault(eng, [None] * self.nslots)
            s = self.dma_count[eng] % self.nslots
            self.dma_count[eng] += 1
            if q[s] is not None:
                deps.add(q[s])
            q[s] = op
            op.slot = (eng, s)
        deps.discard(op)
        op.deps = deps
        for k in reads:
            self.readers.setdefault(k, []).append(op)
        for k in writes:
            self.lastw[k] = op
            self.readers[k] = []
        self.ops[eng].append(op)
        return op

    def fence(self):
        deps = set()
        for e in ENGS:
            last = None
            for op in reversed(self.ops[e]):
                if not op.is_dma:
                    last = op
                    break
            if last is not None:
                deps.add(last)
        for q in self.dma_slots.values():
            for op in q:
                if op is not None:
                    deps.add(op)
        self.fence_deps = deps
        self.lastw = {}
        self.readers = {}

    def emit(self, nc, final_wait_ops=()):
        needed = set()
        for e in ENGS:
            for op in self.ops[e]:
                for d in op.deps:
                    if e == 'pe' and d.eng == 'pe' and not d.is_dma:
                        continue
                    needed.add(d)
        for d in final_wait_ops:
            needed.add(d)
        n_epochs = {}
        for e in ENGS:
            c = 0
            ep = 0
            for op in self.ops[e]:
                if (not op.is_dma) and op in needed:
                    if c >= SEM_CAP:
                        ep += 1
                        c = 0
                    c += 1
                    op.epoch = ep
                    op.msval = c
            n_epochs[e] = ep + 1
        slot_cnt = {}
        for e in ENGS:
            for op in self.ops[e]:
                if op.is_dma:
                    slot_cnt[op.slot] = slot_cnt.get(op.slot, 0) + 16 * op.ndma
                    op.target = slot_cnt[op.slot]
        with contextlib.ExitStack() as st:
            esem = {}
            for e in ENGS:
                for ep in range(n_epochs[e]):
                    esem[(e, ep)] = st.enter_context(nc.semaphore(f"s_{e}_{ep}"))
            dsem = {}
            for slot in slot_cnt:
                dsem[slot] = st.enter_context(nc.semaphore(f"d_{slot[0]}_{slot[1]}"))
            block = st.enter_context(nc.Block())
            ops = self.ops

            def run(e, eng):
                seen_e = {}
                seen_d = {}
                for op in ops[e]:
                    self._waits(e, eng, op.deps, seen_e, seen_d, esem, dsem)
                    name, (pa, kw) = op.fn
                    if op.is_dma:
                        getattr(eng, name)(*pa, **kw).then_inc(dsem[op.slot], 16)
                    else:
                        ins = getattr(eng, name)(*pa, **kw)
                        if op.msval is not None:
                            ins.then_inc(esem[(e, op.epoch)], 1)
                if e == 'sp' and final_wait_ops:
                    self._waits(e, eng, final_wait_ops, seen_e, seen_d, esem, dsem)

            @block.tensor
            def _(eng):
                run('pe', eng)

            @block.scalar
            def _(eng):
                run('act', eng)

            @block.vector
            def _(eng):
                run('dve', eng)

            @block.gpsimd
            def _(eng):
                run('pool', eng)

            @block.sync
            def _(eng):
                run('sp', eng)

    @staticmethod
    def _waits(e, eng, deps, seen_e, seen_d, esem, dsem):
        best_e = {}
        best_d = {}
        for d in deps:
            if d.is_dma:
                if best_d.get(d.slot, 0) < d.target:
                    best_d[d.slot] = d.target
            else:
                if e == 'pe' and d.eng == 'pe':
                    continue
                v = (d.epoch, d.msval)
                if best_e.get(d.eng, (-1, 0)) < v:
                    best_e[d.eng] = v
        for slot, t in best_d.items():
            if seen_d.get(slot, 0) >= t:
                continue
            seen_d[slot] = t
            eng.wait_ge(dsem[slot], t)
        for de, v in best_e.items():
            if seen_e.get(de, (-1, 0)) >= v:
                continue
            seen_e[de] = v
            eng.wait_ge(esem[(de, v[0])], v[1])


class Region:
    def __init__(self, raw, nbytes):
        self.raw = raw
        self.nbytes = nbytes
        self.off = 0

    def reset(self):
        self.off = 0

    def alloc(self, cols, dt):
        esz = 4 if dt == F32 else 2
        nb = cols * esz
        nb_al = (nb + 63) // 64 * 64
        assert self.off + nb_al <= self.nbytes, (self.off, nb_al, self.nbytes)
        v = self.raw[:, self.off:self.off + nb].bitcast(dt)
        self.off += nb_al
        return v


def build_program(S_LEN=4096, layers=(0, 1, 2, 3), final_norm=True, passes=None):
    nblk = S_LEN // TB
    nc = bass.Bass("TRN2", target_bir_lowering=False)

    def din(name, shape):
        return nc.dram_tensor(name, list(shape), F32, kind="ExternalInput").ap()

    x_in = din("x", [S_LEN, D])
    norm_mix = din("norm_mix", [4, D])
    norm_ffn = din("norm_ffn", [4, D])
    norm_final = din("norm_final", [D])
    ab_w_in = din("ab_w_in", [2, D, 2560])
    ab_w_out = din("ab_w_out", [2, D, D])
    diff_l = din("diff_l", [2, 256])
    pvec = din("pvec", [2, 128, 33])
    lru_wa = din("lru_wa", [2, 8, 64, 64])
    lru_wx = din("lru_wx", [2, 8, 64, 64])
    c_w_in = din("c_w_in", [2, D, 2048])
    c_ln_g = din("c_ln_g", [2, D])
    c_ln_b = din("c_ln_b", [2, D])
    c_w_sT = din("c_w_sT", [2, 128, 8, 128])
    c_b_sT = din("c_b_sT", [2, 128, 8])
    c_w_out = din("c_w_out", [2, D, D])
    ffn_w1 = din("ffn_w1", [4, D, 4096])
    ffn_w2 = din("ffn_w2", [4, 4096, D])
    out = nc.dram_tensor("out", [S_LEN, D], F32, kind="ExternalOutput").ap()
    scrA = nc.dram_tensor("scrA", [S_LEN, D], F32, kind="Internal").ap()
    scrB = nc.dram_tensor("scrB", [S_LEN, D], F32, kind="Internal").ap()

    S = Sched()
    st = contextlib.ExitStack()
    with st:
        def sb(name, shape, dt):
            return st.enter_context(nc.sbuf_tensor(name, shape, dt))

        WBYTES = DBG.get('WBYTES', 126976)
        ABYTES = 77824
        wraw = sb("wraw", [128, WBYTES], U8)
        araw = sb("araw", [128, ABYTES], U8)
        WR = Region(wraw, WBYTES)
        AR = Region(araw, ABYTES)
        ident = sb("ident", [128, 128], BF16)
        identf = sb("identf", [128, 128], F32)
        ones_bf = sb("ones_bf", [128, 128], BF16)
        ones_f = sb("ones_f", [128, 128], F32)
        neghalf = sb("neghalf", [128, 1], F32)
        poshalf = sb("poshalf", [128, 1], F32)
        expbias = sb("expbias", [128, 1], F32)
        qbias = sb("qbias", [128, 1], F32)
        epsb = sb("epsb", [128, 1], F32)
        banks = [st.enter_context(nc.psum_tensor(f"bank{i}", [128, 512], F32)) for i in range(8)]

        def bank_bf(i):
            return banks[i][:, :].bitcast(BF16).rearrange("p (k t) -> p k t", k=8)

        S.add('pool', ('memset', A(identf[:], 1.0)), writes=['identf'])
        S.add('pool', ('affine_select', A(out=identf[:], in_=identf[:], pattern=[[-1, 128]],
                                                compare_op=ALU.is_equal, fill=0.0, base=0,
                                                channel_multiplier=1)), reads=['identf'], writes=['identf'])
        S.add('pool', ('tensor_copy', A(out=ident[:], in_=identf[:])), reads=['identf'], writes=['ident'])
        S.add('pool', ('memset', A(ones_bf[:], 1.0)), writes=['ones_bf'])
        S.add('pool', ('memset', A(ones_f[:], 1.0)), writes=['ones_f'])
        S.add('pool', ('memset', A(neghalf[:], -0.5)), writes=['neghalf'])
        S.add('pool', ('memset', A(poshalf[:], 0.5)), writes=['poshalf'])
        S.add('pool', ('memset', A(expbias[:], 0.0)), writes=['expbias'])
        S.add('pool', ('memset', A(qbias[:], 0.25)), writes=['qbias'])
        S.add('pool', ('memset', A(epsb[:], EPS)), writes=['epsb'])
        S.add('pool', ('memset', A(expbias[64:128, :], -30000.0)), reads=['expbias'], writes=['expbias'])
        CONST_KEYS = ['ident', 'ones_bf', 'ones_f', 'neghalf', 'poshalf', 'expbias']

        def after_fence_consts():
            pass

        def load_w(dst3, src2, name, kc_n, cols):
            src3 = src2.rearrange("(kc p) n -> p kc n", p=128)
            cstep = cols
            while cstep > 2048:
                cstep //= 2
            kstep = max(1, 2048 // cstep)
            keys = []
            for k0 in range(0, kc_n, kstep):
                for c0 in range(0, cols, cstep):
                    key = f"{name}_{k0}_{c0}"
                    keys.append(key)
                    S.add('pool', ('dma_start', A(
                        out=dst3[:, k0:k0 + kstep, c0:c0 + cstep],
                        in_=src3[:, k0:k0 + kstep, c0:c0 + cstep])),
                        writes=[key], dma=1)
            return keys

        def join(keys, name, dummy):
            S.add('pool', ('memset', A(dummy, 0.0)), reads=keys, writes=[name])

        def load_bcast(dst, src1d, name):
            S.add('sp', ('dma_start', A(out=dst, in_=src1d.partition_broadcast(128))),
                  writes=[name], dma=1)

        def load_x(xt3, src, blk, key):
            S.add('sp', ('dma_start', A(
                out=xt3, in_=src[blk * TB:(blk + 1) * TB, :].rearrange("(s p) d -> p s d", p=128))),
                writes=[key], dma=1)

        def store_x(dst, xo3, blk, key):
            return S.add('sp', ('dma_start', A(
                out=dst[blk * TB:(blk + 1) * TB, :].rearrange("(s p) d -> p s d", p=128), in_=xo3)),
                reads=[key], dma=1)

        def rms_norm(x2, gbc, gkey, h2, xkey, hkey, tmp, tkey):
            stt = tmp[:, 0:12]
            mv = tmp[:, 12:14]
            ms = tmp[:, 14:15]
            rs = tmp[:, 15:16]
            S.add('dve', ('bn_stats', A(out=stt[:, 0:6], in_=x2[:, 0:512])), reads=[xkey], writes=[tkey + 'a'])
            S.add('dve', ('bn_stats', A(out=stt[:, 6:12], in_=x2[:, 512:1024])), reads=[xkey], writes=[tkey + 'b'])
            S.add('dve', ('bn_aggr', A(out=mv, in_=stt)), reads=[tkey + 'a', tkey + 'b'], writes=[tkey + 'mv'])
            S.add('dve', ('tensor_scalar', A(out=ms, in0=mv[:, 0:1], scalar1=mv[:, 0:1], scalar2=mv[:, 1:2],
                                                   op0=ALU.mult, op1=ALU.add)), reads=[tkey + 'mv'], writes=[tkey + 'ms'])
            S.add('dve', ('tensor_scalar', A(out=ms, in0=ms, scalar1=EPS, scalar2=None, op0=ALU.add)),
                  reads=[tkey + 'ms'], writes=[tkey + 'ms'])
            S.add('pool', ('tensor_tensor', A(out=rs, in0=ms, in1=neghalf[:, 0:1], op=ALU.pow)),
                  reads=[tkey + 'ms', 'neghalf'], writes=[tkey + 'rs'])
            S.add('dve', ('scalar_tensor_tensor', A(out=h2, in0=x2, scalar=rs, in1=gbc, op0=ALU.mult, op1=ALU.mult)),
                  reads=[xkey, tkey + 'rs', gkey], writes=[hkey])

        def transpose8(h2, hkey, hT3, col0, hTkey, bank_i, evac_eng):
            tp = bank_bf(bank_i)
            bkey = f"bank{bank_i}"
            for kc in range(8):
                S.add('pe', ('transpose', A(out=tp[:, kc, :], in_=h2[:, kc * 128:(kc + 1) * 128],
                                                         identity=ident[:])),
                      reads=[hkey, 'ident'], writes=[bkey])
            if evac_eng == 'act':
                S.add('act', ('copy', A(out=hT3[:, :, col0:col0 + 128], in_=tp)), writes=[bkey, hTkey])
            else:
                S.add('dve', ('tensor_copy', A(out=hT3[:, :, col0:col0 + 128], in_=tp)), writes=[bkey, hTkey])

        final_ops = []

        def BK(i):
            return f'bank{i}'

        def norm_stats(x2, xkeys, t, tk):
            stt = t[:, 0:12]
            mv = t[:, 12:14]
            ms = t[:, 14:15]
            rs = t[:, 15:16]
            S.add('dve', ('bn_stats', A(out=stt[:, 0:6], in_=x2[:, 0:512])), reads=xkeys, writes=[tk + 'a'])
            S.add('dve', ('bn_stats', A(out=stt[:, 6:12], in_=x2[:, 512:1024])), reads=xkeys, writes=[tk + 'b'])
            S.add('dve', ('bn_aggr', A(out=mv, in_=stt)), reads=[tk + 'a', tk + 'b'], writes=[tk + 'mv'])
            S.add('dve', ('tensor_scalar', A(out=ms, in0=mv[:, 0:1], scalar1=mv[:, 0:1], scalar2=mv[:, 1:2],
                                                   op0=ALU.mult, op1=ALU.add)), reads=[tk + 'mv'], writes=[tk + 'ms'])
            S.add('dve', ('tensor_scalar', A(out=ms, in0=ms, scalar1=EPS, scalar2=None, op0=ALU.add)),
                  reads=[tk + 'ms'], writes=[tk + 'ms'])
            S.add('pool', ('tensor_tensor', A(out=rs, in0=ms, in1=neghalf[:, 0:1], op=ALU.pow)),
                  reads=[tk + 'ms', 'neghalf'], writes=[tk + 'rs'])
            return rs

        def ffn_pass(layer, src, dst, apply_final):
            S.fence()
            WR.reset()
            AR.reset()
            W1 = WR.alloc(8 * 4096, BF16).rearrange("p (k f) -> p k f", k=8)
            W2a = WR.alloc(30 * 1024, BF16).rearrange("p (k f) -> p k f", k=30)
            W2b = AR.alloc(2 * 1024, BF16).rearrange("p (k f) -> p k f", k=2)
            W2v = [W2a[:, fc, :] if fc < 30 else W2b[:, fc - 30, :] for fc in range(32)]
            gbc = AR.alloc(1024, F32)
            gfin = AR.alloc(1024, F32) if apply_final else None
            dummy = AR.alloc(16, F32)
            xts = [AR.alloc(2048, F32).rearrange("p (s d) -> p s d", s=2) for _ in range(2)]
            xos = [AR.alloc(2048, F32).rearrange("p (s d) -> p s d", s=2) for _ in range(2)]
            hb = [AR.alloc(1024, BF16) for _ in range(2)]
            hT = [AR.alloc(8 * TB, BF16).rearrange("p (k t) -> p k t", k=8) for _ in range(2)]
            uT = AR.alloc(32 * TB, BF16).rearrange("p (k t) -> p k t", k=32)
            rt = [AR.alloc(TB, F32) for _ in range(2)]
            tmp = [AR.alloc(16, F32) for _ in range(2)]

            load_bcast(gbc, norm_ffn[layer], 'gbc')
            if apply_final:
                load_bcast(gfin, norm_final, 'gfin')
            load_x(xts[0], src, 0, 'xt0')
            k1 = load_w(W1, ffn_w1[layer], 'W1', 8, 4096)
            k2 = load_w(W2a, ffn_w2[layer][0:30 * 128, :], 'W2a', 30, 1024)
            k2 += load_w(W2b, ffn_w2[layer][30 * 128:32 * 128, :], 'W2b', 2, 1024)
            join(k1, 'W1', dummy[:, 0:4])
            join(k2, 'W2', dummy[:, 4:8])

            def f_norm(b):
                p = b % 2
                for s in range(2):
                    rms_norm(xts[p][:, s, :], gbc, 'gbc', hb[s], f'xt{p}', f'hb{s}', tmp[s], f'tmp{s}')

            def f_tr(b):
                p = b % 2
                for s in range(2):
                    transpose8(hb[s], f'hb{s}', hT[p], s * 128, f'hT{p}', s, 'act')

            def f_w1(b):
                p = b % 2
                for fc in range(32):
                    bi = 2 + fc % 4
                    ups = banks[bi][:, 0:TB]
                    for kc in range(8):
                        S.add('pe', ('matmul', A(
                            out=ups, lhsT=W1[:, kc, fc * 128:(fc + 1) * 128], rhs=hT[p][:, kc, :],
                            start=(kc == 0), stop=(kc == 7))),
                            reads=['W1', f'hT{p}'], writes=[BK(bi)])
                    r = rt[fc % 2]
                    S.add('act', ('activation', A(out=r, in_=ups, func=AF.Relu)),
                          writes=[BK(bi), f'rt{fc % 2}'])
                    S.add('dve', ('tensor_tensor', A(out=uT[:, fc, :], in0=r, in1=r, op=ALU.mult)),
                          reads=[f'rt{fc % 2}'], writes=[f'uT{fc}'])

            def f_w2(b, gi):
                p = b % 2
                xt = xts[p]
                xo = xos[p]
                ts, dh = gi // 2, gi % 2
                bi = 6 + gi % 2
                acc = banks[bi]
                for fc in range(32):
                    S.add('pe', ('matmul', A(
                        out=acc[:, :], lhsT=uT[:, fc, ts * 128:(ts + 1) * 128],
                        rhs=W2v[fc][:, dh * 512:(dh + 1) * 512], start=(fc == 0), stop=(fc == 31))),
                        reads=['W2', f'uT{fc}'], writes=[BK(bi)])
                S.add('dve', ('tensor_tensor', A(
                    out=xo[:, ts, dh * 512:(dh + 1) * 512], in0=acc[:, :], in1=xt[:, ts, dh * 512:(dh + 1) * 512],
                    op=ALU.add)), reads=[f'xt{p}'], writes=[BK(bi), f'xo{p}_{ts}'])

            def f_store(b):
                p = b % 2
                xo = xos[p]
                okeys = [f'xo{p}_0', f'xo{p}_1']
                if apply_final:
                    for s in range(2):
                        x2 = xo[:, s, :]
                        rs = norm_stats(x2, [f'xo{p}_{s}'], tmp[s], f'tmp{s}')
                        S.add('dve', ('scalar_tensor_tensor', A(out=x2, in0=x2, scalar=rs, in1=gfin,
                                                               op0=ALU.mult, op1=ALU.mult)),
                              reads=[f'tmp{s}rs', 'gfin'], writes=[f'xo{p}_{s}'])
                op = S.add('sp', ('dma_start', A(
                    out=dst[b * TB:(b + 1) * TB, :].rearrange("(s p) d -> p s d", p=128), in_=xo)),
                    reads=okeys, dma=1)
                if apply_final:
                    final_ops.append(op)

            if nblk > 1:
                load_x(xts[1], src, 1, 'xt1')
            f_norm(0)
            f_tr(0)
            for b in range(nblk):
                p = b % 2
                f_w1(b)
                if b + 1 < nblk:
                    f_norm(b + 1)
                f_w2(b, 0)
                f_w2(b, 1)
                if b + 1 < nblk:
                    f_tr(b + 1)
                f_w2(b, 2)
                f_w2(b, 3)
                f_store(b)
                if b + 2 < nblk:
                    load_x(xts[p], src, b + 2, f'xt{p}')

        def odd_pass(layer, src, dst):
            o = layer // 2
            S.fence()
            WR.reset()
            AR.reset()
            Wci = WR.alloc(8 * 2048, BF16).rearrange("p (k f) -> p k f", k=8)
            Wco = WR.alloc(8 * 1024, BF16).rearrange("p (k f) -> p k f", k=8)
            WsT = WR.alloc(8 * 128, BF16).rearrange("p (g t) -> p g t", g=8)
            lng = WR.alloc(1024, F32)
            lnb = WR.alloc(1024, F32)
            gbc = WR.alloc(1024, F32)
            bsT = WR.alloc(8, F32)
            dummy = WR.alloc(16, F32)
            zu = [WR.alloc(1024, F32) for _ in range(2)]
            zv = [WR.alloc(1024, F32) for _ in range(2)]
            vt = [WR.alloc(1024, F32) for _ in range(2)]
            xts = [AR.alloc(2048, F32).rearrange("p (s d) -> p s d", s=2) for _ in range(2)]
            xos = [AR.alloc(2048, F32).rearrange("p (s d) -> p s d", s=2) for _ in range(2)]
            hb = [AR.alloc(1024, BF16) for _ in range(2)]
            hT = [AR.alloc(8 * 128, BF16).rearrange("p (k t) -> p k t", k=8) for _ in range(2)]
            vln = [AR.alloc(1024, BF16) for _ in range(2)]
            sg = [AR.alloc(1024, BF16) for _ in range(2)]
            sT = [AR.alloc(8 * 128, BF16).rearrange("p (k t) -> p k t", k=8) for _ in range(2)]
            tmp = [AR.alloc(16, F32) for _ in range(2)]
            tmp2 = [AR.alloc(16, F32) for _ in range(2)]

            load_bcast(gbc, norm_mix[layer], 'gbc')
            load_bcast(lng, c_ln_g[o], 'lng')
            load_bcast(lnb, c_ln_b[o], 'lnb')
            S.add('sp', ('dma_start', A(out=bsT, in_=c_b_sT[o])), writes=['bsT'], dma=1)
            load_x(xts[0], src, 0, 'xt0')
            k1 = load_w(Wci, c_w_in[o], 'Wci', 8, 2048)
            k2 = load_w(Wco, c_w_out[o], 'Wco', 8, 1024)
            S.add('pool', ('dma_start', A(out=WsT, in_=c_w_sT[o])), writes=['WsT_raw'], dma=1)
            join(k1, 'Wci', dummy[:, 0:4])
            join(k2, 'Wco', dummy[:, 4:8])
            S.add('pool', ('memset', A(WsT[64:128, :, 0:64], 0.0)), reads=['WsT_raw'], writes=['WsT'])

            def stage_a(b, s):
                p = b % 2
                xt = xts[p]
                x2 = xt[:, s, :]
                rms_norm(x2, gbc, 'gbc', hb[s], f'xt{p}', f'hb{s}', tmp[s], f'tmp{s}')
                transpose8(hb[s], f'hb{s}', hT[s], 0, f'hT{s}', 0, 'act')
                for cg in range(4):
                    bi = 2 + cg
                    for kc in range(8):
                        S.add('pe', ('matmul', A(
                            out=banks[bi][:, :], lhsT=hT[s][:, kc, :], rhs=Wci[:, kc, cg * 512:(cg + 1) * 512],
                            start=(kc == 0), stop=(kc == 7))), reads=['Wci', f'hT{s}'], writes=[BK(bi)])
                    dstz = (zu[s] if cg < 2 else zv[s])[:, (cg % 2) * 512:(cg % 2 + 1) * 512]
                    S.add('act', ('activation', A(out=dstz, in_=banks[bi][:, :], func=AF.Gelu_apprx_tanh)),
                          writes=[BK(bi), f'z{s}_{cg}'])

            def stage_b(b, s):
                p = b % 2
                xt = xts[p]
                xo = xos[p]
                t2 = tmp2[s]
                stt = t2[:, 0:12]
                mv = t2[:, 12:14]
                ve = t2[:, 14:15]
                rs = t2[:, 15:16]
                zvs = zv[s]
                S.add('dve', ('bn_stats', A(out=stt[:, 0:6], in_=zvs[:, 0:512])), reads=[f'z{s}_2'], writes=[f't2{s}a'])
                S.add('dve', ('bn_stats', A(out=stt[:, 6:12], in_=zvs[:, 512:1024])), reads=[f'z{s}_3'], writes=[f't2{s}b'])
                S.add('dve', ('bn_aggr', A(out=mv, in_=stt)), reads=[f't2{s}a', f't2{s}b'], writes=[f't2{s}mv'])
                S.add('dve', ('tensor_scalar', A(out=ve, in0=mv[:, 1:2], scalar1=EPS, scalar2=None, op0=ALU.add)),
                      reads=[f't2{s}mv'], writes=[f't2{s}ve'])
                S.add('pool', ('tensor_tensor', A(out=rs, in0=ve, in1=neghalf[:, 0:1], op=ALU.pow)),
                      reads=[f't2{s}ve', 'neghalf'], writes=[f't2{s}rs'])
                vts = vt[s]
                S.add('dve', ('tensor_scalar', A(out=vts, in0=zvs, scalar1=mv[:, 0:1], scalar2=rs,
                                                 op0=ALU.subtract, op1=ALU.mult)),
                      reads=[f'z{s}_2', f'z{s}_3', f't2{s}mv', f't2{s}rs'], writes=[f'vt{s}'])
                S.add('pool', ('tensor_tensor', A(out=vts, in0=vts, in1=lng, op=ALU.mult)), reads=['lng'], writes=[f'vt{s}'])
                vl = vln[s]
                S.add('pool', ('tensor_tensor', A(out=vl, in0=vts, in1=lnb, op=ALU.add)), reads=[f'vt{s}', 'lnb'],
                      writes=[f'vln{s}'])
                sgs = sg[s]
                zus = zu[s]
                for gh in range(2):
                    bi = 6 + gh
                    for g in range(4 * gh, 4 * gh + 4):
                        mps = banks[bi][:, (g % 4) * 128:(g % 4 + 1) * 128]
                        S.add('pe', ('matmul', A(out=mps, lhsT=WsT[:, g, :], rhs=vl[:, g * 128:(g + 1) * 128],
                                                 start=True, stop=True)), reads=['WsT', f'vln{s}'], writes=[BK(bi)])
                    for g in range(4 * gh, 4 * gh + 4):
                        mps = banks[bi][:, (g % 4) * 128:(g % 4 + 1) * 128]
                        S.add('dve', ('scalar_tensor_tensor', A(
                            out=sgs[:, g * 128:(g + 1) * 128], in0=mps, scalar=bsT[:, g:g + 1],
                            in1=zus[:, g * 128:(g + 1) * 128], op0=ALU.add, op1=ALU.mult)),
                            reads=['bsT', f'z{s}_{g // 4}'], writes=[BK(bi), f'sg{s}_{g}'])
                tp = bank_bf(1)
                for kc in range(8):
                    S.add('pe', ('transpose', A(out=tp[:, kc, :], in_=sgs[:, kc * 128:(kc + 1) * 128], identity=ident[:])),
                          reads=[f'sg{s}_{kc}', 'ident'], writes=[BK(1)])
                S.add('act', ('copy', A(out=sT[s], in_=tp)), writes=[BK(1), f'sT{s}'])
                for dh in range(2):
                    bi = 6 + dh
                    for kc in range(8):
                        S.add('pe', ('matmul', A(
                            out=banks[bi][:, :], lhsT=sT[s][:, kc, :], rhs=Wco[:, kc, dh * 512:(dh + 1) * 512],
                            start=(kc == 0), stop=(kc == 7))), reads=['Wco', f'sT{s}'], writes=[BK(bi)])
                    S.add('dve', ('tensor_tensor', A(
                        out=xo[:, s, dh * 512:(dh + 1) * 512], in0=banks[bi][:, :],
                        in1=xt[:, s, dh * 512:(dh + 1) * 512], op=ALU.add)),
                        reads=[f'xt{p}'], writes=[BK(bi), f'xo{p}_{s}'])
                if s == 1:
                    S.add('sp', ('dma_start', A(
                        out=dst[b * TB:(b + 1) * TB, :].rearrange("(s p) d -> p s d", p=128), in_=xo)),
                        reads=[f'xo{p}_0', f'xo{p}_1'], dma=1)
                    if b + 2 < nblk:
                        load_x(xts[p], src, b + 2, f'xt{p}')

            nsub = 2 * nblk
            if nblk > 1:
                load_x(xts[1], src, 1, 'xt1')
            stage_a(0, 0)
            for g in range(nsub):
                if g + 1 < nsub:
                    stage_a((g + 1) // 2, (g + 1) % 2)
                stage_b(g // 2, g % 2)

        def even_pass(layer, src, dst):
            ev = layer // 2
            lam_init = 0.8 - 0.6 * math.exp(-0.3 * layer)
            NCH = S_LEN // 128
            S.fence()
            WR.reset()
            AR.reset()
            Win = WR.alloc(8 * 2560, BF16).rearrange("p (k f) -> p k f", k=8)
            Wout = WR.alloc(8 * 1024, BF16).rearrange("p (k f) -> p k f", k=8)
            kT = WR.alloc(4 * S_LEN, BF16).rearrange("p (h t) -> p h t", h=4)
            V = WR.alloc(NCH * 512, BF16).rearrange("p (c v) -> p c v", c=NCH)
            WA = WR.alloc(4 * 128, BF16).rearrange("p (c j) -> p c j", c=4)
            WX = WR.alloc(4 * 128, BF16).rearrange("p (c j) -> p c j", c=4)
            gbc = AR.alloc(1024, F32)
            pv = AR.alloc(33, F32)
            ldl = AR.alloc(256, F32)
            sm = AR.alloc(64, F32)
            dummy = AR.alloc(16, F32)
            ltmp = AR.alloc(64, F32)
            hprev = AR.alloc(4, F32)
            tmp = [AR.alloc(16, F32) for _ in range(2)]
            xt = AR.alloc(2048, F32).rearrange("p (s d) -> p s d", s=2)
            hb = [AR.alloc(1024, BF16) for _ in range(2)]
            hT = AR.alloc(8 * TB, BF16).rearrange("p (k t) -> p k t", k=8)
            qc = AR.alloc(4 * 2 * TB, BF16).rearrange("p (h t) -> p h t", h=4)
            xbh = AR.alloc(4 * (TB + 3), F32).rearrange("p (c t) -> p c t", c=4)
            gg = AR.alloc(4 * TB, F32).rearrange("p (c t) -> p c t", c=4)
            xc = AR.alloc(4 * TB, F32).rearrange("p (c t) -> p c t", c=4)
            xcb = AR.alloc(4 * TB, BF16).rearrange("p (c t) -> p c t", c=4)
            Tr = AR.alloc(4 * TB, F32).rearrange("p (c t) -> p c t", c=4)
            Ti = AR.alloc(4 * TB, F32).rearrange("p (c t) -> p c t", c=4)
            aa = AR.alloc(4 * TB, F32).rearrange("p (c t) -> p c t", c=4)
            a2 = AR.alloc(4 * TB, F32).rearrange("p (c t) -> p c t", c=4)
            bt = Ti
            hs = Tr
            yT = AR.alloc(8 * TB, BF16).rearrange("p (k t) -> p k t", k=8)
            NPT = 4
            pT = [AR.alloc(2 * TB, BF16) for _ in range(NPT)]
            fr = AR.alloc(2 * TB, F32)
            ft = AR.alloc(2 * TB, F32)
            fo4 = AR.alloc(4 * TB, F32).rearrange("p (h t) -> p h t", h=4)
            sd = AR.alloc(4 * TB, F32).rearrange("p (h t) -> p h t", h=4)

            s1 = sm[:, 0:1]
            s2 = sm[:, 1:2]
            e1 = sm[:, 2:3]
            e2 = sm[:, 3:4]
            neglam = sm[:, 4:5]
            g1 = sm[:, 5:6]
            yv = sm[:, 8:12]
            tv = sm[:, 12:16]
            sc = sm[:, 16:20]
            sch = sm[:, 20:24]
            hba = sm[:, 24:28]
            hbx = sm[:, 28:32]
            cw = lambda w, c: pv[:, w * 4 + c:w * 4 + c + 1]
            cb = lambda c: pv[:, 16 + c:17 + c]

            load_bcast(gbc, norm_mix[layer], 'gbc')
            load_bcast(ldl, diff_l[ev], 'ldl')
            S.add('sp', ('dma_start', A(out=pv, in_=pvec[ev])), writes=['pv'], dma=1)
            S.add('sp', ('dma_start', A(
                out=xt, in_=src[0:TB, :].rearrange("(s p) d -> p s d", p=128))), writes=['xt'], dma=1)
            k1 = load_w(Win, ab_w_in[ev], 'Win', 8, 2560)
            k2 = load_w(Wout, ab_w_out[ev], 'Wout', 8, 1024)
            join(k1, 'Win', dummy[:, 0:4])
            join(k2, 'Wout', dummy[:, 4:8])
            S.add('pool', ('memset', A(WA, 0.0)), writes=['WA0'])
            S.add('pool', ('memset', A(WX, 0.0)), writes=['WX0'])
            for nm, wt, srcw in (('WA', WA, lru_wa), ('WX', WX, lru_wx)):
                sv = srcw[ev].rearrange("(c j) i o -> j i c o", j=2)
                wk = []
                for j in range(2):
                    key = f'{nm}_{j}'
                    wk.append(key)
                    S.add('pool', ('dma_start', A(
                        out=wt[j * 64:(j + 1) * 64, :, j * 64:(j + 1) * 64], in_=sv[j])),
                        reads=[nm + '0'], writes=[key], dma=1)
                join(wk, nm, dummy[:, 8:10] if nm == 'WA' else dummy[:, 10:12])
            S.add('pool', ('memset', A(qc[64:128, :, 0:TB], 0.0)), writes=['qcz0'])
            S.add('pool', ('memset', A(qc[0:64, :, TB:2 * TB], 0.0)), writes=['qcz1'])
            S.add('dve', ('tensor_tensor', A(out=ltmp, in0=ldl[:, 0:64], in1=ldl[:, 64:128], op=ALU.mult)),
                  reads=['ldl'], writes=['ltmp'])
            S.add('dve', ('tensor_reduce', A(out=s1, in_=ltmp, axis=mybir.AxisListType.X, op=ALU.add)),
                  reads=['ltmp'], writes=['s1'])
            S.add('dve', ('tensor_tensor', A(out=ltmp, in0=ldl[:, 128:192], in1=ldl[:, 192:256], op=ALU.mult)),
                  reads=['ldl', 's1'], writes=['ltmp'])
            S.add('dve', ('tensor_reduce', A(out=s2, in_=ltmp, axis=mybir.AxisListType.X, op=ALU.add)),
                  reads=['ltmp'], writes=['s2'])
            S.add('act', ('activation', A(out=e1, in_=s1, func=AF.Exp)), reads=['s1'], writes=['e1'])
            S.add('act', ('activation', A(out=e2, in_=s2, func=AF.Exp)), reads=['s2'], writes=['e2'])
            S.add('dve', ('tensor_tensor', A(out=neglam, in0=e2, in1=e1, op=ALU.subtract)), reads=['e1', 'e2'],
                  writes=['neglam'])
            S.add('dve', ('tensor_scalar', A(out=neglam, in0=neglam, scalar1=-lam_init, scalar2=None, op0=ALU.add)),
                  reads=['neglam'], writes=['neglam'])
            S.add('dve', ('tensor_scalar', A(out=g1, in0=pv[:, 32:33], scalar1=(1.0 - lam_init), scalar2=None,
                                                   op0=ALU.mult)), reads=['pv'], writes=['g1'])
            S.add('act', ('activation', A(out=yv, in_=pv[:, 28:32], func=AF.Exp, scale=-1.0)), reads=['pv'], writes=['yv'])
            S.add('dve', ('tensor_scalar', A(out=tv, in0=yv, scalar1=-0.25, scalar2=1.0 / 3.0, op0=ALU.mult, op1=ALU.add)),
                  reads=['yv'], writes=['tv'])
            S.add('dve', ('tensor_tensor', A(out=tv, in0=tv, in1=yv, op=ALU.mult)), reads=['tv', 'yv'], writes=['tv'])
            S.add('dve', ('tensor_scalar', A(out=tv, in0=tv, scalar1=-1.0, scalar2=0.5, op0=ALU.mult, op1=ALU.add)),
                  reads=['tv'], writes=['tv'])
            S.add('dve', ('tensor_tensor', A(out=tv, in0=tv, in1=yv, op=ALU.mult)), reads=['tv', 'yv'], writes=['tv'])
            S.add('dve', ('tensor_scalar', A(out=tv, in0=tv, scalar1=-1.0, scalar2=1.0, op0=ALU.mult, op1=ALU.add)),
                  reads=['tv'], writes=['tv'])
            S.add('dve', ('tensor_tensor', A(out=tv, in0=tv, in1=yv, op=ALU.mult)), reads=['tv', 'yv'], writes=['tv'])
            S.add('dve', ('tensor_scalar', A(out=sc, in0=tv, scalar1=-8.0, scalar2=None, op0=ALU.mult)),
                  reads=['tv'], writes=['sc'])
            S.add('dve', ('tensor_scalar', A(out=sch, in0=tv, scalar1=-4.0, scalar2=None, op0=ALU.mult)),
                  reads=['tv'], writes=['sch'])
            S.add('dve', ('tensor_scalar', A(out=hba, in0=pv[:, 20:24], scalar1=0.5, scalar2=None, op0=ALU.mult)),
                  reads=['pv'], writes=['hba'])
            S.add('dve', ('tensor_scalar', A(out=hbx, in0=pv[:, 24:28], scalar1=0.5, scalar2=None, op0=ALU.mult)),
                  reads=['pv'], writes=['hbx'])
            S.add('pool', ('memset', A(xbh[:, :, 0:3], 0.0)), writes=['xbh_halo'])
            S.add('pool', ('memset', A(hprev, 0.0)), writes=['hprev'])

            gcount = [0]

            def gbank():
                bi = 1 + gcount[0] % 3
                gcount[0] += 1
                return bi

            pcount = [0]
            for b in range(nblk):
                if b > 0:
                    S.add('sp', ('dma_start', A(
                        out=xt, in_=src[b * TB:(b + 1) * TB, :].rearrange("(s p) d -> p s d", p=128))),
                        writes=['xt'], dma=1)
                for s in range(2):
                    rms_norm(xt[:, s, :], gbc, 'gbc', hb[s], 'xt', f'hb{s}', tmp[s], f'tmp{s}')
                    transpose8(hb[s], f'hb{s}', hT, s * 128, 'hT', 0, 'act')

                def proj_fm(col0, evac):
                    bi = gbank()
                    ps_ = banks[bi][:, 0:TB]
                    for kc in range(8):
                        S.add('pe', ('matmul', A(out=ps_, lhsT=Win[:, kc, col0:col0 + 128],
                                                                       rhs=hT[:, kc, :], start=(kc == 0), stop=(kc == 7))),
                              reads=['Win', 'hT'], writes=[BK(bi)])
                    evac(ps_, BK(bi))

                def evac_q(ps_, bk, h):
                    S.add('dve', ('tensor_copy', A(out=qc[0:64, h, 0:TB], in_=ps_[0:64, :])),
                          reads=['qcz0', 'qcz1'], writes=[bk, f'qc{h}'])
                    S.add('dve', ('tensor_copy', A(out=qc[64:128, h, TB:2 * TB], in_=ps_[64:128, :])),
                          writes=[bk, f'qc{h}'])

                for h in range(4):
                    proj_fm(h * 128, lambda ps_, bk, h=h: evac_q(ps_, bk, h))
                for h in range(4):
                    proj_fm(512 + h * 128, lambda ps_, bk, h=h: S.add(
                        'dve', ('tensor_copy', A(out=kT[:, h, b * TB:(b + 1) * TB], in_=ps_)),
                        writes=[bk, f'kT{h}_{b}']))
                for ts in range(2):
                    bi = gbank()
                    for kc in range(8):
                        S.add('pe', ('matmul', A(
                            out=banks[bi][:, :], lhsT=hT[:, kc, ts * 128:(ts + 1) * 128],
                            rhs=Win[:, kc, 1024:1536], start=(kc == 0), stop=(kc == 7))),
                            reads=['Win', 'hT'], writes=[BK(bi)])
                    S.add('dve', ('tensor_copy', A(out=V[:, b * 2 + ts, :], in_=banks[bi][:, :])),
                          writes=[BK(bi), f'V{b * 2 + ts}'])
                for c in range(4):
                    proj_fm(1536 + c * 128, lambda ps_, bk, c=c: S.add(
                        'act', ('copy', A(out=xbh[:, c, 3:3 + TB], in_=ps_)), reads=['xbh_halo'],
                        writes=[bk, f'xbh{c}']))
                for c in range(4):
                    proj_fm(2048 + c * 128, lambda ps_, bk, c=c: S.add(
                        'act', ('activation', A(out=gg[:, c, :], in_=ps_, func=AF.Gelu_apprx_tanh)),
                        writes=[bk, f'gg{c}']))
                for c in range(4):
                    S.add('dve', ('tensor_scalar', A(out=xc[:, c, :], in0=xbh[:, c, 0:TB], scalar1=cw(0, c),
                                                                scalar2=cb(c), op0=ALU.mult, op1=ALU.add)),
                          reads=[f'xbh{c}', 'xbh_halo', 'pv'], writes=[f'xc{c}'])
                    for w in range(1, 4):
                        S.add('dve', ('scalar_tensor_tensor', A(
                            out=xc[:, c, :], in0=xbh[:, c, w:w + TB], scalar=cw(w, c), in1=xc[:, c, :],
                            op0=ALU.mult, op1=ALU.add)), reads=[f'xbh{c}', 'xbh_halo', 'pv'], writes=[f'xc{c}'])
                    S.add('pool', ('tensor_copy', A(out=xcb[:, c, :], in_=xc[:, c, :])), reads=[f'xc{c}'],
                          writes=[f'xcb{c}'])
                S.add('pool', ('tensor_copy', A(out=xbh[:, :, 0:3], in_=xbh[:, :, TB:TB + 3])),
                      reads=[f'xbh{c}' for c in range(4)] + [f'xc{c}' for c in range(4)], writes=['xbh_halo'])
                nkc = 2 * b + 2
                steps = [(h, kc) for h in range(4) for kc in range(nkc)]
                LA = 2
                pts = {}

                def lru_part2():
                    for c in range(4):
                        for nm, wt, Tt, hbias in (('r', WA, Tr, hba), ('i', WX, Ti, hbx)):
                            bi = gbank()
                            ps_ = banks[bi][:, 0:TB]
                            S.add('pe', ('matmul', A(out=ps_, lhsT=wt[:, c, :], rhs=xcb[:, c, :],
                                                                                start=True, stop=True)),
                                  reads=['WA' if nm == 'r' else 'WX', f'xcb{c}'], writes=[BK(bi)])
                            S.add('act', ('activation', A(
                                out=Tt[:, c, :], in_=ps_, func=AF.Tanh, bias=hbias[:, c:c + 1], scale=0.5)),
                                reads=['hba', 'hbx'],
                                writes=[BK(bi), f'T{nm}{c}'] + ([f'hs{c}'] if nm == 'r' else [f'bt{c}']))
                    for c in range(4):
                        S.add('act', ('activation', A(out=aa[:, c, :], in_=Tr[:, c, :], func=AF.Exp,
                                                                 bias=sch[:, c:c + 1], scale=sch[:, c:c + 1])),
                              reads=[f'Tr{c}', 'sch'], writes=[f'aa{c}'])
                        S.add('act', ('activation', A(out=a2[:, c, :], in_=Tr[:, c, :], func=AF.Exp,
                                                                 bias=sc[:, c:c + 1], scale=sc[:, c:c + 1])),
                              reads=[f'Tr{c}', 'sc'], writes=[f'a2{c}'])

                def qk_exp(i):
                    h, kc = steps[i]
                    j = kc - 2 * b
                    si = 2 + i % 2
                    S.add('pe', ('matmul', A(
                        out=banks[si][:, :], lhsT=kT[:, h, kc * 128:(kc + 1) * 128], rhs=qc[:, h, :],
                        start=True, stop=True)), reads=[f'kT{h}_{kc // 2}', f'qc{h}'], writes=[BK(si)])
                    pi = pcount[0] % NPT
                    pcount[0] += 1
                    pt = pT[pi]
                    pkey = f'pT{pi}'
                    pts[i] = (pt, pkey)
                    S.add('act', ('activation', A(out=pt, in_=banks[si][:, :], func=AF.Exp, scale=0.125)),
                          writes=[BK(si), pkey])
                    if j >= 0:
                        c0 = j * 128
                        if c0 > 0:
                            S.add('pool', ('memset', A(pt[:, 0:c0], 0.0)), writes=[pkey])
                            S.add('pool', ('memset', A(pt[:, TB:TB + c0], 0.0)), writes=[pkey])
                        S.add('pool', ('memset', A(pt[64:128, c0:c0 + 64], 0.0)), writes=[pkey])
                        S.add('pool', ('memset', A(pt[64:128, TB + c0:TB + c0 + 64], 0.0)), writes=[pkey])

                def pv_sum(i):
                    h, kc = steps[i]
                    ob = 4 + h % 2
                    sb_ = 6 + h % 2
                    pt, pkey = pts.pop(i)
                    S.add('pe', ('matmul', A(
                        out=banks[ob][:, :], lhsT=V[:, kc, h * 128:(h + 1) * 128], rhs=pt,
                        start=(kc == 0), stop=(kc == nkc - 1))), reads=[f'V{kc}', pkey], writes=[BK(ob)])
                    S.add('pe', ('matmul', A(
                        out=banks[sb_][:, :], lhsT=ones_bf[:], rhs=pt,
                        start=(kc == 0), stop=(kc == nkc - 1))), reads=['ones_bf', pkey], writes=[BK(sb_)])
                    if kc == nkc - 1:
                        S.add('dve', ('reciprocal', A(out=fr, in_=banks[sb_][:, :])), writes=[BK(sb_), 'fr'])
                        S.add('dve', ('tensor_tensor', A(out=ft, in0=banks[ob][:, :], in1=fr, op=ALU.mult)),
                              reads=['fr'], writes=[BK(ob), 'ft'])
                        S.add('dve', ('scalar_tensor_tensor', A(out=fo4[:, h, :], in0=ft[:, TB:2 * TB], scalar=neglam,
                                                               in1=ft[:, 0:TB], op0=ALU.mult, op1=ALU.add)),
                              reads=['ft', 'neglam'], writes=[f'fo{h}'])
                        S.add('dve', ('tensor_tensor', A(out=sd[:, h, :], in0=fo4[:, h, :], in1=fo4[:, h, :], op=ALU.mult)),
                              reads=[f'fo{h}'], writes=[f'sd{h}'])

                nst = len(steps)
                for i in range(min(LA, nst)):
                    qk_exp(i)
                lru2_at = min(3, nst - 1)
                for i in range(nst):
                    pv_sum(i)
                    if i + LA < nst:
                        qk_exp(i + LA)
                    if i == lru2_at:
                        lru_part2()
                S.add('act', ('activation', A(out=a2.rearrange("p c t -> p (c t)"), in_=a2.rearrange("p c t -> p (c t)"),
                                              func=AF.Sqrt, bias=qbias[:, 0:1], scale=-0.25)),
                      reads=['qbias'], writes=[f'a2{c}' for c in range(4)])
                for c in range(4):
                    S.add('dve', ('scalar_tensor_tensor', A(out=bt[:, c, :], in0=Ti[:, c, :], scalar=1.0,
                                                                       in1=xc[:, c, :], op0=ALU.add, op1=ALU.mult)),
                          reads=[f'Ti{c}', f'xc{c}'], writes=[f'bt{c}', f'Ti{c}'])
                    S.add('dve', ('tensor_tensor', A(out=bt[:, c, :], in0=bt[:, c, :], in1=a2[:, c, :], op=ALU.mult)),
                          reads=[f'a2{c}'], writes=[f'bt{c}'])
                    S.add('dve', ('tensor_tensor_scan', A(out=hs[:, c, :], data0=aa[:, c, :], data1=bt[:, c, :],
                                                                     initial=hprev[:, c:c + 1], op0=ALU.mult, op1=ALU.add)),
                          reads=[f'aa{c}', f'bt{c}', 'hprev'], writes=[f'hs{c}', f'Tr{c}'])
                    S.add('dve', ('tensor_tensor', A(out=yT[:, 4 + c, :], in0=gg[:, c, :], in1=hs[:, c, :], op=ALU.mult)),
                          reads=[f'gg{c}', f'hs{c}'], writes=[f'yT{4 + c}'])
                S.add('pool', ('tensor_copy', A(out=hprev, in_=hs[:, :, TB - 1])),
                      reads=[f'hs{c}' for c in range(4)], writes=['hprev'])
                for h in range(4):
                    mb = 1 - h // 2
                    S.add('pe', ('matmul', A(out=banks[mb][:, (h % 2) * TB:(h % 2 + 1) * TB], lhsT=ones_f[:], rhs=sd[:, h, :],
                                             start=True, stop=True)),
                          reads=['ones_f', f'sd{h}'], writes=[BK(mb)])
                for hh in range(2):
                    mb = 1 - hh
                    S.add('act', ('activation', A(out=sd[:, 2 * hh:2 * hh + 2, :].rearrange("p h t -> p (h t)"),
                                                  in_=banks[mb][:, :], func=AF.Sqrt, bias=epsb[:, 0:1], scale=1.0 / 128.0)),
                          reads=['epsb'], writes=[BK(mb), f'sd{2 * hh}', f'sd{2 * hh + 1}'])
                S.add('dve', ('reciprocal', A(out=sd.rearrange("p h t -> p (h t)"), in_=sd.rearrange("p h t -> p (h t)"))),
                      writes=[f'sd{h}' for h in range(4)])
                S.add('dve', ('scalar_tensor_tensor', A(out=yT[:, 0:4, :].rearrange("p h t -> p (h t)"),
                                                       in0=fo4.rearrange("p h t -> p (h t)"), scalar=g1,
                                                       in1=sd.rearrange("p h t -> p (h t)"), op0=ALU.mult, op1=ALU.mult)),
                      reads=[f'fo{h}' for h in range(4)] + ['g1'] + [f'sd{h}' for h in range(4)],
                      writes=[f'yT{h}' for h in range(4)])
                for ts in range(2):
                    for dh in range(2):
                        bi = gbank()
                        for kc in range(8):
                            S.add('pe', ('matmul', A(
                                out=banks[bi][:, :], lhsT=yT[:, kc, ts * 128:(ts + 1) * 128],
                                rhs=Wout[:, kc, dh * 512:(dh + 1) * 512], start=(kc == 0), stop=(kc == 7))),
                                reads=['Wout', f'yT{kc}'], writes=[BK(bi)])
                        S.add('dve', ('tensor_tensor', A(
                            out=xt[:, ts, dh * 512:(dh + 1) * 512], in0=banks[bi][:, :],
                            in1=xt[:, ts, dh * 512:(dh + 1) * 512], op=ALU.add)),
                            writes=[BK(bi), 'xt'])
                S.add('sp', ('dma_start', A(
                    out=dst[b * TB:(b + 1) * TB, :].rearrange("(s p) d -> p s d", p=128), in_=xt)),
                    reads=['xt'], dma=1)

        bufs = [scrA, scrB]
        cur = x_in
        nb = 0
        if passes is None:
            passes_l = []
            for layer in layers:
                passes_l.append(('even' if layer % 2 == 0 else 'odd', layer))
                passes_l.append(('ffn', layer))
        else:
            passes_l = list(passes)
        for pi, (kind, layer) in enumerate(passes_l):
            last = (pi == len(passes_l) - 1)
            d = out if last else bufs[nb % 2]
            nb += 1
            if kind == 'even':
                even_pass(layer, cur, d)
            elif kind == 'odd':
                odd_pass(layer, cur, d)
            else:
                ffn_pass(layer, cur, d, apply_final=(last and final_norm))
            cur = d
        if not final_ops:
            for q in S.dma_slots.values():
                for op in q:
                    if op is not None:
                        final_ops.append(op)
        else:
            for q in S.dma_slots.values():
                for op in q:
                    if op is not None and op not in final_ops:
                        final_ops.append(op)
        S.emit(nc, final_wait_ops=final_ops)
    return nc


def prep_weights(inp):
    f = lambda a: np.ascontiguousarray(np.asarray(a, dtype=np.float32))
    w = {}
    for k in ('norm_mix', 'norm_ffn', 'norm_final', 'ab_w_in', 'ab_w_out', 'lru_wa', 'lru_wx', 'c_w_in', 'c_ln_g',
              'c_ln_b', 'c_w_out', 'ffn_w1', 'ffn_w2'):
        w[k] = f(inp[k])
    w['diff_l'] = f(np.concatenate([inp['diff_lq1'], inp['diff_lk1'], inp['diff_lq2'], inp['diff_lk2']], axis=1))
    pv = []
    for e in range(2):
        cols = []
        cwv = np.asarray(inp['lru_conv_w'][e])
        cols.append(cwv.reshape(4, 4, 128).transpose(2, 0, 1).reshape(128, 16))
        for nm in ('lru_conv_b', 'lru_ba', 'lru_bx', 'lru_lambda'):
            cols.append(np.asarray(inp[nm][e]).reshape(4, 128).T)
        cols.append(np.asarray(inp['diff_subln'][e]).reshape(128, 1))
        pv.append(np.concatenate(cols, axis=1))
    w['pvec'] = f(np.stack(pv))
    w['c_w_sT'] = f(np.transpose(np.asarray(inp['c_w_s']), (0, 3, 1, 2)))
    w['c_b_sT'] = f(np.transpose(np.asarray(inp['c_b_s']), (0, 2, 1)))
    return w


_NC_CACHE = {}


def kernel(**inputs):
    x = np.asarray(inputs['x'], dtype=np.float32)
    B, S_LEN, _ = x.shape
    w = prep_weights(inputs)
    key = (S_LEN,)
    if key not in _NC_CACHE:
        _NC_CACHE[key] = build_program(S_LEN)
    nc = _NC_CACHE[key]
    in_maps = []
    for c in range(B):
        m = dict(w)
        m['x'] = np.ascontiguousarray(x[c])
        in_maps.append(m)
    res = run_bass_kernel_spmd(nc, in_maps, core_ids=list(range(B)))
    return np.stack([np.asarray(r['out'], dtype=np.float32) for r in res.results], axis=0)
```

```python
import contextlib
import math

import numpy as np
import concourse.bass as bass
import concourse.mybir as mybir
from concourse.bass_utils import run_bass_kernel_spmd

F32 = mybir.dt.float32
BF16 = mybir.dt.bfloat16
U8 = mybir.dt.uint8
AF = mybir.ActivationFunctionType
ALU = mybir.AluOpType

D = 1024
TB = 256
EPS = 1e-6
ENGS = ('pe', 'act', 'dve', 'pool', 'sp')
SEM_CAP = 30000
DBG = {'ffn': 9, 'noW': 0}


def A(*a, **k):
    return (a, k)


class Op:
    __slots__ = ('eng', 'fn', 'deps', 'is_dma', 'ndma', 'slot', 'target', 'epoch', 'msval')


class Sched:
    def __init__(self, nslots=8):
        self.ops = {e: [] for e in ENGS}
        self.lastw = {}
        self.readers = {}
        self.nslots = nslots
        self.dma_slots = {}
        self.dma_count = {e: 0 for e in ENGS}
        self.fence_deps = set()

    def add(self, eng, fn, reads=(), writes=(), dma=0):
        op = Op()
        op.eng = eng
        op.fn = fn
        op.is_dma = dma > 0
        op.ndma = dma
        op.slot = None
        op.target = None
        op.epoch = None
        op.msval = None
        deps = set(self.fence_deps)
        for k in reads:
            w = self.lastw.get(k)
            if w is not None:
                deps.add(w)
        for k in writes:
            w = self.lastw.get(k)
            if w is not None:
                deps.add(w)
            for r in self.readers.get(k, ()):
                deps.add(r)
        if dma:
            q = self.dma_slots.setdefault(eng, [None] * self.nslots)
            s = self.dma_count[eng] % self.nslots
            self.dma_count[eng] += 1
            if q[s] is not None:
                deps.add(q[s])
            q[s] = op
            op.slot = (eng, s)
        deps.discard(op)
        op.deps = deps
        for k in reads:
            self.readers.setdefault(k, []).append(op)
        for k in writes:
            self.lastw[k] = op
            self.readers[k] = []
        self.ops[eng].append(op)
        return op

    def fence(self):
        deps = set()
        for e in ENGS:
            last = None
            for op in reversed(self.ops[e]):
                if not op.is_dma:
                    last = op
                    break
            if last is not None:
                deps.add(last)
        for q in self.dma_slots.values():
            for op in q:
                if op is not None:
                    deps.add(op)
        self.fence_deps = deps
        self.lastw = {}
        self.readers = {}

    def emit(self, nc, final_wait_ops=()):
        needed = set()
        for e in ENGS:
            for op in self.ops[e]:
                for d in op.deps:
                    if e == 'pe' and d.eng == 'pe' and not d.is_dma:
                        continue
                    needed.add(d)
        for d in final_wait_ops:
            needed.add(d)
        n_epochs = {}
        for e in ENGS:
            c = 0
            ep = 0
            for op in self.ops[e]:
                if (not op.is_dma) and op in needed:
                    if c >= SEM_CAP:
                        ep += 1
                        c = 0
                    c += 1
                    op.epoch = ep
                    op.msval = c
            n_epochs[e] = ep + 1
        slot_cnt = {}
        for e in ENGS:
            for op in self.ops[e]:
                if op.is_dma:
                    slot_cnt[op.slot] = slot_cnt.get(op.slot, 0) + 16 * op.ndma
                    op.target = slot_cnt[op.slot]
        with contextlib.ExitStack() as st:
            esem = {}
            for e in ENGS:
                for ep in range(n_epochs[e]):
                    esem[(e, ep)] = st.enter_context(nc.semaphore(f"s_{e}_{ep}"))
            dsem = {}
            for slot in slot_cnt:
                dsem[slot] = st.enter_context(nc.semaphore(f"d_{slot[0]}_{slot[1]}"))
            block = st.enter_context(nc.Block())
            ops = self.ops

            def run(e, eng):
                seen_e = {}
                seen_d = {}
                for op in ops[e]:
                    self._waits(e, eng, op.deps, seen_e, seen_d, esem, dsem)
                    name, (pa, kw) = op.fn
                    if op.is_dma:
                        getattr(eng, name)(*pa, **kw).then_inc(dsem[op.slot], 16)
                    else:
                        ins = getattr(eng, name)(*pa, **kw)
                        if op.msval is not None:
                            ins.then_inc(esem[(e, op.epoch)], 1)
                if e == 'sp' and final_wait_ops:
                    self._waits(e, eng, final_wait_ops, seen_e, seen_d, esem, dsem)

            @block.tensor
            def _(eng):
                run('pe', eng)

            @block.scalar
            def _(eng):
                run('act', eng)

            @block.vector
            def _(eng):
                run('dve', eng)

            @block.gpsimd
            def _(eng):
                run('pool', eng)

            @block.sync
            def _(eng):
                run('sp', eng)

    @staticmethod
    def _waits(e, eng, deps, seen_e, seen_d, esem, dsem):
        best_e = {}
        best_d = {}
        for d in deps:
            if d.is_dma:
                if best_d.get(d.slot, 0) < d.target:
                    best_d[d.slot] = d.target
            else:
                if e == 'pe' and d.eng == 'pe':
                    continue
                v = (d.epoch, d.msval)
                if best_e.get(d.eng, (-1, 0)) < v:
                    best_e[d.eng] = v
        for slot, t in best_d.items():
            if seen_d.get(slot, 0) >= t:
                continue
            seen_d[slot] = t
            eng.wait_ge(dsem[slot], t)
        for de, v in best_e.items():
            if seen_e.get(de, (-1, 0)) >= v:
                continue
            seen_e[de] = v
            eng.wait_ge(esem[(de, v[0])], v[1])


class Region:
    def __init__(self, raw, nbytes):
        self.raw = raw
        self.nbytes = nbytes
        self.off = 0

    def reset(self):
        self.off = 0

    def alloc(self, cols, dt):
        esz = 4 if dt == F32 else 2
        nb = cols * esz
        nb_al = (nb + 63) // 64 * 64
        assert self.off + nb_al <= self.nbytes, (self.off, nb_al, self.nbytes)
        v = self.raw[:, self.off:self.off + nb].bitcast(dt)
        self.off += nb_al
        return v


def build_program(S_LEN=4096, layers=(0, 1, 2, 3), final_norm=True, passes=None):
    nblk = S_LEN // TB
    nc = bass.Bass("TRN2", target_bir_lowering=False)

    def din(name, shape):
        return nc.dram_tensor(name, list(shape), F32, kind="ExternalInput").ap()

    x_in = din("x", [S_LEN, D])
    norm_mix = din("norm_mix", [4, D])
    norm_ffn = din("norm_ffn", [4, D])
    norm_final = din("norm_final", [D])
    ab_w_in = din("ab_w_in", [2, D, 2560])
    ab_w_out = din("ab_w_out", [2, D, D])
    diff_l = din("diff_l", [2, 256])
    pvec = din("pvec", [2, 128, 33])
    lru_wa = din("lru_wa", [2, 8, 64, 64])
    lru_wx = din("lru_wx", [2, 8, 64, 64])
    c_w_in = din("c_w_in", [2, D, 2048])
    c_ln_g = din("c_ln_g", [2, D])
    c_ln_b = din("c_ln_b", [2, D])
    c_w_sT = din("c_w_sT", [2, 128, 8, 128])
    c_b_sT = din("c_b_sT", [2, 128, 8])
    c_w_out = din("c_w_out", [2, D, D])
    ffn_w1 = din("ffn_w1", [4, D, 4096])
    ffn_w2 = din("ffn_w2", [4, 4096, D])
    out = nc.dram_tensor("out", [S_LEN, D], F32, kind="ExternalOutput").ap()
    scrA = nc.dram_tensor("scrA", [S_LEN, D], F32, kind="Internal").ap()
    scrB = nc.dram_tensor("scrB", [S_LEN, D], F32, kind="Internal").ap()

    S = Sched(nslots=16)
    st = contextlib.ExitStack()
    with st:
        def sb(name, shape, dt):
            return st.enter_context(nc.sbuf_tensor(name, shape, dt))

        WBYTES = DBG.get('WBYTES', 126976)
        ABYTES = 77824
        wraw = sb("wraw", [128, WBYTES], U8)
        araw = sb("araw", [128, ABYTES], U8)
        WR = Region(wraw, WBYTES)
        AR = Region(araw, ABYTES)
        ident = sb("ident", [128, 128], BF16)
        identf = sb("identf", [128, 128], F32)
        ones_bf = sb("ones_bf", [128, 128], BF16)
        ones_f = sb("ones_f", [128, 128], F32)
        neghalf = sb("neghalf", [128, 1], F32)
        poshalf = sb("poshalf", [128, 1], F32)
        expbias = sb("expbias", [128, 1], F32)
        qbias = sb("qbias", [128, 1], F32)
        epsb = sb("epsb", [128, 1], F32)
        banks = [st.enter_context(nc.psum_tensor(f"bank{i}", [128, 512], F32)) for i in range(8)]

        def bank_bf(i):
            return banks[i][:, :].bitcast(BF16).rearrange("p (k t) -> p k t", k=8)

        S.add('pool', ('memset', A(identf[:], 1.0)), writes=['identf'])
        S.add('pool', ('affine_select', A(out=identf[:], in_=identf[:], pattern=[[-1, 128]],
                                                compare_op=ALU.is_equal, fill=0.0, base=0,
                                                channel_multiplier=1)), reads=['identf'], writes=['identf'])
        S.add('pool', ('tensor_copy', A(out=ident[:], in_=identf[:])), reads=['identf'], writes=['ident'])
        S.add('pool', ('memset', A(ones_bf[:], 1.0)), writes=['ones_bf'])
        S.add('pool', ('memset', A(ones_f[:], 1.0)), writes=['ones_f'])
        S.add('pool', ('memset', A(neghalf[:], -0.5)), writes=['neghalf'])
        S.add('pool', ('memset', A(poshalf[:], 0.5)), writes=['poshalf'])
        S.add('pool', ('memset', A(expbias[:], 0.0)), writes=['expbias'])
        S.add('pool', ('memset', A(qbias[:], 0.25)), writes=['qbias'])
        S.add('pool', ('memset', A(epsb[:], EPS)), writes=['epsb'])
        S.add('pool', ('memset', A(expbias[64:128, :], -30000.0)), reads=['expbias'], writes=['expbias'])
        CONST_KEYS = ['ident', 'ones_bf', 'ones_f', 'neghalf', 'poshalf', 'expbias']

        def after_fence_consts():
            pass

        def load_w(dst3, src2, name, kc_n, cols):
            src3 = src2.rearrange("(kc p) n -> p kc n", p=128)
            cstep = cols
            while cstep > 2048:
                cstep //= 2
            kstep = max(1, 2048 // cstep)
            keys = []
            for k0 in range(0, kc_n, kstep):
                for c0 in range(0, cols, cstep):
                    key = f"{name}_{k0}_{c0}"
                    keys.append(key)
                    S.add('pool', ('dma_start', A(
                        out=dst3[:, k0:k0 + kstep, c0:c0 + cstep],
                        in_=src3[:, k0:k0 + kstep, c0:c0 + cstep])),
                        writes=[key], dma=1)
            return keys

        def join(keys, name, dummy):
            S.add('pool', ('memset', A(dummy, 0.0)), reads=keys, writes=[name])

        def load_bcast(dst, src1d, name):
            S.add('sp', ('dma_start', A(out=dst, in_=src1d.partition_broadcast(128))),
                  writes=[name], dma=1)

        def load_x(xt3, src, blk, key):
            S.add('sp', ('dma_start', A(
                out=xt3, in_=src[blk * TB:(blk + 1) * TB, :].rearrange("(s p) d -> p s d", p=128))),
                writes=[key], dma=1)

        def store_x(dst, xo3, blk, key):
            return S.add('sp', ('dma_start', A(
                out=dst[blk * TB:(blk + 1) * TB, :].rearrange("(s p) d -> p s d", p=128), in_=xo3)),
                reads=[key], dma=1)

        def rms_norm(x2, gbc, gkey, h2, xkey, hkey, tmp, tkey):
            stt = tmp[:, 0:12]
            mv = tmp[:, 12:14]
            ms = tmp[:, 14:15]
            rs = tmp[:, 15:16]
            S.add('dve', ('bn_stats', A(out=stt[:, 0:6], in_=x2[:, 0:512])), reads=[xkey], writes=[tkey + 'a'])
            S.add('dve', ('bn_stats', A(out=stt[:, 6:12], in_=x2[:, 512:1024])), reads=[xkey], writes=[tkey + 'b'])
            S.add('dve', ('bn_aggr', A(out=mv, in_=stt)), reads=[tkey + 'a', tkey + 'b'], writes=[tkey + 'mv'])
            S.add('dve', ('tensor_scalar', A(out=ms, in0=mv[:, 0:1], scalar1=mv[:, 0:1], scalar2=mv[:, 1:2],
                                                   op0=ALU.mult, op1=ALU.add)), reads=[tkey + 'mv'], writes=[tkey + 'ms'])
            S.add('dve', ('tensor_scalar', A(out=ms, in0=ms, scalar1=EPS, scalar2=None, op0=ALU.add)),
                  reads=[tkey + 'ms'], writes=[tkey + 'ms'])
            S.add('pool', ('tensor_tensor', A(out=rs, in0=ms, in1=neghalf[:, 0:1], op=ALU.pow)),
                  reads=[tkey + 'ms', 'neghalf'], writes=[tkey + 'rs'])
            S.add('dve', ('scalar_tensor_tensor', A(out=h2, in0=x2, scalar=rs, in1=gbc, op0=ALU.mult, op1=ALU.mult)),
                  reads=[xkey, tkey + 'rs', gkey], writes=[hkey])

        def transpose8(h2, hkey, hT3, col0, hTkey, bank_i, evac_eng):
            tp = bank_bf(bank_i)
            bkey = f"bank{bank_i}"
            for kc in range(8):
                S.add('pe', ('transpose', A(out=tp[:, kc, :], in_=h2[:, kc * 128:(kc + 1) * 128],
                                                         identity=ident[:])),
                      reads=[hkey, 'ident'], writes=[bkey])
            if evac_eng == 'act':
                S.add('act', ('copy', A(out=hT3[:, :, col0:col0 + 128], in_=tp)), writes=[bkey, hTkey])
            else:
                S.add('dve', ('tensor_copy', A(out=hT3[:, :, col0:col0 + 128], in_=tp)), writes=[bkey, hTkey])

        final_ops = []

        def BK(i):
            return f'bank{i}'

        def norm_stats(x2, xkeys, t, tk):
            stt = t[:, 0:12]
            mv = t[:, 12:14]
            ms = t[:, 14:15]
            rs = t[:, 15:16]
            S.add('dve', ('bn_stats', A(out=stt[:, 0:6], in_=x2[:, 0:512])), reads=xkeys, writes=[tk + 'a'])
            S.add('dve', ('bn_stats', A(out=stt[:, 6:12], in_=x2[:, 512:1024])), reads=xkeys, writes=[tk + 'b'])
            S.add('dve', ('bn_aggr', A(out=mv, in_=stt)), reads=[tk + 'a', tk + 'b'], writes=[tk + 'mv'])
            S.add('dve', ('tensor_scalar', A(out=ms, in0=mv[:, 0:1], scalar1=mv[:, 0:1], scalar2=mv[:, 1:2],
                                                   op0=ALU.mult, op1=ALU.add)), reads=[tk + 'mv'], writes=[tk + 'ms'])
            S.add('dve', ('tensor_scalar', A(out=ms, in0=ms, scalar1=EPS, scalar2=None, op0=ALU.add)),
                  reads=[tk + 'ms'], writes=[tk + 'ms'])
            S.add('pool', ('tensor_tensor', A(out=rs, in0=ms, in1=neghalf[:, 0:1], op=ALU.pow)),
                  reads=[tk + 'ms', 'neghalf'], writes=[tk + 'rs'])
            return rs

        def ffn_pass(layer, src, dst, apply_final):
            S.fence()
            WR.reset()
            AR.reset()
            W1 = WR.alloc(8 * 4096, BF16).rearrange("p (k f) -> p k f", k=8)
            W2a = WR.alloc(30 * 1024, BF16).rearrange("p (k f) -> p k f", k=30)
            W2b = AR.alloc(2 * 1024, BF16).rearrange("p (k f) -> p k f", k=2)
            W2v = [W2a[:, fc, :] if fc < 30 else W2b[:, fc - 30, :] for fc in range(32)]
            gbc = AR.alloc(1024, F32)
            gfin = AR.alloc(1024, F32) if apply_final else None
            dummy = AR.alloc(16, F32)
            xts = [AR.alloc(2048, F32).rearrange("p (s d) -> p s d", s=2) for _ in range(2)]
            xos = [AR.alloc(2048, F32).rearrange("p (s d) -> p s d", s=2) for _ in range(2)]
            hb = [AR.alloc(1024, BF16) for _ in range(2)]
            hT = [AR.alloc(8 * TB, BF16).rearrange("p (k t) -> p k t", k=8) for _ in range(2)]
            uT = AR.alloc(32 * TB, BF16).rearrange("p (k t) -> p k t", k=32)
            rt = [AR.alloc(TB, F32) for _ in range(2)]
            tmp = [AR.alloc(16, F32) for _ in range(2)]

            load_bcast(gbc, norm_ffn[layer], 'gbc')
            if apply_final:
                load_bcast(gfin, norm_final, 'gfin')
            load_x(xts[0], src, 0, 'xt0')
            k1 = load_w(W1, ffn_w1[layer], 'W1', 8, 4096)
            k2 = load_w(W2a, ffn_w2[layer][0:30 * 128, :], 'W2a', 30, 1024)
            k2 += load_w(W2b, ffn_w2[layer][30 * 128:32 * 128, :], 'W2b', 2, 1024)
            join(k1, 'W1', dummy[:, 0:4])
            join(k2, 'W2', dummy[:, 4:8])

            def f_norm(b):
                p = b % 2
                for s in range(2):
                    rms_norm(xts[p][:, s, :], gbc, 'gbc', hb[s], f'xt{p}', f'hb{s}', tmp[s], f'tmp{s}')

            def f_tr(b):
                p = b % 2
                for s in range(2):
                    transpose8(hb[s], f'hb{s}', hT[p], s * 128, f'hT{p}', s, 'act')

            def f_w1(b):
                p = b % 2
                for fc in range(32):
                    bi = 2 + fc % 4
                    ups = banks[bi][:, 0:TB]
                    for kc in range(8):
                        S.add('pe', ('matmul', A(
                            out=ups, lhsT=W1[:, kc, fc * 128:(fc + 1) * 128], rhs=hT[p][:, kc, :],
                            start=(kc == 0), stop=(kc == 7))),
                            reads=['W1', f'hT{p}'], writes=[BK(bi)])
                    r = rt[fc % 2]
                    S.add('act', ('activation', A(out=r, in_=ups, func=AF.Relu)),
                          writes=[BK(bi), f'rt{fc % 2}'])
                    S.add('dve', ('tensor_tensor', A(out=uT[:, fc, :], in0=r, in1=r, op=ALU.mult)),
                          reads=[f'rt{fc % 2}'], writes=[f'uT{fc}'])

            def f_w2(b, gi):
                p = b % 2
                xt = xts[p]
                xo = xos[p]
                ts, dh = gi // 2, gi % 2
                bi = 6 + gi % 2
                acc = banks[bi]
                for fc in range(32):
                    S.add('pe', ('matmul', A(
                        out=acc[:, :], lhsT=uT[:, fc, ts * 128:(ts + 1) * 128],
                        rhs=W2v[fc][:, dh * 512:(dh + 1) * 512], start=(fc == 0), stop=(fc == 31))),
                        reads=['W2', f'uT{fc}'], writes=[BK(bi)])
                S.add('dve', ('tensor_tensor', A(
                    out=xo[:, ts, dh * 512:(dh + 1) * 512], in0=acc[:, :], in1=xt[:, ts, dh * 512:(dh + 1) * 512],
                    op=ALU.add)), reads=[f'xt{p}'], writes=[BK(bi), f'xo{p}_{ts}'])

            def f_store(b):
                p = b % 2
                xo = xos[p]
                okeys = [f'xo{p}_0', f'xo{p}_1']
                if apply_final:
                    for s in range(2):
                        x2 = xo[:, s, :]
                        rs = norm_stats(x2, [f'xo{p}_{s}'], tmp[s], f'tmp{s}')
                        S.add('dve', ('scalar_tensor_tensor', A(out=x2, in0=x2, scalar=rs, in1=gfin,
                                                               op0=ALU.mult, op1=ALU.mult)),
                              reads=[f'tmp{s}rs', 'gfin'], writes=[f'xo{p}_{s}'])
                op = S.add('sp', ('dma_start', A(
                    out=dst[b * TB:(b + 1) * TB, :].rearrange("(s p) d -> p s d", p=128), in_=xo)),
                    reads=okeys, dma=1)
                if apply_final:
                    final_ops.append(op)

            if nblk > 1:
                load_x(xts[1], src, 1, 'xt1')
            f_norm(0)
            f_tr(0)
            for b in range(nblk):
                p = b % 2
                f_w1(b)
                if b + 1 < nblk:
                    f_norm(b + 1)
                f_w2(b, 0)
                f_w2(b, 1)
                if b + 1 < nblk:
                    f_tr(b + 1)
                f_w2(b, 2)
                f_w2(b, 3)
                f_store(b)
                if b + 2 < nblk:
                    load_x(xts[p], src, b + 2, f'xt{p}')

        def odd_pass(layer, src, dst):
            o = layer // 2
            S.fence()
            WR.reset()
            AR.reset()
            Wci = WR.alloc(8 * 2048, BF16).rearrange("p (k f) -> p k f", k=8)
            Wco = WR.alloc(8 * 1024, BF16).rearrange("p (k f) -> p k f", k=8)
            WsT = WR.alloc(8 * 128, BF16).rearrange("p (g t) -> p g t", g=8)
            lng = WR.alloc(1024, F32)
            lnb = WR.alloc(1024, F32)
            gbc = WR.alloc(1024, F32)
            bsT = WR.alloc(8, F32)
            dummy = WR.alloc(16, F32)
            NZ = 3
            NX = 3
            zu = [WR.alloc(1024, F32) for _ in range(NZ)]
            zv = [WR.alloc(1024, F32) for _ in range(NZ)]
            vt = [WR.alloc(1024, F32) for _ in range(2)]
            xts = [AR.alloc(2048, F32).rearrange("p (s d) -> p s d", s=2) for _ in range(NX)]
            xos = [AR.alloc(2048, F32).rearrange("p (s d) -> p s d", s=2) for _ in range(2)]
            hb = [AR.alloc(1024, BF16) for _ in range(2)]
            hT = [AR.alloc(8 * 128, BF16).rearrange("p (k t) -> p k t", k=8) for _ in range(2)]
            vln = [AR.alloc(1024, BF16) for _ in range(2)]
            sg = [AR.alloc(1024, BF16) for _ in range(2)]
            sT = [AR.alloc(8 * 128, BF16).rearrange("p (k t) -> p k t", k=8) for _ in range(2)]
            tmp = [AR.alloc(16, F32) for _ in range(2)]
            tmp2 = [AR.alloc(16, F32) for _ in range(2)]

            load_bcast(gbc, norm_mix[layer], 'gbc')
            load_bcast(lng, c_ln_g[o], 'lng')
            load_bcast(lnb, c_ln_b[o], 'lnb')
            S.add('sp', ('dma_start', A(out=bsT, in_=c_b_sT[o])), writes=['bsT'], dma=1)
            load_x(xts[0], src, 0, 'xt0')
            k1 = load_w(Wci, c_w_in[o], 'Wci', 8, 2048)
            k2 = load_w(Wco, c_w_out[o], 'Wco', 8, 1024)
            S.add('pool', ('dma_start', A(out=WsT, in_=c_w_sT[o])), writes=['WsT_raw'], dma=1)
            join(k1, 'Wci', dummy[:, 0:4])
            join(k2, 'Wco', dummy[:, 4:8])
            S.add('pool', ('memset', A(WsT[64:128, :, 0:64], 0.0)), reads=['WsT_raw'], writes=['WsT'])

            def st_a(g):
                b, s = g // 2, g % 2
                q = g % 2
                z = g % NZ
                xk = f'xt{b % NX}'
                x2 = xts[b % NX][:, s, :]
                rms_norm(x2, gbc, 'gbc', hb[q], xk, f'hb{q}', tmp[q], f'tmp{q}')
                transpose8(hb[q], f'hb{q}', hT[q], 0, f'hT{q}', 0, 'act')
                for cg in range(4):
                    bi = 2 + cg % 2
                    for kc in range(8):
                        S.add('pe', ('matmul', A(
                            out=banks[bi][:, :], lhsT=hT[q][:, kc, :], rhs=Wci[:, kc, cg * 512:(cg + 1) * 512],
                            start=(kc == 0), stop=(kc == 7))), reads=['Wci', f'hT{q}'], writes=[BK(bi)])
                    dstz = (zu[z] if cg < 2 else zv[z])[:, (cg % 2) * 512:(cg % 2 + 1) * 512]
                    S.add('act', ('activation', A(out=dstz, in_=banks[bi][:, :], func=AF.Gelu_apprx_tanh)),
                          writes=[BK(bi), f'z{z}_{cg}'])

            def st_b1(g):
                q = g % 2
                z = g % NZ
                t2 = tmp2[q]
                stt = t2[:, 0:12]
                mv = t2[:, 12:14]
                ve = t2[:, 14:15]
                rs = t2[:, 15:16]
                zvs = zv[z]
                S.add('dve', ('bn_stats', A(out=stt[:, 0:6], in_=zvs[:, 0:512])), reads=[f'z{z}_2'], writes=[f't2{q}a'])
                S.add('dve', ('bn_stats', A(out=stt[:, 6:12], in_=zvs[:, 512:1024])), reads=[f'z{z}_3'], writes=[f't2{q}b'])
                S.add('dve', ('bn_aggr', A(out=mv, in_=stt)), reads=[f't2{q}a', f't2{q}b'], writes=[f't2{q}mv'])
                S.add('dve', ('tensor_scalar', A(out=ve, in0=mv[:, 1:2], scalar1=EPS, scalar2=None, op0=ALU.add)),
                      reads=[f't2{q}mv'], writes=[f't2{q}ve'])
                S.add('pool', ('tensor_tensor', A(out=rs, in0=ve, in1=neghalf[:, 0:1], op=ALU.pow)),
                      reads=[f't2{q}ve', 'neghalf'], writes=[f't2{q}rs'])
                vts = vt[q]
                S.add('dve', ('tensor_scalar', A(out=vts, in0=zvs, scalar1=mv[:, 0:1], scalar2=rs,
                                                 op0=ALU.subtract, op1=ALU.mult)),
                      reads=[f'z{z}_2', f'z{z}_3', f't2{q}mv', f't2{q}rs'], writes=[f'vt{q}'])
                S.add('pool', ('tensor_tensor', A(out=vts, in0=vts, in1=lng, op=ALU.mult)), reads=['lng'], writes=[f'vt{q}'])
                S.add('pool', ('tensor_tensor', A(out=vln[q], in0=vts, in1=lnb, op=ALU.add)), reads=[f'vt{q}', 'lnb'],
                      writes=[f'vln{q}'])

            def st_b2(g):
                q = g % 2
                z = g % NZ
                vl = vln[q]
                sgs = sg[q]
                zus = zu[z]
                for gh in range(2):
                    bi = 4 + gh
                    for gg_ in range(4 * gh, 4 * gh + 4):
                        mps = banks[bi][:, (gg_ % 4) * 128:(gg_ % 4 + 1) * 128]
                        S.add('pe', ('matmul', A(out=mps, lhsT=WsT[:, gg_, :], rhs=vl[:, gg_ * 128:(gg_ + 1) * 128],
                                                 start=True, stop=True)), reads=['WsT', f'vln{q}'], writes=[BK(bi)])
                    for gg_ in range(4 * gh, 4 * gh + 4):
                        mps = banks[bi][:, (gg_ % 4) * 128:(gg_ % 4 + 1) * 128]
                        S.add('dve', ('scalar_tensor_tensor', A(
                            out=sgs[:, gg_ * 128:(gg_ + 1) * 128], in0=mps, scalar=bsT[:, gg_:gg_ + 1],
                            in1=zus[:, gg_ * 128:(gg_ + 1) * 128], op0=ALU.add, op1=ALU.mult)),
                            reads=['bsT', f'z{z}_{gg_ // 4}'], writes=[BK(bi), f'sg{q}_{gg_}'])
                tp = bank_bf(1)
                for kc in range(8):
                    S.add('pe', ('transpose', A(out=tp[:, kc, :], in_=sgs[:, kc * 128:(kc + 1) * 128], identity=ident[:])),
                          reads=[f'sg{q}_{kc}', 'ident'], writes=[BK(1)])
                S.add('act', ('copy', A(out=sT[q], in_=tp)), writes=[BK(1), f'sT{q}'])

            def st_b3(g):
                b, s = g // 2, g % 2
                q = g % 2
                p = b % 2
                xt = xts[b % NX]
                xo = xos[p]
                for dh in range(2):
                    bi = 6 + dh
                    for kc in range(8):
                        S.add('pe', ('matmul', A(
                            out=banks[bi][:, :], lhsT=sT[q][:, kc, :], rhs=Wco[:, kc, dh * 512:(dh + 1) * 512],
                            start=(kc == 0), stop=(kc == 7))), reads=['Wco', f'sT{q}'], writes=[BK(bi)])
                    S.add('dve', ('tensor_tensor', A(
                        out=xo[:, s, dh * 512:(dh + 1) * 512], in0=banks[bi][:, :],
                        in1=xt[:, s, dh * 512:(dh + 1) * 512], op=ALU.add)),
                        reads=[f'xt{b % NX}'], writes=[BK(bi), f'xo{p}_{s}'])
                if s == 1:
                    S.add('sp', ('dma_start', A(
                        out=dst[b * TB:(b + 1) * TB, :].rearrange("(s p) d -> p s d", p=128), in_=xo)),
                        reads=[f'xo{p}_0', f'xo{p}_1'], dma=1)
                    if b + NX < nblk:
                        load_x(xts[b % NX], src, b + NX, f'xt{b % NX}')

            nsub = 2 * nblk
            for bb in range(1, min(NX, nblk)):
                load_x(xts[bb], src, bb, f'xt{bb}')
            for t in range(nsub + 3):
                if t < nsub:
                    st_a(t)
                if 0 <= t - 1 < nsub:
                    st_b1(t - 1)
                if 0 <= t - 2 < nsub:
                    st_b2(t - 2)
                if 0 <= t - 3 < nsub:
                    st_b3(t - 3)

        def even_pass(layer, src, dst):
            ev = layer // 2
            lam_init = 0.8 - 0.6 * math.exp(-0.3 * layer)
            NCH = S_LEN // 128
            S.fence()
            WR.reset()
            AR.reset()
            Win = WR.alloc(8 * 2560, BF16).rearrange("p (k f) -> p k f", k=8)
            Wout = WR.alloc(8 * 1024, BF16).rearrange("p (k f) -> p k f", k=8)
            kT = WR.alloc(4 * S_LEN, BF16).rearrange("p (h t) -> p h t", h=4)
            V = WR.alloc(NCH * 512, BF16).rearrange("p (c v) -> p c v", c=NCH)
            WA = WR.alloc(4 * 128, BF16).rearrange("p (c j) -> p c j", c=4)
            WX = WR.alloc(4 * 128, BF16).rearrange("p (c j) -> p c j", c=4)
            gbc = AR.alloc(1024, F32)
            pv = AR.alloc(33, F32)
            ldl = AR.alloc(256, F32)
            sm = AR.alloc(64, F32)
            dummy = AR.alloc(16, F32)
            ltmp = AR.alloc(64, F32)
            hprev = AR.alloc(4, F32)
            tmp = [AR.alloc(16, F32) for _ in range(2)]
            xt = AR.alloc(2048, F32).rearrange("p (s d) -> p s d", s=2)
            hb = [AR.alloc(1024, BF16) for _ in range(2)]
            hT = AR.alloc(8 * TB, BF16).rearrange("p (k t) -> p k t", k=8)
            qc = AR.alloc(4 * 2 * TB, BF16).rearrange("p (h t) -> p h t", h=4)
            xbh = AR.alloc(4 * (TB + 3), F32).rearrange("p (c t) -> p c t", c=4)
            gg = AR.alloc(4 * TB, F32).rearrange("p (c t) -> p c t", c=4)
            xc = AR.alloc(4 * TB, F32).rearrange("p (c t) -> p c t", c=4)
            xcb = AR.alloc(4 * TB, BF16).rearrange("p (c t) -> p c t", c=4)
            Tr = AR.alloc(4 * TB, F32).rearrange("p (c t) -> p c t", c=4)
            Ti = AR.alloc(4 * TB, F32).rearrange("p (c t) -> p c t", c=4)
            aa = AR.alloc(4 * TB, F32).rearrange("p (c t) -> p c t", c=4)
            a2 = AR.alloc(4 * TB, F32).rearrange("p (c t) -> p c t", c=4)
            bt = Ti
            hs = Tr
            yT = AR.alloc(8 * TB, BF16).rearrange("p (k t) -> p k t", k=8)
            NPT = 4
            pT = [AR.alloc(2 * TB, BF16) for _ in range(NPT)]
            fr = AR.alloc(2 * TB, F32)
            ft = AR.alloc(2 * TB, F32)
            fo4 = AR.alloc(4 * TB, F32).rearrange("p (h t) -> p h t", h=4)
            sd = AR.alloc(4 * TB, F32).rearrange("p (h t) -> p h t", h=4)

            s1 = sm[:, 0:1]
            s2 = sm[:, 1:2]
            e1 = sm[:, 2:3]
            e2 = sm[:, 3:4]
            neglam = sm[:, 4:5]
            g1 = sm[:, 5:6]
            yv = sm[:, 8:12]
            tv = sm[:, 12:16]
            sc = sm[:, 16:20]
            sch = sm[:, 20:24]
            hba = sm[:, 24:28]
            hbx = sm[:, 28:32]
            cw = lambda w, c: pv[:, w * 4 + c:w * 4 + c + 1]
            cb = lambda c: pv[:, 16 + c:17 + c]

            load_bcast(gbc, norm_mix[layer], 'gbc')
            load_bcast(ldl, diff_l[ev], 'ldl')
            S.add('sp', ('dma_start', A(out=pv, in_=pvec[ev])), writes=['pv'], dma=1)
            S.add('sp', ('dma_start', A(
                out=xt, in_=src[0:TB, :].rearrange("(s p) d -> p s d", p=128))), writes=['xt'], dma=1)
            k1 = load_w(Win, ab_w_in[ev], 'Win', 8, 2560)
            k2 = load_w(Wout, ab_w_out[ev], 'Wout', 8, 1024)
            join(k1, 'Win', dummy[:, 0:4])
            join(k2, 'Wout', dummy[:, 4:8])
            S.add('pool', ('memset', A(WA, 0.0)), writes=['WA0'])
            S.add('pool', ('memset', A(WX, 0.0)), writes=['WX0'])
            for nm, wt, srcw in (('WA', WA, lru_wa), ('WX', WX, lru_wx)):
                sv = srcw[ev].rearrange("(c j) i o -> j i c o", j=2)
                wk = []
                for j in range(2):
                    key = f'{nm}_{j}'
                    wk.append(key)
                    S.add('pool', ('dma_start', A(
                        out=wt[j * 64:(j + 1) * 64, :, j * 64:(j + 1) * 64], in_=sv[j])),
                        reads=[nm + '0'], writes=[key], dma=1)
                join(wk, nm, dummy[:, 8:10] if nm == 'WA' else dummy[:, 10:12])
            S.add('pool', ('memset', A(qc[64:128, :, 0:TB], 0.0)), writes=['qcz0'])
            S.add('pool', ('memset', A(qc[0:64, :, TB:2 * TB], 0.0)), writes=['qcz1'])
            S.add('dve', ('tensor_tensor', A(out=ltmp, in0=ldl[:, 0:64], in1=ldl[:, 64:128], op=ALU.mult)),
                  reads=['ldl'], writes=['ltmp'])
            S.add('dve', ('tensor_reduce', A(out=s1, in_=ltmp, axis=mybir.AxisListType.X, op=ALU.add)),
                  reads=['ltmp'], writes=['s1'])
            S.add('dve', ('tensor_tensor', A(out=ltmp, in0=ldl[:, 128:192], in1=ldl[:, 192:256], op=ALU.mult)),
                  reads=['ldl', 's1'], writes=['ltmp'])
            S.add('dve', ('tensor_reduce', A(out=s2, in_=ltmp, axis=mybir.AxisListType.X, op=ALU.add)),
                  reads=['ltmp'], writes=['s2'])
            S.add('act', ('activation', A(out=e1, in_=s1, func=AF.Exp)), reads=['s1'], writes=['e1'])
            S.add('act', ('activation', A(out=e2, in_=s2, func=AF.Exp)), reads=['s2'], writes=['e2'])
            S.add('dve', ('tensor_tensor', A(out=neglam, in0=e2, in1=e1, op=ALU.subtract)), reads=['e1', 'e2'],
                  writes=['neglam'])
            S.add('dve', ('tensor_scalar', A(out=neglam, in0=neglam, scalar1=-lam_init, scalar2=None, op0=ALU.add)),
                  reads=['neglam'], writes=['neglam'])
            S.add('dve', ('tensor_scalar', A(out=g1, in0=pv[:, 32:33], scalar1=(1.0 - lam_init), scalar2=None,
                                                   op0=ALU.mult)), reads=['pv'], writes=['g1'])
            S.add('act', ('activation', A(out=yv, in_=pv[:, 28:32], func=AF.Exp, scale=-1.0)), reads=['pv'], writes=['yv'])
            S.add('dve', ('tensor_scalar', A(out=tv, in0=yv, scalar1=-0.25, scalar2=1.0 / 3.0, op0=ALU.mult, op1=ALU.add)),
                  reads=['yv'], writes=['tv'])
            S.add('dve', ('tensor_tensor', A(out=tv, in0=tv, in1=yv, op=ALU.mult)), reads=['tv', 'yv'], writes=['tv'])
            S.add('dve', ('tensor_scalar', A(out=tv, in0=tv, scalar1=-1.0, scalar2=0.5, op0=ALU.mult, op1=ALU.add)),
                  reads=['tv'], writes=['tv'])
            S.add('dve', ('tensor_tensor', A(out=tv, in0=tv, in1=yv, op=ALU.mult)), reads=['tv', 'yv'], writes=['tv'])
            S.add('dve', ('tensor_scalar', A(out=tv, in0=tv, scalar1=-1.0, scalar2=1.0, op0=ALU.mult, op1=ALU.add)),
                  reads=['tv'], writes=['tv'])
            S.add('dve', ('tensor_tensor', A(out=tv, in0=tv, in1=yv, op=ALU.mult)), reads=['tv', 'yv'], writes=['tv'])
            S.add('dve', ('tensor_scalar', A(out=sc, in0=tv, scalar1=-8.0, scalar2=None, op0=ALU.mult)),
                  reads=['tv'], writes=['sc'])
            S.add('dve', ('tensor_scalar', A(out=sch, in0=tv, scalar1=-4.0, scalar2=None, op0=ALU.mult)),
                  reads=['tv'], writes=['sch'])
            S.add('dve', ('tensor_scalar', A(out=hba, in0=pv[:, 20:24], scalar1=0.5, scalar2=None, op0=ALU.mult)),
                  reads=['pv'], writes=['hba'])
            S.add('dve', ('tensor_scalar', A(out=hbx, in0=pv[:, 24:28], scalar1=0.5, scalar2=None, op0=ALU.mult)),
                  reads=['pv'], writes=['hbx'])
            S.add('pool', ('memset', A(xbh[:, :, 0:3], 0.0)), writes=['xbh_halo'])
            S.add('pool', ('memset', A(hprev, 0.0)), writes=['hprev'])

            gcount = [0]

            def gbank():
                bi = 1 + gcount[0] % 3
                gcount[0] += 1
                return bi

            pcount = [0]
            for b in range(nblk):
                if b > 0:
                    S.add('sp', ('dma_start', A(
                        out=xt, in_=src[b * TB:(b + 1) * TB, :].rearrange("(s p) d -> p s d", p=128))),
                        writes=['xt'], dma=1)
                for s in range(2):
                    rms_norm(xt[:, s, :], gbc, 'gbc', hb[s], 'xt', f'hb{s}', tmp[s], f'tmp{s}')
                    transpose8(hb[s], f'hb{s}', hT, s * 128, 'hT', 0, 'act')

                def proj_fm(col0, evac):
                    bi = gbank()
                    ps_ = banks[bi][:, 0:TB]
                    for kc in range(8):
                        S.add('pe', ('matmul', A(out=ps_, lhsT=Win[:, kc, col0:col0 + 128],
                                                                       rhs=hT[:, kc, :], start=(kc == 0), stop=(kc == 7))),
                              reads=['Win', 'hT'], writes=[BK(bi)])
                    evac(ps_, BK(bi))

                def evac_q(ps_, bk, h):
                    S.add('dve', ('tensor_copy', A(out=qc[0:64, h, 0:TB], in_=ps_[0:64, :])),
                          reads=['qcz0', 'qcz1'], writes=[bk, f'qc{h}'])
                    S.add('dve', ('tensor_copy', A(out=qc[64:128, h, TB:2 * TB], in_=ps_[64:128, :])),
                          writes=[bk, f'qc{h}'])

                for h in range(4):
                    proj_fm(h * 128, lambda ps_, bk, h=h: evac_q(ps_, bk, h))
                for h in range(4):
                    proj_fm(512 + h * 128, lambda ps_, bk, h=h: S.add(
                        'dve', ('tensor_copy', A(out=kT[:, h, b * TB:(b + 1) * TB], in_=ps_)),
                        writes=[bk, f'kT{h}_{b}']))
                for ts in range(2):
                    bi = gbank()
                    for kc in range(8):
                        S.add('pe', ('matmul', A(
                            out=banks[bi][:, :], lhsT=hT[:, kc, ts * 128:(ts + 1) * 128],
                            rhs=Win[:, kc, 1024:1536], start=(kc == 0), stop=(kc == 7))),
                            reads=['Win', 'hT'], writes=[BK(bi)])
                    S.add('dve', ('tensor_copy', A(out=V[:, b * 2 + ts, :], in_=banks[bi][:, :])),
                          writes=[BK(bi), f'V{b * 2 + ts}'])
                for c in range(4):
                    proj_fm(1536 + c * 128, lambda ps_, bk, c=c: S.add(
                        'act', ('copy', A(out=xbh[:, c, 3:3 + TB], in_=ps_)), reads=['xbh_halo'],
                        writes=[bk, f'xbh{c}']))
                for c in range(4):
                    proj_fm(2048 + c * 128, lambda ps_, bk, c=c: S.add(
                        'act', ('activation', A(out=gg[:, c, :], in_=ps_, func=AF.Gelu_apprx_tanh)),
                        writes=[bk, f'gg{c}']))
                for c in range(4):
                    S.add('dve', ('tensor_scalar', A(out=xc[:, c, :], in0=xbh[:, c, 0:TB], scalar1=cw(0, c),
                                                                scalar2=cb(c), op0=ALU.mult, op1=ALU.add)),
                          reads=[f'xbh{c}', 'xbh_halo', 'pv'], writes=[f'xc{c}'])
                    for w in range(1, 4):
                        S.add('dve', ('scalar_tensor_tensor', A(
                            out=xc[:, c, :], in0=xbh[:, c, w:w + TB], scalar=cw(w, c), in1=xc[:, c, :],
                            op0=ALU.mult, op1=ALU.add)), reads=[f'xbh{c}', 'xbh_halo', 'pv'], writes=[f'xc{c}'])
                    S.add('pool', ('tensor_copy', A(out=xcb[:, c, :], in_=xc[:, c, :])), reads=[f'xc{c}'],
                          writes=[f'xcb{c}'])
                S.add('pool', ('tensor_copy', A(out=xbh[:, :, 0:3], in_=xbh[:, :, TB:TB + 3])),
                      reads=[f'xbh{c}' for c in range(4)] + [f'xc{c}' for c in range(4)], writes=['xbh_halo'])
                nkc = 2 * b + 2
                steps = [(h, kc) for h in range(4) for kc in range(nkc)]
                LA = 2
                pts = {}

                def lru_part2():
                    for c in range(4):
                        for nm, wt, Tt, hbias in (('r', WA, Tr, hba), ('i', WX, Ti, hbx)):
                            bi = gbank()
                            ps_ = banks[bi][:, 0:TB]
                            S.add('pe', ('matmul', A(out=ps_, lhsT=wt[:, c, :], rhs=xcb[:, c, :],
                                                                                start=True, stop=True)),
                                  reads=['WA' if nm == 'r' else 'WX', f'xcb{c}'], writes=[BK(bi)])
                            S.add('act', ('activation', A(
                                out=Tt[:, c, :], in_=ps_, func=AF.Tanh, bias=hbias[:, c:c + 1], scale=0.5)),
                                reads=['hba', 'hbx'],
                                writes=[BK(bi), f'T{nm}{c}'] + ([f'hs{c}'] if nm == 'r' else [f'bt{c}']))
                    for c in range(4):
                        S.add('act', ('activation', A(out=aa[:, c, :], in_=Tr[:, c, :], func=AF.Exp,
                                                                 bias=sch[:, c:c + 1], scale=sch[:, c:c + 1])),
                              reads=[f'Tr{c}', 'sch'], writes=[f'aa{c}'])
                        S.add('act', ('activation', A(out=a2[:, c, :], in_=Tr[:, c, :], func=AF.Exp,
                                                                 bias=sc[:, c:c + 1], scale=sc[:, c:c + 1])),
                              reads=[f'Tr{c}', 'sc'], writes=[f'a2{c}'])

                def qk_exp(i):
                    h, kc = steps[i]
                    j = kc - 2 * b
                    si = 2 + i % 2
                    S.add('pe', ('matmul', A(
                        out=banks[si][:, :], lhsT=kT[:, h, kc * 128:(kc + 1) * 128], rhs=qc[:, h, :],
                        start=True, stop=True)), reads=[f'kT{h}_{kc // 2}', f'qc{h}'], writes=[BK(si)])
                    pi = pcount[0] % NPT
                    pcount[0] += 1
                    pt = pT[pi]
                    pkey = f'pT{pi}'
                    pts[i] = (pt, pkey)
                    S.add('act', ('activation', A(out=pt, in_=banks[si][:, :], func=AF.Exp, scale=0.125)),
                          writes=[BK(si), pkey])
                    if j >= 0:
                        c0 = j * 128
                        if c0 > 0:
                            S.add('pool', ('memset', A(pt[:, 0:c0], 0.0)), writes=[pkey])
                            S.add('pool', ('memset', A(pt[:, TB:TB + c0], 0.0)), writes=[pkey])
                        S.add('pool', ('memset', A(pt[64:128, c0:c0 + 64], 0.0)), writes=[pkey])
                        S.add('pool', ('memset', A(pt[64:128, TB + c0:TB + c0 + 64], 0.0)), writes=[pkey])

                def pv_sum(i):
                    h, kc = steps[i]
                    ob = 4 + h % 2
                    sb_ = 6 + h % 2
                    pt, pkey = pts.pop(i)
                    S.add('pe', ('matmul', A(
                        out=banks[ob][:, :], lhsT=V[:, kc, h * 128:(h + 1) * 128], rhs=pt,
                        start=(kc == 0), stop=(kc == nkc - 1))), reads=[f'V{kc}', pkey], writes=[BK(ob)])
                    S.add('pe', ('matmul', A(
                        out=banks[sb_][:, :], lhsT=ones_bf[:], rhs=pt,
                        start=(kc == 0), stop=(kc == nkc - 1))), reads=['ones_bf', pkey], writes=[BK(sb_)])
                    if kc == nkc - 1:
                        S.add('dve', ('reciprocal', A(out=fr, in_=banks[sb_][:, :])), writes=[BK(sb_), 'fr'])
                        S.add('dve', ('tensor_tensor', A(out=ft, in0=banks[ob][:, :], in1=fr, op=ALU.mult)),
                              reads=['fr'], writes=[BK(ob), 'ft'])
                        S.add('dve', ('scalar_tensor_tensor', A(out=fo4[:, h, :], in0=ft[:, TB:2 * TB], scalar=neglam,
                                                               in1=ft[:, 0:TB], op0=ALU.mult, op1=ALU.add)),
                              reads=['ft', 'neglam'], writes=[f'fo{h}'])
                        S.add('dve', ('tensor_tensor', A(out=sd[:, h, :], in0=fo4[:, h, :], in1=fo4[:, h, :], op=ALU.mult)),
                              reads=[f'fo{h}'], writes=[f'sd{h}'])

                nst = len(steps)
                for i in range(min(LA, nst)):
                    qk_exp(i)
                lru2_at = min(3, nst - 1)
                for i in range(nst):
                    pv_sum(i)
                    if i + LA < nst:
                        qk_exp(i + LA)
                    if i == lru2_at:
                        lru_part2()
                S.add('act', ('activation', A(out=a2.rearrange("p c t -> p (c t)"), in_=a2.rearrange("p c t -> p (c t)"),
                                              func=AF.Sqrt, bias=qbias[:, 0:1], scale=-0.25)),
                      reads=['qbias'], writes=[f'a2{c}' for c in range(4)])
                for c in range(4):
                    S.add('dve', ('scalar_tensor_tensor', A(out=bt[:, c, :], in0=Ti[:, c, :], scalar=1.0,
                                                                       in1=xc[:, c, :], op0=ALU.add, op1=ALU.mult)),
                          reads=[f'Ti{c}', f'xc{c}'], writes=[f'bt{c}', f'Ti{c}'])
                    S.add('dve', ('tensor_tensor', A(out=bt[:, c, :], in0=bt[:, c, :], in1=a2[:, c, :], op=ALU.mult)),
                          reads=[f'a2{c}'], writes=[f'bt{c}'])
                    S.add('dve', ('tensor_tensor_scan', A(out=hs[:, c, :], data0=aa[:, c, :], data1=bt[:, c, :],
                                                                     initial=hprev[:, c:c + 1], op0=ALU.mult, op1=ALU.add)),
                          reads=[f'aa{c}', f'bt{c}', 'hprev'], writes=[f'hs{c}', f'Tr{c}'])
                    S.add('dve', ('tensor_tensor', A(out=yT[:, 4 + c, :], in0=gg[:, c, :], in1=hs[:, c, :], op=ALU.mult)),
                          reads=[f'gg{c}', f'hs{c}'], writes=[f'yT{4 + c}'])
                S.add('pool', ('tensor_copy', A(out=hprev, in_=hs[:, :, TB - 1])),
                      reads=[f'hs{c}' for c in range(4)], writes=['hprev'])
                for h in range(4):
                    mb = 1 - h // 2
                    S.add('pe', ('matmul', A(out=banks[mb][:, (h % 2) * TB:(h % 2 + 1) * TB], lhsT=ones_f[:], rhs=sd[:, h, :],
                                             start=True, stop=True)),
                          reads=['ones_f', f'sd{h}'], writes=[BK(mb)])
                for hh in range(2):
                    mb = 1 - hh
                    S.add('act', ('activation', A(out=sd[:, 2 * hh:2 * hh + 2, :].rearrange("p h t -> p (h t)"),
                                                  in_=banks[mb][:, :], func=AF.Sqrt, bias=epsb[:, 0:1], scale=1.0 / 128.0)),
                          reads=['epsb'], writes=[BK(mb), f'sd{2 * hh}', f'sd{2 * hh + 1}'])
                S.add('dve', ('reciprocal', A(out=sd.rearrange("p h t -> p (h t)"), in_=sd.rearrange("p h t -> p (h t)"))),
                      writes=[f'sd{h}' for h in range(4)])
                S.add('dve', ('scalar_tensor_tensor', A(out=yT[:, 0:4, :].rearrange("p h t -> p (h t)"),
                                                       in0=fo4.rearrange("p h t -> p (h t)"), scalar=g1,
                                                       in1=sd.rearrange("p h t -> p (h t)"), op0=ALU.mult, op1=ALU.mult)),
                      reads=[f'fo{h}' for h in range(4)] + ['g1'] + [f'sd{h}' for h in range(4)],
                      writes=[f'yT{h}' for h in range(4)])
                for ts in range(2):
                    for dh in range(2):
                        bi = gbank()
                        for kc in range(8):
                            S.add('pe', ('matmul', A(
                                out=banks[bi][:, :], lhsT=yT[:, kc, ts * 128:(ts + 1) * 128],
                                rhs=Wout[:, kc, dh * 512:(dh + 1) * 512], start=(kc == 0), stop=(kc == 7))),
                                reads=['Wout', f'yT{kc}'], writes=[BK(bi)])
                        S.add('dve', ('tensor_tensor', A(
                            out=xt[:, ts, dh * 512:(dh + 1) * 512], in0=banks[bi][:, :],
                            in1=xt[:, ts, dh * 512:(dh + 1) * 512], op=ALU.add)),
                            writes=[BK(bi), 'xt'])
                S.add('sp', ('dma_start', A(
                    out=dst[b * TB:(b + 1) * TB, :].rearrange("(s p) d -> p s d", p=128), in_=xt)),
                    reads=['xt'], dma=1)

        bufs = [scrA, scrB]
        cur = x_in
        nb = 0
        if passes is None:
            passes_l = []
            for layer in layers:
                passes_l.append(('even' if layer % 2 == 0 else 'odd', layer))
                passes_l.append(('ffn', layer))
        else:
            passes_l = list(passes)
        for pi, (kind, layer) in enumerate(passes_l):
            last = (pi == len(passes_l) - 1)
            d = out if last else bufs[nb % 2]
            nb += 1
            if kind == 'even':
                even_pass(layer, cur, d)
            elif kind == 'odd':
                odd_pass(layer, cur, d)
            else:
                ffn_pass(layer, cur, d, apply_final=(last and final_norm))
            cur = d
        if not final_ops:
            for q in S.dma_slots.values():
                for op in q:
                    if op is not None:
                        final_ops.append(op)
        else:
            for q in S.dma_slots.values():
                for op in q:
                    if op is not None and op not in final_ops:
                        final_ops.append(op)
        S.emit(nc, final_wait_ops=final_ops)
    return nc


def prep_weights(inp):
    f = lambda a: np.ascontiguousarray(np.asarray(a, dtype=np.float32))
    w = {}
    for k in ('norm_mix', 'norm_ffn', 'norm_final', 'ab_w_in', 'ab_w_out', 'lru_wa', 'lru_wx', 'c_w_in', 'c_ln_g',
              'c_ln_b', 'c_w_out', 'ffn_w1', 'ffn_w2'):
        w[k] = f(inp[k])
    w['diff_l'] = f(np.concatenate([inp['diff_lq1'], inp['diff_lk1'], inp['diff_lq2'], inp['diff_lk2']], axis=1))
    pv = []
    for e in range(2):
        cols = []
        cwv = np.asarray(inp['lru_conv_w'][e])
        cols.append(cwv.reshape(4, 4, 128).transpose(2, 0, 1).reshape(128, 16))
        for nm in ('lru_conv_b', 'lru_ba', 'lru_bx', 'lru_lambda'):
            cols.append(np.asarray(inp[nm][e]).reshape(4, 128).T)
        cols.append(np.asarray(inp['diff_subln'][e]).reshape(128, 1))
        pv.append(np.concatenate(cols, axis=1))
    w['pvec'] = f(np.stack(pv))
    w['c_w_sT'] = f(np.transpose(np.asarray(inp['c_w_s']), (0, 3, 1, 2)))
    w['c_b_sT'] = f(np.transpose(np.asarray(inp['c_b_s']), (0, 2, 1)))
    return w


_NC_CACHE = {}


def kernel(**inputs):
    x = np.asarray(inputs['x'], dtype=np.float32)
    B, S_LEN, _ = x.shape
    w = prep_weights(inputs)
    key = (S_LEN,)
    if key not in _NC_CACHE:
        _NC_CACHE[key] = build_program(S_LEN)
    nc = _NC_CACHE[key]
    in_maps = []
    for c in range(B):
        m = dict(w)
        m['x'] = np.ascontiguousarray(x[c])
        in_maps.append(m)
    res = run_bass_kernel_spmd(nc, in_maps, core_ids=list(range(B)))
    return np.stack([np.asarray(r['out'], dtype=np.float32) for r in res.results], axis=0)
```

```python
import contextlib
import math

import numpy as np
import concourse.bass as bass
import concourse.mybir as mybir
from concourse.bass_utils import run_bass_kernel_spmd

F32 = mybir.dt.float32
BF16 = mybir.dt.bfloat16
U8 = mybir.dt.uint8
AF = mybir.ActivationFunctionType
ALU = mybir.AluOpType

D = 1024
TB = 256
EPS = 1e-6
ENGS = ('pe', 'act', 'dve', 'pool', 'sp')
SEM_CAP = 30000
DBG = {'ffn': 9, 'noW': 0}


def A(*a, **k):
    return (a, k)


class Op:
    __slots__ = ('eng', 'fn', 'deps', 'is_dma', 'ndma', 'slot', 'target', 'epoch', 'msval')


class Sched:
    def __init__(self, nslots=8):
        self.ops = {e: [] for e in ENGS}
        self.lastw = {}
        self.readers = {}
        self.nslots = nslots
        self.dma_slots = {}
        self.dma_count = {e: 0 for e in ENGS}
        self.fence_deps = set()

    def add(self, eng, fn, reads=(), writes=(), dma=0):
        op = Op()
        op.eng = eng
        op.fn = fn
        op.is_dma = dma > 0
        op.ndma = dma
        op.slot = None
        op.target = None
        op.epoch = None
        op.msval = None
        deps = set(self.fence_deps)
        for k in reads:
            w = self.lastw.get(k)
            if w is not None:
                deps.add(w)
        for k in writes:
            w = self.lastw.get(k)
            if w is not None:
                deps.add(w)
            for r in self.readers.get(k, ()):
                deps.add(r)
        if dma:
            q = self.dma_slots.setdefault(eng, [None] * self.nslots)
            s = self.dma_count[eng] % self.nslots
            self.dma_count[eng] += 1
            if q[s] is not None:
                deps.add(q[s])
            q[s] = op
            op.slot = (eng, s)
        deps.discard(op)
        op.deps = deps
        for k in reads:
            self.readers.setdefault(k, []).append(op)
        for k in writes:
            self.lastw[k] = op
            self.readers[k] = []
        self.ops[eng].append(op)
        return op

    def fence(self):
        deps = set()
        for e in ENGS:
            last = None
            for op in reversed(self.ops[e]):
                if not op.is_dma:
                    last = op
                    break
            if last is not None:
                deps.add(last)
        for q in self.dma_slots.values():
            for op in q:
                if op is not None:
                    deps.add(op)
        self.fence_deps = deps
        self.lastw = {}
        self.readers = {}

    def emit(self, nc, final_wait_ops=()):
        needed = set()
        for e in ENGS:
            for op in self.ops[e]:
                for d in op.deps:
                    if e == 'pe' and d.eng == 'pe' and not d.is_dma:
                        continue
                    needed.add(d)
        for d in final_wait_ops:
            needed.add(d)
        n_epochs = {}
        for e in ENGS:
            c = 0
            ep = 0
            for op in self.ops[e]:
                if (not op.is_dma) and op in needed:
                    if c >= SEM_CAP:
                        ep += 1
                        c = 0
                    c += 1
                    op.epoch = ep
                    op.msval = c
            n_epochs[e] = ep + 1
        slot_cnt = {}
        for e in ENGS:
            for op in self.ops[e]:
                if op.is_dma:
                    slot_cnt[op.slot] = slot_cnt.get(op.slot, 0) + 16 * op.ndma
                    op.target = slot_cnt[op.slot]
        with contextlib.ExitStack() as st:
            esem = {}
            for e in ENGS:
                for ep in range(n_epochs[e]):
                    esem[(e, ep)] = st.enter_context(nc.semaphore(f"s_{e}_{ep}"))
            dsem = {}
            for slot in slot_cnt:
                dsem[slot] = st.enter_context(nc.semaphore(f"d_{slot[0]}_{slot[1]}"))
            block = st.enter_context(nc.Block())
            ops = self.ops

            def run(e, eng):
                seen_e = {}
                seen_d = {}
                for op in ops[e]:
                    self._waits(e, eng, op.deps, seen_e, seen_d, esem, dsem)
                    name, (pa, kw) = op.fn
                    if op.is_dma:
                        getattr(eng, name)(*pa, **kw).then_inc(dsem[op.slot], 16)
                    else:
                        ins = getattr(eng, name)(*pa, **kw)
                        if op.msval is not None:
                            ins.then_inc(esem[(e, op.epoch)], 1)
                if e == 'sp' and final_wait_ops:
                    self._waits(e, eng, final_wait_ops, seen_e, seen_d, esem, dsem)

            @block.tensor
            def _(eng):
                run('pe', eng)

            @block.scalar
            def _(eng):
                run('act', eng)

            @block.vector
            def _(eng):
                run('dve', eng)

            @block.gpsimd
            def _(eng):
                run('pool', eng)

            @block.sync
            def _(eng):
                run('sp', eng)

    @staticmethod
    def _waits(e, eng, deps, seen_e, seen_d, esem, dsem):
        best_e = {}
        best_d = {}
        for d in deps:
            if d.is_dma:
                if best_d.get(d.slot, 0) < d.target:
                    best_d[d.slot] = d.target
            else:
                if e == 'pe' and d.eng == 'pe':
                    continue
                v = (d.epoch, d.msval)
                if best_e.get(d.eng, (-1, 0)) < v:
                    best_e[d.eng] = v
        for slot, t in best_d.items():
            if seen_d.get(slot, 0) >= t:
                continue
            seen_d[slot] = t
            eng.wait_ge(dsem[slot], t)
        for de, v in best_e.items():
            if seen_e.get(de, (-1, 0)) >= v:
                continue
            seen_e[de] = v
            eng.wait_ge(esem[(de, v[0])], v[1])


class Region:
    def __init__(self, raw, nbytes):
        self.raw = raw
        self.nbytes = nbytes
        self.off = 0

    def reset(self):
        self.off = 0

    def alloc(self, cols, dt):
        esz = 4 if dt == F32 else 2
        nb = cols * esz
        nb_al = (nb + 63) // 64 * 64
        assert self.off + nb_al <= self.nbytes, (self.off, nb_al, self.nbytes)
        v = self.raw[:, self.off:self.off + nb].bitcast(dt)
        self.off += nb_al
        return v


def build_program(S_LEN=4096, layers=(0, 1, 2, 3), final_norm=True, passes=None):
    nblk = S_LEN // TB
    nc = bass.Bass("TRN2", target_bir_lowering=False)

    def din(name, shape):
        return nc.dram_tensor(name, list(shape), F32, kind="ExternalInput").ap()

    x_in = din("x", [S_LEN, D])
    norm_mix = din("norm_mix", [4, D])
    norm_ffn = din("norm_ffn", [4, D])
    norm_final = din("norm_final", [D])
    ab_w_in = din("ab_w_in", [2, D, 2560])
    ab_w_out = din("ab_w_out", [2, D, D])
    diff_l = din("diff_l", [2, 256])
    pvec = din("pvec", [2, 128, 33])
    lru_wa = din("lru_wa", [2, 8, 64, 64])
    lru_wx = din("lru_wx", [2, 8, 64, 64])
    c_w_in = din("c_w_in", [2, D, 2048])
    c_ln_g = din("c_ln_g", [2, D])
    c_ln_b = din("c_ln_b", [2, D])
    c_w_sT = din("c_w_sT", [2, 128, 8, 128])
    c_b_sT = din("c_b_sT", [2, 128, 8])
    c_w_out = din("c_w_out", [2, D, D])
    ffn_w1 = din("ffn_w1", [4, D, 4096])
    ffn_w2 = din("ffn_w2", [4, 4096, D])
    out = nc.dram_tensor("out", [S_LEN, D], F32, kind="ExternalOutput").ap()
    scrA = nc.dram_tensor("scrA", [S_LEN, D], F32, kind="Internal").ap()
    scrB = nc.dram_tensor("scrB", [S_LEN, D], F32, kind="Internal").ap()

    S = Sched(nslots=16)
    st = contextlib.ExitStack()
    with st:
        def sb(name, shape, dt):
            return st.enter_context(nc.sbuf_tensor(name, shape, dt))

        WBYTES = DBG.get('WBYTES', 126976)
        ABYTES = 77824
        wraw = sb("wraw", [128, WBYTES], U8)
        araw = sb("araw", [128, ABYTES], U8)
        WR = Region(wraw, WBYTES)
        AR = Region(araw, ABYTES)
        ident = sb("ident", [128, 128], BF16)
        identf = sb("identf", [128, 128], F32)
        ones_bf = sb("ones_bf", [128, 128], BF16)
        ones_f = sb("ones_f", [128, 128], F32)
        neghalf = sb("neghalf", [128, 1], F32)
        poshalf = sb("poshalf", [128, 1], F32)
        expbias = sb("expbias", [128, 1], F32)
        qbias = sb("qbias", [128, 1], F32)
        epsb = sb("epsb", [128, 1], F32)
        banks = [st.enter_context(nc.psum_tensor(f"bank{i}", [128, 512], F32)) for i in range(8)]

        def bank_bf(i):
            return banks[i][:, :].bitcast(BF16).rearrange("p (k t) -> p k t", k=8)

        S.add('pool', ('memset', A(identf[:], 1.0)), writes=['identf'])
        S.add('pool', ('affine_select', A(out=identf[:], in_=identf[:], pattern=[[-1, 128]],
                                                compare_op=ALU.is_equal, fill=0.0, base=0,
                                                channel_multiplier=1)), reads=['identf'], writes=['identf'])
        S.add('pool', ('tensor_copy', A(out=ident[:], in_=identf[:])), reads=['identf'], writes=['ident'])
        S.add('pool', ('memset', A(ones_bf[:], 1.0)), writes=['ones_bf'])
        S.add('pool', ('memset', A(ones_f[:], 1.0)), writes=['ones_f'])
        S.add('pool', ('memset', A(neghalf[:], -0.5)), writes=['neghalf'])
        S.add('pool', ('memset', A(poshalf[:], 0.5)), writes=['poshalf'])
        S.add('pool', ('memset', A(expbias[:], 0.0)), writes=['expbias'])
        S.add('pool', ('memset', A(qbias[:], 0.25)), writes=['qbias'])
        S.add('pool', ('memset', A(epsb[:], EPS)), writes=['epsb'])
        S.add('pool', ('memset', A(expbias[64:128, :], -30000.0)), reads=['expbias'], writes=['expbias'])
        CONST_KEYS = ['ident', 'ones_bf', 'ones_f', 'neghalf', 'poshalf', 'expbias']

        def after_fence_consts():
            pass

        def load_w(dst3, src2, name, kc_n, cols):
            src3 = src2.rearrange("(kc p) n -> p kc n", p=128)
            cstep = cols
            while cstep > 2048:
                cstep //= 2
            kstep = max(1, 2048 // cstep)
            keys = []
            for k0 in range(0, kc_n, kstep):
                for c0 in range(0, cols, cstep):
                    key = f"{name}_{k0}_{c0}"
                    keys.append(key)
                    S.add('pool', ('dma_start', A(
                        out=dst3[:, k0:k0 + kstep, c0:c0 + cstep],
                        in_=src3[:, k0:k0 + kstep, c0:c0 + cstep])),
                        writes=[key], dma=1)
            return keys

        def join(keys, name, dummy):
            S.add('pool', ('memset', A(dummy, 0.0)), reads=keys, writes=[name])

        def load_bcast(dst, src1d, name):
            S.add('sp', ('dma_start', A(out=dst, in_=src1d.partition_broadcast(128))),
                  writes=[name], dma=1)

        def load_x(xt3, src, blk, key):
            S.add('sp', ('dma_start', A(
                out=xt3, in_=src[blk * TB:(blk + 1) * TB, :].rearrange("(s p) d -> p s d", p=128))),
                writes=[key], dma=1)

        def store_x(dst, xo3, blk, key):
            return S.add('sp', ('dma_start', A(
                out=dst[blk * TB:(blk + 1) * TB, :].rearrange("(s p) d -> p s d", p=128), in_=xo3)),
                reads=[key], dma=1)

        def rms_norm(x2, gbc, gkey, h2, xkey, hkey, tmp, tkey):
            stt = tmp[:, 0:12]
            mv = tmp[:, 12:14]
            ms = tmp[:, 14:15]
            rs = tmp[:, 15:16]
            S.add('dve', ('bn_stats', A(out=stt[:, 0:6], in_=x2[:, 0:512])), reads=[xkey], writes=[tkey + 'a'])
            S.add('dve', ('bn_stats', A(out=stt[:, 6:12], in_=x2[:, 512:1024])), reads=[xkey], writes=[tkey + 'b'])
            S.add('dve', ('bn_aggr', A(out=mv, in_=stt)), reads=[tkey + 'a', tkey + 'b'], writes=[tkey + 'mv'])
            S.add('dve', ('tensor_scalar', A(out=ms, in0=mv[:, 0:1], scalar1=mv[:, 0:1], scalar2=mv[:, 1:2],
                                                   op0=ALU.mult, op1=ALU.add)), reads=[tkey + 'mv'], writes=[tkey + 'ms'])
            S.add('dve', ('tensor_scalar', A(out=ms, in0=ms, scalar1=EPS, scalar2=None, op0=ALU.add)),
                  reads=[tkey + 'ms'], writes=[tkey + 'ms'])
            S.add('pool', ('tensor_tensor', A(out=rs, in0=ms, in1=neghalf[:, 0:1], op=ALU.pow)),
                  reads=[tkey + 'ms', 'neghalf'], writes=[tkey + 'rs'])
            S.add('dve', ('scalar_tensor_tensor', A(out=h2, in0=x2, scalar=rs, in1=gbc, op0=ALU.mult, op1=ALU.mult)),
                  reads=[xkey, tkey + 'rs', gkey], writes=[hkey])

        def transpose8(h2, hkey, hT3, col0, hTkey, bank_i, evac_eng):
            tp = bank_bf(bank_i)
            bkey = f"bank{bank_i}"
            for kc in range(8):
                S.add('pe', ('transpose', A(out=tp[:, kc, :], in_=h2[:, kc * 128:(kc + 1) * 128],
                                                         identity=ident[:])),
                      reads=[hkey, 'ident'], writes=[bkey])
            if evac_eng == 'act':
                S.add('act', ('copy', A(out=hT3[:, :, col0:col0 + 128], in_=tp)), writes=[bkey, hTkey])
            else:
                S.add('dve', ('tensor_copy', A(out=hT3[:, :, col0:col0 + 128], in_=tp)), writes=[bkey, hTkey])

        final_ops = []

        def BK(i):
            return f'bank{i}'

        def norm_stats(x2, xkeys, t, tk):
            stt = t[:, 0:12]
            mv = t[:, 12:14]
            ms = t[:, 14:15]
            rs = t[:, 15:16]
            S.add('dve', ('bn_stats', A(out=stt[:, 0:6], in_=x2[:, 0:512])), reads=xkeys, writes=[tk + 'a'])
            S.add('dve', ('bn_stats', A(out=stt[:, 6:12], in_=x2[:, 512:1024])), reads=xkeys, writes=[tk + 'b'])
            S.add('dve', ('bn_aggr', A(out=mv, in_=stt)), reads=[tk + 'a', tk + 'b'], writes=[tk + 'mv'])
            S.add('dve', ('tensor_scalar', A(out=ms, in0=mv[:, 0:1], scalar1=mv[:, 0:1], scalar2=mv[:, 1:2],
                                                   op0=ALU.mult, op1=ALU.add)), reads=[tk + 'mv'], writes=[tk + 'ms'])
            S.add('dve', ('tensor_scalar', A(out=ms, in0=ms, scalar1=EPS, scalar2=None, op0=ALU.add)),
                  reads=[tk + 'ms'], writes=[tk + 'ms'])
            S.add('pool', ('tensor_tensor', A(out=rs, in0=ms, in1=neghalf[:, 0:1], op=ALU.pow)),
                  reads=[tk + 'ms', 'neghalf'], writes=[tk + 'rs'])
            return rs

        def ffn_pass(layer, src, dst, apply_final):
            S.fence()
            WR.reset()
            AR.reset()
            W1 = WR.alloc(8 * 4096, BF16).rearrange("p (k f) -> p k f", k=8)
            W2a = WR.alloc(30 * 1024, BF16).rearrange("p (k f) -> p k f", k=30)
            W2b = AR.alloc(2 * 1024, BF16).rearrange("p (k f) -> p k f", k=2)
            W2v = [W2a[:, fc, :] if fc < 30 else W2b[:, fc - 30, :] for fc in range(32)]
            gbc = AR.alloc(1024, F32)
            gfin = AR.alloc(1024, F32) if apply_final else None
            dummy = AR.alloc(16, F32)
            xts = [AR.alloc(2048, F32).rearrange("p (s d) -> p s d", s=2) for _ in range(2)]
            xos = [AR.alloc(2048, F32).rearrange("p (s d) -> p s d", s=2) for _ in range(2)]
            hb = [AR.alloc(1024, BF16) for _ in range(2)]
            hT = [AR.alloc(8 * TB, BF16).rearrange("p (k t) -> p k t", k=8) for _ in range(2)]
            uT = AR.alloc(32 * TB, BF16).rearrange("p (k t) -> p k t", k=32)
            rt = [AR.alloc(TB, F32) for _ in range(2)]
            tmp = [AR.alloc(16, F32) for _ in range(2)]

            load_bcast(gbc, norm_ffn[layer], 'gbc')
            if apply_final:
                load_bcast(gfin, norm_final, 'gfin')
            load_x(xts[0], src, 0, 'xt0')
            k1 = load_w(W1, ffn_w1[layer], 'W1', 8, 4096)
            k2 = load_w(W2a, ffn_w2[layer][0:30 * 128, :], 'W2a', 30, 1024)
            k2 += load_w(W2b, ffn_w2[layer][30 * 128:32 * 128, :], 'W2b', 2, 1024)
            join(k1, 'W1', dummy[:, 0:4])
            join(k2, 'W2', dummy[:, 4:8])

            def f_norm(b):
                p = b % 2
                for s in range(2):
                    rms_norm(xts[p][:, s, :], gbc, 'gbc', hb[s], f'xt{p}', f'hb{s}', tmp[s], f'tmp{s}')

            def f_tr(b):
                p = b % 2
                for s in range(2):
                    transpose8(hb[s], f'hb{s}', hT[p], s * 128, f'hT{p}', s, 'act')

            def f_w1(b):
                p = b % 2
                for fc in range(32):
                    bi = 2 + fc % 4
                    ups = banks[bi][:, 0:TB]
                    for kc in range(8):
                        S.add('pe', ('matmul', A(
                            out=ups, lhsT=W1[:, kc, fc * 128:(fc + 1) * 128], rhs=hT[p][:, kc, :],
                            start=(kc == 0), stop=(kc == 7))),
                            reads=['W1', f'hT{p}'], writes=[BK(bi)])
                    r = rt[fc % 2]
                    S.add('act', ('activation', A(out=r, in_=ups, func=AF.Relu)),
                          writes=[BK(bi), f'rt{fc % 2}'])
                    S.add('dve', ('tensor_tensor', A(out=uT[:, fc, :], in0=r, in1=r, op=ALU.mult)),
                          reads=[f'rt{fc % 2}'], writes=[f'uT{fc}'])

            def f_w2(b, gi):
                p = b % 2
                xt = xts[p]
                xo = xos[p]
                ts, dh = gi // 2, gi % 2
                bi = 6 + gi % 2
                acc = banks[bi]
                for fc in range(32):
                    S.add('pe', ('matmul', A(
                        out=acc[:, :], lhsT=uT[:, fc, ts * 128:(ts + 1) * 128],
                        rhs=W2v[fc][:, dh * 512:(dh + 1) * 512], start=(fc == 0), stop=(fc == 31))),
                        reads=['W2', f'uT{fc}'], writes=[BK(bi)])
                S.add('dve', ('tensor_tensor', A(
                    out=xo[:, ts, dh * 512:(dh + 1) * 512], in0=acc[:, :], in1=xt[:, ts, dh * 512:(dh + 1) * 512],
                    op=ALU.add)), reads=[f'xt{p}'], writes=[BK(bi), f'xo{p}_{ts}'])

            def f_store(b):
                p = b % 2
                xo = xos[p]
                okeys = [f'xo{p}_0', f'xo{p}_1']
                if apply_final:
                    for s in range(2):
                        x2 = xo[:, s, :]
                        rs = norm_stats(x2, [f'xo{p}_{s}'], tmp[s], f'tmp{s}')
                        S.add('dve', ('scalar_tensor_tensor', A(out=x2, in0=x2, scalar=rs, in1=gfin,
                                                               op0=ALU.mult, op1=ALU.mult)),
                              reads=[f'tmp{s}rs', 'gfin'], writes=[f'xo{p}_{s}'])
                op = S.add('sp', ('dma_start', A(
                    out=dst[b * TB:(b + 1) * TB, :].rearrange("(s p) d -> p s d", p=128), in_=xo)),
                    reads=okeys, dma=1)
                if apply_final:
                    final_ops.append(op)

            if nblk > 1:
                load_x(xts[1], src, 1, 'xt1')
            f_norm(0)
            f_tr(0)
            for b in range(nblk):
                p = b % 2
                f_w1(b)
                if b + 1 < nblk:
                    f_norm(b + 1)
                f_w2(b, 0)
                f_w2(b, 1)
                if b + 1 < nblk:
                    f_tr(b + 1)
                f_w2(b, 2)
                f_w2(b, 3)
                f_store(b)
                if b + 2 < nblk:
                    load_x(xts[p], src, b + 2, f'xt{p}')

        def odd_pass(layer, src, dst):
            o = layer // 2
            S.fence()
            WR.reset()
            AR.reset()
            Wci = WR.alloc(8 * 2048, BF16).rearrange("p (k f) -> p k f", k=8)
            Wco = WR.alloc(8 * 1024, BF16).rearrange("p (k f) -> p k f", k=8)
            WsT = WR.alloc(8 * 128, BF16).rearrange("p (g t) -> p g t", g=8)
            lng = WR.alloc(1024, F32)
            lnb = WR.alloc(1024, F32)
            gbc = WR.alloc(1024, F32)
            bsT = WR.alloc(8, F32)
            dummy = WR.alloc(16, F32)
            NZ = 3
            NX = 4
            zu = [WR.alloc(1024, F32) for _ in range(NZ)]
            zv = [WR.alloc(1024, F32) for _ in range(NZ)]
            vt = [WR.alloc(1024, F32) for _ in range(2)]
            xts = [AR.alloc(2048, F32).rearrange("p (s d) -> p s d", s=2) for _ in range(NX)]
            xos = [AR.alloc(2048, F32).rearrange("p (s d) -> p s d", s=2) for _ in range(2)]
            hb = [AR.alloc(1024, BF16) for _ in range(2)]
            hT = [AR.alloc(8 * 128, BF16).rearrange("p (k t) -> p k t", k=8) for _ in range(2)]
            vln = [AR.alloc(1024, BF16) for _ in range(2)]
            sg = [AR.alloc(1024, BF16) for _ in range(2)]
            sT = [AR.alloc(8 * 128, BF16).rearrange("p (k t) -> p k t", k=8) for _ in range(2)]
            tmp = [AR.alloc(16, F32) for _ in range(2)]
            tmp2 = [AR.alloc(16, F32) for _ in range(2)]

            load_bcast(gbc, norm_mix[layer], 'gbc')
            load_bcast(lng, c_ln_g[o], 'lng')
            load_bcast(lnb, c_ln_b[o], 'lnb')
            S.add('sp', ('dma_start', A(out=bsT, in_=c_b_sT[o])), writes=['bsT'], dma=1)
            load_x(xts[0], src, 0, 'xt0')
            k1 = load_w(Wci, c_w_in[o], 'Wci', 8, 2048)
            k2 = load_w(Wco, c_w_out[o], 'Wco', 8, 1024)
            S.add('pool', ('dma_start', A(out=WsT, in_=c_w_sT[o])), writes=['WsT_raw'], dma=1)
            join(k1, 'Wci', dummy[:, 0:4])
            join(k2, 'Wco', dummy[:, 4:8])
            S.add('pool', ('memset', A(WsT[64:128, :, 0:64], 0.0)), reads=['WsT_raw'], writes=['WsT'])

            def st_norm(g):
                b, s = g // 2, g % 2
                q = g % 2
                rms_norm(xts[b % NX][:, s, :], gbc, 'gbc', hb[q], f'xt{b % NX}', f'hb{q}', tmp[q], f'tmp{q}')

            def st_tr(g):
                q = g % 2
                transpose8(hb[q], f'hb{q}', hT[q], 0, f'hT{q}', 0, 'act')

            def st_z(g):
                q = g % 2
                z = g % NZ
                for cg in range(4):
                    bi = 2 + cg % 2
                    for kc in range(8):
                        S.add('pe', ('matmul', A(
                            out=banks[bi][:, :], lhsT=hT[q][:, kc, :], rhs=Wci[:, kc, cg * 512:(cg + 1) * 512],
                            start=(kc == 0), stop=(kc == 7))), reads=['Wci', f'hT{q}'], writes=[BK(bi)])
                    dstz = (zu[z] if cg < 2 else zv[z])[:, (cg % 2) * 512:(cg % 2 + 1) * 512]
                    S.add('act', ('activation', A(out=dstz, in_=banks[bi][:, :], func=AF.Gelu_apprx_tanh)),
                          writes=[BK(bi), f'z{z}_{cg}'])

            def st_ln(g):
                q = g % 2
                z = g % NZ
                t2 = tmp2[q]
                stt = t2[:, 0:12]
                mv = t2[:, 12:14]
                ve = t2[:, 14:15]
                rs = t2[:, 15:16]
                zvs = zv[z]
                S.add('dve', ('bn_stats', A(out=stt[:, 0:6], in_=zvs[:, 0:512])), reads=[f'z{z}_2'], writes=[f't2{q}a'])
                S.add('dve', ('bn_stats', A(out=stt[:, 6:12], in_=zvs[:, 512:1024])), reads=[f'z{z}_3'], writes=[f't2{q}b'])
                S.add('dve', ('bn_aggr', A(out=mv, in_=stt)), reads=[f't2{q}a', f't2{q}b'], writes=[f't2{q}mv'])
                S.add('dve', ('tensor_scalar', A(out=ve, in0=mv[:, 1:2], scalar1=EPS, scalar2=None, op0=ALU.add)),
                      reads=[f't2{q}mv'], writes=[f't2{q}ve'])
                S.add('pool', ('tensor_tensor', A(out=rs, in0=ve, in1=neghalf[:, 0:1], op=ALU.pow)),
                      reads=[f't2{q}ve', 'neghalf'], writes=[f't2{q}rs'])
                vts = vt[q]
                S.add('dve', ('tensor_scalar', A(out=vts, in0=zvs, scalar1=mv[:, 0:1], scalar2=rs,
                                                 op0=ALU.subtract, op1=ALU.mult)),
                      reads=[f'z{z}_2', f'z{z}_3', f't2{q}mv', f't2{q}rs'], writes=[f'vt{q}'])
                S.add('pool', ('tensor_tensor', A(out=vts, in0=vts, in1=lng, op=ALU.mult)), reads=['lng'], writes=[f'vt{q}'])
                S.add('pool', ('tensor_tensor', A(out=vln[q], in0=vts, in1=lnb, op=ALU.add)), reads=[f'vt{q}', 'lnb'],
                      writes=[f'vln{q}'])

            def st_sp_pe(g):
                q = g % 2
                vl = vln[q]
                for gg_ in range(8):
                    bi = 4 + gg_ // 4
                    mps = banks[bi][:, (gg_ % 4) * 128:(gg_ % 4 + 1) * 128]
                    S.add('pe', ('matmul', A(out=mps, lhsT=WsT[:, gg_, :], rhs=vl[:, gg_ * 128:(gg_ + 1) * 128],
                                             start=True, stop=True)), reads=['WsT', f'vln{q}'], writes=[BK(bi)])

            def st_sp_dve(g):
                q = g % 2
                z = g % NZ
                sgs = sg[q]
                zus = zu[z]
                for gg_ in range(8):
                    bi = 4 + gg_ // 4
                    mps = banks[bi][:, (gg_ % 4) * 128:(gg_ % 4 + 1) * 128]
                    S.add('dve', ('scalar_tensor_tensor', A(
                        out=sgs[:, gg_ * 128:(gg_ + 1) * 128], in0=mps, scalar=bsT[:, gg_:gg_ + 1],
                        in1=zus[:, gg_ * 128:(gg_ + 1) * 128], op0=ALU.add, op1=ALU.mult)),
                        reads=['bsT', f'z{z}_{gg_ // 4}'], writes=[BK(bi), f'sg{q}_{gg_}'])

            def st_str(g):
                q = g % 2
                sgs = sg[q]
                tp = bank_bf(1)
                for kc in range(8):
                    S.add('pe', ('transpose', A(out=tp[:, kc, :], in_=sgs[:, kc * 128:(kc + 1) * 128], identity=ident[:])),
                          reads=[f'sg{q}_{kc}', 'ident'], writes=[BK(1)])
                S.add('act', ('copy', A(out=sT[q], in_=tp)), writes=[BK(1), f'sT{q}'])

            def st_out(g):
                b, s = g // 2, g % 2
                q = g % 2
                p = b % 2
                xt = xts[b % NX]
                xo = xos[p]
                for dh in range(2):
                    bi = 6 + dh
                    for kc in range(8):
                        S.add('pe', ('matmul', A(
                            out=banks[bi][:, :], lhsT=sT[q][:, kc, :], rhs=Wco[:, kc, dh * 512:(dh + 1) * 512],
                            start=(kc == 0), stop=(kc == 7))), reads=['Wco', f'sT{q}'], writes=[BK(bi)])
                    S.add('dve', ('tensor_tensor', A(
                        out=xo[:, s, dh * 512:(dh + 1) * 512], in0=banks[bi][:, :],
                        in1=xt[:, s, dh * 512:(dh + 1) * 512], op=ALU.add)),
                        reads=[f'xt{b % NX}'], writes=[BK(bi), f'xo{p}_{s}'])
                if s == 1:
                    S.add('sp', ('dma_start', A(
                        out=dst[b * TB:(b + 1) * TB, :].rearrange("(s p) d -> p s d", p=128), in_=xo)),
                        reads=[f'xo{p}_0', f'xo{p}_1'], dma=1)
                    if b + NX < nblk:
                        load_x(xts[b % NX], src, b + NX, f'xt{b % NX}')

            nsub = 2 * nblk
            for bb in range(1, min(NX, nblk)):
                load_x(xts[bb], src, bb, f'xt{bb}')
            st_norm(0)
            for t in range(nsub + 3):
                if 0 <= t - 2 < nsub:
                    st_sp_pe(t - 2)
                if t + 1 < nsub:
                    st_norm(t + 1)
                if t < nsub:
                    st_tr(t)
                if 0 <= t - 3 < nsub:
                    st_out(t - 3)
                if t < nsub:
                    st_z(t)
                if 0 <= t - 1 < nsub:
                    st_ln(t - 1)
                if 0 <= t - 2 < nsub:
                    st_sp_dve(t - 2)
                    st_str(t - 2)

        def even_pass(layer, src, dst):
            ev = layer // 2
            lam_init = 0.8 - 0.6 * math.exp(-0.3 * layer)
            NCH = S_LEN // 128
            S.fence()
            WR.reset()
            AR.reset()
            Win = WR.alloc(8 * 2560, BF16).rearrange("p (k f) -> p k f", k=8)
            Wout = WR.alloc(8 * 1024, BF16).rearrange("p (k f) -> p k f", k=8)
            kT = WR.alloc(4 * S_LEN, BF16).rearrange("p (h t) -> p h t", h=4)
            V = WR.alloc(NCH * 512, BF16).rearrange("p (c v) -> p c v", c=NCH)
            WA = WR.alloc(4 * 128, BF16).rearrange("p (c j) -> p c j", c=4)
            WX = WR.alloc(4 * 128, BF16).rearrange("p (c j) -> p c j", c=4)
            gbc = AR.alloc(1024, F32)
            pv = AR.alloc(33, F32)
            ldl = WR.alloc(256, F32)
            sm = AR.alloc(64, F32)
            dummy = AR.alloc(16, F32)
            ltmp = AR.alloc(64, F32)
            hprev = AR.alloc(4, F32)
            tmp = [AR.alloc(16, F32) for _ in range(2)]
            xts = [AR.alloc(2048, F32).rearrange("p (s d) -> p s d", s=2) for _ in range(2)]
            hb = [AR.alloc(1024, BF16) for _ in range(2)]
            hT = AR.alloc(8 * TB, BF16).rearrange("p (k t) -> p k t", k=8)
            qc = AR.alloc(4 * 2 * TB, BF16).rearrange("p (h t) -> p h t", h=4)
            xbh = AR.alloc(4 * (TB + 3), F32).rearrange("p (c t) -> p c t", c=4)
            gg = AR.alloc(4 * TB, BF16).rearrange("p (c t) -> p c t", c=4)
            xc = AR.alloc(4 * TB, F32).rearrange("p (c t) -> p c t", c=4)
            xcb = hb[1].rearrange("p (c t) -> p c t", c=4)
            Tr = AR.alloc(4 * TB, F32).rearrange("p (c t) -> p c t", c=4)
            Ti = AR.alloc(4 * TB, F32).rearrange("p (c t) -> p c t", c=4)
            aa = AR.alloc(4 * TB, F32).rearrange("p (c t) -> p c t", c=4)
            a2 = AR.alloc(4 * TB, F32).rearrange("p (c t) -> p c t", c=4)
            bt = Ti
            hs = Tr
            yT = AR.alloc(8 * TB, BF16).rearrange("p (k t) -> p k t", k=8)
            NPT = 4
            pT = [AR.alloc(2 * TB, BF16) for _ in range(NPT - 1)] + [WR.alloc(2 * TB, BF16)]
            fr = AR.alloc(2 * TB, F32)
            fo4 = AR.alloc(4 * TB, F32).rearrange("p (h t) -> p h t", h=4)
            sd = AR.alloc(4 * TB, F32).rearrange("p (h t) -> p h t", h=4)

            s1 = sm[:, 0:1]
            s2 = sm[:, 1:2]
            e1 = sm[:, 2:3]
            e2 = sm[:, 3:4]
            neglam = sm[:, 4:5]
            g1 = sm[:, 5:6]
            yv = sm[:, 8:12]
            tv = sm[:, 12:16]
            sc = sm[:, 16:20]
            sch = sm[:, 20:24]
            hba = sm[:, 24:28]
            hbx = sm[:, 28:32]
            cw = lambda w, c: pv[:, w * 4 + c:w * 4 + c + 1]
            cb = lambda c: pv[:, 16 + c:17 + c]

            load_bcast(gbc, norm_mix[layer], 'gbc')
            load_bcast(ldl, diff_l[ev], 'ldl')
            S.add('sp', ('dma_start', A(out=pv, in_=pvec[ev])), writes=['pv'], dma=1)
            k1 = load_w(Win, ab_w_in[ev], 'Win', 8, 2560)
            k2 = load_w(Wout, ab_w_out[ev], 'Wout', 8, 1024)
            join(k1, 'Win', dummy[:, 0:4])
            join(k2, 'Wout', dummy[:, 4:8])
            S.add('pool', ('memset', A(WA, 0.0)), writes=['WA0'])
            S.add('pool', ('memset', A(WX, 0.0)), writes=['WX0'])
            for nm, wt, srcw in (('WA', WA, lru_wa), ('WX', WX, lru_wx)):
                sv = srcw[ev].rearrange("(c j) i o -> j i c o", j=2)
                wk = []
                for j in range(2):
                    key = f'{nm}_{j}'
                    wk.append(key)
                    S.add('pool', ('dma_start', A(
                        out=wt[j * 64:(j + 1) * 64, :, j * 64:(j + 1) * 64], in_=sv[j])),
                        reads=[nm + '0'], writes=[key], dma=1)
                join(wk, nm, dummy[:, 8:10] if nm == 'WA' else dummy[:, 10:12])
            S.add('pool', ('memset', A(qc[64:128, :, 0:TB], 0.0)), writes=['qcz0'])
            S.add('pool', ('memset', A(qc[0:64, :, TB:2 * TB], 0.0)), writes=['qcz1'])
            S.add('dve', ('tensor_tensor', A(out=ltmp, in0=ldl[:, 0:64], in1=ldl[:, 64:128], op=ALU.mult)),
                  reads=['ldl'], writes=['ltmp'])
            S.add('dve', ('tensor_reduce', A(out=s1, in_=ltmp, axis=mybir.AxisListType.X, op=ALU.add)),
                  reads=['ltmp'], writes=['s1'])
            S.add('dve', ('tensor_tensor', A(out=ltmp, in0=ldl[:, 128:192], in1=ldl[:, 192:256], op=ALU.mult)),
                  reads=['ldl', 's1'], writes=['ltmp'])
            S.add('dve', ('tensor_reduce', A(out=s2, in_=ltmp, axis=mybir.AxisListType.X, op=ALU.add)),
                  reads=['ltmp'], writes=['s2'])
            S.add('act', ('activation', A(out=e1, in_=s1, func=AF.Exp)), reads=['s1'], writes=['e1'])
            S.add('act', ('activation', A(out=e2, in_=s2, func=AF.Exp)), reads=['s2'], writes=['e2'])
            S.add('dve', ('tensor_tensor', A(out=neglam, in0=e2, in1=e1, op=ALU.subtract)), reads=['e1', 'e2'],
                  writes=['neglam'])
            S.add('dve', ('tensor_scalar', A(out=neglam, in0=neglam, scalar1=-lam_init, scalar2=None, op0=ALU.add)),
                  reads=['neglam'], writes=['neglam'])
            S.add('dve', ('tensor_scalar', A(out=g1, in0=pv[:, 32:33], scalar1=(1.0 - lam_init), scalar2=None,
                                                   op0=ALU.mult)), reads=['pv'], writes=['g1'])
            S.add('act', ('activation', A(out=yv, in_=pv[:, 28:32], func=AF.Exp, scale=-1.0)), reads=['pv'], writes=['yv'])
            S.add('dve', ('tensor_scalar', A(out=tv, in0=yv, scalar1=-0.25, scalar2=1.0 / 3.0, op0=ALU.mult, op1=ALU.add)),
                  reads=['yv'], writes=['tv'])
            S.add('dve', ('tensor_tensor', A(out=tv, in0=tv, in1=yv, op=ALU.mult)), reads=['tv', 'yv'], writes=['tv'])
            S.add('dve', ('tensor_scalar', A(out=tv, in0=tv, scalar1=-1.0, scalar2=0.5, op0=ALU.mult, op1=ALU.add)),
                  reads=['tv'], writes=['tv'])
            S.add('dve', ('tensor_tensor', A(out=tv, in0=tv, in1=yv, op=ALU.mult)), reads=['tv', 'yv'], writes=['tv'])
            S.add('dve', ('tensor_scalar', A(out=tv, in0=tv, scalar1=-1.0, scalar2=1.0, op0=ALU.mult, op1=ALU.add)),
                  reads=['tv'], writes=['tv'])
            S.add('dve', ('tensor_tensor', A(out=tv, in0=tv, in1=yv, op=ALU.mult)), reads=['tv', 'yv'], writes=['tv'])
            S.add('dve', ('tensor_scalar', A(out=sc, in0=tv, scalar1=-8.0, scalar2=None, op0=ALU.mult)),
                  reads=['tv'], writes=['sc'])
            S.add('dve', ('tensor_scalar', A(out=sch, in0=tv, scalar1=-4.0, scalar2=None, op0=ALU.mult)),
                  reads=['tv'], writes=['sch'])
            S.add('dve', ('tensor_scalar', A(out=hba, in0=pv[:, 20:24], scalar1=0.5, scalar2=None, op0=ALU.mult)),
                  reads=['pv'], writes=['hba'])
            S.add('dve', ('tensor_scalar', A(out=hbx, in0=pv[:, 24:28], scalar1=0.5, scalar2=None, op0=ALU.mult)),
                  reads=['pv'], writes=['hbx'])
            S.add('pool', ('memset', A(xbh[:, :, 0:3], 0.0)), writes=['xbh_halo'])
            S.add('pool', ('memset', A(hprev, 0.0)), writes=['hprev'])

            gcount = [0]

            def gbank():
                bi = 1 + gcount[0] % 3
                gcount[0] += 1
                return bi

            pcount = [0]

            def e_load(b):
                p = b % 2
                S.add('sp', ('dma_start', A(
                    out=xts[p], in_=src[b * TB:(b + 1) * TB, :].rearrange("(s p) d -> p s d", p=128))),
                    writes=[f'xt{p}'], dma=1)

            def e_norm(b):
                p = b % 2
                for s in range(2):
                    rms_norm(xts[p][:, s, :], gbc, 'gbc', hb[s], f'xt{p}', f'hb{s}', tmp[s], f'tmp{s}')

            def e_tr(b):
                for s in range(2):
                    transpose8(hb[s], f'hb{s}', hT, s * 128, 'hT', 0, 'act')

            def proj_fm(col0, evac):
                bi = gbank()
                ps_ = banks[bi][:, 0:TB]
                for kc in range(8):
                    S.add('pe', ('matmul', A(out=ps_, lhsT=Win[:, kc, col0:col0 + 128],
                                             rhs=hT[:, kc, :], start=(kc == 0), stop=(kc == 7))),
                          reads=['Win', 'hT'], writes=[BK(bi)])
                evac(ps_, BK(bi))

            def evac_q(ps_, bk, h):
                S.add('dve', ('tensor_copy', A(out=qc[0:64, h, 0:TB], in_=ps_[0:64, :])),
                      reads=['qcz0', 'qcz1'], writes=[bk, f'qc{h}'])
                S.add('dve', ('tensor_copy', A(out=qc[64:128, h, TB:2 * TB], in_=ps_[64:128, :])),
                      writes=[bk, f'qc{h}'])

            def e_inproj(b):
                for h in range(4):
                    proj_fm(h * 128, lambda ps_, bk, h=h: evac_q(ps_, bk, h))
                for h in range(4):
                    proj_fm(512 + h * 128, lambda ps_, bk, h=h: S.add(
                        'dve', ('tensor_copy', A(out=kT[:, h, b * TB:(b + 1) * TB], in_=ps_)),
                        writes=[bk, f'kT{h}_{b}']))
                for ts in range(2):
                    bi = gbank()
                    for kc in range(8):
                        S.add('pe', ('matmul', A(
                            out=banks[bi][:, :], lhsT=hT[:, kc, ts * 128:(ts + 1) * 128],
                            rhs=Win[:, kc, 1024:1536], start=(kc == 0), stop=(kc == 7))),
                            reads=['Win', 'hT'], writes=[BK(bi)])
                    S.add('dve', ('tensor_copy', A(out=V[:, b * 2 + ts, :], in_=banks[bi][:, :])),
                          writes=[BK(bi), f'V{b * 2 + ts}'])
                for c in range(4):
                    proj_fm(1536 + c * 128, lambda ps_, bk, c=c: S.add(
                        'act', ('copy', A(out=xbh[:, c, 3:3 + TB], in_=ps_)), reads=['xbh_halo'],
                        writes=[bk, f'xbh{c}']))
                for c in range(4):
                    proj_fm(2048 + c * 128, lambda ps_, bk, c=c: S.add(
                        'act', ('activation', A(out=gg[:, c, :], in_=ps_, func=AF.Gelu_apprx_tanh)),
                        writes=[bk, f'gg{c}']))

            def e_lru1(b):
                for c in range(4):
                    S.add('dve', ('tensor_scalar', A(out=xc[:, c, :], in0=xbh[:, c, 0:TB], scalar1=cw(0, c),
                                                     scalar2=cb(c), op0=ALU.mult, op1=ALU.add)),
                          reads=[f'xbh{c}', 'xbh_halo', 'pv'], writes=[f'xc{c}'])
                    for w in range(1, 4):
                        S.add('dve', ('scalar_tensor_tensor', A(
                            out=xc[:, c, :], in0=xbh[:, c, w:w + TB], scalar=cw(w, c), in1=xc[:, c, :],
                            op0=ALU.mult, op1=ALU.add)), reads=[f'xbh{c}', 'xbh_halo', 'pv'], writes=[f'xc{c}'])
                    S.add('pool', ('tensor_copy', A(out=xcb[:, c, :], in_=xc[:, c, :])), reads=[f'xc{c}'],
                          writes=[f'xcb{c}', 'hb1'])
                S.add('pool', ('tensor_copy', A(out=xbh[:, :, 0:3], in_=xbh[:, :, TB:TB + 3])),
                      reads=[f'xbh{c}' for c in range(4)] + [f'xc{c}' for c in range(4)], writes=['xbh_halo'])

            def lru_part2():
                for c in range(4):
                    for nm, wt, Tt, hbias in (('r', WA, Tr, hba), ('i', WX, Ti, hbx)):
                        bi = 0
                        ps_ = banks[bi][:, 0:TB]
                        S.add('pe', ('matmul', A(out=ps_, lhsT=wt[:, c, :], rhs=xcb[:, c, :], start=True, stop=True)),
                              reads=['WA' if nm == 'r' else 'WX', f'xcb{c}', 'hb1'], writes=[BK(bi)])
                        S.add('act', ('activation', A(
                            out=Tt[:, c, :], in_=ps_, func=AF.Tanh, bias=hbias[:, c:c + 1], scale=0.5)),
                            reads=['hba', 'hbx'],
                            writes=[BK(bi), f'T{nm}{c}'] + ([f'hs{c}'] if nm == 'r' else [f'bt{c}']))
                for c in range(4):
                    S.add('act', ('activation', A(out=aa[:, c, :], in_=Tr[:, c, :], func=AF.Exp,
                                                  bias=sch[:, c:c + 1], scale=sch[:, c:c + 1])),
                          reads=[f'Tr{c}', 'sch'], writes=[f'aa{c}'])
                    S.add('act', ('activation', A(out=a2[:, c, :], in_=Tr[:, c, :], func=AF.Exp,
                                                  bias=sc[:, c:c + 1], scale=sc[:, c:c + 1])),
                          reads=[f'Tr{c}', 'sc'], writes=[f'a2{c}'])

            def e_lru_tail(b):
                S.add('act', ('activation', A(out=a2.rearrange("p c t -> p (c t)"), in_=a2.rearrange("p c t -> p (c t)"),
                                              func=AF.Sqrt, bias=qbias[:, 0:1], scale=-0.25)),
                      reads=['qbias'], writes=[f'a2{c}' for c in range(4)])
                for c in range(4):
                    S.add('dve', ('scalar_tensor_tensor', A(out=bt[:, c, :], in0=Ti[:, c, :], scalar=1.0,
                                                           in1=xc[:, c, :], op0=ALU.add, op1=ALU.mult)),
                          reads=[f'Ti{c}', f'xc{c}'], writes=[f'bt{c}', f'Ti{c}'])
                    S.add('dve', ('tensor_tensor', A(out=bt[:, c, :], in0=bt[:, c, :], in1=a2[:, c, :], op=ALU.mult)),
                          reads=[f'a2{c}'], writes=[f'bt{c}'])
                    S.add('dve', ('tensor_tensor_scan', A(out=hs[:, c, :], data0=aa[:, c, :], data1=bt[:, c, :],
                                                         initial=hprev[:, c:c + 1], op0=ALU.mult, op1=ALU.add)),
                          reads=[f'aa{c}', f'bt{c}', 'hprev'], writes=[f'hs{c}', f'Tr{c}'])
                    S.add('dve', ('tensor_tensor', A(out=yT[:, 4 + c, :], in0=gg[:, c, :], in1=hs[:, c, :], op=ALU.mult)),
                          reads=[f'gg{c}', f'hs{c}'], writes=[f'yT{4 + c}'])
                S.add('pool', ('tensor_copy', A(out=hprev, in_=hs[:, :, TB - 1])),
                      reads=[f'hs{c}' for c in range(4)], writes=['hprev'])

            def e_tail_a(b):
                for h in range(4):
                    mb = 1 - h // 2
                    S.add('pe', ('matmul', A(out=banks[mb][:, (h % 2) * TB:(h % 2 + 1) * TB], lhsT=ones_f[:], rhs=sd[:, h, :],
                                             start=True, stop=True)),
                          reads=['ones_f', f'sd{h}'], writes=[BK(mb)])
                for hh in range(2):
                    mb = 1 - hh
                    S.add('act', ('activation', A(out=sd[:, 2 * hh:2 * hh + 2, :].rearrange("p h t -> p (h t)"),
                                                  in_=banks[mb][:, :], func=AF.Sqrt, bias=epsb[:, 0:1], scale=1.0 / 128.0)),
                          reads=['epsb'], writes=[BK(mb), f'sd{2 * hh}', f'sd{2 * hh + 1}'])
                S.add('dve', ('reciprocal', A(out=sd.rearrange("p h t -> p (h t)"), in_=sd.rearrange("p h t -> p (h t)"))),
                      writes=[f'sd{h}' for h in range(4)])
                S.add('dve', ('scalar_tensor_tensor', A(out=yT[:, 0:4, :].rearrange("p h t -> p (h t)"),
                                                       in0=fo4.rearrange("p h t -> p (h t)"), scalar=g1,
                                                       in1=sd.rearrange("p h t -> p (h t)"), op0=ALU.mult, op1=ALU.mult)),
                      reads=[f'fo{h}' for h in range(4)] + ['g1'] + [f'sd{h}' for h in range(4)],
                      writes=[f'yT{h}' for h in range(4)])

            def e_outproj(b, gi, bi):
                p = b % 2
                xt = xts[p]
                ts, dh = gi // 2, gi % 2
                for kc in range(8):
                    S.add('pe', ('matmul', A(
                        out=banks[bi][:, :], lhsT=yT[:, kc, ts * 128:(ts + 1) * 128],
                        rhs=Wout[:, kc, dh * 512:(dh + 1) * 512], start=(kc == 0), stop=(kc == 7))),
                        reads=['Wout', f'yT{kc}'], writes=[BK(bi)])
                S.add('dve', ('tensor_tensor', A(
                    out=xt[:, ts, dh * 512:(dh + 1) * 512], in0=banks[bi][:, :],
                    in1=xt[:, ts, dh * 512:(dh + 1) * 512], op=ALU.add)),
                    writes=[BK(bi), f'xt{p}'])

            def e_store(b):
                p = b % 2
                S.add('sp', ('dma_start', A(
                    out=dst[b * TB:(b + 1) * TB, :].rearrange("(s p) d -> p s d", p=128), in_=xts[p])),
                    reads=[f'xt{p}'], dma=1)
                if b + 2 < nblk:
                    e_load(b + 2)

            def e_att(b):
                nkc = 2 * b + 2
                steps = [(h, kc) for h in range(4) for kc in range(nkc)]
                nst = len(steps)
                LA = 3
                pts = {}

                def qk_exp(i):
                    h, kc = steps[i]
                    j = kc - 2 * b
                    si = 1 + i % 3
                    S.add('pe', ('matmul', A(
                        out=banks[si][:, :], lhsT=kT[:, h, kc * 128:(kc + 1) * 128], rhs=qc[:, h, :],
                        start=True, stop=True)), reads=[f'kT{h}_{kc // 2}', f'qc{h}'], writes=[BK(si)])
                    pi = pcount[0] % NPT
                    pcount[0] += 1
                    pt = pT[pi]
                    pkey = f'pT{pi}'
                    pts[i] = (pt, pkey)
                    S.add('act', ('activation', A(out=pt, in_=banks[si][:, :], func=AF.Exp, scale=0.125)),
                          writes=[BK(si), pkey])
                    if j >= 0:
                        c0 = j * 128
                        if c0 > 0:
                            S.add('pool', ('memset', A(pt[:, 0:c0], 0.0)), writes=[pkey])
                            S.add('pool', ('memset', A(pt[:, TB:TB + c0], 0.0)), writes=[pkey])
                        S.add('pool', ('memset', A(pt[64:128, c0:c0 + 64], 0.0)), writes=[pkey])
                        S.add('pool', ('memset', A(pt[64:128, TB + c0:TB + c0 + 64], 0.0)), writes=[pkey])

                def pv_sum(i):
                    h, kc = steps[i]
                    ob = 4 + h % 2
                    sb_ = 6 + h % 2
                    pt, pkey = pts.pop(i)
                    S.add('pe', ('matmul', A(
                        out=banks[ob][:, :], lhsT=V[:, kc, h * 128:(h + 1) * 128], rhs=pt,
                        start=(kc == 0), stop=(kc == nkc - 1))), reads=[f'V{kc}', pkey], writes=[BK(ob)])
                    S.add('pe', ('matmul', A(
                        out=banks[sb_][:, :], lhsT=ones_bf[:], rhs=pt,
                        start=(kc == 0), stop=(kc == nkc - 1))), reads=['ones_bf', pkey], writes=[BK(sb_)])
                    if kc == nkc - 1:
                        S.add('dve', ('reciprocal', A(out=fr, in_=banks[sb_][:, :])), writes=[BK(sb_), 'fr'])
                        S.add('dve', ('tensor_tensor', A(out=fr, in0=banks[ob][:, :], in1=fr, op=ALU.mult)),
                              writes=[BK(ob), 'fr'])
                        S.add('dve', ('scalar_tensor_tensor', A(out=fo4[:, h, :], in0=fr[:, TB:2 * TB], scalar=neglam,
                                                               in1=fr[:, 0:TB], op0=ALU.mult, op1=ALU.add)),
                              reads=['fr', 'neglam'], writes=[f'fo{h}'])
                        S.add('dve', ('tensor_tensor', A(out=sd[:, h, :], in0=fo4[:, h, :], in1=fo4[:, h, :], op=ALU.mult)),
                              reads=[f'fo{h}'], writes=[f'sd{h}'])

                ins = {}
                def at(step, fn):
                    ins.setdefault(min(max(step, 0), nst - 1), []).append(fn)
                at(3, lru_part2)
                if b > 0:
                    for gi in range(4):
                        at(8 + 3 * gi, lambda gi=gi: e_outproj(b - 1, gi, 0))
                    at(8 + 3 * 3 + 1, lambda: e_store(b - 1))
                if b + 1 < nblk:
                    at(8 + 3 * 3 + 6, lambda: e_norm(b + 1))

                for i in range(min(LA, nst)):
                    qk_exp(i)
                for i in range(nst):
                    pv_sum(i)
                    if i + LA < nst:
                        qk_exp(i + LA)
                    for fn in ins.get(i, ()):
                        fn()

            e_load(0)
            if nblk > 1:
                e_load(1)
            e_norm(0)
            for b in range(nblk):
                e_tr(b)
                e_inproj(b)
                if b > 0:
                    e_tail_a(b - 1)
                e_lru1(b)
                e_att(b)
                e_lru_tail(b)
            e_tail_a(nblk - 1)
            for gi in range(4):
                e_outproj(nblk - 1, gi, gbank())
            e_store(nblk - 1)

        bufs = [scrA, scrB]
        cur = x_in
        nb = 0
        if passes is None:
            passes_l = []
            for layer in layers:
                passes_l.append(('even' if layer % 2 == 0 else 'odd', layer))
                passes_l.append(('ffn', layer))
        else:
            passes_l = list(passes)
        for pi, (kind, layer) in enumerate(passes_l):
            last = (pi == len(passes_l) - 1)
            d = out if last else bufs[nb % 2]
            nb += 1
            if kind == 'even':
                even_pass(layer, cur, d)
            elif kind == 'odd':
                odd_pass(layer, cur, d)
            else:
                ffn_pass(layer, cur, d, apply_final=(last and final_norm))
            cur = d
        if not final_ops:
            for q in S.dma_slots.values():
                for op in q:
                    if op is not None:
                        final_ops.append(op)
        else:
            for q in S.dma_slots.values():
                for op in q:
                    if op is not None and op not in final_ops:
                        final_ops.append(op)
        S.emit(nc, final_wait_ops=final_ops)
    return nc


def prep_weights(inp):
    f = lambda a: np.ascontiguousarray(np.asarray(a, dtype=np.float32))
    w = {}
    for k in ('norm_mix', 'norm_ffn', 'norm_final', 'ab_w_in', 'ab_w_out', 'lru_wa', 'lru_wx', 'c_w_in', 'c_ln_g',
              'c_ln_b', 'c_w_out', 'ffn_w1', 'ffn_w2'):
        w[k] = f(inp[k])
    w['diff_l'] = f(np.concatenate([inp['diff_lq1'], inp['diff_lk1'], inp['diff_lq2'], inp['diff_lk2']], axis=1))
    pv = []
    for e in range(2):
        cols = []
        cwv = np.asarray(inp['lru_conv_w'][e])
        cols.append(cwv.reshape(4, 4, 128).transpose(2, 0, 1).reshape(128, 16))
        for nm in ('lru_conv_b', 'lru_ba', 'lru_bx', 'lru_lambda'):
            cols.append(np.asarray(inp[nm][e]).reshape(4, 128).T)
        cols.append(np.asarray(inp['diff_subln'][e]).reshape(128, 1))
        pv.append(np.concatenate(cols, axis=1))
    w['pvec'] = f(np.stack(pv))
    w['c_w_sT'] = f(np.transpose(np.asarray(inp['c_w_s']), (0, 3, 1, 2)))
    w['c_b_sT'] = f(np.transpose(np.asarray(inp['c_b_s']), (0, 2, 1)))
    return w


_NC_CACHE = {}


def kernel(**inputs):
    x = np.asarray(inputs['x'], dtype=np.float32)
    B, S_LEN, _ = x.shape
    w = prep_weights(inputs)
    key = (S_LEN,)
    if key not in _NC_CACHE:
        _NC_CACHE[key] = build_program(S_LEN)
    nc = _NC_CACHE[key]
    in_maps = []
    for c in range(B):
        m = dict(w)
        m['x'] = np.ascontiguousarray(x[c])
        in_maps.append(m)
    res = run_bass_kernel_spmd(nc, in_maps, core_ids=list(range(B)))
    return np.stack([np.asarray(r['out'], dtype=np.float32) for r in res.results], axis=0)
```

```python
import contextlib
import math

import numpy as np
import concourse.bass as bass
import concourse.mybir as mybir
from concourse.bass_utils import run_bass_kernel_spmd

F32 = mybir.dt.float32
BF16 = mybir.dt.bfloat16
U8 = mybir.dt.uint8
AF = mybir.ActivationFunctionType
ALU = mybir.AluOpType

D = 1024
TB = 256
EPS = 1e-6
ENGS = ('pe', 'act', 'dve', 'pool', 'sp')
SEM_CAP = 30000
DBG = {'ffn': 9, 'noW': 0}


def A(*a, **k):
    return (a, k)


class Op:
    __slots__ = ('eng', 'fn', 'deps', 'is_dma', 'ndma', 'slot', 'target', 'epoch', 'msval')


class Sched:
    def __init__(self, nslots=8):
        self.ops = {e: [] for e in ENGS}
        self.lastw = {}
        self.readers = {}
        self.nslots = nslots
        self.dma_slots = {}
        self.dma_count = {e: 0 for e in ENGS}
        self.fence_deps = set()

    def add(self, eng, fn, reads=(), writes=(), dma=0):
        op = Op()
        op.eng = eng
        op.fn = fn
        op.is_dma = dma > 0
        op.ndma = dma
        op.slot = None
        op.target = None
        op.epoch = None
        op.msval = None
        deps = set(self.fence_deps)
        for k in reads:
            w = self.lastw.get(k)
            if w is not None:
                deps.add(w)
        for k in writes:
            w = self.lastw.get(k)
            if w is not None:
                deps.add(w)
            for r in self.readers.get(k, ()):
                deps.add(r)
        if dma:
            q = self.dma_slots.setdefault(eng, [None] * self.nslots)
            s = self.dma_count[eng] % self.nslots
            self.dma_count[eng] += 1
            if q[s] is not None:
                deps.add(q[s])
            q[s] = op
            op.slot = (eng, s)
        deps.discard(op)
        op.deps = deps
        for k in reads:
            self.readers.setdefault(k, []).append(op)
        for k in writes:
            self.lastw[k] = op
            self.readers[k] = []
        self.ops[eng].append(op)
        return op

    def fence(self):
        deps = set()
        for e in ENGS:
            last = None
            for op in reversed(self.ops[e]):
                if not op.is_dma:
                    last = op
                    break
            if last is not None:
                deps.add(last)
        for q in self.dma_slots.values():
            for op in q:
                if op is not None:
                    deps.add(op)
        self.fence_deps = deps
        self.lastw = {}
        self.readers = {}

    def emit(self, nc, final_wait_ops=()):
        needed = set()
        for e in ENGS:
            for op in self.ops[e]:
                for d in op.deps:
                    if e == 'pe' and d.eng == 'pe' and not d.is_dma:
                        continue
                    needed.add(d)
        for d in final_wait_ops:
            needed.add(d)
        n_epochs = {}
        for e in ENGS:
            c = 0
            ep = 0
            for op in self.ops[e]:
                if (not op.is_dma) and op in needed:
                    if c >= SEM_CAP:
                        ep += 1
                        c = 0
                    c += 1
                    op.epoch = ep
                    op.msval = c
            n_epochs[e] = ep + 1
        slot_cnt = {}
        for e in ENGS:
            for op in self.ops[e]:
                if op.is_dma:
                    slot_cnt[op.slot] = slot_cnt.get(op.slot, 0) + 16 * op.ndma
                    op.target = slot_cnt[op.slot]
        with contextlib.ExitStack() as st:
            esem = {}
            for e in ENGS:
                for ep in range(n_epochs[e]):
                    esem[(e, ep)] = st.enter_context(nc.semaphore(f"s_{e}_{ep}"))
            dsem = {}
            for slot in slot_cnt:
                dsem[slot] = st.enter_context(nc.semaphore(f"d_{slot[0]}_{slot[1]}"))
            block = st.enter_context(nc.Block())
            ops = self.ops

            def run(e, eng):
                seen_e = {}
                seen_d = {}
                for op in ops[e]:
                    self._waits(e, eng, op.deps, seen_e, seen_d, esem, dsem)
                    name, (pa, kw) = op.fn
                    if op.is_dma:
                        getattr(eng, name)(*pa, **kw).then_inc(dsem[op.slot], 16)
                    else:
                        ins = getattr(eng, name)(*pa, **kw)
                        if op.msval is not None:
                            ins.then_inc(esem[(e, op.epoch)], 1)
                if e == 'sp' and final_wait_ops:
                    self._waits(e, eng, final_wait_ops, seen_e, seen_d, esem, dsem)

            @block.tensor
            def _(eng):
                run('pe', eng)

            @block.scalar
            def _(eng):
                run('act', eng)

            @block.vector
            def _(eng):
                run('dve', eng)

            @block.gpsimd
            def _(eng):
                run('pool', eng)

            @block.sync
            def _(eng):
                run('sp', eng)

    @staticmethod
    def _waits(e, eng, deps, seen_e, seen_d, esem, dsem):
        best_e = {}
        best_d = {}
        for d in deps:
            if d.is_dma:
                if best_d.get(d.slot, 0) < d.target:
                    best_d[d.slot] = d.target
            else:
                if e == 'pe' and d.eng == 'pe':
                    continue
                v = (d.epoch, d.msval)
                if best_e.get(d.eng, (-1, 0)) < v:
                    best_e[d.eng] = v
        for slot, t in best_d.items():
            if seen_d.get(slot, 0) >= t:
                continue
            seen_d[slot] = t
            eng.wait_ge(dsem[slot], t)
        for de, v in best_e.items():
            if seen_e.get(de, (-1, 0)) >= v:
                continue
            seen_e[de] = v
            eng.wait_ge(esem[(de, v[0])], v[1])


class Region:
    def __init__(self, raw, nbytes):
        self.raw = raw
        self.nbytes = nbytes
        self.off = 0

    def reset(self):
        self.off = 0

    def alloc(self, cols, dt):
        esz = 4 if dt == F32 else 2
        nb = cols * esz
        nb_al = (nb + 63) // 64 * 64
        assert self.off + nb_al <= self.nbytes, (self.off, nb_al, self.nbytes)
        v = self.raw[:, self.off:self.off + nb].bitcast(dt)
        self.off += nb_al
        return v


def build_program(S_LEN=4096, layers=(0, 1, 2, 3), final_norm=True, passes=None):
    nblk = S_LEN // TB
    nc = bass.Bass("TRN2", target_bir_lowering=False)

    def din(name, shape):
        return nc.dram_tensor(name, list(shape), F32, kind="ExternalInput").ap()

    x_in = din("x", [S_LEN, D])
    norm_mix = din("norm_mix", [4, D])
    norm_ffn = din("norm_ffn", [4, D])
    norm_final = din("norm_final", [D])
    ab_w_in = din("ab_w_in", [2, D, 2560])
    ab_w_out = din("ab_w_out", [2, D, D])
    diff_l = din("diff_l", [2, 256])
    pvec = din("pvec", [2, 128, 33])
    lru_wa = din("lru_wa", [2, 8, 64, 64])
    lru_wx = din("lru_wx", [2, 8, 64, 64])
    c_w_in = din("c_w_in", [2, D, 2048])
    c_ln_g = din("c_ln_g", [2, D])
    c_ln_b = din("c_ln_b", [2, D])
    c_w_sT = din("c_w_sT", [2, 128, 8, 128])
    c_b_sT = din("c_b_sT", [2, 128, 8])
    c_w_out = din("c_w_out", [2, D, D])
    ffn_w1 = din("ffn_w1", [4, D, 4096])
    ffn_w2 = din("ffn_w2", [4, 4096, D])
    out = nc.dram_tensor("out", [S_LEN, D], F32, kind="ExternalOutput").ap()
    scrA = nc.dram_tensor("scrA", [S_LEN, D], F32, kind="Internal").ap()
    scrB = nc.dram_tensor("scrB", [S_LEN, D], F32, kind="Internal").ap()

    S = Sched(nslots=16)
    st = contextlib.ExitStack()
    with st:
        def sb(name, shape, dt):
            return st.enter_context(nc.sbuf_tensor(name, shape, dt))

        WBYTES = DBG.get('WBYTES', 126976)
        ABYTES = 77824
        wraw = sb("wraw", [128, WBYTES], U8)
        araw = sb("araw", [128, ABYTES], U8)
        WR = Region(wraw, WBYTES)
        AR = Region(araw, ABYTES)
        ident = sb("ident", [128, 128], BF16)
        identf = sb("identf", [128, 128], F32)
        ones_bf = sb("ones_bf", [128, 128], BF16)
        ones_f = sb("ones_f", [128, 128], F32)
        neghalf = sb("neghalf", [128, 1], F32)
        poshalf = sb("poshalf", [128, 1], F32)
        expbias = sb("expbias", [128, 1], F32)
        qbias = sb("qbias", [128, 1], F32)
        epsb = sb("epsb", [128, 1], F32)
        banks = [st.enter_context(nc.psum_tensor(f"bank{i}", [128, 512], F32)) for i in range(8)]

        def bank_bf(i):
            return banks[i][:, :].bitcast(BF16).rearrange("p (k t) -> p k t", k=8)

        S.add('pool', ('memset', A(identf[:], 1.0)), writes=['identf'])
        S.add('pool', ('affine_select', A(out=identf[:], in_=identf[:], pattern=[[-1, 128]],
                                                compare_op=ALU.is_equal, fill=0.0, base=0,
                                                channel_multiplier=1)), reads=['identf'], writes=['identf'])
        S.add('pool', ('tensor_copy', A(out=ident[:], in_=identf[:])), reads=['identf'], writes=['ident'])
        S.add('pool', ('memset', A(ones_bf[:], 1.0)), writes=['ones_bf'])
        S.add('pool', ('memset', A(ones_f[:], 1.0)), writes=['ones_f'])
        S.add('pool', ('memset', A(neghalf[:], -0.5)), writes=['neghalf'])
        S.add('pool', ('memset', A(poshalf[:], 0.5)), writes=['poshalf'])
        S.add('pool', ('memset', A(expbias[:], 0.0)), writes=['expbias'])
        S.add('pool', ('memset', A(qbias[:], 0.25)), writes=['qbias'])
        S.add('pool', ('memset', A(epsb[:], EPS)), writes=['epsb'])
        S.add('pool', ('memset', A(expbias[64:128, :], -30000.0)), reads=['expbias'], writes=['expbias'])
        CONST_KEYS = ['ident', 'ones_bf', 'ones_f', 'neghalf', 'poshalf', 'expbias']

        def after_fence_consts():
            pass

        def load_w(dst3, src2, name, kc_n, cols):
            src3 = src2.rearrange("(kc p) n -> p kc n", p=128)
            cstep = cols
            while cstep > 2048:
                cstep //= 2
            kstep = max(1, 2048 // cstep)
            keys = []
            for k0 in range(0, kc_n, kstep):
                for c0 in range(0, cols, cstep):
                    key = f"{name}_{k0}_{c0}"
                    keys.append(key)
                    S.add('pool', ('dma_start', A(
                        out=dst3[:, k0:k0 + kstep, c0:c0 + cstep],
                        in_=src3[:, k0:k0 + kstep, c0:c0 + cstep])),
                        writes=[key], dma=1)
            return keys

        def join(keys, name, dummy):
            S.add('pool', ('memset', A(dummy, 0.0)), reads=keys, writes=[name])

        def load_bcast(dst, src1d, name):
            S.add('sp', ('dma_start', A(out=dst, in_=src1d.partition_broadcast(128))),
                  writes=[name], dma=1)

        def load_x(xt3, src, blk, key):
            S.add('sp', ('dma_start', A(
                out=xt3, in_=src[blk * TB:(blk + 1) * TB, :].rearrange("(s p) d -> p s d", p=128))),
                writes=[key], dma=1)

        def store_x(dst, xo3, blk, key):
            return S.add('sp', ('dma_start', A(
                out=dst[blk * TB:(blk + 1) * TB, :].rearrange("(s p) d -> p s d", p=128), in_=xo3)),
                reads=[key], dma=1)

        def rms_norm(x2, gbc, gkey, h2, xkey, hkey, tmp, tkey):
            stt = tmp[:, 0:12]
            mv = tmp[:, 12:14]
            ms = tmp[:, 14:15]
            rs = tmp[:, 15:16]
            S.add('dve', ('bn_stats', A(out=stt[:, 0:6], in_=x2[:, 0:512])), reads=[xkey], writes=[tkey + 'a'])
            S.add('dve', ('bn_stats', A(out=stt[:, 6:12], in_=x2[:, 512:1024])), reads=[xkey], writes=[tkey + 'b'])
            S.add('dve', ('bn_aggr', A(out=mv, in_=stt)), reads=[tkey + 'a', tkey + 'b'], writes=[tkey + 'mv'])
            S.add('dve', ('tensor_scalar', A(out=ms, in0=mv[:, 0:1], scalar1=mv[:, 0:1], scalar2=mv[:, 1:2],
                                                   op0=ALU.mult, op1=ALU.add)), reads=[tkey + 'mv'], writes=[tkey + 'ms'])
            S.add('dve', ('tensor_scalar', A(out=ms, in0=ms, scalar1=EPS, scalar2=None, op0=ALU.add)),
                  reads=[tkey + 'ms'], writes=[tkey + 'ms'])
            S.add('pool', ('tensor_tensor', A(out=rs, in0=ms, in1=neghalf[:, 0:1], op=ALU.pow)),
                  reads=[tkey + 'ms', 'neghalf'], writes=[tkey + 'rs'])
            S.add('dve', ('scalar_tensor_tensor', A(out=h2, in0=x2, scalar=rs, in1=gbc, op0=ALU.mult, op1=ALU.mult)),
                  reads=[xkey, tkey + 'rs', gkey], writes=[hkey])

        def transpose8(h2, hkey, hT3, col0, hTkey, bank_i, evac_eng):
            tp = bank_bf(bank_i)
            bkey = f"bank{bank_i}"
            for kc in range(8):
                S.add('pe', ('transpose', A(out=tp[:, kc, :], in_=h2[:, kc * 128:(kc + 1) * 128],
                                                         identity=ident[:])),
                      reads=[hkey, 'ident'], writes=[bkey])
            if evac_eng == 'act':
                S.add('act', ('copy', A(out=hT3[:, :, col0:col0 + 128], in_=tp)), writes=[bkey, hTkey])
            else:
                S.add('dve', ('tensor_copy', A(out=hT3[:, :, col0:col0 + 128], in_=tp)), writes=[bkey, hTkey])

        final_ops = []

        def BK(i):
            return f'bank{i}'

        def norm_stats(x2, xkeys, t, tk):
            stt = t[:, 0:12]
            mv = t[:, 12:14]
            ms = t[:, 14:15]
            rs = t[:, 15:16]
            S.add('dve', ('bn_stats', A(out=stt[:, 0:6], in_=x2[:, 0:512])), reads=xkeys, writes=[tk + 'a'])
            S.add('dve', ('bn_stats', A(out=stt[:, 6:12], in_=x2[:, 512:1024])), reads=xkeys, writes=[tk + 'b'])
            S.add('dve', ('bn_aggr', A(out=mv, in_=stt)), reads=[tk + 'a', tk + 'b'], writes=[tk + 'mv'])
            S.add('dve', ('tensor_scalar', A(out=ms, in0=mv[:, 0:1], scalar1=mv[:, 0:1], scalar2=mv[:, 1:2],
                                                   op0=ALU.mult, op1=ALU.add)), reads=[tk + 'mv'], writes=[tk + 'ms'])
            S.add('dve', ('tensor_scalar', A(out=ms, in0=ms, scalar1=EPS, scalar2=None, op0=ALU.add)),
                  reads=[tk + 'ms'], writes=[tk + 'ms'])
            S.add('pool', ('tensor_tensor', A(out=rs, in0=ms, in1=neghalf[:, 0:1], op=ALU.pow)),
                  reads=[tk + 'ms', 'neghalf'], writes=[tk + 'rs'])
            return rs

        def ffn_pass(layer, src, dst, apply_final):
            S.fence()
            WR.reset()
            AR.reset()
            W1 = WR.alloc(8 * 4096, BF16).rearrange("p (k f) -> p k f", k=8)
            W2a = WR.alloc(30 * 1024, BF16).rearrange("p (k f) -> p k f", k=30)
            W2b = AR.alloc(2 * 1024, BF16).rearrange("p (k f) -> p k f", k=2)
            W2v = [W2a[:, fc, :] if fc < 30 else W2b[:, fc - 30, :] for fc in range(32)]
            gbc = AR.alloc(1024, F32)
            gfin = AR.alloc(1024, F32) if apply_final else None
            dummy = AR.alloc(16, F32)
            xts = [AR.alloc(2048, F32).rearrange("p (s d) -> p s d", s=2) for _ in range(2)]
            xos = [AR.alloc(2048, F32).rearrange("p (s d) -> p s d", s=2) for _ in range(2)]
            hb = [AR.alloc(1024, BF16) for _ in range(2)]
            hT = [AR.alloc(8 * TB, BF16).rearrange("p (k t) -> p k t", k=8) for _ in range(2)]
            uT = AR.alloc(32 * TB, BF16).rearrange("p (k t) -> p k t", k=32)
            rt = [AR.alloc(TB, F32) for _ in range(2)]
            tmp = [AR.alloc(16, F32) for _ in range(2)]

            load_bcast(gbc, norm_ffn[layer], 'gbc')
            if apply_final:
                load_bcast(gfin, norm_final, 'gfin')
            load_x(xts[0], src, 0, 'xt0')
            k1 = load_w(W1, ffn_w1[layer], 'W1', 8, 4096)
            k2 = load_w(W2a, ffn_w2[layer][0:30 * 128, :], 'W2a', 30, 1024)
            k2 += load_w(W2b, ffn_w2[layer][30 * 128:32 * 128, :], 'W2b', 2, 1024)
            join(k1, 'W1', dummy[:, 0:4])
            join(k2, 'W2', dummy[:, 4:8])

            def f_norm(b):
                p = b % 2
                for s in range(2):
                    rms_norm(xts[p][:, s, :], gbc, 'gbc', hb[s], f'xt{p}', f'hb{s}', tmp[s], f'tmp{s}')

            def f_tr(b):
                p = b % 2
                for s in range(2):
                    transpose8(hb[s], f'hb{s}', hT[p], s * 128, f'hT{p}', s, 'act')

            def f_w1(b):
                p = b % 2
                for fc in range(32):
                    bi = 2 + fc % 4
                    ups = banks[bi][:, 0:TB]
                    for kc in range(8):
                        S.add('pe', ('matmul', A(
                            out=ups, lhsT=W1[:, kc, fc * 128:(fc + 1) * 128], rhs=hT[p][:, kc, :],
                            start=(kc == 0), stop=(kc == 7))),
                            reads=['W1', f'hT{p}'], writes=[BK(bi)])
                    r = rt[fc % 2]
                    S.add('act', ('activation', A(out=r, in_=ups, func=AF.Relu)),
                          writes=[BK(bi), f'rt{fc % 2}'])
                    S.add('dve', ('tensor_tensor', A(out=uT[:, fc, :], in0=r, in1=r, op=ALU.mult)),
                          reads=[f'rt{fc % 2}'], writes=[f'uT{fc}'])

            def f_w2(b, gi):
                p = b % 2
                xt = xts[p]
                xo = xos[p]
                ts, dh = gi // 2, gi % 2
                bi = 6 + gi % 2
                acc = banks[bi]
                for fc in range(32):
                    S.add('pe', ('matmul', A(
                        out=acc[:, :], lhsT=uT[:, fc, ts * 128:(ts + 1) * 128],
                        rhs=W2v[fc][:, dh * 512:(dh + 1) * 512], start=(fc == 0), stop=(fc == 31))),
                        reads=['W2', f'uT{fc}'], writes=[BK(bi)])
                S.add('dve', ('tensor_tensor', A(
                    out=xo[:, ts, dh * 512:(dh + 1) * 512], in0=acc[:, :], in1=xt[:, ts, dh * 512:(dh + 1) * 512],
                    op=ALU.add)), reads=[f'xt{p}'], writes=[BK(bi), f'xo{p}_{ts}'])

            def f_store(b):
                p = b % 2
                xo = xos[p]
                okeys = [f'xo{p}_0', f'xo{p}_1']
                if apply_final:
                    for s in range(2):
                        x2 = xo[:, s, :]
                        rs = norm_stats(x2, [f'xo{p}_{s}'], tmp[s], f'tmp{s}')
                        S.add('dve', ('scalar_tensor_tensor', A(out=x2, in0=x2, scalar=rs, in1=gfin,
                                                               op0=ALU.mult, op1=ALU.mult)),
                              reads=[f'tmp{s}rs', 'gfin'], writes=[f'xo{p}_{s}'])
                op = S.add('sp', ('dma_start', A(
                    out=dst[b * TB:(b + 1) * TB, :].rearrange("(s p) d -> p s d", p=128), in_=xo)),
                    reads=okeys, dma=1)
                if apply_final:
                    final_ops.append(op)

            if nblk > 1:
                load_x(xts[1], src, 1, 'xt1')
            f_norm(0)
            f_tr(0)
            for b in range(nblk):
                p = b % 2
                f_w1(b)
                if b + 1 < nblk:
                    f_norm(b + 1)
                f_w2(b, 0)
                f_w2(b, 1)
                if b + 1 < nblk:
                    f_tr(b + 1)
                f_w2(b, 2)
                f_w2(b, 3)
                f_store(b)
                if b + 2 < nblk:
                    load_x(xts[p], src, b + 2, f'xt{p}')

        def odd_pass(layer, src, dst):
            o = layer // 2
            S.fence()
            WR.reset()
            AR.reset()
            Wci = WR.alloc(8 * 2048, BF16).rearrange("p (k f) -> p k f", k=8)
            Wco = WR.alloc(8 * 1024, BF16).rearrange("p (k f) -> p k f", k=8)
            WsT = WR.alloc(8 * 128, BF16).rearrange("p (g t) -> p g t", g=8)
            lng = WR.alloc(1024, F32)
            lnb = WR.alloc(1024, F32)
            gbc = WR.alloc(1024, F32)
            bsT = WR.alloc(8, F32)
            dummy = WR.alloc(16, F32)
            NZ = 3
            NX = 4
            zu = [WR.alloc(1024, F32) for _ in range(NZ)]
            zv = [WR.alloc(1024, F32) for _ in range(NZ)]
            vt = [WR.alloc(1024, F32) for _ in range(2)]
            xts = [AR.alloc(2048, F32).rearrange("p (s d) -> p s d", s=2) for _ in range(NX)]
            xos = [AR.alloc(2048, F32).rearrange("p (s d) -> p s d", s=2) for _ in range(2)]
            hb = [AR.alloc(1024, BF16) for _ in range(2)]
            hT = [AR.alloc(8 * 128, BF16).rearrange("p (k t) -> p k t", k=8) for _ in range(2)]
            vln = [AR.alloc(1024, BF16) for _ in range(2)]
            sg = [AR.alloc(1024, BF16) for _ in range(2)]
            sT = [AR.alloc(8 * 128, BF16).rearrange("p (k t) -> p k t", k=8) for _ in range(2)]
            tmp = [AR.alloc(16, F32) for _ in range(2)]
            tmp2 = [AR.alloc(16, F32) for _ in range(2)]

            load_bcast(gbc, norm_mix[layer], 'gbc')
            load_bcast(lng, c_ln_g[o], 'lng')
            load_bcast(lnb, c_ln_b[o], 'lnb')
            S.add('sp', ('dma_start', A(out=bsT, in_=c_b_sT[o])), writes=['bsT'], dma=1)
            load_x(xts[0], src, 0, 'xt0')
            k1 = load_w(Wci, c_w_in[o], 'Wci', 8, 2048)
            k2 = load_w(Wco, c_w_out[o], 'Wco', 8, 1024)
            S.add('pool', ('dma_start', A(out=WsT, in_=c_w_sT[o])), writes=['WsT_raw'], dma=1)
            join(k1, 'Wci', dummy[:, 0:4])
            join(k2, 'Wco', dummy[:, 4:8])
            S.add('pool', ('memset', A(WsT[64:128, :, 0:64], 0.0)), reads=['WsT_raw'], writes=['WsT'])

            def st_norm(g):
                b, s = g // 2, g % 2
                q = g % 2
                rms_norm(xts[b % NX][:, s, :], gbc, 'gbc', hb[q], f'xt{b % NX}', f'hb{q}', tmp[q], f'tmp{q}')

            def st_tr(g):
                q = g % 2
                transpose8(hb[q], f'hb{q}', hT[q], 0, f'hT{q}', 0, 'act')

            def st_z(g):
                q = g % 2
                z = g % NZ
                for cg in range(4):
                    bi = 2 + cg % 2
                    for kc in range(8):
                        S.add('pe', ('matmul', A(
                            out=banks[bi][:, :], lhsT=hT[q][:, kc, :], rhs=Wci[:, kc, cg * 512:(cg + 1) * 512],
                            start=(kc == 0), stop=(kc == 7))), reads=['Wci', f'hT{q}'], writes=[BK(bi)])
                    dstz = (zu[z] if cg < 2 else zv[z])[:, (cg % 2) * 512:(cg % 2 + 1) * 512]
                    S.add('act', ('activation', A(out=dstz, in_=banks[bi][:, :], func=AF.Gelu_apprx_tanh)),
                          writes=[BK(bi), f'z{z}_{cg}'])

            def st_ln(g):
                q = g % 2
                z = g % NZ
                t2 = tmp2[q]
                stt = t2[:, 0:12]
                mv = t2[:, 12:14]
                ve = t2[:, 14:15]
                rs = t2[:, 15:16]
                zvs = zv[z]
                S.add('dve', ('bn_stats', A(out=stt[:, 0:6], in_=zvs[:, 0:512])), reads=[f'z{z}_2'], writes=[f't2{q}a'])
                S.add('dve', ('bn_stats', A(out=stt[:, 6:12], in_=zvs[:, 512:1024])), reads=[f'z{z}_3'], writes=[f't2{q}b'])
                S.add('dve', ('bn_aggr', A(out=mv, in_=stt)), reads=[f't2{q}a', f't2{q}b'], writes=[f't2{q}mv'])
                S.add('dve', ('tensor_scalar', A(out=ve, in0=mv[:, 1:2], scalar1=EPS, scalar2=None, op0=ALU.add)),
                      reads=[f't2{q}mv'], writes=[f't2{q}ve'])
                S.add('pool', ('tensor_tensor', A(out=rs, in0=ve, in1=neghalf[:, 0:1], op=ALU.pow)),
                      reads=[f't2{q}ve', 'neghalf'], writes=[f't2{q}rs'])
                vts = vt[q]
                S.add('dve', ('tensor_scalar', A(out=vts, in0=zvs, scalar1=mv[:, 0:1], scalar2=rs,
                                                 op0=ALU.subtract, op1=ALU.mult)),
                      reads=[f'z{z}_2', f'z{z}_3', f't2{q}mv', f't2{q}rs'], writes=[f'vt{q}'])
                S.add('pool', ('tensor_tensor', A(out=vts, in0=vts, in1=lng, op=ALU.mult)), reads=['lng'], writes=[f'vt{q}'])
                S.add('pool', ('tensor_tensor', A(out=vln[q], in0=vts, in1=lnb, op=ALU.add)), reads=[f'vt{q}', 'lnb'],
                      writes=[f'vln{q}'])

            def st_sp_pe(g):
                q = g % 2
                vl = vln[q]
                for gg_ in range(8):
                    bi = 4 + gg_ // 4
                    mps = banks[bi][:, (gg_ % 4) * 128:(gg_ % 4 + 1) * 128]
                    S.add('pe', ('matmul', A(out=mps, lhsT=WsT[:, gg_, :], rhs=vl[:, gg_ * 128:(gg_ + 1) * 128],
                                             start=True, stop=True)), reads=['WsT', f'vln{q}'], writes=[BK(bi)])

            def st_sp_dve(g):
                q = g % 2
                z = g % NZ
                sgs = sg[q]
                zus = zu[z]
                for gg_ in range(8):
                    bi = 4 + gg_ // 4
                    mps = banks[bi][:, (gg_ % 4) * 128:(gg_ % 4 + 1) * 128]
                    S.add('dve', ('scalar_tensor_tensor', A(
                        out=sgs[:, gg_ * 128:(gg_ + 1) * 128], in0=mps, scalar=bsT[:, gg_:gg_ + 1],
                        in1=zus[:, gg_ * 128:(gg_ + 1) * 128], op0=ALU.add, op1=ALU.mult)),
                        reads=['bsT', f'z{z}_{gg_ // 4}'], writes=[BK(bi), f'sg{q}_{gg_}'])

            def st_str(g):
                q = g % 2
                sgs = sg[q]
                tp = bank_bf(1)
                for kc in range(8):
                    S.add('pe', ('transpose', A(out=tp[:, kc, :], in_=sgs[:, kc * 128:(kc + 1) * 128], identity=ident[:])),
                          reads=[f'sg{q}_{kc}', 'ident'], writes=[BK(1)])
                S.add('act', ('copy', A(out=sT[q], in_=tp)), writes=[BK(1), f'sT{q}'])

            def st_out(g):
                b, s = g // 2, g % 2
                q = g % 2
                p = b % 2
                xt = xts[b % NX]
                xo = xos[p]
                for dh in range(2):
                    bi = 6 + dh
                    for kc in range(8):
                        S.add('pe', ('matmul', A(
                            out=banks[bi][:, :], lhsT=sT[q][:, kc, :], rhs=Wco[:, kc, dh * 512:(dh + 1) * 512],
                            start=(kc == 0), stop=(kc == 7))), reads=['Wco', f'sT{q}'], writes=[BK(bi)])
                    S.add('dve', ('tensor_tensor', A(
                        out=xo[:, s, dh * 512:(dh + 1) * 512], in0=banks[bi][:, :],
                        in1=xt[:, s, dh * 512:(dh + 1) * 512], op=ALU.add)),
                        reads=[f'xt{b % NX}'], writes=[BK(bi), f'xo{p}_{s}'])
                if s == 1:
                    S.add('sp', ('dma_start', A(
                        out=dst[b * TB:(b + 1) * TB, :].rearrange("(s p) d -> p s d", p=128), in_=xo)),
                        reads=[f'xo{p}_0', f'xo{p}_1'], dma=1)
                    if b + NX < nblk:
                        load_x(xts[b % NX], src, b + NX, f'xt{b % NX}')

            nsub = 2 * nblk
            for bb in range(1, min(NX, nblk)):
                load_x(xts[bb], src, bb, f'xt{bb}')
            st_norm(0)
            for t in range(nsub + 3):
                if 0 <= t - 2 < nsub:
                    st_sp_pe(t - 2)
                if t + 1 < nsub:
                    st_norm(t + 1)
                if t < nsub:
                    st_tr(t)
                if 0 <= t - 3 < nsub:
                    st_out(t - 3)
                if t < nsub:
                    st_z(t)
                if 0 <= t - 1 < nsub:
                    st_ln(t - 1)
                if 0 <= t - 2 < nsub:
                    st_sp_dve(t - 2)
                    st_str(t - 2)

        def even_pass(layer, src, dst):
            ev = layer // 2
            lam_init = 0.8 - 0.6 * math.exp(-0.3 * layer)
            NCH = S_LEN // 128
            S.fence()
            WR.reset()
            AR.reset()
            Win = WR.alloc(8 * 2560, BF16).rearrange("p (k f) -> p k f", k=8)
            Wout = WR.alloc(8 * 1024, BF16).rearrange("p (k f) -> p k f", k=8)
            kT = WR.alloc(4 * S_LEN, BF16).rearrange("p (h t) -> p h t", h=4)
            V = WR.alloc(NCH * 512, BF16).rearrange("p (c v) -> p c v", c=NCH)
            WA = WR.alloc(4 * 128, BF16).rearrange("p (c j) -> p c j", c=4)
            WX = WR.alloc(4 * 128, BF16).rearrange("p (c j) -> p c j", c=4)
            gbc = AR.alloc(1024, F32)
            pv = AR.alloc(33, F32)
            ldl = WR.alloc(256, F32)
            sm = AR.alloc(64, F32)
            dummy = AR.alloc(16, F32)
            ltmp = AR.alloc(64, F32)
            hprev = AR.alloc(4, F32)
            tmp = [AR.alloc(16, F32) for _ in range(2)]
            xts = [AR.alloc(2048, F32).rearrange("p (s d) -> p s d", s=2) for _ in range(2)]
            hb = [AR.alloc(1024, BF16) for _ in range(2)]
            hT = AR.alloc(8 * TB, BF16).rearrange("p (k t) -> p k t", k=8)
            qc = AR.alloc(4 * 2 * TB, BF16).rearrange("p (h t) -> p h t", h=4)
            xbh = AR.alloc(4 * (TB + 3), F32).rearrange("p (c t) -> p c t", c=4)
            gg = AR.alloc(4 * TB, BF16).rearrange("p (c t) -> p c t", c=4)
            xc = AR.alloc(4 * TB, F32).rearrange("p (c t) -> p c t", c=4)
            xcb = hb[1].rearrange("p (c t) -> p c t", c=4)
            Tr = AR.alloc(4 * TB, F32).rearrange("p (c t) -> p c t", c=4)
            Ti = AR.alloc(4 * TB, F32).rearrange("p (c t) -> p c t", c=4)
            aa = AR.alloc(4 * TB, F32).rearrange("p (c t) -> p c t", c=4)
            a2 = AR.alloc(4 * TB, F32).rearrange("p (c t) -> p c t", c=4)
            bt = Ti
            hs = Tr
            yT = AR.alloc(8 * TB, BF16).rearrange("p (k t) -> p k t", k=8)
            NPT = 4
            pT = [AR.alloc(2 * TB, BF16) for _ in range(NPT - 1)] + [WR.alloc(2 * TB, BF16)]
            fr = AR.alloc(2 * TB, F32)
            fo4 = AR.alloc(4 * TB, F32).rearrange("p (h t) -> p h t", h=4)
            sd = AR.alloc(4 * TB, F32).rearrange("p (h t) -> p h t", h=4)

            s1 = sm[:, 0:1]
            s2 = sm[:, 1:2]
            e1 = sm[:, 2:3]
            e2 = sm[:, 3:4]
            neglam = sm[:, 4:5]
            g1 = sm[:, 5:6]
            yv = sm[:, 8:12]
            tv = sm[:, 12:16]
            sc = sm[:, 16:20]
            sch = sm[:, 20:24]
            hba = sm[:, 24:28]
            hbx = sm[:, 28:32]
            cw = lambda w, c: pv[:, w * 4 + c:w * 4 + c + 1]
            cb = lambda c: pv[:, 16 + c:17 + c]

            load_bcast(gbc, norm_mix[layer], 'gbc')
            load_bcast(ldl, diff_l[ev], 'ldl')
            S.add('sp', ('dma_start', A(out=pv, in_=pvec[ev])), writes=['pv'], dma=1)
            k1 = load_w(Win, ab_w_in[ev], 'Win', 8, 2560)
            k2 = load_w(Wout, ab_w_out[ev], 'Wout', 8, 1024)
            join(k1, 'Win', dummy[:, 0:4])
            join(k2, 'Wout', dummy[:, 4:8])
            S.add('pool', ('memset', A(WA, 0.0)), writes=['WA0'])
            S.add('pool', ('memset', A(WX, 0.0)), writes=['WX0'])
            for nm, wt, srcw in (('WA', WA, lru_wa), ('WX', WX, lru_wx)):
                sv = srcw[ev].rearrange("(c j) i o -> j i c o", j=2)
                wk = []
                for j in range(2):
                    key = f'{nm}_{j}'
                    wk.append(key)
                    S.add('pool', ('dma_start', A(
                        out=wt[j * 64:(j + 1) * 64, :, j * 64:(j + 1) * 64], in_=sv[j])),
                        reads=[nm + '0'], writes=[key], dma=1)
                join(wk, nm, dummy[:, 8:10] if nm == 'WA' else dummy[:, 10:12])
            S.add('pool', ('memset', A(qc[64:128, :, 0:TB], 0.0)), writes=['qcz0'])
            S.add('pool', ('memset', A(qc[0:64, :, TB:2 * TB], 0.0)), writes=['qcz1'])
            S.add('dve', ('tensor_tensor', A(out=ltmp, in0=ldl[:, 0:64], in1=ldl[:, 64:128], op=ALU.mult)),
                  reads=['ldl'], writes=['ltmp'])
            S.add('dve', ('tensor_reduce', A(out=s1, in_=ltmp, axis=mybir.AxisListType.X, op=ALU.add)),
                  reads=['ltmp'], writes=['s1'])
            S.add('dve', ('tensor_tensor', A(out=ltmp, in0=ldl[:, 128:192], in1=ldl[:, 192:256], op=ALU.mult)),
                  reads=['ldl', 's1'], writes=['ltmp'])
            S.add('dve', ('tensor_reduce', A(out=s2, in_=ltmp, axis=mybir.AxisListType.X, op=ALU.add)),
                  reads=['ltmp'], writes=['s2'])
            S.add('act', ('activation', A(out=e1, in_=s1, func=AF.Exp)), reads=['s1'], writes=['e1'])
            S.add('act', ('activation', A(out=e2, in_=s2, func=AF.Exp)), reads=['s2'], writes=['e2'])
            S.add('dve', ('tensor_tensor', A(out=neglam, in0=e2, in1=e1, op=ALU.subtract)), reads=['e1', 'e2'],
                  writes=['neglam'])
            S.add('dve', ('tensor_scalar', A(out=neglam, in0=neglam, scalar1=-lam_init, scalar2=None, op0=ALU.add)),
                  reads=['neglam'], writes=['neglam'])
            S.add('dve', ('tensor_scalar', A(out=g1, in0=pv[:, 32:33], scalar1=(1.0 - lam_init), scalar2=None,
                                                   op0=ALU.mult)), reads=['pv'], writes=['g1'])
            S.add('act', ('activation', A(out=yv, in_=pv[:, 28:32], func=AF.Exp, scale=-1.0)), reads=['pv'], writes=['yv'])
            S.add('dve', ('tensor_scalar', A(out=tv, in0=yv, scalar1=-0.25, scalar2=1.0 / 3.0, op0=ALU.mult, op1=ALU.add)),
                  reads=['yv'], writes=['tv'])
            S.add('dve', ('tensor_tensor', A(out=tv, in0=tv, in1=yv, op=ALU.mult)), reads=['tv', 'yv'], writes=['tv'])
            S.add('dve', ('tensor_scalar', A(out=tv, in0=tv, scalar1=-1.0, scalar2=0.5, op0=ALU.mult, op1=ALU.add)),
                  reads=['tv'], writes=['tv'])
            S.add('dve', ('tensor_tensor', A(out=tv, in0=tv, in1=yv, op=ALU.mult)), reads=['tv', 'yv'], writes=['tv'])
            S.add('dve', ('tensor_scalar', A(out=tv, in0=tv, scalar1=-1.0, scalar2=1.0, op0=ALU.mult, op1=ALU.add)),
                  reads=['tv'], writes=['tv'])
            S.add('dve', ('tensor_tensor', A(out=tv, in0=tv, in1=yv, op=ALU.mult)), reads=['tv', 'yv'], writes=['tv'])
            S.add('dve', ('tensor_scalar', A(out=sc, in0=tv, scalar1=-8.0, scalar2=None, op0=ALU.mult)),
                  reads=['tv'], writes=['sc'])
            S.add('dve', ('tensor_scalar', A(out=sch, in0=tv, scalar1=-4.0, scalar2=None, op0=ALU.mult)),
                  reads=['tv'], writes=['sch'])
            S.add('dve', ('tensor_scalar', A(out=hba, in0=pv[:, 20:24], scalar1=0.5, scalar2=None, op0=ALU.mult)),
                  reads=['pv'], writes=['hba'])
            S.add('dve', ('tensor_scalar', A(out=hbx, in0=pv[:, 24:28], scalar1=0.5, scalar2=None, op0=ALU.mult)),
                  reads=['pv'], writes=['hbx'])
            S.add('pool', ('memset', A(xbh[:, :, 0:3], 0.0)), writes=['xbh_halo'])
            S.add('pool', ('memset', A(hprev, 0.0)), writes=['hprev'])

            gcount = [0]

            def gbank():
                bi = 1 + gcount[0] % 3
                gcount[0] += 1
                return bi

            pcount = [0]

            def e_load(b):
                p = b % 2
                S.add('sp', ('dma_start', A(
                    out=xts[p], in_=src[b * TB:(b + 1) * TB, :].rearrange("(s p) d -> p s d", p=128))),
                    writes=[f'xt{p}'], dma=1)

            def e_norm(b):
                p = b % 2
                for s in range(2):
                    rms_norm(xts[p][:, s, :], gbc, 'gbc', hb[s], f'xt{p}', f'hb{s}', tmp[s], f'tmp{s}')

            def e_tr(b):
                for s in range(2):
                    transpose8(hb[s], f'hb{s}', hT, s * 128, 'hT', 0, 'act')

            def proj_fm(col0, evac):
                bi = gbank()
                ps_ = banks[bi][:, 0:TB]
                for kc in range(8):
                    S.add('pe', ('matmul', A(out=ps_, lhsT=Win[:, kc, col0:col0 + 128],
                                             rhs=hT[:, kc, :], start=(kc == 0), stop=(kc == 7))),
                          reads=['Win', 'hT'], writes=[BK(bi)])
                evac(ps_, BK(bi))

            def evac_q(ps_, bk, h):
                S.add('act', ('copy', A(out=qc[0:64, h, 0:TB], in_=ps_[0:64, :])),
                      reads=['qcz0', 'qcz1'], writes=[bk, f'qc{h}'])
                S.add('act', ('copy', A(out=qc[64:128, h, TB:2 * TB], in_=ps_[64:128, :])),
                      writes=[bk, f'qc{h}'])

            def e_inproj(b):
                for c in range(4):
                    proj_fm(1536 + c * 128, lambda ps_, bk, c=c: S.add(
                        'act', ('copy', A(out=xbh[:, c, 3:3 + TB], in_=ps_)), reads=['xbh_halo'],
                        writes=[bk, f'xbh{c}']))
                for c in range(4):
                    proj_fm(2048 + c * 128, lambda ps_, bk, c=c: S.add(
                        'act', ('activation', A(out=gg[:, c, :], in_=ps_, func=AF.Gelu_apprx_tanh)),
                        writes=[bk, f'gg{c}']))
                e_lru1(b)
                for h in range(4):
                    proj_fm(h * 128, lambda ps_, bk, h=h: evac_q(ps_, bk, h))
                for h in range(4):
                    proj_fm(512 + h * 128, lambda ps_, bk, h=h: S.add(
                        'act', ('copy', A(out=kT[:, h, b * TB:(b + 1) * TB], in_=ps_)),
                        writes=[bk, f'kT{h}_{b}']))
                for ts in range(2):
                    bi = gbank()
                    for kc in range(8):
                        S.add('pe', ('matmul', A(
                            out=banks[bi][:, :], lhsT=hT[:, kc, ts * 128:(ts + 1) * 128],
                            rhs=Win[:, kc, 1024:1536], start=(kc == 0), stop=(kc == 7))),
                            reads=['Win', 'hT'], writes=[BK(bi)])
                    S.add('act', ('copy', A(out=V[:, b * 2 + ts, :], in_=banks[bi][:, :])),
                          writes=[BK(bi), f'V{b * 2 + ts}'])

            def e_lru1(b):
                for c in range(4):
                    S.add('dve', ('tensor_scalar', A(out=xc[:, c, :], in0=xbh[:, c, 0:TB], scalar1=cw(0, c),
                                                     scalar2=cb(c), op0=ALU.mult, op1=ALU.add)),
                          reads=[f'xbh{c}', 'xbh_halo', 'pv'], writes=[f'xc{c}'])
                    for w in range(1, 4):
                        S.add('dve', ('scalar_tensor_tensor', A(
                            out=xc[:, c, :], in0=xbh[:, c, w:w + TB], scalar=cw(w, c), in1=xc[:, c, :],
                            op0=ALU.mult, op1=ALU.add)), reads=[f'xbh{c}', 'xbh_halo', 'pv'], writes=[f'xc{c}'])
                    S.add('pool', ('tensor_copy', A(out=xcb[:, c, :], in_=xc[:, c, :])), reads=[f'xc{c}'],
                          writes=[f'xcb{c}', 'hb1'])
                S.add('pool', ('tensor_copy', A(out=xbh[:, :, 0:3], in_=xbh[:, :, TB:TB + 3])),
                      reads=[f'xbh{c}' for c in range(4)] + [f'xc{c}' for c in range(4)], writes=['xbh_halo'])

            def lru_part2():
                for c in range(4):
                    for nm, wt, Tt, hbias in (('r', WA, Tr, hba), ('i', WX, Ti, hbx)):
                        bi = 0
                        ps_ = banks[bi][:, 0:TB]
                        S.add('pe', ('matmul', A(out=ps_, lhsT=wt[:, c, :], rhs=xcb[:, c, :], start=True, stop=True)),
                              reads=['WA' if nm == 'r' else 'WX', f'xcb{c}', 'hb1'], writes=[BK(bi)])
                        S.add('act', ('activation', A(
                            out=Tt[:, c, :], in_=ps_, func=AF.Tanh, bias=hbias[:, c:c + 1], scale=0.5)),
                            reads=['hba', 'hbx'],
                            writes=[BK(bi), f'T{nm}{c}'] + ([f'hs{c}'] if nm == 'r' else [f'bt{c}']))
                for c in range(4):
                    S.add('act', ('activation', A(out=aa[:, c, :], in_=Tr[:, c, :], func=AF.Exp,
                                                  bias=sch[:, c:c + 1], scale=sch[:, c:c + 1])),
                          reads=[f'Tr{c}', 'sch'], writes=[f'aa{c}'])
                    S.add('act', ('activation', A(out=a2[:, c, :], in_=Tr[:, c, :], func=AF.Exp,
                                                  bias=sc[:, c:c + 1], scale=sc[:, c:c + 1])),
                          reads=[f'Tr{c}', 'sc'], writes=[f'a2{c}'])

            def e_lru_tail(b):
                S.add('act', ('activation', A(out=a2.rearrange("p c t -> p (c t)"), in_=a2.rearrange("p c t -> p (c t)"),
                                              func=AF.Sqrt, bias=qbias[:, 0:1], scale=-0.25)),
                      reads=['qbias'], writes=[f'a2{c}' for c in range(4)])
                for c in range(4):
                    S.add('dve', ('scalar_tensor_tensor', A(out=bt[:, c, :], in0=Ti[:, c, :], scalar=1.0,
                                                           in1=xc[:, c, :], op0=ALU.add, op1=ALU.mult)),
                          reads=[f'Ti{c}', f'xc{c}'], writes=[f'bt{c}', f'Ti{c}'])
                    S.add('dve', ('tensor_tensor', A(out=bt[:, c, :], in0=bt[:, c, :], in1=a2[:, c, :], op=ALU.mult)),
                          reads=[f'a2{c}'], writes=[f'bt{c}'])
                    S.add('dve', ('tensor_tensor_scan', A(out=hs[:, c, :], data0=aa[:, c, :], data1=bt[:, c, :],
                                                         initial=hprev[:, c:c + 1], op0=ALU.mult, op1=ALU.add)),
                          reads=[f'aa{c}', f'bt{c}', 'hprev'], writes=[f'hs{c}', f'Tr{c}'])
                    S.add('dve', ('tensor_tensor', A(out=yT[:, 4 + c, :], in0=gg[:, c, :], in1=hs[:, c, :], op=ALU.mult)),
                          reads=[f'gg{c}', f'hs{c}'], writes=[f'yT{4 + c}'])
                S.add('pool', ('tensor_copy', A(out=hprev, in_=hs[:, :, TB - 1])),
                      reads=[f'hs{c}' for c in range(4)], writes=['hprev'])

            def e_tail_a(b):
                for h in range(4):
                    mb = 1 - h // 2
                    S.add('pe', ('matmul', A(out=banks[mb][:, (h % 2) * TB:(h % 2 + 1) * TB], lhsT=ones_f[:], rhs=sd[:, h, :],
                                             start=True, stop=True)),
                          reads=['ones_f', f'sd{h}'], writes=[BK(mb)])
                for hh in range(2):
                    mb = 1 - hh
                    S.add('act', ('activation', A(out=sd[:, 2 * hh:2 * hh + 2, :].rearrange("p h t -> p (h t)"),
                                                  in_=banks[mb][:, :], func=AF.Sqrt, bias=epsb[:, 0:1], scale=1.0 / 128.0)),
                          reads=['epsb'], writes=[BK(mb), f'sd{2 * hh}', f'sd{2 * hh + 1}'])
                S.add('dve', ('reciprocal', A(out=sd.rearrange("p h t -> p (h t)"), in_=sd.rearrange("p h t -> p (h t)"))),
                      writes=[f'sd{h}' for h in range(4)])
                S.add('dve', ('scalar_tensor_tensor', A(out=yT[:, 0:4, :].rearrange("p h t -> p (h t)"),
                                                       in0=fo4.rearrange("p h t -> p (h t)"), scalar=g1,
                                                       in1=sd.rearrange("p h t -> p (h t)"), op0=ALU.mult, op1=ALU.mult)),
                      reads=[f'fo{h}' for h in range(4)] + ['g1'] + [f'sd{h}' for h in range(4)],
                      writes=[f'yT{h}' for h in range(4)])

            def e_outproj(b, gi, bi):
                p = b % 2
                xt = xts[p]
                ts, dh = gi // 2, gi % 2
                for kc in range(8):
                    S.add('pe', ('matmul', A(
                        out=banks[bi][:, :], lhsT=yT[:, kc, ts * 128:(ts + 1) * 128],
                        rhs=Wout[:, kc, dh * 512:(dh + 1) * 512], start=(kc == 0), stop=(kc == 7))),
                        reads=['Wout', f'yT{kc}'], writes=[BK(bi)])
                S.add('dve', ('tensor_tensor', A(
                    out=xt[:, ts, dh * 512:(dh + 1) * 512], in0=banks[bi][:, :],
                    in1=xt[:, ts, dh * 512:(dh + 1) * 512], op=ALU.add)),
                    writes=[BK(bi), f'xt{p}'])

            def e_store(b):
                p = b % 2
                S.add('sp', ('dma_start', A(
                    out=dst[b * TB:(b + 1) * TB, :].rearrange("(s p) d -> p s d", p=128), in_=xts[p])),
                    reads=[f'xt{p}'], dma=1)
                if b + 2 < nblk:
                    e_load(b + 2)

            def e_att(b):
                nkc = 2 * b + 2
                steps = [(h, kc) for h in range(4) for kc in range(nkc)]
                nst = len(steps)
                LA = 3
                pts = {}

                def qk_exp(i):
                    h, kc = steps[i]
                    j = kc - 2 * b
                    si = 1 + i % 3
                    S.add('pe', ('matmul', A(
                        out=banks[si][:, :], lhsT=kT[:, h, kc * 128:(kc + 1) * 128], rhs=qc[:, h, :],
                        start=True, stop=True)), reads=[f'kT{h}_{kc // 2}', f'qc{h}'], writes=[BK(si)])
                    pi = pcount[0] % NPT
                    pcount[0] += 1
                    pt = pT[pi]
                    pkey = f'pT{pi}'
                    pts[i] = (pt, pkey)
                    S.add('act', ('activation', A(out=pt, in_=banks[si][:, :], func=AF.Exp, scale=0.125)),
                          writes=[BK(si), pkey])
                    if j >= 0:
                        c0 = j * 128
                        if c0 > 0:
                            S.add('pool', ('memset', A(pt[:, 0:c0], 0.0)), writes=[pkey])
                            S.add('pool', ('memset', A(pt[:, TB:TB + c0], 0.0)), writes=[pkey])
                        S.add('pool', ('memset', A(pt[64:128, c0:c0 + 64], 0.0)), writes=[pkey])
                        S.add('pool', ('memset', A(pt[64:128, TB + c0:TB + c0 + 64], 0.0)), writes=[pkey])

                def pv_sum(i):
                    h, kc = steps[i]
                    ob = 4 + h % 2
                    sb_ = 6 + h % 2
                    pt, pkey = pts.pop(i)
                    S.add('pe', ('matmul', A(
                        out=banks[ob][:, :], lhsT=V[:, kc, h * 128:(h + 1) * 128], rhs=pt,
                        start=(kc == 0), stop=(kc == nkc - 1))), reads=[f'V{kc}', pkey], writes=[BK(ob)])
                    S.add('pe', ('matmul', A(
                        out=banks[sb_][:, :], lhsT=ones_bf[:], rhs=pt,
                        start=(kc == 0), stop=(kc == nkc - 1))), reads=['ones_bf', pkey], writes=[BK(sb_)])
                    if kc == nkc - 1:
                        S.add('dve', ('reciprocal', A(out=fr, in_=banks[sb_][:, :])), writes=[BK(sb_), 'fr'])
                        S.add('dve', ('tensor_tensor', A(out=fr, in0=banks[ob][:, :], in1=fr, op=ALU.mult)),
                              writes=[BK(ob), 'fr'])
                        S.add('dve', ('scalar_tensor_tensor', A(out=fo4[:, h, :], in0=fr[:, TB:2 * TB], scalar=neglam,
                                                               in1=fr[:, 0:TB], op0=ALU.mult, op1=ALU.add)),
                              reads=['fr', 'neglam'], writes=[f'fo{h}'])
                        S.add('dve', ('tensor_tensor', A(out=sd[:, h, :], in0=fo4[:, h, :], in1=fo4[:, h, :], op=ALU.mult)),
                              reads=[f'fo{h}'], writes=[f'sd{h}'])

                ins = {}
                def at(step, fn):
                    ins.setdefault(min(max(step, 0), nst - 1), []).append(fn)
                at(3, lru_part2)
                if b > 0:
                    for gi in range(4):
                        at(14 + 3 * gi, lambda gi=gi: e_outproj(b - 1, gi, 0))
                    at(14 + 3 * 3 + 1, lambda: e_store(b - 1))
                if b + 1 < nblk:
                    at(14 + 3 * 3 + 6, lambda: e_norm(b + 1))

                for i in range(min(LA, nst)):
                    qk_exp(i)
                for i in range(nst):
                    pv_sum(i)
                    if i + LA < nst:
                        qk_exp(i + LA)
                    for fn in ins.get(i, ()):
                        fn()

            e_load(0)
            if nblk > 1:
                e_load(1)
            e_norm(0)
            for b in range(nblk):
                e_tr(b)
                e_inproj(b)
                if b > 0:
                    e_tail_a(b - 1)
                e_att(b)
                e_lru_tail(b)
            e_tail_a(nblk - 1)
            for gi in range(4):
                e_outproj(nblk - 1, gi, gbank())
            e_store(nblk - 1)

        bufs = [scrA, scrB]
        cur = x_in
        nb = 0
        if passes is None:
            passes_l = []
            for layer in layers:
                passes_l.append(('even' if layer % 2 == 0 else 'odd', layer))
                passes_l.append(('ffn', layer))
        else:
            passes_l = list(passes)
        for pi, (kind, layer) in enumerate(passes_l):
            last = (pi == len(passes_l) - 1)
            d = out if last else bufs[nb % 2]
            nb += 1
            if kind == 'even':
                even_pass(layer, cur, d)
            elif kind == 'odd':
                odd_pass(layer, cur, d)
            else:
                ffn_pass(layer, cur, d, apply_final=(last and final_norm))
            cur = d
        if not final_ops:
            for q in S.dma_slots.values():
                for op in q:
                    if op is not None:
                        final_ops.append(op)
        else:
            for q in S.dma_slots.values():
                for op in q:
                    if op is not None and op not in final_ops:
                        final_ops.append(op)
        S.emit(nc, final_wait_ops=final_ops)
    return nc


def prep_weights(inp):
    f = lambda a: np.ascontiguousarray(np.asarray(a, dtype=np.float32))
    w = {}
    for k in ('norm_mix', 'norm_ffn', 'norm_final', 'ab_w_in', 'ab_w_out', 'lru_wa', 'lru_wx', 'c_w_in', 'c_ln_g',
              'c_ln_b', 'c_w_out', 'ffn_w1', 'ffn_w2'):
        w[k] = f(inp[k])
    w['diff_l'] = f(np.concatenate([inp['diff_lq1'], inp['diff_lk1'], inp['diff_lq2'], inp['diff_lk2']], axis=1))
    pv = []
    for e in range(2):
        cols = []
        cwv = np.asarray(inp['lru_conv_w'][e])
        cols.append(cwv.reshape(4, 4, 128).transpose(2, 0, 1).reshape(128, 16))
        for nm in ('lru_conv_b', 'lru_ba', 'lru_bx', 'lru_lambda'):
            cols.append(np.asarray(inp[nm][e]).reshape(4, 128).T)
        cols.append(np.asarray(inp['diff_subln'][e]).reshape(128, 1))
        pv.append(np.concatenate(cols, axis=1))
    w['pvec'] = f(np.stack(pv))
    w['c_w_sT'] = f(np.transpose(np.asarray(inp['c_w_s']), (0, 3, 1, 2)))
    w['c_b_sT'] = f(np.transpose(np.asarray(inp['c_b_s']), (0, 2, 1)))
    return w


_NC_CACHE = {}


def kernel(**inputs):
    x = np.asarray(inputs['x'], dtype=np.float32)
    B, S_LEN, _ = x.shape
    w = prep_weights(inputs)
    key = (S_LEN,)
    if key not in _NC_CACHE:
        _NC_CACHE[key] = build_program(S_LEN)
    nc = _NC_CACHE[key]
    in_maps = []
    for c in range(B):
        m = dict(w)
        m['x'] = np.ascontiguousarray(x[c])
        in_maps.append(m)
    res = run_bass_kernel_spmd(nc, in_maps, core_ids=list(range(B)))
    return np.stack([np.asarray(r['out'], dtype=np.float32) for r in res.results], axis=0)
```

```python
import contextlib
import math

import numpy as np
import concourse.bass as bass
import concourse.mybir as mybir
from concourse.bass_utils import run_bass_kernel_spmd

F32 = mybir.dt.float32
BF16 = mybir.dt.bfloat16
U8 = mybir.dt.uint8
AF = mybir.ActivationFunctionType
ALU = mybir.AluOpType

D = 1024
TB = 256
EPS = 1e-6
ENGS = ('pe', 'act', 'dve', 'pool', 'sp')
SEM_CAP = 30000
DBG = {'ffn': 9, 'noW': 0}


def A(*a, **k):
    return (a, k)


class Op:
    __slots__ = ('eng', 'fn', 'deps', 'is_dma', 'ndma', 'slot', 'target', 'epoch', 'msval')


class Sched:
    def __init__(self, nslots=8):
        self.ops = {e: [] for e in ENGS}
        self.lastw = {}
        self.readers = {}
        self.nslots = nslots
        self.dma_slots = {}
        self.dma_count = {e: 0 for e in ENGS}
        self.fence_deps = set()

    def add(self, eng, fn, reads=(), writes=(), dma=0):
        op = Op()
        op.eng = eng
        op.fn = fn
        op.is_dma = dma > 0
        op.ndma = dma
        op.slot = None
        op.target = None
        op.epoch = None
        op.msval = None
        deps = set(self.fence_deps)
        for k in reads:
            w = self.lastw.get(k)
            if w is not None:
                deps.add(w)
        for k in writes:
            w = self.lastw.get(k)
            if w is not None:
                deps.add(w)
            for r in self.readers.get(k, ()):
                deps.add(r)
        if dma:
            q = self.dma_slots.setdefault(eng, [None] * self.nslots)
            s = self.dma_count[eng] % self.nslots
            self.dma_count[eng] += 1
            if q[s] is not None:
                deps.add(q[s])
            q[s] = op
            op.slot = (eng, s)
        deps.discard(op)
        op.deps = deps
        for k in reads:
            self.readers.setdefault(k, []).append(op)
        for k in writes:
            self.lastw[k] = op
            self.readers[k] = []
        self.ops[eng].append(op)
        return op

    def fence(self):
        deps = set()
        for e in ENGS:
            last = None
            for op in reversed(self.ops[e]):
                if not op.is_dma:
                    last = op
                    break
            if last is not None:
                deps.add(last)
        for q in self.dma_slots.values():
            for op in q:
                if op is not None:
                    deps.add(op)
        self.fence_deps = deps
        self.lastw = {}
        self.readers = {}

    def emit(self, nc, final_wait_ops=()):
        needed = set()
        for e in ENGS:
            for op in self.ops[e]:
                for d in op.deps:
                    if e == 'pe' and d.eng == 'pe' and not d.is_dma:
                        continue
                    needed.add(d)
        for d in final_wait_ops:
            needed.add(d)
        n_epochs = {}
        for e in ENGS:
            c = 0
            ep = 0
            for op in self.ops[e]:
                if (not op.is_dma) and op in needed:
                    if c >= SEM_CAP:
                        ep += 1
                        c = 0
                    c += 1
                    op.epoch = ep
                    op.msval = c
            n_epochs[e] = ep + 1
        slot_cnt = {}
        for e in ENGS:
            for op in self.ops[e]:
                if op.is_dma:
                    slot_cnt[op.slot] = slot_cnt.get(op.slot, 0) + 16 * op.ndma
                    op.target = slot_cnt[op.slot]
        with contextlib.ExitStack() as st:
            esem = {}
            for e in ENGS:
                for ep in range(n_epochs[e]):
                    esem[(e, ep)] = st.enter_context(nc.semaphore(f"s_{e}_{ep}"))
            dsem = {}
            for slot in slot_cnt:
                dsem[slot] = st.enter_context(nc.semaphore(f"d_{slot[0]}_{slot[1]}"))
            block = st.enter_context(nc.Block())
            ops = self.ops

            def run(e, eng):
                seen_e = {}
                seen_d = {}
                for op in ops[e]:
                    self._waits(e, eng, op.deps, seen_e, seen_d, esem, dsem)
                    name, (pa, kw) = op.fn
                    if op.is_dma:
                        getattr(eng, name)(*pa, **kw).then_inc(dsem[op.slot], 16)
                    else:
                        ins = getattr(eng, name)(*pa, **kw)
                        if op.msval is not None:
                            ins.then_inc(esem[(e, op.epoch)], 1)
                if e == 'sp' and final_wait_ops:
                    self._waits(e, eng, final_wait_ops, seen_e, seen_d, esem, dsem)

            @block.tensor
            def _(eng):
                run('pe', eng)

            @block.scalar
            def _(eng):
                run('act', eng)

            @block.vector
            def _(eng):
                run('dve', eng)

            @block.gpsimd
            def _(eng):
                run('pool', eng)

            @block.sync
            def _(eng):
                run('sp', eng)

    @staticmethod
    def _waits(e, eng, deps, seen_e, seen_d, esem, dsem):
        best_e = {}
        best_d = {}
        for d in deps:
            if d.is_dma:
                if best_d.get(d.slot, 0) < d.target:
                    best_d[d.slot] = d.target
            else:
                if e == 'pe' and d.eng == 'pe':
                    continue
                v = (d.epoch, d.msval)
                if best_e.get(d.eng, (-1, 0)) < v:
                    best_e[d.eng] = v
        for slot, t in best_d.items():
            if seen_d.get(slot, 0) >= t:
                continue
            seen_d[slot] = t
            eng.wait_ge(dsem[slot], t)
        for de, v in best_e.items():
            if seen_e.get(de, (-1, 0)) >= v:
                continue
            seen_e[de] = v
            eng.wait_ge(esem[(de, v[0])], v[1])


class Region:
    def __init__(self, raw, nbytes):
        self.raw = raw
        self.nbytes = nbytes
        self.off = 0

    def reset(self):
        self.off = 0

    def alloc(self, cols, dt):
        esz = 4 if dt == F32 else 2
        nb = cols * esz
        nb_al = (nb + 63) // 64 * 64
        assert self.off + nb_al <= self.nbytes, (self.off, nb_al, self.nbytes)
        v = self.raw[:, self.off:self.off + nb].bitcast(dt)
        self.off += nb_al
        return v


def build_program(S_LEN=4096, layers=(0, 1, 2, 3), final_norm=True, passes=None):
    nblk = S_LEN // TB
    nc = bass.Bass("TRN2", target_bir_lowering=False)

    def din(name, shape):
        return nc.dram_tensor(name, list(shape), F32, kind="ExternalInput").ap()

    x_in = din("x", [S_LEN, D])
    norm_mix = din("norm_mix", [4, D])
    norm_ffn = din("norm_ffn", [4, D])
    norm_final = din("norm_final", [D])
    ab_w_in = din("ab_w_in", [2, D, 2560])
    ab_w_out = din("ab_w_out", [2, D, D])
    diff_l = din("diff_l", [2, 256])
    pvec = din("pvec", [2, 128, 33])
    lru_wa = din("lru_wa", [2, 8, 64, 64])
    lru_wx = din("lru_wx", [2, 8, 64, 64])
    c_w_in = din("c_w_in", [2, D, 2048])
    c_ln_g = din("c_ln_g", [2, D])
    c_ln_b = din("c_ln_b", [2, D])
    c_w_sT = din("c_w_sT", [2, 128, 8, 128])
    c_b_sT = din("c_b_sT", [2, 128, 8])
    c_w_out = din("c_w_out", [2, D, D])
    ffn_w1 = din("ffn_w1", [4, D, 4096])
    ffn_w2 = din("ffn_w2", [4, 4096, D])
    out = nc.dram_tensor("out", [S_LEN, D], F32, kind="ExternalOutput").ap()
    scrA = nc.dram_tensor("scrA", [S_LEN, D], F32, kind="Internal").ap()
    scrB = nc.dram_tensor("scrB", [S_LEN, D], F32, kind="Internal").ap()

    S = Sched(nslots=16)
    st = contextlib.ExitStack()
    with st:
        def sb(name, shape, dt):
            return st.enter_context(nc.sbuf_tensor(name, shape, dt))

        WBYTES = DBG.get('WBYTES', 126976)
        ABYTES = 77824
        wraw = sb("wraw", [128, WBYTES], U8)
        araw = sb("araw", [128, ABYTES], U8)
        WR = Region(wraw, WBYTES)
        AR = Region(araw, ABYTES)
        ident = sb("ident", [128, 128], BF16)
        identf = sb("identf", [128, 128], F32)
        ones_bf = sb("ones_bf", [128, 128], BF16)
        ones_f = sb("ones_f", [128, 128], F32)
        neghalf = sb("neghalf", [128, 1], F32)
        poshalf = sb("poshalf", [128, 1], F32)
        expbias = sb("expbias", [128, 1], F32)
        qbias = sb("qbias", [128, 1], F32)
        epsb = sb("epsb", [128, 1], F32)
        umask = sb("umask", [2, 128], BF16)
        wm0 = sb("wm0", [2, 256], BF16)
        wm1 = sb("wm1", [2, 256], BF16)
        banks = [st.enter_context(nc.psum_tensor(f"bank{i}", [128, 512], F32)) for i in range(8)]

        def bank_bf(i):
            return banks[i][:, :].bitcast(BF16).rearrange("p (k t) -> p k t", k=8)

        S.add('pool', ('memset', A(identf[:], 1.0)), writes=['identf'])
        S.add('pool', ('affine_select', A(out=identf[:], in_=identf[:], pattern=[[-1, 128]],
                                                compare_op=ALU.is_equal, fill=0.0, base=0,
                                                channel_multiplier=1)), reads=['identf'], writes=['identf'])
        S.add('pool', ('tensor_copy', A(out=ident[:], in_=identf[:])), reads=['identf'], writes=['ident'])
        S.add('pool', ('memset', A(ones_bf[:], 1.0)), writes=['ones_bf'])
        S.add('pool', ('memset', A(ones_f[:], 1.0)), writes=['ones_f'])
        S.add('pool', ('memset', A(neghalf[:], -0.5)), writes=['neghalf'])
        S.add('pool', ('memset', A(poshalf[:], 0.5)), writes=['poshalf'])
        S.add('pool', ('memset', A(expbias[:], 0.0)), writes=['expbias'])
        S.add('pool', ('memset', A(qbias[:], 0.25)), writes=['qbias'])
        S.add('pool', ('memset', A(epsb[:], EPS)), writes=['epsb'])
        S.add('pool', ('memset', A(umask[:], 1.0)), writes=['umask'])
        S.add('pool', ('affine_select', A(out=umask[:], in_=umask[:], pattern=[[1, 128]], compare_op=ALU.is_ge,
                                          fill=0.0, base=0, channel_multiplier=-64)), writes=['umask'])
        S.add('pool', ('memset', A(wm0[:], -30000.0)), writes=['wm0'])
        S.add('pool', ('affine_select', A(out=wm0[:], in_=wm0[:], pattern=[[-1, 256]], compare_op=ALU.is_ge,
                                          fill=0.0, base=-1, channel_multiplier=64)), writes=['wm0'])
        S.add('pool', ('memset', A(wm1[:], -30000.0)), writes=['wm1'])
        S.add('pool', ('affine_select', A(out=wm1[:], in_=wm1[:], pattern=[[-1, 256]], compare_op=ALU.is_ge,
                                          fill=0.0, base=127, channel_multiplier=64)), writes=['wm1'])
        S.add('pool', ('affine_select', A(out=wm1[:], in_=wm1[:], pattern=[[1, 256]], compare_op=ALU.is_ge,
                                          fill=0.0, base=0, channel_multiplier=-128)), writes=['wm1'])
        S.add('pool', ('memset', A(expbias[64:128, :], -30000.0)), reads=['expbias'], writes=['expbias'])
        CONST_KEYS = ['ident', 'ones_bf', 'ones_f', 'neghalf', 'poshalf', 'expbias']

        def after_fence_consts():
            pass

        def load_w(dst3, src2, name, kc_n, cols):
            src3 = src2.rearrange("(kc p) n -> p kc n", p=128)
            cstep = cols
            while cstep > 2048:
                cstep //= 2
            kstep = max(1, 2048 // cstep)
            while kc_n % kstep:
                kstep -= 1
            keys = []
            for k0 in range(0, kc_n, kstep):
                for c0 in range(0, cols, cstep):
                    key = f"{name}_{k0}_{c0}"
                    keys.append(key)
                    S.add('pool', ('dma_start', A(
                        out=dst3[:, k0:k0 + kstep, c0:c0 + cstep],
                        in_=src3[:, k0:k0 + kstep, c0:c0 + cstep])),
                        writes=[key], dma=1)
            return keys

        def join(keys, name, dummy):
            S.add('pool', ('memset', A(dummy, 0.0)), reads=keys, writes=[name])

        def load_bcast(dst, src1d, name):
            S.add('sp', ('dma_start', A(out=dst, in_=src1d.partition_broadcast(128))),
                  writes=[name], dma=1)

        def load_x(xt3, src, blk, key):
            S.add('sp', ('dma_start', A(
                out=xt3, in_=src[blk * TB:(blk + 1) * TB, :].rearrange("(s p) d -> p s d", p=128))),
                writes=[key], dma=1)

        def store_x(dst, xo3, blk, key):
            return S.add('sp', ('dma_start', A(
                out=dst[blk * TB:(blk + 1) * TB, :].rearrange("(s p) d -> p s d", p=128), in_=xo3)),
                reads=[key], dma=1)

        def rms_norm(x2, gbc, gkey, h2, xkey, hkey, tmp, tkey):
            stt = tmp[:, 0:12]
            mv = tmp[:, 12:14]
            ms = tmp[:, 14:15]
            rs = tmp[:, 15:16]
            S.add('dve', ('bn_stats', A(out=stt[:, 0:6], in_=x2[:, 0:512])), reads=[xkey], writes=[tkey + 'a'])
            S.add('dve', ('bn_stats', A(out=stt[:, 6:12], in_=x2[:, 512:1024])), reads=[xkey], writes=[tkey + 'b'])
            S.add('dve', ('bn_aggr', A(out=mv, in_=stt)), reads=[tkey + 'a', tkey + 'b'], writes=[tkey + 'mv'])
            S.add('dve', ('tensor_scalar', A(out=ms, in0=mv[:, 0:1], scalar1=mv[:, 0:1], scalar2=mv[:, 1:2],
                                                   op0=ALU.mult, op1=ALU.add)), reads=[tkey + 'mv'], writes=[tkey + 'ms'])
            S.add('dve', ('tensor_scalar', A(out=ms, in0=ms, scalar1=EPS, scalar2=None, op0=ALU.add)),
                  reads=[tkey + 'ms'], writes=[tkey + 'ms'])
            S.add('pool', ('tensor_tensor', A(out=rs, in0=ms, in1=neghalf[:, 0:1], op=ALU.pow)),
                  reads=[tkey + 'ms', 'neghalf'], writes=[tkey + 'rs'])
            S.add('dve', ('scalar_tensor_tensor', A(out=h2, in0=x2, scalar=rs, in1=gbc, op0=ALU.mult, op1=ALU.mult)),
                  reads=[xkey, tkey + 'rs', gkey], writes=[hkey])

        def transpose8(h2, hkey, hT3, col0, hTkey, bank_i, evac_eng):
            tp = bank_bf(bank_i)
            bkey = f"bank{bank_i}"
            for kc in range(8):
                S.add('pe', ('transpose', A(out=tp[:, kc, :], in_=h2[:, kc * 128:(kc + 1) * 128],
                                                         identity=ident[:])),
                      reads=[hkey, 'ident'], writes=[bkey])
            if evac_eng == 'act':
                S.add('act', ('copy', A(out=hT3[:, :, col0:col0 + 128], in_=tp)), writes=[bkey, hTkey])
            else:
                S.add('dve', ('tensor_copy', A(out=hT3[:, :, col0:col0 + 128], in_=tp)), writes=[bkey, hTkey])

        final_ops = []

        def BK(i):
            return f'bank{i}'

        def norm_stats(x2, xkeys, t, tk):
            stt = t[:, 0:12]
            mv = t[:, 12:14]
            ms = t[:, 14:15]
            rs = t[:, 15:16]
            S.add('dve', ('bn_stats', A(out=stt[:, 0:6], in_=x2[:, 0:512])), reads=xkeys, writes=[tk + 'a'])
            S.add('dve', ('bn_stats', A(out=stt[:, 6:12], in_=x2[:, 512:1024])), reads=xkeys, writes=[tk + 'b'])
            S.add('dve', ('bn_aggr', A(out=mv, in_=stt)), reads=[tk + 'a', tk + 'b'], writes=[tk + 'mv'])
            S.add('dve', ('tensor_scalar', A(out=ms, in0=mv[:, 0:1], scalar1=mv[:, 0:1], scalar2=mv[:, 1:2],
                                                   op0=ALU.mult, op1=ALU.add)), reads=[tk + 'mv'], writes=[tk + 'ms'])
            S.add('dve', ('tensor_scalar', A(out=ms, in0=ms, scalar1=EPS, scalar2=None, op0=ALU.add)),
                  reads=[tk + 'ms'], writes=[tk + 'ms'])
            S.add('pool', ('tensor_tensor', A(out=rs, in0=ms, in1=neghalf[:, 0:1], op=ALU.pow)),
                  reads=[tk + 'ms', 'neghalf'], writes=[tk + 'rs'])
            return rs

        def ffn_pass(layer, src, dst, apply_final):
            S.fence()
            WR.reset()
            AR.reset()
            W1 = WR.alloc(8 * 4096, BF16).rearrange("p (k f) -> p k f", k=8)
            W2a = WR.alloc(30 * 1024, BF16).rearrange("p (k f) -> p k f", k=30)
            W2b = AR.alloc(2 * 1024, BF16).rearrange("p (k f) -> p k f", k=2)
            W2v = [W2a[:, fc, :] if fc < 30 else W2b[:, fc - 30, :] for fc in range(32)]
            gbc = AR.alloc(1024, F32)
            gfin = AR.alloc(1024, F32) if apply_final else None
            dummy = AR.alloc(16, F32)
            xts = [AR.alloc(2048, F32).rearrange("p (s d) -> p s d", s=2) for _ in range(2)]
            xos = [AR.alloc(2048, F32).rearrange("p (s d) -> p s d", s=2) for _ in range(2)]
            hb = [AR.alloc(1024, BF16) for _ in range(2)]
            hT = [AR.alloc(8 * TB, BF16).rearrange("p (k t) -> p k t", k=8) for _ in range(2)]
            uT = AR.alloc(32 * TB, BF16).rearrange("p (k t) -> p k t", k=32)
            rt = [AR.alloc(TB, F32) for _ in range(2)]
            tmp = [AR.alloc(16, F32) for _ in range(2)]

            load_bcast(gbc, norm_ffn[layer], 'gbc')
            if apply_final:
                load_bcast(gfin, norm_final, 'gfin')
            load_x(xts[0], src, 0, 'xt0')
            k1 = load_w(W1, ffn_w1[layer], 'W1', 8, 4096)

            def w_rest():
                for dh in range(2):
                    cs = slice(dh * 512, (dh + 1) * 512)
                    k2 = load_w(W2a[:, :, cs], ffn_w2[layer][0:30 * 128, cs], f'W2a{dh}', 30, 512)
                    k2 += load_w(W2b[:, :, cs], ffn_w2[layer][30 * 128:32 * 128, cs], f'W2b{dh}', 2, 512)
                    if dh == 0:
                        join(k1, 'W1', dummy[:, 0:4])
                    join(k2, f'W2h{dh}', dummy[:, 4 + 2 * dh:6 + 2 * dh])

            def f_norm(b):
                p = b % 2
                for s in range(2):
                    rms_norm(xts[p][:, s, :], gbc, 'gbc', hb[s], f'xt{p}', f'hb{s}', tmp[s], f'tmp{s}')

            def f_tr(b):
                p = b % 2
                for s in range(2):
                    transpose8(hb[s], f'hb{s}', hT[p], s * 128, f'hT{p}', s, 'act')

            def f_w1(b):
                p = b % 2
                for fc in range(32):
                    bi = 2 + fc % 4
                    ups = banks[bi][:, 0:TB]
                    for kc in range(8):
                        S.add('pe', ('matmul', A(
                            out=ups, lhsT=W1[:, kc, fc * 128:(fc + 1) * 128], rhs=hT[p][:, kc, :],
                            start=(kc == 0), stop=(kc == 7))),
                            reads=['W1', f'hT{p}'], writes=[BK(bi)])
                    r = rt[fc % 2]
                    S.add('act', ('activation', A(out=r, in_=ups, func=AF.Relu)),
                          writes=[BK(bi), f'rt{fc % 2}'])
                    S.add('dve', ('tensor_tensor', A(out=uT[:, fc, :], in0=r, in1=r, op=ALU.mult)),
                          reads=[f'rt{fc % 2}'], writes=[f'uT{fc}'])

            def f_w2(b, gi):
                p = b % 2
                xt = xts[p]
                xo = xos[p]
                dh, ts = gi // 2, gi % 2
                bi = 6 + gi % 2
                acc = banks[bi]
                for fc in range(32):
                    S.add('pe', ('matmul', A(
                        out=acc[:, :], lhsT=uT[:, fc, ts * 128:(ts + 1) * 128],
                        rhs=W2v[fc][:, dh * 512:(dh + 1) * 512], start=(fc == 0), stop=(fc == 31))),
                        reads=[f'W2h{dh}', f'uT{fc}'], writes=[BK(bi)])
                S.add('dve', ('tensor_tensor', A(
                    out=xo[:, ts, dh * 512:(dh + 1) * 512], in0=acc[:, :], in1=xt[:, ts, dh * 512:(dh + 1) * 512],
                    op=ALU.add)), reads=[f'xt{p}'], writes=[BK(bi), f'xo{p}_{ts}'])

            def f_store(b):
                p = b % 2
                xo = xos[p]
                okeys = [f'xo{p}_0', f'xo{p}_1']
                if apply_final:
                    for s in range(2):
                        x2 = xo[:, s, :]
                        rs = norm_stats(x2, [f'xo{p}_{s}'], tmp[s], f'tmp{s}')
                        S.add('dve', ('scalar_tensor_tensor', A(out=x2, in0=x2, scalar=rs, in1=gfin,
                                                               op0=ALU.mult, op1=ALU.mult)),
                              reads=[f'tmp{s}rs', 'gfin'], writes=[f'xo{p}_{s}'])
                op = S.add('sp', ('dma_start', A(
                    out=dst[b * TB:(b + 1) * TB, :].rearrange("(s p) d -> p s d", p=128), in_=xo)),
                    reads=okeys, dma=1)
                if apply_final:
                    final_ops.append(op)

            if nblk > 1:
                load_x(xts[1], src, 1, 'xt1')
            f_norm(0)
            f_tr(0)
            w_rest()
            for b in range(nblk):
                p = b % 2
                f_w1(b)
                if b + 1 < nblk:
                    f_norm(b + 1)
                f_w2(b, 0)
                f_w2(b, 1)
                if b + 1 < nblk:
                    f_tr(b + 1)
                f_w2(b, 2)
                f_w2(b, 3)
                f_store(b)
                if b + 2 < nblk:
                    load_x(xts[p], src, b + 2, f'xt{p}')

        def odd_pass(layer, src, dst):
            o = layer // 2
            S.fence()
            WR.reset()
            AR.reset()
            Wci = WR.alloc(8 * 2048, BF16).rearrange("p (k f) -> p k f", k=8)
            Wco = WR.alloc(8 * 1024, BF16).rearrange("p (k f) -> p k f", k=8)
            WsT = WR.alloc(8 * 128, BF16).rearrange("p (g t) -> p g t", g=8)
            lng = WR.alloc(1024, F32)
            lnb = WR.alloc(1024, F32)
            gbc = WR.alloc(1024, F32)
            bsT = WR.alloc(8, F32)
            dummy = WR.alloc(16, F32)
            NZ = 3
            NX = 4
            zu = [WR.alloc(1024, F32) for _ in range(NZ)]
            zv = [WR.alloc(1024, F32) for _ in range(NZ)]
            vt = [WR.alloc(1024, F32) for _ in range(2)]
            xts = [AR.alloc(2048, F32).rearrange("p (s d) -> p s d", s=2) for _ in range(NX)]
            xos = [AR.alloc(2048, F32).rearrange("p (s d) -> p s d", s=2) for _ in range(2)]
            hb = [AR.alloc(1024, BF16) for _ in range(2)]
            hT = [AR.alloc(8 * 128, BF16).rearrange("p (k t) -> p k t", k=8) for _ in range(2)]
            vln = [AR.alloc(1024, BF16) for _ in range(2)]
            sg = [AR.alloc(1024, BF16) for _ in range(2)]
            sT = [AR.alloc(8 * 128, BF16).rearrange("p (k t) -> p k t", k=8) for _ in range(2)]
            tmp = [AR.alloc(16, F32) for _ in range(2)]
            tmp2 = [AR.alloc(16, F32) for _ in range(2)]

            load_bcast(gbc, norm_mix[layer], 'gbc')
            load_bcast(lng, c_ln_g[o], 'lng')
            load_bcast(lnb, c_ln_b[o], 'lnb')
            S.add('sp', ('dma_start', A(out=bsT, in_=c_b_sT[o])), writes=['bsT'], dma=1)
            load_x(xts[0], src, 0, 'xt0')
            k1 = load_w(Wci, c_w_in[o], 'Wci', 8, 2048)

            def w_rest():
                k2 = load_w(Wco, c_w_out[o], 'Wco', 8, 1024)
                S.add('pool', ('dma_start', A(out=WsT, in_=c_w_sT[o])), writes=['WsT_raw'], dma=1)
                join(k1, 'Wci', dummy[:, 0:4])
                join(k2, 'Wco', dummy[:, 4:8])
                S.add('pool', ('memset', A(WsT[64:128, :, 0:64], 0.0)), reads=['WsT_raw'], writes=['WsT'])

            def st_norm(g):
                b, s = g // 2, g % 2
                q = g % 2
                rms_norm(xts[b % NX][:, s, :], gbc, 'gbc', hb[q], f'xt{b % NX}', f'hb{q}', tmp[q], f'tmp{q}')

            def st_tr(g):
                q = g % 2
                transpose8(hb[q], f'hb{q}', hT[q], 0, f'hT{q}', 0, 'act')

            def st_z(g):
                q = g % 2
                z = g % NZ
                for cg in range(4):
                    bi = 2 + cg % 2
                    for kc in range(8):
                        S.add('pe', ('matmul', A(
                            out=banks[bi][:, :], lhsT=hT[q][:, kc, :], rhs=Wci[:, kc, cg * 512:(cg + 1) * 512],
                            start=(kc == 0), stop=(kc == 7))), reads=['Wci', f'hT{q}'], writes=[BK(bi)])
                    dstz = (zu[z] if cg < 2 else zv[z])[:, (cg % 2) * 512:(cg % 2 + 1) * 512]
                    S.add('act', ('activation', A(out=dstz, in_=banks[bi][:, :], func=AF.Gelu_apprx_tanh)),
                          writes=[BK(bi), f'z{z}_{cg}'])

            def st_ln(g):
                q = g % 2
                z = g % NZ
                t2 = tmp2[q]
                stt = t2[:, 0:12]
                mv = t2[:, 12:14]
                ve = t2[:, 14:15]
                rs = t2[:, 15:16]
                zvs = zv[z]
                S.add('dve', ('bn_stats', A(out=stt[:, 0:6], in_=zvs[:, 0:512])), reads=[f'z{z}_2'], writes=[f't2{q}a'])
                S.add('dve', ('bn_stats', A(out=stt[:, 6:12], in_=zvs[:, 512:1024])), reads=[f'z{z}_3'], writes=[f't2{q}b'])
                S.add('dve', ('bn_aggr', A(out=mv, in_=stt)), reads=[f't2{q}a', f't2{q}b'], writes=[f't2{q}mv'])
                S.add('dve', ('tensor_scalar', A(out=ve, in0=mv[:, 1:2], scalar1=EPS, scalar2=None, op0=ALU.add)),
                      reads=[f't2{q}mv'], writes=[f't2{q}ve'])
                S.add('pool', ('tensor_tensor', A(out=rs, in0=ve, in1=neghalf[:, 0:1], op=ALU.pow)),
                      reads=[f't2{q}ve', 'neghalf'], writes=[f't2{q}rs'])
                vts = vt[q]
                S.add('dve', ('tensor_scalar', A(out=vts, in0=zvs, scalar1=mv[:, 0:1], scalar2=rs,
                                                 op0=ALU.subtract, op1=ALU.mult)),
                      reads=[f'z{z}_2', f'z{z}_3', f't2{q}mv', f't2{q}rs'], writes=[f'vt{q}'])
                S.add('pool', ('tensor_tensor', A(out=vts, in0=vts, in1=lng, op=ALU.mult)), reads=['lng'], writes=[f'vt{q}'])
                S.add('pool', ('tensor_tensor', A(out=vln[q], in0=vts, in1=lnb, op=ALU.add)), reads=[f'vt{q}', 'lnb'],
                      writes=[f'vln{q}'])

            def st_sp_pe(g):
                q = g % 2
                vl = vln[q]
                for gg_ in range(8):
                    bi = 4 + gg_ // 4
                    mps = banks[bi][:, (gg_ % 4) * 128:(gg_ % 4 + 1) * 128]
                    S.add('pe', ('matmul', A(out=mps, lhsT=WsT[:, gg_, :], rhs=vl[:, gg_ * 128:(gg_ + 1) * 128],
                                             start=True, stop=True)), reads=['WsT', f'vln{q}'], writes=[BK(bi)])

            def st_sp_dve(g):
                q = g % 2
                z = g % NZ
                sgs = sg[q]
                zus = zu[z]
                for gg_ in range(8):
                    bi = 4 + gg_ // 4
                    mps = banks[bi][:, (gg_ % 4) * 128:(gg_ % 4 + 1) * 128]
                    S.add('dve', ('scalar_tensor_tensor', A(
                        out=sgs[:, gg_ * 128:(gg_ + 1) * 128], in0=mps, scalar=bsT[:, gg_:gg_ + 1],
                        in1=zus[:, gg_ * 128:(gg_ + 1) * 128], op0=ALU.add, op1=ALU.mult)),
                        reads=['bsT', f'z{z}_{gg_ // 4}'], writes=[BK(bi), f'sg{q}_{gg_}'])

            def st_str(g):
                q = g % 2
                sgs = sg[q]
                tp = bank_bf(1)
                for kc in range(8):
                    S.add('pe', ('transpose', A(out=tp[:, kc, :], in_=sgs[:, kc * 128:(kc + 1) * 128], identity=ident[:])),
                          reads=[f'sg{q}_{kc}', 'ident'], writes=[BK(1)])
                S.add('act', ('copy', A(out=sT[q], in_=tp)), writes=[BK(1), f'sT{q}'])

            def st_out(g):
                b, s = g // 2, g % 2
                q = g % 2
                p = b % 2
                xt = xts[b % NX]
                xo = xos[p]
                for dh in range(2):
                    bi = 6 + dh
                    for kc in range(8):
                        S.add('pe', ('matmul', A(
                            out=banks[bi][:, :], lhsT=sT[q][:, kc, :], rhs=Wco[:, kc, dh * 512:(dh + 1) * 512],
                            start=(kc == 0), stop=(kc == 7))), reads=['Wco', f'sT{q}'], writes=[BK(bi)])
                    S.add('dve', ('tensor_tensor', A(
                        out=xo[:, s, dh * 512:(dh + 1) * 512], in0=banks[bi][:, :],
                        in1=xt[:, s, dh * 512:(dh + 1) * 512], op=ALU.add)),
                        reads=[f'xt{b % NX}'], writes=[BK(bi), f'xo{p}_{s}'])
                if s == 1:
                    S.add('sp', ('dma_start', A(
                        out=dst[b * TB:(b + 1) * TB, :].rearrange("(s p) d -> p s d", p=128), in_=xo)),
                        reads=[f'xo{p}_0', f'xo{p}_1'], dma=1)
                    if b + NX < nblk:
                        load_x(xts[b % NX], src, b + NX, f'xt{b % NX}')

            nsub = 2 * nblk
            for bb in range(1, min(NX, nblk)):
                load_x(xts[bb], src, bb, f'xt{bb}')
            st_norm(0)
            w_rest()
            for t in range(nsub + 3):
                if 0 <= t - 2 < nsub:
                    st_sp_pe(t - 2)
                if t + 1 < nsub:
                    st_norm(t + 1)
                if t < nsub:
                    st_tr(t)
                if 0 <= t - 3 < nsub:
                    st_out(t - 3)
                if t < nsub:
                    st_z(t)
                if 0 <= t - 1 < nsub:
                    st_ln(t - 1)
                if 0 <= t - 2 < nsub:
                    st_sp_dve(t - 2)
                    st_str(t - 2)

        def even_pass(layer, src, dst):
            ev = layer // 2
            lam_init = 0.8 - 0.6 * math.exp(-0.3 * layer)
            NCH = S_LEN // 128
            S.fence()
            WR.reset()
            AR.reset()
            Win = WR.alloc(8 * 2560, BF16).rearrange("p (k f) -> p k f", k=8)
            Wout = WR.alloc(8 * 1024, BF16).rearrange("p (k f) -> p k f", k=8)
            kT = WR.alloc(4 * S_LEN, BF16).rearrange("p (h t) -> p h t", h=4)
            V = WR.alloc(NCH * 512, BF16).rearrange("p (c v) -> p c v", c=NCH)
            WA = WR.alloc(4 * 128, BF16).rearrange("p (c j) -> p c j", c=4)
            WX = WR.alloc(4 * 128, BF16).rearrange("p (c j) -> p c j", c=4)
            gbc = AR.alloc(1024, F32)
            pv = AR.alloc(33, F32)
            ldl = WR.alloc(256, F32)
            sm = AR.alloc(64, F32)
            dummy = AR.alloc(16, F32)
            ltmp = AR.alloc(64, F32)
            hprev = AR.alloc(4, F32)
            tmp = [AR.alloc(16, F32) for _ in range(2)]
            xts = [AR.alloc(2048, F32).rearrange("p (s d) -> p s d", s=2) for _ in range(2)]
            hb = [AR.alloc(1024, BF16) for _ in range(2)]
            hT = AR.alloc(8 * TB, BF16).rearrange("p (k t) -> p k t", k=8)
            qc = AR.alloc(4 * 2 * TB, BF16).rearrange("p (h t) -> p h t", h=4)
            xbh = AR.alloc(4 * (TB + 3), F32).rearrange("p (c t) -> p c t", c=4)
            gg = AR.alloc(4 * TB, BF16).rearrange("p (c t) -> p c t", c=4)
            xc = AR.alloc(4 * TB, F32).rearrange("p (c t) -> p c t", c=4)
            xcb = hb[1].rearrange("p (c t) -> p c t", c=4)
            Tr = AR.alloc(4 * TB, F32).rearrange("p (c t) -> p c t", c=4)
            Ti = AR.alloc(4 * TB, F32).rearrange("p (c t) -> p c t", c=4)
            aa = AR.alloc(4 * TB, F32).rearrange("p (c t) -> p c t", c=4)
            a2 = AR.alloc(4 * TB, F32).rearrange("p (c t) -> p c t", c=4)
            bt = Ti
            hs = Tr
            yT = AR.alloc(8 * TB, BF16).rearrange("p (k t) -> p k t", k=8)
            NPT = 4
            pT = [AR.alloc(2 * TB, BF16) for _ in range(NPT - 1)] + [WR.alloc(2 * TB, BF16)]
            fr = AR.alloc(2 * TB, F32)
            fo4 = AR.alloc(4 * TB, F32).rearrange("p (h t) -> p h t", h=4)
            sd = AR.alloc(4 * TB, F32).rearrange("p (h t) -> p h t", h=4)

            s1 = sm[:, 0:1]
            s2 = sm[:, 1:2]
            e1 = sm[:, 2:3]
            e2 = sm[:, 3:4]
            neglam = sm[:, 4:5]
            g1 = sm[:, 5:6]
            yv = sm[:, 8:12]
            tv = sm[:, 12:16]
            sc = sm[:, 16:20]
            sch = sm[:, 20:24]
            hba = sm[:, 24:28]
            hbx = sm[:, 28:32]
            cw = lambda w, c: pv[:, w * 4 + c:w * 4 + c + 1]
            cb = lambda c: pv[:, 16 + c:17 + c]

            load_bcast(gbc, norm_mix[layer], 'gbc')
            load_bcast(ldl, diff_l[ev], 'ldl')
            S.add('sp', ('dma_start', A(out=pv, in_=pvec[ev])), writes=['pv'], dma=1)
            k1 = load_w(Win, ab_w_in[ev], 'Win', 8, 2560)
            def w_rest():
                k2 = load_w(Wout, ab_w_out[ev], 'Wout', 8, 1024)
                join(k1, 'Win', dummy[:, 0:4])
                join(k2, 'Wout', dummy[:, 4:8])
                S.add('pool', ('memset', A(WA, 0.0)), writes=['WA0'])
                S.add('pool', ('memset', A(WX, 0.0)), writes=['WX0'])
                for nm, wt, srcw in (('WA', WA, lru_wa), ('WX', WX, lru_wx)):
                    sv = srcw[ev].rearrange("(c j) i o -> j i c o", j=2)
                    wk = []
                    for j in range(2):
                        key = f'{nm}_{j}'
                        wk.append(key)
                        S.add('pool', ('dma_start', A(
                            out=wt[j * 64:(j + 1) * 64, :, j * 64:(j + 1) * 64], in_=sv[j])),
                            reads=[nm + '0'], writes=[key], dma=1)
                    join(wk, nm, dummy[:, 8:10] if nm == 'WA' else dummy[:, 10:12])
                S.add('pool', ('memset', A(qc[64:128, :, 0:TB], 0.0)), writes=['qcz0'])
                S.add('pool', ('memset', A(qc[0:64, :, TB:2 * TB], 0.0)), writes=['qcz1'])
                S.add('dve', ('tensor_tensor', A(out=ltmp, in0=ldl[:, 0:64], in1=ldl[:, 64:128], op=ALU.mult)),
                      reads=['ldl'], writes=['ltmp'])
                S.add('dve', ('tensor_reduce', A(out=s1, in_=ltmp, axis=mybir.AxisListType.X, op=ALU.add)),
                      reads=['ltmp'], writes=['s1'])
                S.add('dve', ('tensor_tensor', A(out=ltmp, in0=ldl[:, 128:192], in1=ldl[:, 192:256], op=ALU.mult)),
                      reads=['ldl', 's1'], writes=['ltmp'])
                S.add('dve', ('tensor_reduce', A(out=s2, in_=ltmp, axis=mybir.AxisListType.X, op=ALU.add)),
                      reads=['ltmp'], writes=['s2'])
                S.add('act', ('activation', A(out=e1, in_=s1, func=AF.Exp)), reads=['s1'], writes=['e1'])
                S.add('act', ('activation', A(out=e2, in_=s2, func=AF.Exp)), reads=['s2'], writes=['e2'])
                S.add('dve', ('tensor_tensor', A(out=neglam, in0=e2, in1=e1, op=ALU.subtract)), reads=['e1', 'e2'],
                      writes=['neglam'])
                S.add('dve', ('tensor_scalar', A(out=neglam, in0=neglam, scalar1=-lam_init, scalar2=None, op0=ALU.add)),
                      reads=['neglam'], writes=['neglam'])
                S.add('dve', ('tensor_scalar', A(out=g1, in0=pv[:, 32:33], scalar1=(1.0 - lam_init), scalar2=None,
                                                       op0=ALU.mult)), reads=['pv'], writes=['g1'])
                S.add('act', ('activation', A(out=yv, in_=pv[:, 28:32], func=AF.Exp, scale=-1.0)), reads=['pv'], writes=['yv'])
                S.add('dve', ('tensor_scalar', A(out=tv, in0=yv, scalar1=-0.25, scalar2=1.0 / 3.0, op0=ALU.mult, op1=ALU.add)),
                      reads=['yv'], writes=['tv'])
                S.add('dve', ('tensor_tensor', A(out=tv, in0=tv, in1=yv, op=ALU.mult)), reads=['tv', 'yv'], writes=['tv'])
                S.add('dve', ('tensor_scalar', A(out=tv, in0=tv, scalar1=-1.0, scalar2=0.5, op0=ALU.mult, op1=ALU.add)),
                      reads=['tv'], writes=['tv'])
                S.add('dve', ('tensor_tensor', A(out=tv, in0=tv, in1=yv, op=ALU.mult)), reads=['tv', 'yv'], writes=['tv'])
                S.add('dve', ('tensor_scalar', A(out=tv, in0=tv, scalar1=-1.0, scalar2=1.0, op0=ALU.mult, op1=ALU.add)),
                      reads=['tv'], writes=['tv'])
                S.add('dve', ('tensor_tensor', A(out=tv, in0=tv, in1=yv, op=ALU.mult)), reads=['tv', 'yv'], writes=['tv'])
                S.add('dve', ('tensor_scalar', A(out=sc, in0=tv, scalar1=-8.0, scalar2=None, op0=ALU.mult)),
                      reads=['tv'], writes=['sc'])
                S.add('dve', ('tensor_scalar', A(out=sch, in0=tv, scalar1=-4.0, scalar2=None, op0=ALU.mult)),
                      reads=['tv'], writes=['sch'])
                S.add('dve', ('tensor_scalar', A(out=hba, in0=pv[:, 20:24], scalar1=0.5, scalar2=None, op0=ALU.mult)),
                      reads=['pv'], writes=['hba'])
                S.add('dve', ('tensor_scalar', A(out=hbx, in0=pv[:, 24:28], scalar1=0.5, scalar2=None, op0=ALU.mult)),
                      reads=['pv'], writes=['hbx'])
                S.add('pool', ('memset', A(xbh[:, :, 0:3], 0.0)), writes=['xbh_halo'])
                S.add('pool', ('memset', A(hprev, 0.0)), writes=['hprev'])


            gcount = [0]

            def gbank():
                bi = 1 + gcount[0] % 3
                gcount[0] += 1
                return bi

            pcount = [0]

            def e_load(b):
                p = b % 2
                S.add('sp', ('dma_start', A(
                    out=xts[p], in_=src[b * TB:(b + 1) * TB, :].rearrange("(s p) d -> p s d", p=128))),
                    writes=[f'xt{p}'], dma=1)

            def e_norm(b):
                p = b % 2
                for s in range(2):
                    rms_norm(xts[p][:, s, :], gbc, 'gbc', hb[s], f'xt{p}', f'hb{s}', tmp[s], f'tmp{s}')

            def e_tr(b):
                for s in range(2):
                    transpose8(hb[s], f'hb{s}', hT, s * 128, 'hT', 0, 'act')

            def proj_fm(col0, evac):
                bi = gbank()
                ps_ = banks[bi][:, 0:TB]
                for kc in range(8):
                    S.add('pe', ('matmul', A(out=ps_, lhsT=Win[:, kc, col0:col0 + 128],
                                             rhs=hT[:, kc, :], start=(kc == 0), stop=(kc == 7))),
                          reads=['Win', 'hT'], writes=[BK(bi)])
                evac(ps_, BK(bi))

            def evac_q(ps_, bk, h):
                S.add('act', ('copy', A(out=qc[0:64, h, 0:TB], in_=ps_[0:64, :])),
                      reads=['qcz0', 'qcz1'], writes=[bk, f'qc{h}'])
                S.add('act', ('copy', A(out=qc[64:128, h, TB:2 * TB], in_=ps_[64:128, :])),
                      writes=[bk, f'qc{h}'])

            def e_inproj(b):
                for c in range(4):
                    proj_fm(1536 + c * 128, lambda ps_, bk, c=c: S.add(
                        'act', ('copy', A(out=xbh[:, c, 3:3 + TB], in_=ps_)), reads=['xbh_halo'],
                        writes=[bk, f'xbh{c}']))
                for c in range(4):
                    proj_fm(2048 + c * 128, lambda ps_, bk, c=c: S.add(
                        'act', ('activation', A(out=gg[:, c, :], in_=ps_, func=AF.Gelu_apprx_tanh)),
                        writes=[bk, f'gg{c}']))
                e_lru1(b)
                for h in range(4):
                    proj_fm(h * 128, lambda ps_, bk, h=h: evac_q(ps_, bk, h))
                for h in range(4):
                    proj_fm(512 + h * 128, lambda ps_, bk, h=h: S.add(
                        'act', ('copy', A(out=kT[:, h, b * TB:(b + 1) * TB], in_=ps_)),
                        writes=[bk, f'kT{h}_{b}']))
                for ts in range(2):
                    bi = gbank()
                    for kc in range(8):
                        S.add('pe', ('matmul', A(
                            out=banks[bi][:, :], lhsT=hT[:, kc, ts * 128:(ts + 1) * 128],
                            rhs=Win[:, kc, 1024:1536], start=(kc == 0), stop=(kc == 7))),
                            reads=['Win', 'hT'], writes=[BK(bi)])
                    S.add('act', ('copy', A(out=V[:, b * 2 + ts, :], in_=banks[bi][:, :])),
                          writes=[BK(bi), f'V{b * 2 + ts}'])

            def e_lru1(b):
                for c in range(4):
                    S.add('dve', ('tensor_scalar', A(out=xc[:, c, :], in0=xbh[:, c, 0:TB], scalar1=cw(0, c),
                                                     scalar2=cb(c), op0=ALU.mult, op1=ALU.add)),
                          reads=[f'xbh{c}', 'xbh_halo', 'pv'], writes=[f'xc{c}'])
                    for w in range(1, 4):
                        S.add('dve', ('scalar_tensor_tensor', A(
                            out=xc[:, c, :], in0=xbh[:, c, w:w + TB], scalar=cw(w, c), in1=xc[:, c, :],
                            op0=ALU.mult, op1=ALU.add)), reads=[f'xbh{c}', 'xbh_halo', 'pv'], writes=[f'xc{c}'])
                    S.add('pool', ('tensor_copy', A(out=xcb[:, c, :], in_=xc[:, c, :])), reads=[f'xc{c}'],
                          writes=[f'xcb{c}', 'hb1'])
                S.add('pool', ('tensor_copy', A(out=xbh[:, :, 0:3], in_=xbh[:, :, TB:TB + 3])),
                      reads=[f'xbh{c}' for c in range(4)] + [f'xc{c}' for c in range(4)], writes=['xbh_halo'])

            def lru_gate(c, nm):
                wt, Tt, hbias = (WA, Tr, hba) if nm == 'r' else (WX, Ti, hbx)
                bi = 0
                ps_ = banks[bi][:, 0:TB]
                S.add('pe', ('matmul', A(out=ps_, lhsT=wt[:, c, :], rhs=xcb[:, c, :], start=True, stop=True)),
                      reads=['WA' if nm == 'r' else 'WX', f'xcb{c}', 'hb1'], writes=[BK(bi)])
                S.add('act', ('activation', A(
                    out=Tt[:, c, :], in_=ps_, func=AF.Tanh, bias=hbias[:, c:c + 1], scale=0.5)),
                    reads=['hba', 'hbx'],
                    writes=[BK(bi), f'T{nm}{c}'] + ([f'hs{c}'] if nm == 'r' else [f'bt{c}']))

            def lru_exps():
                for c in range(4):
                    S.add('act', ('activation', A(out=aa[:, c, :], in_=Tr[:, c, :], func=AF.Exp,
                                                  bias=sch[:, c:c + 1], scale=sch[:, c:c + 1])),
                          reads=[f'Tr{c}', 'sch'], writes=[f'aa{c}'])
                    S.add('act', ('activation', A(out=a2[:, c, :], in_=Tr[:, c, :], func=AF.Exp,
                                                  bias=sc[:, c:c + 1], scale=sc[:, c:c + 1])),
                          reads=[f'Tr{c}', 'sc'], writes=[f'a2{c}'])

            def e_lru_tail(b):
                S.add('act', ('activation', A(out=a2.rearrange("p c t -> p (c t)"), in_=a2.rearrange("p c t -> p (c t)"),
                                              func=AF.Sqrt, bias=qbias[:, 0:1], scale=-0.25)),
                      reads=['qbias'], writes=[f'a2{c}' for c in range(4)])
                for c in range(4):
                    S.add('dve', ('scalar_tensor_tensor', A(out=bt[:, c, :], in0=Ti[:, c, :], scalar=1.0,
                                                           in1=xc[:, c, :], op0=ALU.add, op1=ALU.mult)),
                          reads=[f'Ti{c}', f'xc{c}'], writes=[f'bt{c}', f'Ti{c}'])
                    S.add('dve', ('tensor_tensor', A(out=bt[:, c, :], in0=bt[:, c, :], in1=a2[:, c, :], op=ALU.mult)),
                          reads=[f'a2{c}'], writes=[f'bt{c}'])
                    S.add('dve', ('tensor_tensor_scan', A(out=hs[:, c, :], data0=aa[:, c, :], data1=bt[:, c, :],
                                                         initial=hprev[:, c:c + 1], op0=ALU.mult, op1=ALU.add)),
                          reads=[f'aa{c}', f'bt{c}', 'hprev'], writes=[f'hs{c}', f'Tr{c}'])
                    S.add('dve', ('tensor_tensor', A(out=yT[:, 4 + c, :], in0=gg[:, c, :], in1=hs[:, c, :], op=ALU.mult)),
                          reads=[f'gg{c}', f'hs{c}'], writes=[f'yT{4 + c}'])
                S.add('pool', ('tensor_copy', A(out=hprev, in_=hs[:, :, TB - 1])),
                      reads=[f'hs{c}' for c in range(4)], writes=['hprev'])

            def e_tail_a(b):
                for h in range(4):
                    mb = 1 - h // 2
                    S.add('pe', ('matmul', A(out=banks[mb][:, (h % 2) * TB:(h % 2 + 1) * TB], lhsT=ones_f[:], rhs=sd[:, h, :],
                                             start=True, stop=True)),
                          reads=['ones_f', f'sd{h}'], writes=[BK(mb)])
                for hh in range(2):
                    mb = 1 - hh
                    S.add('act', ('activation', A(out=sd[:, 2 * hh:2 * hh + 2, :].rearrange("p h t -> p (h t)"),
                                                  in_=banks[mb][:, :], func=AF.Sqrt, bias=epsb[:, 0:1], scale=1.0 / 128.0)),
                          reads=['epsb'], writes=[BK(mb), f'sd{2 * hh}', f'sd{2 * hh + 1}'])
                S.add('dve', ('reciprocal', A(out=sd.rearrange("p h t -> p (h t)"), in_=sd.rearrange("p h t -> p (h t)"))),
                      writes=[f'sd{h}' for h in range(4)])
                S.add('dve', ('scalar_tensor_tensor', A(out=yT[:, 0:4, :].rearrange("p h t -> p (h t)"),
                                                       in0=fo4.rearrange("p h t -> p (h t)"), scalar=g1,
                                                       in1=sd.rearrange("p h t -> p (h t)"), op0=ALU.mult, op1=ALU.mult)),
                      reads=[f'fo{h}' for h in range(4)] + ['g1'] + [f'sd{h}' for h in range(4)],
                      writes=[f'yT{h}' for h in range(4)])

            def e_outproj(b, gi, bi):
                p = b % 2
                xt = xts[p]
                ts, dh = gi // 2, gi % 2
                for kc in range(8):
                    S.add('pe', ('matmul', A(
                        out=banks[bi][:, :], lhsT=yT[:, kc, ts * 128:(ts + 1) * 128],
                        rhs=Wout[:, kc, dh * 512:(dh + 1) * 512], start=(kc == 0), stop=(kc == 7))),
                        reads=['Wout', f'yT{kc}'], writes=[BK(bi)])
                S.add('dve', ('tensor_tensor', A(
                    out=xt[:, ts, dh * 512:(dh + 1) * 512], in0=banks[bi][:, :],
                    in1=xt[:, ts, dh * 512:(dh + 1) * 512], op=ALU.add)),
                    writes=[BK(bi), f'xt{p}'])

            def e_store(b):
                p = b % 2
                S.add('sp', ('dma_start', A(
                    out=dst[b * TB:(b + 1) * TB, :].rearrange("(s p) d -> p s d", p=128), in_=xts[p])),
                    reads=[f'xt{p}'], dma=1)
                if b + 2 < nblk:
                    e_load(b + 2)

            def e_att(b):
                nkc = 2 * b + 2
                steps = [(h, kc) for h in range(4) for kc in range(nkc)]
                nst = len(steps)
                LA = 3
                pts = {}

                def qk_exp(i):
                    h, kc = steps[i]
                    j = kc - 2 * b
                    si = 1 + i % 3
                    S.add('pe', ('matmul', A(
                        out=banks[si][:, :], lhsT=kT[:, h, kc * 128:(kc + 1) * 128], rhs=qc[:, h, :],
                        start=True, stop=(j < 0))), reads=[f'kT{h}_{kc // 2}', f'qc{h}'], writes=[BK(si)])
                    if j >= 0:
                        wm = wm0 if j == 0 else wm1
                        for m in range(2):
                            S.add('pe', ('matmul', A(
                                out=banks[si][:, m * TB:(m + 1) * TB], lhsT=umask[:, :], rhs=wm[:, :],
                                start=False, stop=(m == 1))), reads=['umask', 'wm0', 'wm1'], writes=[BK(si)])
                    pi = pcount[0] % NPT
                    pcount[0] += 1
                    pt = pT[pi]
                    pkey = f'pT{pi}'
                    pts[i] = (pt, pkey)
                    S.add('act', ('activation', A(out=pt, in_=banks[si][:, :], func=AF.Exp, scale=0.125)),
                          writes=[BK(si), pkey])

                def pv_sum(i):
                    h, kc = steps[i]
                    ob = 4 + h % 2
                    sb_ = 6 + h % 2
                    pt, pkey = pts.pop(i)
                    S.add('pe', ('matmul', A(
                        out=banks[ob][:, :], lhsT=V[:, kc, h * 128:(h + 1) * 128], rhs=pt,
                        start=(kc == 0), stop=(kc == nkc - 1))), reads=[f'V{kc}', pkey], writes=[BK(ob)])
                    S.add('pe', ('matmul', A(
                        out=banks[sb_][:, :], lhsT=ones_bf[:], rhs=pt,
                        start=(kc == 0), stop=(kc == nkc - 1))), reads=['ones_bf', pkey], writes=[BK(sb_)])
                    if kc == nkc - 1:
                        S.add('dve', ('reciprocal', A(out=fr, in_=banks[sb_][:, :])), writes=[BK(sb_), 'fr'])
                        S.add('dve', ('tensor_tensor', A(out=fr, in0=banks[ob][:, :], in1=fr, op=ALU.mult)),
                              writes=[BK(ob), 'fr'])
                        S.add('dve', ('scalar_tensor_tensor', A(out=fo4[:, h, :], in0=fr[:, TB:2 * TB], scalar=neglam,
                                                               in1=fr[:, 0:TB], op0=ALU.mult, op1=ALU.add)),
                              reads=['fr', 'neglam'], writes=[f'fo{h}'])
                        S.add('dve', ('tensor_tensor', A(out=sd[:, h, :], in0=fo4[:, h, :], in1=fo4[:, h, :], op=ALU.mult)),
                              reads=[f'fo{h}'], writes=[f'sd{h}'])

                ins = {}
                def at(step, fn):
                    ins.setdefault(min(max(step, 0), nst - 1), []).append(fn)
                gi_ = 0
                for c in range(4):
                    for nm in ('r', 'i'):
                        at(3 + 2 * gi_, lambda c=c, nm=nm: lru_gate(c, nm))
                        gi_ += 1
                at(3 + 2 * 8, lru_exps)
                if b > 0:
                    for gi in range(4):
                        at(22 + 3 * gi, lambda gi=gi: e_outproj(b - 1, gi, 0))
                    at(22 + 3 * 3 + 1, lambda: e_store(b - 1))
                if b + 1 < nblk:
                    at(22 + 3 * 3 + 1 + 24, lambda: e_norm(b + 1))

                for i in range(min(LA, nst)):
                    qk_exp(i)
                for i in range(nst):
                    pv_sum(i)
                    if i + LA < nst:
                        qk_exp(i + LA)
                    for fn in ins.get(i, ()):
                        fn()

            e_load(0)
            if nblk > 1:
                e_load(1)
            e_norm(0)
            w_rest()
            for b in range(nblk):
                e_tr(b)
                e_inproj(b)
                if b > 0:
                    e_tail_a(b - 1)
                e_att(b)
                e_lru_tail(b)
            e_tail_a(nblk - 1)
            for gi in range(4):
                e_outproj(nblk - 1, gi, gbank())
            e_store(nblk - 1)

        bufs = [scrA, scrB]
        cur = x_in
        nb = 0
        if passes is None:
            passes_l = []
            for layer in layers:
                passes_l.append(('even' if layer % 2 == 0 else 'odd', layer))
                passes_l.append(('ffn', layer))
        else:
            passes_l = list(passes)
        for pi, (kind, layer) in enumerate(passes_l):
            last = (pi == len(passes_l) - 1)
            d = out if last else bufs[nb % 2]
            nb += 1
            if kind == 'even':
                even_pass(layer, cur, d)
            elif kind == 'odd':
                odd_pass(layer, cur, d)
            else:
                ffn_pass(layer, cur, d, apply_final=(last and final_norm))
            cur = d
        if not final_ops:
            for q in S.dma_slots.values():
                for op in q:
                    if op is not None:
                        final_ops.append(op)
        else:
            for q in S.dma_slots.values():
                for op in q:
                    if op is not None and op not in final_ops:
                        final_ops.append(op)
        S.emit(nc, final_wait_ops=final_ops)
    return nc


def prep_weights(inp):
    f = lambda a: np.ascontiguousarray(np.asarray(a, dtype=np.float32))
    w = {}
    for k in ('norm_mix', 'norm_ffn', 'norm_final', 'ab_w_in', 'ab_w_out', 'lru_wa', 'lru_wx', 'c_w_in', 'c_ln_g',
              'c_ln_b', 'c_w_out', 'ffn_w1', 'ffn_w2'):
        w[k] = f(inp[k])
    w['diff_l'] = f(np.concatenate([inp['diff_lq1'], inp['diff_lk1'], inp['diff_lq2'], inp['diff_lk2']], axis=1))
    pv = []
    for e in range(2):
        cols = []
        cwv = np.asarray(inp['lru_conv_w'][e])
        cols.append(cwv.reshape(4, 4, 128).transpose(2, 0, 1).reshape(128, 16))
        for nm in ('lru_conv_b', 'lru_ba', 'lru_bx', 'lru_lambda'):
            cols.append(np.asarray(inp[nm][e]).reshape(4, 128).T)
        cols.append(np.asarray(inp['diff_subln'][e]).reshape(128, 1))
        pv.append(np.concatenate(cols, axis=1))
    w['pvec'] = f(np.stack(pv))
    w['c_w_sT'] = f(np.transpose(np.asarray(inp['c_w_s']), (0, 3, 1, 2)))
    w['c_b_sT'] = f(np.transpose(np.asarray(inp['c_b_s']), (0, 2, 1)))
    return w


_NC_CACHE = {}


def kernel(**inputs):
    x = np.asarray(inputs['x'], dtype=np.float32)
    B, S_LEN, _ = x.shape
    w = prep_weights(inputs)
    key = (S_LEN,)
    if key not in _NC_CACHE:
        _NC_CACHE[key] = build_program(S_LEN)
    nc = _NC_CACHE[key]
    in_maps = []
    for c in range(B):
        m = dict(w)
        m['x'] = np.ascontiguousarray(x[c])
        in_maps.append(m)
    res = run_bass_kernel_spmd(nc, in_maps, core_ids=list(range(B)))
    return np.stack([np.asarray(r['out'], dtype=np.float32) for r in res.results], axis=0)
```

```python
import contextlib
import math

import numpy as np
import concourse.bass as bass
import concourse.mybir as mybir
from concourse.bass_utils import run_bass_kernel_spmd

F32 = mybir.dt.float32
BF16 = mybir.dt.bfloat16
U8 = mybir.dt.uint8
AF = mybir.ActivationFunctionType
ALU = mybir.AluOpType

D = 1024
TB = 256
EPS = 1e-6
ENGS = ('pe', 'act', 'dve', 'pool', 'sp')
SEM_CAP = 30000
DBG = {'ffn': 9, 'noW': 0}


def A(*a, **k):
    return (a, k)


class Op:
    __slots__ = ('eng', 'fn', 'deps', 'is_dma', 'ndma', 'slot', 'target', 'epoch', 'msval')


class Sched:
    def __init__(self, nslots=8):
        self.ops = {e: [] for e in ENGS}
        self.lastw = {}
        self.readers = {}
        self.nslots = nslots
        self.dma_slots = {}
        self.dma_count = {e: 0 for e in ENGS}
        self.fence_deps = set()

    def add(self, eng, fn, reads=(), writes=(), dma=0):
        op = Op()
        op.eng = eng
        op.fn = fn
        op.is_dma = dma > 0
        op.ndma = dma
        op.slot = None
        op.target = None
        op.epoch = None
        op.msval = None
        deps = set(self.fence_deps)
        for k in reads:
            w = self.lastw.get(k)
            if w is not None:
                deps.add(w)
        for k in writes:
            w = self.lastw.get(k)
            if w is not None:
                deps.add(w)
            for r in self.readers.get(k, ()):
                deps.add(r)
        if dma:
            q = self.dma_slots.setdefault(eng, [None] * self.nslots)
            s = self.dma_count[eng] % self.nslots
            self.dma_count[eng] += 1
            if q[s] is not None:
                deps.add(q[s])
            q[s] = op
            op.slot = (eng, s)
        deps.discard(op)
        op.deps = deps
        for k in reads:
            self.readers.setdefault(k, []).append(op)
        for k in writes:
            self.lastw[k] = op
            self.readers[k] = []
        self.ops[eng].append(op)
        return op

    def fence(self):
        deps = set()
        for e in ENGS:
            last = None
            for op in reversed(self.ops[e]):
                if not op.is_dma:
                    last = op
                    break
            if last is not None:
                deps.add(last)
        for q in self.dma_slots.values():
            for op in q:
                if op is not None:
                    deps.add(op)
        self.fence_deps = deps
        self.lastw = {}
        self.readers = {}

    def emit(self, nc, final_wait_ops=()):
        needed = set()
        for e in ENGS:
            for op in self.ops[e]:
                for d in op.deps:
                    if e == 'pe' and d.eng == 'pe' and not d.is_dma:
                        continue
                    needed.add(d)
        for d in final_wait_ops:
            needed.add(d)
        n_epochs = {}
        for e in ENGS:
            c = 0
            ep = 0
            for op in self.ops[e]:
                if (not op.is_dma) and op in needed:
                    if c >= SEM_CAP:
                        ep += 1
                        c = 0
                    c += 1
                    op.epoch = ep
                    op.msval = c
            n_epochs[e] = ep + 1
        slot_cnt = {}
        for e in ENGS:
            for op in self.ops[e]:
                if op.is_dma:
                    slot_cnt[op.slot] = slot_cnt.get(op.slot, 0) + 16 * op.ndma
                    op.target = slot_cnt[op.slot]
        with contextlib.ExitStack() as st:
            esem = {}
            for e in ENGS:
                for ep in range(n_epochs[e]):
                    esem[(e, ep)] = st.enter_context(nc.semaphore(f"s_{e}_{ep}"))
            dsem = {}
            for slot in slot_cnt:
                dsem[slot] = st.enter_context(nc.semaphore(f"d_{slot[0]}_{slot[1]}"))
            block = st.enter_context(nc.Block())
            ops = self.ops

            def run(e, eng):
                seen_e = {}
                seen_d = {}
                for op in ops[e]:
                    self._waits(e, eng, op.deps, seen_e, seen_d, esem, dsem)
                    name, (pa, kw) = op.fn
                    if op.is_dma:
                        getattr(eng, name)(*pa, **kw).then_inc(dsem[op.slot], 16)
                    else:
                        ins = getattr(eng, name)(*pa, **kw)
                        if op.msval is not None:
                            ins.then_inc(esem[(e, op.epoch)], 1)
                if e == 'sp' and final_wait_ops:
                    self._waits(e, eng, final_wait_ops, seen_e, seen_d, esem, dsem)

            @block.tensor
            def _(eng):
                run('pe', eng)

            @block.scalar
            def _(eng):
                run('act', eng)

            @block.vector
            def _(eng):
                run('dve', eng)

            @block.gpsimd
            def _(eng):
                run('pool', eng)

            @block.sync
            def _(eng):
                run('sp', eng)

    @staticmethod
    def _waits(e, eng, deps, seen_e, seen_d, esem, dsem):
        best_e = {}
        best_d = {}
        for d in deps:
            if d.is_dma:
                if best_d.get(d.slot, 0) < d.target:
                    best_d[d.slot] = d.target
            else:
                if e == 'pe' and d.eng == 'pe':
                    continue
                v = (d.epoch, d.msval)
                if best_e.get(d.eng, (-1, 0)) < v:
                    best_e[d.eng] = v
        for slot, t in best_d.items():
            if seen_d.get(slot, 0) >= t:
                continue
            seen_d[slot] = t
            eng.wait_ge(dsem[slot], t)
        for de, v in best_e.items():
            if seen_e.get(de, (-1, 0)) >= v:
                continue
            seen_e[de] = v
            eng.wait_ge(esem[(de, v[0])], v[1])


class Region:
    def __init__(self, raw, nbytes):
        self.raw = raw
        self.nbytes = nbytes
        self.off = 0

    def reset(self):
        self.off = 0

    def alloc(self, cols, dt):
        esz = 4 if dt == F32 else 2
        nb = cols * esz
        nb_al = (nb + 63) // 64 * 64
        assert self.off + nb_al <= self.nbytes, (self.off, nb_al, self.nbytes)
        v = self.raw[:, self.off:self.off + nb].bitcast(dt)
        self.off += nb_al
        return v


def build_program(S_LEN=4096, layers=(0, 1, 2, 3), final_norm=True, passes=None):
    nblk = S_LEN // TB
    nc = bass.Bass("TRN2", target_bir_lowering=False)

    def din(name, shape):
        return nc.dram_tensor(name, list(shape), F32, kind="ExternalInput").ap()

    x_in = din("x", [S_LEN, D])
    norm_mix = din("norm_mix", [4, D])
    norm_ffn = din("norm_ffn", [4, D])
    norm_final = din("norm_final", [D])
    ab_w_in = din("ab_w_in", [2, D, 2560])
    ab_w_out = din("ab_w_out", [2, D, D])
    diff_l = din("diff_l", [2, 256])
    pvec = din("pvec", [2, 128, 33])
    lru_wa = din("lru_wa", [2, 8, 64, 64])
    lru_wx = din("lru_wx", [2, 8, 64, 64])
    c_w_in = din("c_w_in", [2, D, 2048])
    c_ln_g = din("c_ln_g", [2, D])
    c_ln_b = din("c_ln_b", [2, D])
    c_w_sT = din("c_w_sT", [2, 128, 8, 128])
    c_b_sT = din("c_b_sT", [2, 128, 8])
    c_w_out = din("c_w_out", [2, D, D])
    ffn_w1 = din("ffn_w1", [4, D, 4096])
    ffn_w2 = din("ffn_w2", [4, 4096, D])
    out = nc.dram_tensor("out", [S_LEN, D], F32, kind="ExternalOutput").ap()
    scrA = nc.dram_tensor("scrA", [S_LEN, D], F32, kind="Internal").ap()
    scrB = nc.dram_tensor("scrB", [S_LEN, D], F32, kind="Internal").ap()

    S = Sched(nslots=16)
    st = contextlib.ExitStack()
    with st:
        def sb(name, shape, dt):
            return st.enter_context(nc.sbuf_tensor(name, shape, dt))

        WBYTES = DBG.get('WBYTES', 126976)
        ABYTES = 77824
        wraw = sb("wraw", [128, WBYTES], U8)
        araw = sb("araw", [128, ABYTES], U8)
        WR = Region(wraw, WBYTES)
        AR = Region(araw, ABYTES)
        ident = sb("ident", [128, 128], BF16)
        identf = sb("identf", [128, 128], F32)
        ones_bf = sb("ones_bf", [128, 128], BF16)
        ones_f = sb("ones_f", [128, 128], F32)
        neghalf = sb("neghalf", [128, 1], F32)
        poshalf = sb("poshalf", [128, 1], F32)
        expbias = sb("expbias", [128, 1], F32)
        qbias = sb("qbias", [128, 1], F32)
        epsb = sb("epsb", [128, 1], F32)
        umask = sb("umask", [2, 128], BF16)
        wm0 = sb("wm0", [2, 256], BF16)
        wm1 = sb("wm1", [2, 256], BF16)
        banks = [st.enter_context(nc.psum_tensor(f"bank{i}", [128, 512], F32)) for i in range(8)]

        def bank_bf(i):
            return banks[i][:, :].bitcast(BF16).rearrange("p (k t) -> p k t", k=8)

        S.add('pool', ('memset', A(identf[:], 1.0)), writes=['identf'])
        S.add('pool', ('affine_select', A(out=identf[:], in_=identf[:], pattern=[[-1, 128]],
                                                compare_op=ALU.is_equal, fill=0.0, base=0,
                                                channel_multiplier=1)), reads=['identf'], writes=['identf'])
        S.add('pool', ('tensor_copy', A(out=ident[:], in_=identf[:])), reads=['identf'], writes=['ident'])
        S.add('pool', ('memset', A(ones_bf[:], 1.0)), writes=['ones_bf'])
        S.add('pool', ('memset', A(ones_f[:], 1.0)), writes=['ones_f'])
        S.add('pool', ('memset', A(neghalf[:], -0.5)), writes=['neghalf'])
        S.add('pool', ('memset', A(poshalf[:], 0.5)), writes=['poshalf'])
        S.add('pool', ('memset', A(expbias[:], 0.0)), writes=['expbias'])
        S.add('pool', ('memset', A(qbias[:], 0.25)), writes=['qbias'])
        S.add('pool', ('memset', A(epsb[:], EPS)), writes=['epsb'])
        S.add('pool', ('memset', A(umask[:], 1.0)), writes=['umask'])
        S.add('pool', ('affine_select', A(out=umask[:], in_=umask[:], pattern=[[1, 128]], compare_op=ALU.is_ge,
                                          fill=0.0, base=0, channel_multiplier=-64)), writes=['umask'])
        S.add('pool', ('memset', A(wm0[:], -30000.0)), writes=['wm0'])
        S.add('pool', ('affine_select', A(out=wm0[:], in_=wm0[:], pattern=[[-1, 256]], compare_op=ALU.is_ge,
                                          fill=0.0, base=-1, channel_multiplier=64)), writes=['wm0'])
        S.add('pool', ('memset', A(wm1[:], -30000.0)), writes=['wm1'])
        S.add('pool', ('affine_select', A(out=wm1[:], in_=wm1[:], pattern=[[-1, 256]], compare_op=ALU.is_ge,
                                          fill=0.0, base=127, channel_multiplier=64)), writes=['wm1'])
        S.add('pool', ('affine_select', A(out=wm1[:], in_=wm1[:], pattern=[[1, 256]], compare_op=ALU.is_ge,
                                          fill=0.0, base=0, channel_multiplier=-128)), writes=['wm1'])
        S.add('pool', ('memset', A(expbias[64:128, :], -30000.0)), reads=['expbias'], writes=['expbias'])
        CONST_KEYS = ['ident', 'ones_bf', 'ones_f', 'neghalf', 'poshalf', 'expbias']

        def after_fence_consts():
            pass

        def load_w(dst3, src2, name, kc_n, cols, col_order=None):
            src3 = src2.rearrange("(kc p) n -> p kc n", p=128)
            cstep = cols
            while cstep > 2048:
                cstep //= 2
            kstep = max(1, 2048 // cstep)
            while kc_n % kstep:
                kstep -= 1
            keys = []
            if col_order is not None:
                bycol = {}
                for ci in col_order:
                    c0 = ci * cstep
                    for k0 in range(0, kc_n, kstep):
                        key = f"{name}_{k0}_{c0}"
                        bycol.setdefault(ci, []).append(key)
                        S.add('pool', ('dma_start', A(
                            out=dst3[:, k0:k0 + kstep, c0:c0 + cstep],
                            in_=src3[:, k0:k0 + kstep, c0:c0 + cstep])),
                            writes=[key], dma=1)
                return bycol
            for k0 in range(0, kc_n, kstep):
                for c0 in range(0, cols, cstep):
                    key = f"{name}_{k0}_{c0}"
                    keys.append(key)
                    S.add('pool', ('dma_start', A(
                        out=dst3[:, k0:k0 + kstep, c0:c0 + cstep],
                        in_=src3[:, k0:k0 + kstep, c0:c0 + cstep])),
                        writes=[key], dma=1)
            return keys

        def join(keys, name, dummy):
            S.add('pool', ('memset', A(dummy, 0.0)), reads=keys, writes=[name])

        def load_bcast(dst, src1d, name):
            S.add('sp', ('dma_start', A(out=dst, in_=src1d.partition_broadcast(128))),
                  writes=[name], dma=1)

        def load_x(xt3, src, blk, key):
            S.add('sp', ('dma_start', A(
                out=xt3, in_=src[blk * TB:(blk + 1) * TB, :].rearrange("(s p) d -> p s d", p=128))),
                writes=[key], dma=1)

        def store_x(dst, xo3, blk, key):
            return S.add('sp', ('dma_start', A(
                out=dst[blk * TB:(blk + 1) * TB, :].rearrange("(s p) d -> p s d", p=128), in_=xo3)),
                reads=[key], dma=1)

        def rms_norm(x2, gbc, gkey, h2, xkey, hkey, tmp, tkey):
            stt = tmp[:, 0:12]
            mv = tmp[:, 12:14]
            ms = tmp[:, 14:15]
            rs = tmp[:, 15:16]
            S.add('dve', ('bn_stats', A(out=stt[:, 0:6], in_=x2[:, 0:512])), reads=[xkey], writes=[tkey + 'a'])
            S.add('dve', ('bn_stats', A(out=stt[:, 6:12], in_=x2[:, 512:1024])), reads=[xkey], writes=[tkey + 'b'])
            S.add('dve', ('bn_aggr', A(out=mv, in_=stt)), reads=[tkey + 'a', tkey + 'b'], writes=[tkey + 'mv'])
            S.add('dve', ('tensor_scalar', A(out=ms, in0=mv[:, 0:1], scalar1=mv[:, 0:1], scalar2=mv[:, 1:2],
                                                   op0=ALU.mult, op1=ALU.add)), reads=[tkey + 'mv'], writes=[tkey + 'ms'])
            S.add('dve', ('tensor_scalar', A(out=ms, in0=ms, scalar1=EPS, scalar2=None, op0=ALU.add)),
                  reads=[tkey + 'ms'], writes=[tkey + 'ms'])
            S.add('pool', ('tensor_tensor', A(out=rs, in0=ms, in1=neghalf[:, 0:1], op=ALU.pow)),
                  reads=[tkey + 'ms', 'neghalf'], writes=[tkey + 'rs'])
            S.add('dve', ('scalar_tensor_tensor', A(out=h2, in0=x2, scalar=rs, in1=gbc, op0=ALU.mult, op1=ALU.mult)),
                  reads=[xkey, tkey + 'rs', gkey], writes=[hkey])

        def transpose8(h2, hkey, hT3, col0, hTkey, bank_i, evac_eng):
            tp = bank_bf(bank_i)
            bkey = f"bank{bank_i}"
            for kc in range(8):
                S.add('pe', ('transpose', A(out=tp[:, kc, :], in_=h2[:, kc * 128:(kc + 1) * 128],
                                                         identity=ident[:])),
                      reads=[hkey, 'ident'], writes=[bkey])
            if evac_eng == 'act':
                S.add('act', ('copy', A(out=hT3[:, :, col0:col0 + 128], in_=tp)), writes=[bkey, hTkey])
            else:
                S.add('dve', ('tensor_copy', A(out=hT3[:, :, col0:col0 + 128], in_=tp)), writes=[bkey, hTkey])

        final_ops = []

        def BK(i):
            return f'bank{i}'

        def norm_stats(x2, xkeys, t, tk):
            stt = t[:, 0:12]
            mv = t[:, 12:14]
            ms = t[:, 14:15]
            rs = t[:, 15:16]
            S.add('dve', ('bn_stats', A(out=stt[:, 0:6], in_=x2[:, 0:512])), reads=xkeys, writes=[tk + 'a'])
            S.add('dve', ('bn_stats', A(out=stt[:, 6:12], in_=x2[:, 512:1024])), reads=xkeys, writes=[tk + 'b'])
            S.add('dve', ('bn_aggr', A(out=mv, in_=stt)), reads=[tk + 'a', tk + 'b'], writes=[tk + 'mv'])
            S.add('dve', ('tensor_scalar', A(out=ms, in0=mv[:, 0:1], scalar1=mv[:, 0:1], scalar2=mv[:, 1:2],
                                                   op0=ALU.mult, op1=ALU.add)), reads=[tk + 'mv'], writes=[tk + 'ms'])
            S.add('dve', ('tensor_scalar', A(out=ms, in0=ms, scalar1=EPS, scalar2=None, op0=ALU.add)),
                  reads=[tk + 'ms'], writes=[tk + 'ms'])
            S.add('pool', ('tensor_tensor', A(out=rs, in0=ms, in1=neghalf[:, 0:1], op=ALU.pow)),
                  reads=[tk + 'ms', 'neghalf'], writes=[tk + 'rs'])
            return rs

        def ffn_pass(layer, src, dst, apply_final):
            S.fence()
            WR.reset()
            AR.reset()
            W1 = WR.alloc(8 * 4096, BF16).rearrange("p (k f) -> p k f", k=8)
            W2a = WR.alloc(30 * 1024, BF16).rearrange("p (k f) -> p k f", k=30)
            W2b = AR.alloc(2 * 1024, BF16).rearrange("p (k f) -> p k f", k=2)
            W2v = [W2a[:, fc, :] if fc < 30 else W2b[:, fc - 30, :] for fc in range(32)]
            gbc = AR.alloc(1024, F32)
            gfin = AR.alloc(1024, F32) if apply_final else None
            dummy = AR.alloc(16, F32)
            xts = [AR.alloc(2048, F32).rearrange("p (s d) -> p s d", s=2) for _ in range(2)]
            xos = [AR.alloc(2048, F32).rearrange("p (s d) -> p s d", s=2) for _ in range(2)]
            hb = [AR.alloc(1024, BF16) for _ in range(2)]
            hT = [AR.alloc(8 * TB, BF16).rearrange("p (k t) -> p k t", k=8) for _ in range(2)]
            uT = AR.alloc(32 * TB, BF16).rearrange("p (k t) -> p k t", k=32)
            rt = [AR.alloc(TB, F32) for _ in range(2)]
            tmp = [AR.alloc(16, F32) for _ in range(2)]

            load_bcast(gbc, norm_ffn[layer], 'gbc')
            if apply_final:
                load_bcast(gfin, norm_final, 'gfin')
            load_x(xts[0], src, 0, 'xt0')
            k1 = load_w(W1, ffn_w1[layer], 'W1', 8, 4096, col_order=[0, 1])

            def w_rest():
                for dh in range(2):
                    cs = slice(dh * 512, (dh + 1) * 512)
                    k2 = load_w(W2a[:, :, cs], ffn_w2[layer][0:30 * 128, cs], f'W2a{dh}', 30, 512)
                    k2 += load_w(W2b[:, :, cs], ffn_w2[layer][30 * 128:32 * 128, cs], f'W2b{dh}', 2, 512)
                    if dh == 0:
                        join(k1[0], 'W1c0', dummy[:, 0:2])
                        join(k1[1], 'W1c1', dummy[:, 2:4])
                    join(k2, f'W2h{dh}', dummy[:, 4 + 2 * dh:6 + 2 * dh])

            def f_norm(b):
                p = b % 2
                for s in range(2):
                    rms_norm(xts[p][:, s, :], gbc, 'gbc', hb[s], f'xt{p}', f'hb{s}', tmp[s], f'tmp{s}')

            def f_tr(b):
                p = b % 2
                for s in range(2):
                    transpose8(hb[s], f'hb{s}', hT[p], s * 128, f'hT{p}', s, 'act')

            def f_w1(b):
                p = b % 2
                for fc in range(32):
                    bi = 2 + fc % 4
                    ups = banks[bi][:, 0:TB]
                    for kc in range(8):
                        S.add('pe', ('matmul', A(
                            out=ups, lhsT=W1[:, kc, fc * 128:(fc + 1) * 128], rhs=hT[p][:, kc, :],
                            start=(kc == 0), stop=(kc == 7))),
                            reads=[f'W1c{fc // 16}', f'hT{p}'], writes=[BK(bi)])
                    r = rt[fc % 2]
                    S.add('act', ('activation', A(out=r, in_=ups, func=AF.Relu)),
                          writes=[BK(bi), f'rt{fc % 2}'])
                    S.add('dve', ('tensor_tensor', A(out=uT[:, fc, :], in0=r, in1=r, op=ALU.mult)),
                          reads=[f'rt{fc % 2}'], writes=[f'uT{fc}'])

            def f_w2(b, gi):
                p = b % 2
                xt = xts[p]
                xo = xos[p]
                dh, ts = gi // 2, gi % 2
                bi = 6 + gi % 2
                acc = banks[bi]
                for fc in range(32):
                    S.add('pe', ('matmul', A(
                        out=acc[:, :], lhsT=uT[:, fc, ts * 128:(ts + 1) * 128],
                        rhs=W2v[fc][:, dh * 512:(dh + 1) * 512], start=(fc == 0), stop=(fc == 31))),
                        reads=[f'W2h{dh}', f'uT{fc}'], writes=[BK(bi)])
                S.add('dve', ('tensor_tensor', A(
                    out=xo[:, ts, dh * 512:(dh + 1) * 512], in0=acc[:, :], in1=xt[:, ts, dh * 512:(dh + 1) * 512],
                    op=ALU.add)), reads=[f'xt{p}'], writes=[BK(bi), f'xo{p}_{ts}'])

            def f_store(b):
                p = b % 2
                xo = xos[p]
                okeys = [f'xo{p}_0', f'xo{p}_1']
                if apply_final:
                    for s in range(2):
                        x2 = xo[:, s, :]
                        rs = norm_stats(x2, [f'xo{p}_{s}'], tmp[s], f'tmp{s}')
                        S.add('dve', ('scalar_tensor_tensor', A(out=x2, in0=x2, scalar=rs, in1=gfin,
                                                               op0=ALU.mult, op1=ALU.mult)),
                              reads=[f'tmp{s}rs', 'gfin'], writes=[f'xo{p}_{s}'])
                op = S.add('sp', ('dma_start', A(
                    out=dst[b * TB:(b + 1) * TB, :].rearrange("(s p) d -> p s d", p=128), in_=xo)),
                    reads=okeys, dma=1)
                if apply_final:
                    final_ops.append(op)

            if nblk > 1:
                load_x(xts[1], src, 1, 'xt1')
            f_norm(0)
            f_tr(0)
            w_rest()
            for b in range(nblk):
                p = b % 2
                f_w1(b)
                if b + 1 < nblk:
                    f_norm(b + 1)
                f_w2(b, 0)
                f_w2(b, 1)
                if b + 1 < nblk:
                    f_tr(b + 1)
                f_w2(b, 2)
                f_w2(b, 3)
                f_store(b)
                if b + 2 < nblk:
                    load_x(xts[p], src, b + 2, f'xt{p}')

        def odd_pass(layer, src, dst):
            o = layer // 2
            S.fence()
            WR.reset()
            AR.reset()
            Wci = WR.alloc(8 * 2048, BF16).rearrange("p (k f) -> p k f", k=8)
            Wco = WR.alloc(8 * 1024, BF16).rearrange("p (k f) -> p k f", k=8)
            WsT = WR.alloc(8 * 128, BF16).rearrange("p (g t) -> p g t", g=8)
            lng = WR.alloc(1024, F32)
            lnb = WR.alloc(1024, F32)
            gbc = WR.alloc(1024, F32)
            bsT = WR.alloc(8, F32)
            dummy = WR.alloc(16, F32)
            NZ = 3
            NX = 4
            zu = [WR.alloc(1024, F32) for _ in range(NZ)]
            zv = [WR.alloc(1024, F32) for _ in range(NZ)]
            vt = [WR.alloc(1024, F32) for _ in range(2)]
            xts = [AR.alloc(2048, F32).rearrange("p (s d) -> p s d", s=2) for _ in range(NX)]
            xos = [AR.alloc(2048, F32).rearrange("p (s d) -> p s d", s=2) for _ in range(2)]
            hb = [AR.alloc(1024, BF16) for _ in range(2)]
            hT = [AR.alloc(8 * 128, BF16).rearrange("p (k t) -> p k t", k=8) for _ in range(2)]
            vln = [AR.alloc(1024, BF16) for _ in range(2)]
            sg = [AR.alloc(1024, BF16) for _ in range(2)]
            sT = [AR.alloc(8 * 128, BF16).rearrange("p (k t) -> p k t", k=8) for _ in range(2)]
            tmp = [AR.alloc(16, F32) for _ in range(2)]
            tmp2 = [AR.alloc(16, F32) for _ in range(2)]

            load_bcast(gbc, norm_mix[layer], 'gbc')
            load_bcast(lng, c_ln_g[o], 'lng')
            load_bcast(lnb, c_ln_b[o], 'lnb')
            S.add('sp', ('dma_start', A(out=bsT, in_=c_b_sT[o])), writes=['bsT'], dma=1)
            load_x(xts[0], src, 0, 'xt0')
            k1 = load_w(Wci, c_w_in[o], 'Wci', 8, 2048)

            def w_rest():
                k2 = load_w(Wco, c_w_out[o], 'Wco', 8, 1024)
                S.add('pool', ('dma_start', A(out=WsT, in_=c_w_sT[o])), writes=['WsT_raw'], dma=1)
                join(k1, 'Wci', dummy[:, 0:4])
                join(k2, 'Wco', dummy[:, 4:8])
                S.add('pool', ('memset', A(WsT[64:128, :, 0:64], 0.0)), reads=['WsT_raw'], writes=['WsT'])

            def st_norm(g):
                b, s = g // 2, g % 2
                q = g % 2
                rms_norm(xts[b % NX][:, s, :], gbc, 'gbc', hb[q], f'xt{b % NX}', f'hb{q}', tmp[q], f'tmp{q}')

            def st_tr(g):
                q = g % 2
                transpose8(hb[q], f'hb{q}', hT[q], 0, f'hT{q}', 0, 'act')

            def st_z(g):
                q = g % 2
                z = g % NZ
                for cg in range(4):
                    bi = 2 + cg % 2
                    for kc in range(8):
                        S.add('pe', ('matmul', A(
                            out=banks[bi][:, :], lhsT=hT[q][:, kc, :], rhs=Wci[:, kc, cg * 512:(cg + 1) * 512],
                            start=(kc == 0), stop=(kc == 7))), reads=['Wci', f'hT{q}'], writes=[BK(bi)])
                    dstz = (zu[z] if cg < 2 else zv[z])[:, (cg % 2) * 512:(cg % 2 + 1) * 512]
                    S.add('act', ('activation', A(out=dstz, in_=banks[bi][:, :], func=AF.Gelu_apprx_tanh)),
                          writes=[BK(bi), f'z{z}_{cg}'])

            def st_ln(g):
                q = g % 2
                z = g % NZ
                t2 = tmp2[q]
                stt = t2[:, 0:12]
                mv = t2[:, 12:14]
                ve = t2[:, 14:15]
                rs = t2[:, 15:16]
                zvs = zv[z]
                S.add('dve', ('bn_stats', A(out=stt[:, 0:6], in_=zvs[:, 0:512])), reads=[f'z{z}_2'], writes=[f't2{q}a'])
                S.add('dve', ('bn_stats', A(out=stt[:, 6:12], in_=zvs[:, 512:1024])), reads=[f'z{z}_3'], writes=[f't2{q}b'])
                S.add('dve', ('bn_aggr', A(out=mv, in_=stt)), reads=[f't2{q}a', f't2{q}b'], writes=[f't2{q}mv'])
                S.add('dve', ('tensor_scalar', A(out=ve, in0=mv[:, 1:2], scalar1=EPS, scalar2=None, op0=ALU.add)),
                      reads=[f't2{q}mv'], writes=[f't2{q}ve'])
                S.add('pool', ('tensor_tensor', A(out=rs, in0=ve, in1=neghalf[:, 0:1], op=ALU.pow)),
                      reads=[f't2{q}ve', 'neghalf'], writes=[f't2{q}rs'])
                vts = vt[q]
                S.add('dve', ('tensor_scalar', A(out=vts, in0=zvs, scalar1=mv[:, 0:1], scalar2=rs,
                                                 op0=ALU.subtract, op1=ALU.mult)),
                      reads=[f'z{z}_2', f'z{z}_3', f't2{q}mv', f't2{q}rs'], writes=[f'vt{q}'])
                S.add('pool', ('tensor_tensor', A(out=vts, in0=vts, in1=lng, op=ALU.mult)), reads=['lng'], writes=[f'vt{q}'])
                S.add('pool', ('tensor_tensor', A(out=vln[q], in0=vts, in1=lnb, op=ALU.add)), reads=[f'vt{q}', 'lnb'],
                      writes=[f'vln{q}'])

            def st_sp_pe(g):
                q = g % 2
                vl = vln[q]
                for gg_ in range(8):
                    bi = 4 + gg_ // 4
                    mps = banks[bi][:, (gg_ % 4) * 128:(gg_ % 4 + 1) * 128]
                    S.add('pe', ('matmul', A(out=mps, lhsT=WsT[:, gg_, :], rhs=vl[:, gg_ * 128:(gg_ + 1) * 128],
                                             start=True, stop=True)), reads=['WsT', f'vln{q}'], writes=[BK(bi)])

            def st_sp_dve(g):
                q = g % 2
                z = g % NZ
                sgs = sg[q]
                zus = zu[z]
                for gg_ in range(8):
                    bi = 4 + gg_ // 4
                    mps = banks[bi][:, (gg_ % 4) * 128:(gg_ % 4 + 1) * 128]
                    S.add('dve', ('scalar_tensor_tensor', A(
                        out=sgs[:, gg_ * 128:(gg_ + 1) * 128], in0=mps, scalar=bsT[:, gg_:gg_ + 1],
                        in1=zus[:, gg_ * 128:(gg_ + 1) * 128], op0=ALU.add, op1=ALU.mult)),
                        reads=['bsT', f'z{z}_{gg_ // 4}'], writes=[BK(bi), f'sg{q}_{gg_}'])

            def st_str(g):
                q = g % 2
                sgs = sg[q]
                tp = bank_bf(1)
                for kc in range(8):
                    S.add('pe', ('transpose', A(out=tp[:, kc, :], in_=sgs[:, kc * 128:(kc + 1) * 128], identity=ident[:])),
                          reads=[f'sg{q}_{kc}', 'ident'], writes=[BK(1)])
                S.add('act', ('copy', A(out=sT[q], in_=tp)), writes=[BK(1), f'sT{q}'])

            def st_out(g):
                b, s = g // 2, g % 2
                q = g % 2
                p = b % 2
                xt = xts[b % NX]
                xo = xos[p]
                for dh in range(2):
                    bi = 6 + dh
                    for kc in range(8):
                        S.add('pe', ('matmul', A(
                            out=banks[bi][:, :], lhsT=sT[q][:, kc, :], rhs=Wco[:, kc, dh * 512:(dh + 1) * 512],
                            start=(kc == 0), stop=(kc == 7))), reads=['Wco', f'sT{q}'], writes=[BK(bi)])
                    S.add('dve', ('tensor_tensor', A(
                        out=xo[:, s, dh * 512:(dh + 1) * 512], in0=banks[bi][:, :],
                        in1=xt[:, s, dh * 512:(dh + 1) * 512], op=ALU.add)),
                        reads=[f'xt{b % NX}'], writes=[BK(bi), f'xo{p}_{s}'])
                if s == 1:
                    S.add('sp', ('dma_start', A(
                        out=dst[b * TB:(b + 1) * TB, :].rearrange("(s p) d -> p s d", p=128), in_=xo)),
                        reads=[f'xo{p}_0', f'xo{p}_1'], dma=1)
                    if b + NX < nblk:
                        load_x(xts[b % NX], src, b + NX, f'xt{b % NX}')

            nsub = 2 * nblk
            for bb in range(1, min(NX, nblk)):
                load_x(xts[bb], src, bb, f'xt{bb}')
            st_norm(0)
            w_rest()
            for t in range(nsub + 3):
                if 0 <= t - 2 < nsub:
                    st_sp_pe(t - 2)
                if t + 1 < nsub:
                    st_norm(t + 1)
                if t < nsub:
                    st_tr(t)
                if 0 <= t - 3 < nsub:
                    st_out(t - 3)
                if t < nsub:
                    st_z(t)
                if 0 <= t - 1 < nsub:
                    st_ln(t - 1)
                if 0 <= t - 2 < nsub:
                    st_sp_dve(t - 2)
                    st_str(t - 2)

        def even_pass(layer, src, dst):
            ev = layer // 2
            lam_init = 0.8 - 0.6 * math.exp(-0.3 * layer)
            NCH = S_LEN // 128
            S.fence()
            WR.reset()
            AR.reset()
            Win = WR.alloc(8 * 2560, BF16).rearrange("p (k f) -> p k f", k=8)
            Wout = WR.alloc(8 * 1024, BF16).rearrange("p (k f) -> p k f", k=8)
            kT = WR.alloc(4 * S_LEN, BF16).rearrange("p (h t) -> p h t", h=4)
            V = WR.alloc(NCH * 512, BF16).rearrange("p (c v) -> p c v", c=NCH)
            WA = WR.alloc(4 * 128, BF16).rearrange("p (c j) -> p c j", c=4)
            WX = WR.alloc(4 * 128, BF16).rearrange("p (c j) -> p c j", c=4)
            gbc = AR.alloc(1024, F32)
            pv = AR.alloc(33, F32)
            ldl = WR.alloc(256, F32)
            sm = AR.alloc(64, F32)
            dummy = AR.alloc(16, F32)
            ltmp = AR.alloc(64, F32)
            hprev = AR.alloc(4, F32)
            tmp = [AR.alloc(16, F32) for _ in range(2)]
            xts = [AR.alloc(2048, F32).rearrange("p (s d) -> p s d", s=2) for _ in range(2)]
            hb = [AR.alloc(1024, BF16) for _ in range(2)]
            hT = AR.alloc(8 * TB, BF16).rearrange("p (k t) -> p k t", k=8)
            qc = AR.alloc(4 * 2 * TB, BF16).rearrange("p (h t) -> p h t", h=4)
            xbh = AR.alloc(4 * (TB + 3), F32).rearrange("p (c t) -> p c t", c=4)
            gg = AR.alloc(4 * TB, BF16).rearrange("p (c t) -> p c t", c=4)
            xc = AR.alloc(4 * TB, F32).rearrange("p (c t) -> p c t", c=4)
            xcb = hb[1].rearrange("p (c t) -> p c t", c=4)
            Tr = AR.alloc(4 * TB, F32).rearrange("p (c t) -> p c t", c=4)
            Ti = AR.alloc(4 * TB, F32).rearrange("p (c t) -> p c t", c=4)
            aa = AR.alloc(4 * TB, F32).rearrange("p (c t) -> p c t", c=4)
            a2 = AR.alloc(4 * TB, F32).rearrange("p (c t) -> p c t", c=4)
            bt = Ti
            hs = Tr
            yT = AR.alloc(8 * TB, BF16).rearrange("p (k t) -> p k t", k=8)
            NPT = 4
            pT = [AR.alloc(2 * TB, BF16) for _ in range(NPT - 1)] + [WR.alloc(2 * TB, BF16)]
            fr = AR.alloc(2 * TB, F32)
            fo4 = AR.alloc(4 * TB, F32).rearrange("p (h t) -> p h t", h=4)
            sd = AR.alloc(4 * TB, F32).rearrange("p (h t) -> p h t", h=4)

            s1 = sm[:, 0:1]
            s2 = sm[:, 1:2]
            e1 = sm[:, 2:3]
            e2 = sm[:, 3:4]
            neglam = sm[:, 4:5]
            g1 = sm[:, 5:6]
            yv = sm[:, 8:12]
            tv = sm[:, 12:16]
            sc = sm[:, 16:20]
            sch = sm[:, 20:24]
            hba = sm[:, 24:28]
            hbx = sm[:, 28:32]
            cw = lambda w, c: pv[:, w * 4 + c:w * 4 + c + 1]
            cb = lambda c: pv[:, 16 + c:17 + c]

            load_bcast(gbc, norm_mix[layer], 'gbc')
            load_bcast(ldl, diff_l[ev], 'ldl')
            S.add('sp', ('dma_start', A(out=pv, in_=pvec[ev])), writes=['pv'], dma=1)
            k1 = load_w(Win, ab_w_in[ev], 'Win', 8, 2560, col_order=[1, 0])
            def w_rest():
                k2 = load_w(Wout, ab_w_out[ev], 'Wout', 8, 1024)
                join(k1[1], 'Winc1', dummy[:, 0:2])
                join(k1[0], 'Winc0', dummy[:, 2:4])
                join(k2, 'Wout', dummy[:, 4:8])
                S.add('pool', ('memset', A(WA, 0.0)), writes=['WA0'])
                S.add('pool', ('memset', A(WX, 0.0)), writes=['WX0'])
                for nm, wt, srcw in (('WA', WA, lru_wa), ('WX', WX, lru_wx)):
                    sv = srcw[ev].rearrange("(c j) i o -> j i c o", j=2)
                    wk = []
                    for j in range(2):
                        key = f'{nm}_{j}'
                        wk.append(key)
                        S.add('pool', ('dma_start', A(
                            out=wt[j * 64:(j + 1) * 64, :, j * 64:(j + 1) * 64], in_=sv[j])),
                            reads=[nm + '0'], writes=[key], dma=1)
                    join(wk, nm, dummy[:, 8:10] if nm == 'WA' else dummy[:, 10:12])
                S.add('pool', ('memset', A(qc[64:128, :, 0:TB], 0.0)), writes=['qcz0'])
                S.add('pool', ('memset', A(qc[0:64, :, TB:2 * TB], 0.0)), writes=['qcz1'])
                S.add('dve', ('tensor_tensor', A(out=ltmp, in0=ldl[:, 0:64], in1=ldl[:, 64:128], op=ALU.mult)),
                      reads=['ldl'], writes=['ltmp'])
                S.add('dve', ('tensor_reduce', A(out=s1, in_=ltmp, axis=mybir.AxisListType.X, op=ALU.add)),
                      reads=['ltmp'], writes=['s1'])
                S.add('dve', ('tensor_tensor', A(out=ltmp, in0=ldl[:, 128:192], in1=ldl[:, 192:256], op=ALU.mult)),
                      reads=['ldl', 's1'], writes=['ltmp'])
                S.add('dve', ('tensor_reduce', A(out=s2, in_=ltmp, axis=mybir.AxisListType.X, op=ALU.add)),
                      reads=['ltmp'], writes=['s2'])
                S.add('act', ('activation', A(out=e1, in_=s1, func=AF.Exp)), reads=['s1'], writes=['e1'])
                S.add('act', ('activation', A(out=e2, in_=s2, func=AF.Exp)), reads=['s2'], writes=['e2'])
                S.add('dve', ('tensor_tensor', A(out=neglam, in0=e2, in1=e1, op=ALU.subtract)), reads=['e1', 'e2'],
                      writes=['neglam'])
                S.add('dve', ('tensor_scalar', A(out=neglam, in0=neglam, scalar1=-lam_init, scalar2=None, op0=ALU.add)),
                      reads=['neglam'], writes=['neglam'])
                S.add('dve', ('tensor_scalar', A(out=g1, in0=pv[:, 32:33], scalar1=(1.0 - lam_init), scalar2=None,
                                                       op0=ALU.mult)), reads=['pv'], writes=['g1'])
                S.add('act', ('activation', A(out=yv, in_=pv[:, 28:32], func=AF.Exp, scale=-1.0)), reads=['pv'], writes=['yv'])
                S.add('dve', ('tensor_scalar', A(out=tv, in0=yv, scalar1=-0.25, scalar2=1.0 / 3.0, op0=ALU.mult, op1=ALU.add)),
                      reads=['yv'], writes=['tv'])
                S.add('dve', ('tensor_tensor', A(out=tv, in0=tv, in1=yv, op=ALU.mult)), reads=['tv', 'yv'], writes=['tv'])
                S.add('dve', ('tensor_scalar', A(out=tv, in0=tv, scalar1=-1.0, scalar2=0.5, op0=ALU.mult, op1=ALU.add)),
                      reads=['tv'], writes=['tv'])
                S.add('dve', ('tensor_tensor', A(out=tv, in0=tv, in1=yv, op=ALU.mult)), reads=['tv', 'yv'], writes=['tv'])
                S.add('dve', ('tensor_scalar', A(out=tv, in0=tv, scalar1=-1.0, scalar2=1.0, op0=ALU.mult, op1=ALU.add)),
                      reads=['tv'], writes=['tv'])
                S.add('dve', ('tensor_tensor', A(out=tv, in0=tv, in1=yv, op=ALU.mult)), reads=['tv', 'yv'], writes=['tv'])
                S.add('dve', ('tensor_scalar', A(out=sc, in0=tv, scalar1=-8.0, scalar2=None, op0=ALU.mult)),
                      reads=['tv'], writes=['sc'])
                S.add('dve', ('tensor_scalar', A(out=sch, in0=tv, scalar1=-4.0, scalar2=None, op0=ALU.mult)),
                      reads=['tv'], writes=['sch'])
                S.add('dve', ('tensor_scalar', A(out=hba, in0=pv[:, 20:24], scalar1=0.5, scalar2=None, op0=ALU.mult)),
                      reads=['pv'], writes=['hba'])
                S.add('dve', ('tensor_scalar', A(out=hbx, in0=pv[:, 24:28], scalar1=0.5, scalar2=None, op0=ALU.mult)),
                      reads=['pv'], writes=['hbx'])
                S.add('pool', ('memset', A(xbh[:, :, 0:3], 0.0)), writes=['xbh_halo'])
                S.add('pool', ('memset', A(hprev, 0.0)), writes=['hprev'])


            gcount = [0]

            def gbank():
                bi = 1 + gcount[0] % 3
                gcount[0] += 1
                return bi

            pcount = [0]

            def e_load(b):
                p = b % 2
                S.add('sp', ('dma_start', A(
                    out=xts[p], in_=src[b * TB:(b + 1) * TB, :].rearrange("(s p) d -> p s d", p=128))),
                    writes=[f'xt{p}'], dma=1)

            def e_norm(b):
                p = b % 2
                for s in range(2):
                    rms_norm(xts[p][:, s, :], gbc, 'gbc', hb[s], f'xt{p}', f'hb{s}', tmp[s], f'tmp{s}')

            def e_tr(b):
                for s in range(2):
                    transpose8(hb[s], f'hb{s}', hT, s * 128, 'hT', 0, 'act')

            def proj_fm(col0, evac):
                bi = gbank()
                ps_ = banks[bi][:, 0:TB]
                for kc in range(8):
                    S.add('pe', ('matmul', A(out=ps_, lhsT=Win[:, kc, col0:col0 + 128],
                                             rhs=hT[:, kc, :], start=(kc == 0), stop=(kc == 7))),
                          reads=[f'Winc{col0 // 1280}', 'hT'], writes=[BK(bi)])
                evac(ps_, BK(bi))

            def evac_q(ps_, bk, h):
                S.add('act', ('copy', A(out=qc[0:64, h, 0:TB], in_=ps_[0:64, :])),
                      reads=['qcz0', 'qcz1'], writes=[bk, f'qc{h}'])
                S.add('act', ('copy', A(out=qc[64:128, h, TB:2 * TB], in_=ps_[64:128, :])),
                      writes=[bk, f'qc{h}'])

            def e_inproj(b):
                for c in range(4):
                    proj_fm(1536 + c * 128, lambda ps_, bk, c=c: S.add(
                        'act', ('copy', A(out=xbh[:, c, 3:3 + TB], in_=ps_)), reads=['xbh_halo'],
                        writes=[bk, f'xbh{c}']))
                for c in range(4):
                    proj_fm(2048 + c * 128, lambda ps_, bk, c=c: S.add(
                        'act', ('activation', A(out=gg[:, c, :], in_=ps_, func=AF.Gelu_apprx_tanh)),
                        writes=[bk, f'gg{c}']))
                e_lru1(b)
                for h in range(4):
                    proj_fm(h * 128, lambda ps_, bk, h=h: evac_q(ps_, bk, h))
                for h in range(4):
                    proj_fm(512 + h * 128, lambda ps_, bk, h=h: S.add(
                        'act', ('copy', A(out=kT[:, h, b * TB:(b + 1) * TB], in_=ps_)),
                        writes=[bk, f'kT{h}_{b}']))
                for ts in range(2):
                    bi = gbank()
                    for kc in range(8):
                        S.add('pe', ('matmul', A(
                            out=banks[bi][:, :], lhsT=hT[:, kc, ts * 128:(ts + 1) * 128],
                            rhs=Win[:, kc, 1024:1536], start=(kc == 0), stop=(kc == 7))),
                            reads=['Winc0', 'Winc1', 'hT'], writes=[BK(bi)])
                    S.add('act', ('copy', A(out=V[:, b * 2 + ts, :], in_=banks[bi][:, :])),
                          writes=[BK(bi), f'V{b * 2 + ts}'])

            def e_lru1(b):
                for c in range(4):
                    S.add('dve', ('tensor_scalar', A(out=xc[:, c, :], in0=xbh[:, c, 0:TB], scalar1=cw(0, c),
                                                     scalar2=cb(c), op0=ALU.mult, op1=ALU.add)),
                          reads=[f'xbh{c}', 'xbh_halo', 'pv'], writes=[f'xc{c}'])
                    for w in range(1, 4):
                        S.add('dve', ('scalar_tensor_tensor', A(
                            out=xc[:, c, :], in0=xbh[:, c, w:w + TB], scalar=cw(w, c), in1=xc[:, c, :],
                            op0=ALU.mult, op1=ALU.add)), reads=[f'xbh{c}', 'xbh_halo', 'pv'], writes=[f'xc{c}'])
                    S.add('pool', ('tensor_copy', A(out=xcb[:, c, :], in_=xc[:, c, :])), reads=[f'xc{c}'],
                          writes=[f'xcb{c}', 'hb1'])
                S.add('pool', ('tensor_copy', A(out=xbh[:, :, 0:3], in_=xbh[:, :, TB:TB + 3])),
                      reads=[f'xbh{c}' for c in range(4)] + [f'xc{c}' for c in range(4)], writes=['xbh_halo'])

            def lru_gate(c, nm):
                wt, Tt, hbias = (WA, Tr, hba) if nm == 'r' else (WX, Ti, hbx)
                bi = 0
                ps_ = banks[bi][:, 0:TB]
                S.add('pe', ('matmul', A(out=ps_, lhsT=wt[:, c, :], rhs=xcb[:, c, :], start=True, stop=True)),
                      reads=['WA' if nm == 'r' else 'WX', f'xcb{c}', 'hb1'], writes=[BK(bi)])
                S.add('act', ('activation', A(
                    out=Tt[:, c, :], in_=ps_, func=AF.Tanh, bias=hbias[:, c:c + 1], scale=0.5)),
                    reads=['hba', 'hbx'],
                    writes=[BK(bi), f'T{nm}{c}'] + ([f'hs{c}'] if nm == 'r' else [f'bt{c}']))

            def lru_exps():
                for c in range(4):
                    S.add('act', ('activation', A(out=aa[:, c, :], in_=Tr[:, c, :], func=AF.Exp,
                                                  bias=sch[:, c:c + 1], scale=sch[:, c:c + 1])),
                          reads=[f'Tr{c}', 'sch'], writes=[f'aa{c}'])
                    S.add('act', ('activation', A(out=a2[:, c, :], in_=Tr[:, c, :], func=AF.Exp,
                                                  bias=sc[:, c:c + 1], scale=sc[:, c:c + 1])),
                          reads=[f'Tr{c}', 'sc'], writes=[f'a2{c}'])

            def e_lru_tail(b):
                S.add('act', ('activation', A(out=a2.rearrange("p c t -> p (c t)"), in_=a2.rearrange("p c t -> p (c t)"),
                                              func=AF.Sqrt, bias=qbias[:, 0:1], scale=-0.25)),
                      reads=['qbias'], writes=[f'a2{c}' for c in range(4)])
                for c in range(4):
                    S.add('dve', ('scalar_tensor_tensor', A(out=bt[:, c, :], in0=Ti[:, c, :], scalar=1.0,
                                                           in1=xc[:, c, :], op0=ALU.add, op1=ALU.mult)),
                          reads=[f'Ti{c}', f'xc{c}'], writes=[f'bt{c}', f'Ti{c}'])
                    S.add('dve', ('tensor_tensor', A(out=bt[:, c, :], in0=bt[:, c, :], in1=a2[:, c, :], op=ALU.mult)),
                          reads=[f'a2{c}'], writes=[f'bt{c}'])
                    S.add('dve', ('tensor_tensor_scan', A(out=hs[:, c, :], data0=aa[:, c, :], data1=bt[:, c, :],
                                                         initial=hprev[:, c:c + 1], op0=ALU.mult, op1=ALU.add)),
                          reads=[f'aa{c}', f'bt{c}', 'hprev'], writes=[f'hs{c}', f'Tr{c}'])
                    S.add('dve', ('tensor_tensor', A(out=yT[:, 4 + c, :], in0=gg[:, c, :], in1=hs[:, c, :], op=ALU.mult)),
                          reads=[f'gg{c}', f'hs{c}'], writes=[f'yT{4 + c}'])
                S.add('pool', ('tensor_copy', A(out=hprev, in_=hs[:, :, TB - 1])),
                      reads=[f'hs{c}' for c in range(4)], writes=['hprev'])

            def e_tail_a(b):
                for h in range(4):
                    mb = 1 - h // 2
                    S.add('pe', ('matmul', A(out=banks[mb][:, (h % 2) * TB:(h % 2 + 1) * TB], lhsT=ones_f[:], rhs=sd[:, h, :],
                                             start=True, stop=True)),
                          reads=['ones_f', f'sd{h}'], writes=[BK(mb)])
                for hh in range(2):
                    mb = 1 - hh
                    S.add('act', ('activation', A(out=sd[:, 2 * hh:2 * hh + 2, :].rearrange("p h t -> p (h t)"),
                                                  in_=banks[mb][:, :], func=AF.Sqrt, bias=epsb[:, 0:1], scale=1.0 / 128.0)),
                          reads=['epsb'], writes=[BK(mb), f'sd{2 * hh}', f'sd{2 * hh + 1}'])
                S.add('dve', ('reciprocal', A(out=sd.rearrange("p h t -> p (h t)"), in_=sd.rearrange("p h t -> p (h t)"))),
                      writes=[f'sd{h}' for h in range(4)])
                S.add('dve', ('scalar_tensor_tensor', A(out=yT[:, 0:4, :].rearrange("p h t -> p (h t)"),
                                                       in0=fo4.rearrange("p h t -> p (h t)"), scalar=g1,
                                                       in1=sd.rearrange("p h t -> p (h t)"), op0=ALU.mult, op1=ALU.mult)),
                      reads=[f'fo{h}' for h in range(4)] + ['g1'] + [f'sd{h}' for h in range(4)],
                      writes=[f'yT{h}' for h in range(4)])

            def e_outproj(b, gi, bi):
                p = b % 2
                xt = xts[p]
                ts, dh = gi // 2, gi % 2
                for kc in range(8):
                    S.add('pe', ('matmul', A(
                        out=banks[bi][:, :], lhsT=yT[:, kc, ts * 128:(ts + 1) * 128],
                        rhs=Wout[:, kc, dh * 512:(dh + 1) * 512], start=(kc == 0), stop=(kc == 7))),
                        reads=['Wout', f'yT{kc}'], writes=[BK(bi)])
                S.add('dve', ('tensor_tensor', A(
                    out=xt[:, ts, dh * 512:(dh + 1) * 512], in0=banks[bi][:, :],
                    in1=xt[:, ts, dh * 512:(dh + 1) * 512], op=ALU.add)),
                    writes=[BK(bi), f'xt{p}'])

            def e_store(b):
                p = b % 2
                S.add('sp', ('dma_start', A(
                    out=dst[b * TB:(b + 1) * TB, :].rearrange("(s p) d -> p s d", p=128), in_=xts[p])),
                    reads=[f'xt{p}'], dma=1)
                if b + 2 < nblk:
                    e_load(b + 2)

            def e_att(b):
                nkc = 2 * b + 2
                steps = [(h, kc) for h in range(4) for kc in range(nkc)]
                nst = len(steps)
                LA = 3
                pts = {}

                def qk_exp(i):
                    h, kc = steps[i]
                    j = kc - 2 * b
                    si = 1 + i % 3
                    S.add('pe', ('matmul', A(
                        out=banks[si][:, :], lhsT=kT[:, h, kc * 128:(kc + 1) * 128], rhs=qc[:, h, :],
                        start=True, stop=(j < 0))), reads=[f'kT{h}_{kc // 2}', f'qc{h}'], writes=[BK(si)])
                    if j >= 0:
                        wm = wm0 if j == 0 else wm1
                        for m in range(2):
                            S.add('pe', ('matmul', A(
                                out=banks[si][:, m * TB:(m + 1) * TB], lhsT=umask[:, :], rhs=wm[:, :],
                                start=False, stop=(m == 1))), reads=['umask', 'wm0', 'wm1'], writes=[BK(si)])
                    pi = pcount[0] % NPT
                    pcount[0] += 1
                    pt = pT[pi]
                    pkey = f'pT{pi}'
                    pts[i] = (pt, pkey)
                    S.add('act', ('activation', A(out=pt, in_=banks[si][:, :], func=AF.Exp, scale=0.125)),
                          writes=[BK(si), pkey])

                def pv_sum(i):
                    h, kc = steps[i]
                    ob = 4 + h % 2
                    sb_ = 6 + h % 2
                    pt, pkey = pts.pop(i)
                    S.add('pe', ('matmul', A(
                        out=banks[ob][:, :], lhsT=V[:, kc, h * 128:(h + 1) * 128], rhs=pt,
                        start=(kc == 0), stop=(kc == nkc - 1))), reads=[f'V{kc}', pkey], writes=[BK(ob)])
                    S.add('pe', ('matmul', A(
                        out=banks[sb_][:, :], lhsT=ones_bf[:], rhs=pt,
                        start=(kc == 0), stop=(kc == nkc - 1))), reads=['ones_bf', pkey], writes=[BK(sb_)])
                    if kc == nkc - 1:
                        S.add('dve', ('reciprocal', A(out=fr, in_=banks[sb_][:, :])), writes=[BK(sb_), 'fr'])
                        S.add('dve', ('tensor_tensor', A(out=fr, in0=banks[ob][:, :], in1=fr, op=ALU.mult)),
                              writes=[BK(ob), 'fr'])
                        S.add('dve', ('scalar_tensor_tensor', A(out=fo4[:, h, :], in0=fr[:, TB:2 * TB], scalar=neglam,
                                                               in1=fr[:, 0:TB], op0=ALU.mult, op1=ALU.add)),
                              reads=['fr', 'neglam'], writes=[f'fo{h}'])
                        S.add('dve', ('tensor_tensor', A(out=sd[:, h, :], in0=fo4[:, h, :], in1=fo4[:, h, :], op=ALU.mult)),
                              reads=[f'fo{h}'], writes=[f'sd{h}'])

                ins = {}
                def at(step, fn):
                    ins.setdefault(min(max(step, 0), nst - 1), []).append(fn)
                gi_ = 0
                for c in range(4):
                    for nm in ('r', 'i'):
                        at(3 + 2 * gi_, lambda c=c, nm=nm: lru_gate(c, nm))
                        gi_ += 1
                at(3 + 2 * 8, lru_exps)
                if b > 0:
                    for gi in range(4):
                        at(22 + 3 * gi, lambda gi=gi: e_outproj(b - 1, gi, 0))
                    at(22 + 3 * 3 + 1, lambda: e_store(b - 1))
                if b + 1 < nblk:
                    at(22 + 3 * 3 + 1 + 24, lambda: e_norm(b + 1))

                for i in range(min(LA, nst)):
                    qk_exp(i)
                for i in range(nst):
                    pv_sum(i)
                    if i + LA < nst:
                        qk_exp(i + LA)
                    for fn in ins.get(i, ()):
                        fn()

            e_load(0)
            if nblk > 1:
                e_load(1)
            e_norm(0)
            w_rest()
            for b in range(nblk):
                e_tr(b)
                e_inproj(b)
                if b > 0:
                    e_tail_a(b - 1)
                e_att(b)
                e_lru_tail(b)
            e_tail_a(nblk - 1)
            for gi in range(4):
                e_outproj(nblk - 1, gi, gbank())
            e_store(nblk - 1)

        bufs = [scrA, scrB]
        cur = x_in
        nb = 0
        if passes is None:
            passes_l = []
            for layer in layers:
                passes_l.append(('even' if layer % 2 == 0 else 'odd', layer))
                passes_l.append(('ffn', layer))
        else:
            passes_l = list(passes)
        for pi, (kind, layer) in enumerate(passes_l):
            last = (pi == len(passes_l) - 1)
            d = out if last else bufs[nb % 2]
            nb += 1
            if kind == 'even':
                even_pass(layer, cur, d)
            elif kind == 'odd':
                odd_pass(layer, cur, d)
            else:
                ffn_pass(layer, cur, d, apply_final=(last and final_norm))
            cur = d
        if not final_ops:
            for q in S.dma_slots.values():
                for op in q:
                    if op is not None:
                        final_ops.append(op)
        else:
            for q in S.dma_slots.values():
                for op in q:
                    if op is not None and op not in final_ops:
                        final_ops.append(op)
        S.emit(nc, final_wait_ops=final_ops)
    return nc


def prep_weights(inp):
    f = lambda a: np.ascontiguousarray(np.asarray(a, dtype=np.float32))
    w = {}
    for k in ('norm_mix', 'norm_ffn', 'norm_final', 'ab_w_in', 'ab_w_out', 'lru_wa', 'lru_wx', 'c_w_in', 'c_ln_g',
              'c_ln_b', 'c_w_out', 'ffn_w1', 'ffn_w2'):
        w[k] = f(inp[k])
    w['diff_l'] = f(np.concatenate([inp['diff_lq1'], inp['diff_lk1'], inp['diff_lq2'], inp['diff_lk2']], axis=1))
    pv = []
    for e in range(2):
        cols = []
        cwv = np.asarray(inp['lru_conv_w'][e])
        cols.append(cwv.reshape(4, 4, 128).transpose(2, 0, 1).reshape(128, 16))
        for nm in ('lru_conv_b', 'lru_ba', 'lru_bx', 'lru_lambda'):
            cols.append(np.asarray(inp[nm][e]).reshape(4, 128).T)
        cols.append(np.asarray(inp['diff_subln'][e]).reshape(128, 1))
        pv.append(np.concatenate(cols, axis=1))
    w['pvec'] = f(np.stack(pv))
    w['c_w_sT'] = f(np.transpose(np.asarray(inp['c_w_s']), (0, 3, 1, 2)))
    w['c_b_sT'] = f(np.transpose(np.asarray(inp['c_b_s']), (0, 2, 1)))
    return w


_NC_CACHE = {}


def kernel(**inputs):
    x = np.asarray(inputs['x'], dtype=np.float32)
    B, S_LEN, _ = x.shape
    w = prep_weights(inputs)
    key = (S_LEN,)
    if key not in _NC_CACHE:
        _NC_CACHE[key] = build_program(S_LEN)
    nc = _NC_CACHE[key]
    in_maps = []
    for c in range(B):
        m = dict(w)
        m['x'] = np.ascontiguousarray(x[c])
        in_maps.append(m)
    res = run_bass_kernel_spmd(nc, in_maps, core_ids=list(range(B)))
    return np.stack([np.asarray(r['out'], dtype=np.float32) for r in res.results], axis=0)
```
